# Optimizing a Trainium2 kernel written in Bass

```python
import math
import jax
import jax.numpy as jnp
from jax import lax
import numpy as np

D_MODEL = 4096
BATCH = 1
SEQ = 8192
DEPTH = 1

HEAD_DIM = 128
ATTN_PATTERNS = ((128, 1), (512, 4), (2048, 16))
N_ATTN_GROUPS = 3
ATTN_HEADS_PER_GROUP = 8
ATTN_WIDTH = N_ATTN_GROUPS * ATTN_HEADS_PER_GROUP * HEAD_DIM
ATTN_OUT_WIDTH = ATTN_HEADS_PER_GROUP * HEAD_DIM
DN_HEADS = 16
DN_WIDTH = DN_HEADS * HEAD_DIM
CONV_K = 4
DN_CHUNK = 64
D_FF = 4 * D_MODEL
PLE_DIM = 256
EPS = 1e-6
IN_SPLITS = (ATTN_WIDTH, ATTN_WIDTH, ATTN_WIDTH, 3 * DN_WIDTH, DN_WIDTH, DN_HEADS, DN_HEADS, D_MODEL, D_MODEL)
IN_WIDTH = sum(IN_SPLITS)

kernel_name = "hybrid_dilated_attn_gated_deltanet_block"


def _split_points(sizes):
    pts, acc = [], 0
    for n in sizes[:-1]:
        acc += n
        pts.append(acc)
    return pts


def rmsnorm(x, g):
    xf = x.astype(jnp.float32)
    y = xf * lax.rsqrt(jnp.mean(xf * xf, axis=-1, keepdims=True) + EPS)
    return (y * g.astype(jnp.float32)).astype(x.dtype)


def l2norm(t):
    return t * lax.rsqrt(jnp.sum(t * t, axis=-1, keepdims=True) + EPS)


def dilated_window_attention(q, k, v, window, dilation):
    b, s, g, hd = q.shape
    span = window // dilation
    L = s // dilation
    nb = -(-L // span)
    Lp = nb * span

    def to_sub(t):
        t = t.reshape(b, L, dilation, g, hd).transpose(0, 2, 1, 3, 4)
        t = jnp.pad(t, ((0, 0), (0, 0), (0, Lp - L), (0, 0), (0, 0)))
        return t.reshape(b, dilation, nb, span, g, hd)

    def with_prev(t):
        prev = jnp.pad(t, ((0, 0), (0, 0), (1, 0), (0, 0), (0, 0), (0, 0)))[:, :, :-1]
        return jnp.concatenate([prev, t], axis=3)

    qs = to_sub(q)
    kk = with_prev(to_sub(k))
    vv = with_prev(to_sub(v))
    scores = jnp.einsum('brnqhd,brnkhd->brnhqk', qs, kk).astype(jnp.float32) * (hd ** -0.5)
    qi = jnp.arange(span)[:, None]
    kj = jnp.arange(2 * span)[None, :]
    dist = qi + span - kj
    band = (dist >= 0) & (dist <= span)
    valid = band[None] & ((jnp.arange(nb)[:, None, None] > 0) | (kj[None] >= span))
    scores = jnp.where(valid[None, None, :, None], scores, -jnp.inf)
    m = jnp.max(scores, axis=-1, keepdims=True)
    e = jnp.exp(scores - m)
    l = jnp.sum(e, axis=-1, keepdims=True)
    o = jnp.einsum('brnhqk,brnkhd->brnqhd', (e / l).astype(v.dtype), vv)
    lse = (m + jnp.log(l))[..., 0]
    o = o.reshape(b, dilation, Lp, g, hd)[:, :, :L].transpose(0, 2, 1, 3, 4).reshape(b, s, g, hd)
    lse = lse.transpose(0, 1, 2, 4, 3).reshape(b, dilation, Lp, g)[:, :, :L]
    lse = lse.transpose(0, 2, 1, 3).reshape(b, s, g)
    return o, lse


def causal_conv_silu(x, w):
    kw = w.shape[0]
    s = x.shape[1]
    xp = jnp.pad(x, ((0, 0), (kw - 1, 0), (0, 0)))
    y = sum(xp[:, j:j + s] * w[j] for j in range(kw))
    return jax.nn.silu(y)


def chunked_gated_delta_rule(q, k, v, g, beta):
    b, s, h, dk = q.shape
    dv = v.shape[-1]
    c = DN_CHUNK
    nc = s // c
    q = q * dk ** -0.5

    def chunks(t):
        t = jnp.moveaxis(t, 2, 1)
        return t.reshape((b, h, nc, c) + t.shape[3:])

    qc, kc, vc = chunks(q), chunks(k), chunks(v)
    gc = jnp.cumsum(chunks(g), axis=-1)
    bc = chunks(beta)
    lower = jnp.tril(jnp.ones((c, c), dtype=bool))
    strict = jnp.tril(jnp.ones((c, c), dtype=bool), -1)
    decay = jnp.exp(jnp.where(lower, gc[..., :, None] - gc[..., None, :], -jnp.inf))
    kb = kc * bc[..., None]
    a = jnp.where(strict, jnp.einsum('bhnid,bhnjd->bhnij', kb, kc) * decay, 0.0)
    eye = jnp.eye(c, dtype=a.dtype)
    t_inv = lax.linalg.triangular_solve(eye + a, jnp.broadcast_to(eye, a.shape),
                                        left_side=True, lower=True, unit_diagonal=True)
    u = jnp.einsum('bhnij,bhnjd->bhnid', t_inv, vc * bc[..., None])
    w = jnp.einsum('bhnij,bhnjd->bhnid', t_inv, kb * jnp.exp(gc)[..., None])
    intra = jnp.where(lower, jnp.einsum('bhnid,bhnjd->bhnij', qc, kc) * decay, 0.0)
    g_last = gc[..., -1]
    k_dec = kc * jnp.exp(g_last[..., None] - gc)[..., None]
    q_dec = qc * jnp.exp(gc)[..., None]

    def step(state, xs):
        q_i, k_i, u_i, w_i, a_i, gl_i = xs
        v_new = u_i - jnp.einsum('bhcd,bhde->bhce', w_i, state)
        o_i = jnp.einsum('bhcd,bhde->bhce', q_i, state) + jnp.einsum('bhij,bhje->bhie', a_i, v_new)
        state = state * jnp.exp(gl_i)[..., None, None] + jnp.einsum('bhcd,bhce->bhde', k_i, v_new)
        return state, o_i

    xs = tuple(jnp.moveaxis(t, 2, 0) for t in (q_dec, k_dec, u, w, intra, g_last))
    state0 = jnp.zeros((b, h, dk, dv), jnp.float32)
    _, o = lax.scan(step, state0, xs)
    o = jnp.moveaxis(o, 0, 2).reshape(b, h, s, dv)
    return jnp.moveaxis(o, 1, 2)


def setup_inputs(seed: int = 0) -> dict:
    key = jax.random.key(seed)
    ks = jax.random.split(key, 20)
    f32 = jnp.float32

    def nrm(kk, shape, fan_in):
        return jax.random.normal(kk, shape, f32) * fan_in ** -0.5

    def gain(kk, shape):
        return 1.0 + 0.02 * jax.random.normal(kk, shape, f32)

    dt = jnp.exp(jax.random.uniform(ks[5], (DEPTH, DN_HEADS), f32, math.log(1e-3), math.log(1e-1)))
    return {
        "x": jax.random.normal(ks[0], (BATCH, SEQ, D_MODEL), f32),
        "p": jax.random.normal(ks[1], (DEPTH, BATCH, SEQ, PLE_DIM), f32),
        "w_in": nrm(ks[2], (DEPTH, D_MODEL, IN_WIDTH), D_MODEL),
        "conv_w": nrm(ks[3], (DEPTH, CONV_K, 3 * DN_WIDTH), CONV_K),
        "dn_a_log": jnp.log(jax.random.uniform(ks[4], (DEPTH, DN_HEADS), f32, 1.0, 16.0)),
        "dn_dt_bias": dt + jnp.log(-jnp.expm1(-dt)),
        "dn_norm": gain(ks[6], (DEPTH, HEAD_DIM)),
        "w_attn_up": nrm(ks[7], (DEPTH, ATTN_OUT_WIDTH, D_MODEL), ATTN_OUT_WIDTH),
        "w_dn_up": nrm(ks[8], (DEPTH, DN_WIDTH, D_MODEL), DN_WIDTH),
        "w_out": nrm(ks[9], (DEPTH, D_MODEL, D_MODEL), D_MODEL),
        "w_mlp_up": nrm(ks[10], (DEPTH, D_MODEL, D_FF), D_MODEL),
        "w_mlp_down": nrm(ks[11], (DEPTH, D_FF, D_MODEL), D_FF),
        "w_ple_gate": nrm(ks[12], (DEPTH, D_MODEL, D_MODEL), D_MODEL),
        "w_ple_proj": nrm(ks[13], (DEPTH, PLE_DIM, D_MODEL), PLE_DIM),
        "norm_mix": gain(ks[14], (DEPTH, D_MODEL)),
        "norm_mlp": gain(ks[15], (DEPTH, D_MODEL)),
        "norm_ple": gain(ks[16], (DEPTH, D_MODEL)),
        "ple_post_norm": gain(ks[17], (DEPTH, D_MODEL)),
        "final_norm": gain(ks[18], (D_MODEL,)),
    }


def reference(x, p, w_in, conv_w, dn_a_log, dn_dt_bias, dn_norm, w_attn_up, w_dn_up, w_out,
              w_mlp_up, w_mlp_down, w_ple_gate, w_ple_proj, norm_mix, norm_mlp, norm_ple,
              ple_post_norm, final_norm):
    b, s, _ = x.shape
    f32 = jnp.float32
    for i in range(DEPTH):
        h = rmsnorm(x, norm_mix[i])
        q_a, k_a, v_a, qkv_b, z_b, beta_raw, alpha_raw, gate_a, gate_b = jnp.split(
            h @ w_in[i], _split_points(IN_SPLITS), axis=-1)

        def grp(t):
            return t.reshape(b, s, N_ATTN_GROUPS, ATTN_HEADS_PER_GROUP, HEAD_DIM)
        q_a, k_a, v_a = grp(q_a), grp(k_a), grp(v_a)
        outs, lses = [], []
        for gi, (window, dilation) in enumerate(ATTN_PATTERNS):
            o_g, lse_g = dilated_window_attention(q_a[:, :, gi], k_a[:, :, gi], v_a[:, :, gi], window, dilation)
            outs.append(o_g)
            lses.append(lse_g)
        wts = jax.nn.softmax(jnp.stack(lses, axis=0), axis=0)
        o_a = jnp.sum(wts[..., None] * jnp.stack(outs, axis=0).astype(f32), axis=0)
        o_a = o_a.reshape(b, s, ATTN_OUT_WIDTH).astype(x.dtype)

        qkv_b = causal_conv_silu(qkv_b, conv_w[i]).astype(f32)
        q_b, k_b, v_b = [t.reshape(b, s, DN_HEADS, HEAD_DIM) for t in jnp.split(qkv_b, 3, axis=-1)]
        q_b, k_b = l2norm(q_b), l2norm(k_b)
        beta = jax.nn.sigmoid(beta_raw.astype(f32))
        g = -jnp.exp(dn_a_log[i].astype(f32)) * jax.nn.softplus(alpha_raw.astype(f32) + dn_dt_bias[i].astype(f32))
        o_b = chunked_gated_delta_rule(q_b, k_b, v_b, g, beta)
        o_b = rmsnorm(o_b, dn_norm[i]) * jax.nn.silu(z_b.astype(f32).reshape(b, s, DN_HEADS, HEAD_DIM))
        o_b = o_b.reshape(b, s, DN_WIDTH).astype(x.dtype)

        merged = jax.nn.sigmoid(gate_a) * (o_a @ w_attn_up[i]) + jax.nn.sigmoid(gate_b) * (o_b @ w_dn_up[i])
        x = x + merged @ w_out[i]

        h = rmsnorm(x, norm_mlp[i])
        x = x + jnp.square(jax.nn.relu(h @ w_mlp_up[i])) @ w_mlp_down[i]

        gate = jax.nn.sigmoid(rmsnorm(x, norm_ple[i]) @ w_ple_gate[i])
        x = x + gate * rmsnorm(p[i] @ w_ple_proj[i], ple_post_norm[i])
    return rmsnorm(x, final_norm)
```

```python
import numpy as np
from contextlib import ExitStack
import concourse.bass as bass
import concourse.mybir as mybir
from concourse.bass_utils import run_bass_kernel_spmd

F32 = mybir.dt.float32
BF16 = mybir.dt.bfloat16
AF = mybir.ActivationFunctionType
ALU = mybir.AluOpType
AX = mybir.AxisListType
ENGS = ["pe", "act", "dve", "pool", "sp"]

NCORES = 8
S = 8192
D = 4096
DFF = 16384
TOWN = 1024
HALO = 2048
TA = TOWN + HALO
EPS = 1e-6
QA, KA, VA, QB, KB, VB, ZB, BETA, ALPHA, GA = 0, 3072, 6144, 9216, 11264, 13312, 15360, 17408, 17424, 17440
DUMP = []
STAGES = {"attn", "dn", "tail"}
DN_CHUNKS = None
DN_CUT = 99
DN_SET = 99


class Buf:
    def __init__(self, t, kind):
        self.t = t
        self.kind = kind
        self.lastw = None
        self.readers = []
        self.sem = None

    def __getitem__(self, k):
        return self.t[k]


class Prog:
    def __init__(self, nc):
        self.nc = nc
        self.es = ExitStack()
        self.q = {e: [] for e in ENGS}
        self.cnt = {e: 0 for e in ENGS}
        self.waited = {e: {} for e in ENGS}
        self.psem = {e: self.es.enter_context(nc.semaphore("prog_" + e)) for e in ENGS}
        self.dcnt = {}
        self._uid = 0
        self.allsems = []
        self.sem_pool = []
        self.phase_es = None
        self.phase_bufs = []

    def uid(self, p):
        self._uid += 1
        return f"{p}{self._uid}"

    def sbuf(self, shape, dt, name=None):
        es = self.phase_es if self.phase_es is not None else self.es
        b = Buf(es.enter_context(self.nc.sbuf_tensor(name or self.uid("sb"), list(shape), dt)), "sb")
        if self.phase_es is not None:
            self.phase_bufs.append(b)
        return b

    def begin_phase(self):
        assert self.phase_es is None
        self.phase_es = ExitStack()
        self.phase_bufs = []

    def barrier(self):
        deps = [(self.psem[x], self.cnt[x], x) for x in ENGS if self.cnt[x] > 0]
        deps += [v for x, v in getattr(self, "prev_final", {}).items() if self.cnt[x] == 0]
        deps += [(sem, self.dcnt[id(sem)], "dma") for sem in self.allsems if self.dcnt[id(sem)] > 0]
        for e in ENGS:
            w = self._waits(e, [d for d in deps if not (d[2] == e)] + [d for d in deps if d[2] == e and e != "pe"])
            if w:
                self.q[e].append((w, None, None))

    def end_phase(self):
        self.barrier()
        for b in self.phase_bufs:
            if b.sem is not None:
                self.sem_pool.append(b.sem)
        self.phase_es.close()
        self.phase_es = None
        self.phase_bufs = []

    def psum(self, shape, dt, name=None):
        return Buf(self.es.enter_context(self.nc.psum_tensor(name or self.uid("ps"), list(shape), dt)), "ps")

    def dram(self, shape, dt, name=None):
        return Buf(self.nc.dram_tensor(name or self.uid("dr"), list(shape), dt).ap(), "dr")

    def ext(self, ap):
        return Buf(ap, "dr")

    def _bsem(self, b):
        if b.sem is None:
            if self.sem_pool:
                b.sem = self.sem_pool.pop()
            else:
                b.sem = self.es.enter_context(self.nc.semaphore(self.uid("ds")))
                self.dcnt[id(b.sem)] = 0
                self.allsems.append(b.sem)
        return b.sem

    def _waits(self, eng, deps):
        w = []
        for d in deps:
            if d is None:
                continue
            sem, val, src = d
            if src == "pe" and eng == "pe":
                continue
            key = id(sem)
            if self.waited[eng].get(key, 0) >= val:
                continue
            self.waited[eng][key] = val
            w.append((sem, val))
        return w

    def _deps(self, reads, writes, waw=True):
        deps = []
        for b in reads:
            deps.append(b.lastw)
            if b.kind == "ps":
                deps += b.readers
        for b in writes:
            if waw:
                deps.append(b.lastw)
            deps += b.readers
        return deps

    def _commit(self, tok, reads, writes):
        for b in writes:
            b.lastw = tok
            b.readers = []
        for b in reads:
            b.readers.append(tok)

    EPOCH = 30000

    def op(self, eng, fn, reads=(), writes=()):
        if self.cnt[eng] >= self.EPOCH:
            self.prev_final = getattr(self, "prev_final", {})
            self.prev_final[eng] = (self.psem[eng], self.cnt[eng], eng)
            self.psem[eng] = self.es.enter_context(self.nc.semaphore(self.uid("prog_" + eng)))
            self.cnt[eng] = 0
            self.total = getattr(self, "total", {})
            self.total[eng] = self.total.get(eng, 0) + self.EPOCH
        w = self._waits(eng, self._deps(reads, writes))
        self.cnt[eng] += 1
        tok = (self.psem[eng], self.cnt[eng], eng)
        self.q[eng].append((w, fn, (self.psem[eng], 1)))
        self._commit(tok, reads, writes)
        return tok

    def dma(self, eng, fn, src, dst, waw=True):
        w = self._waits(eng, self._deps([src], [dst], waw=waw))
        sem = self._bsem(dst)
        if self.dcnt[id(sem)] + 16 > self.EPOCH:
            dst.sem = None
            sem = self._bsem(dst)
        self.dcnt[id(sem)] += 16
        tok = (sem, self.dcnt[id(sem)], "dma")
        self.q[eng].append((w, fn, (sem, 16)))
        if waw:
            self._commit(tok, [src], [dst])
        else:
            dst.lastw = tok
            src.readers.append(tok)
        return tok

    def wait_all(self, eng, bufs):
        w = self._waits(eng, [b.lastw for b in bufs])
        if w:
            self.q[eng].append((w, None, None))

    def emit(self):
        nc = self.nc
        print("ops per engine:", {e: len(self.q[e]) for e in ENGS}, "sems:", len(self.allsems))
        with nc.Block() as block:
            def run(e, engine):
                for w, fn, inc in self.q[e]:
                    for sem, val in w:
                        engine.wait_ge(sem, val)
                    if fn is None:
                        continue
                    ins = fn(engine)
                    if inc is not None:
                        ins.then_inc(inc[0], inc[1])

            @block.tensor
            def _(t):
                run("pe", t)

            @block.scalar
            def _(t):
                run("act", t)

            @block.vector
            def _(t):
                run("dve", t)

            @block.gpsimd
            def _(t):
                run("pool", t)

            @block.sync
            def _(t):
                run("sp", t)
        self.es.close()


class Ring:
    def __init__(self, bufs):
        self.bufs = bufs
        self.i = 0

    def next(self):
        b = self.bufs[self.i % len(self.bufs)]
        self.i += 1
        return b


def bc_ap(ap, nparts):
    n = ap.shape[-1]
    return bass.AP(ap.tensor, ap.offset, [[0, nparts], [1, n]])


def attn_tiles():
    tiles, kt = [], {}
    for g, d in enumerate((1, 4, 16)):
        n_own, s0 = TOWN // d, HALO // d
        QW = min(128, n_own)
        for r in range(d):
            for a in range(s0, s0 + n_own, QW):
                for key in ((g, r, a - 128, 128), (g, r, a, QW)):
                    if key not in kt:
                        kt[key] = len(kt)
                tiles.append((g, d, r, a, QW))
    return tiles, kt


class Ctx:
    pass


def build():
    nc = bass.Bass("TRN2", target_bir_lowering=False)
    IN_W = GA + 2 * D
    dt_in = lambda n, s: nc.dram_tensor(n, list(s), F32, kind="ExternalInput").ap()
    I = Ctx()
    I.xr = dt_in("xr", [S, D])
    I.pp = dt_in("pp", [TOWN, 256])
    I.w_in = dt_in("w_in", [D, IN_W])
    I.conv_wT = dt_in("conv_wT", [6144, 4])
    I.dn_a_log = dt_in("dn_a_log", [16])
    I.dn_dt_bias = dt_in("dn_dt_bias", [16])
    I.dn_norm = dt_in("dn_norm", [128])
    I.w_attn_up = dt_in("w_attn_up", [1024, D])
    I.w_dn_up = dt_in("w_dn_up", [2048, D])
    I.w_out = dt_in("w_out", [D, D])
    I.w_mlp_up = dt_in("w_mlp_up", [D, DFF])
    I.w_mlp_down = dt_in("w_mlp_down", [DFF, D])
    I.w_ple_gate = dt_in("w_ple_gate", [D, D])
    I.w_ple_proj = dt_in("w_ple_proj", [256, D])
    I.norm_mix = dt_in("norm_mix", [D])
    I.norm_mlp = dt_in("norm_mlp", [D])
    I.norm_ple = dt_in("norm_ple", [D])
    I.ple_post = dt_in("ple_post_norm", [D])
    I.final_norm = dt_in("final_norm", [D])
    I.ident = dt_in("ident", [128, 128])
    I.amask = dt_in("amask", [128, 256])
    ntile, ktab = attn_tiles()
    I.vcol = dt_in("vcol", [128, len(ktab)])
    I.dnc = dt_in("dnc", [128, 6 * 512])
    out = nc.dram_tensor("out", [TOWN, D], F32, kind="ExternalOutput").ap()

    P = Prog(nc)
    C = Ctx()
    C.I, C.P, C.nc, C.out = I, P, nc, out
    C.ps = Ring([P.psum([128, 512], F32) for _ in range(8)])
    C.ident = P.sbuf([128, 128], F32)
    P.dma("sp", lambda e: e.dma_start(out=C.ident[:], in_=I.ident), P.ext(I.ident), C.ident)
    C.dr = {}

    def dram(name, shape, dt):
        C.dr[name] = P.dram(shape, dt, name="scr_" + name)
        return C.dr[name]
    C.dram = dram

    def gemm_bufs():
        C.wring = Ring([P.sbuf([128, 8, 512], BF16) for _ in range(4)])
        C.xring = Ring([P.sbuf([128, 32, 512], BF16) for _ in range(2)])
        C.st32 = Ring([P.sbuf([128, 512], F32) for _ in range(4)])
        C.st16 = Ring([P.sbuf([128, 512], BF16) for _ in range(4)])
        C.big = [P.sbuf([128, D], F32) for _ in range(5)]
        C.junk = P.sbuf([128, D], BF16)
        C.ssr = Ring([P.sbuf([128, 4], F32) for _ in range(2)])
    C.gemm_bufs = gemm_bufs

    def store(sb, sb_ap, dr, dr_ap, q="sp"):
        P.dma(q, lambda e: e.dma_start(out=dr_ap, in_=sb_ap), sb, dr, waw=False)
    C.store = store

    def gemm(xT, t0, T, W, n0, N, K, form, epi):
        KC = K // 128
        wv = W.rearrange("(kc p) n -> p kc n", p=128)
        xv = xT.t.rearrange("(kc p) t -> p kc t", p=128)
        Wb = P.ext(W)
        nsup = max(1, KC // 32)
        kcs = min(KC, 32)
        for tb in range(T // 512):
            ts = t0 + tb * 512
            xs = None
            for nb in range((N + 511) // 512):
                ncols = min(512, N - nb * 512)
                accs = [C.ps.next() for _ in range(4)]
                nj = 4 if form == "B" else (ncols + 127) // 128
                for sup in range(nsup):
                    if xs is None or nsup > 1:
                        xs = C.xring.next()
                        P.dma("sp", lambda e, xs=xs, sup=sup, ts=ts: e.dma_start(
                            out=xs[:, 0:kcs, :], in_=xv[:, sup * 32:sup * 32 + kcs, ts:ts + 512]), xT, xs)
                    for kg in range((kcs + 7) // 8):
                        nk = min(8, kcs - kg * 8)
                        wp = C.wring.next()
                        k0 = sup * 32 + kg * 8
                        P.dma("pool", lambda e, wp=wp, k0=k0, nk=nk, nb=nb, ncols=ncols: e.dma_start(
                            out=wp[:, 0:nk, 0:ncols], in_=wv[:, k0:k0 + nk, n0 + nb * 512:n0 + nb * 512 + ncols]), Wb, wp)
                        for kc in range(nk):
                            first = (sup == 0 and kg == 0 and kc == 0)
                            last = (sup == nsup - 1 and kg * 8 + kc == kcs - 1)
                            for j in range(nj):
                                if form == "A":
                                    mc = min(128, ncols - j * 128)
                                    P.op("pe", lambda e, a=accs[j], wp=wp, xs=xs, kc=kc, j=j, mc=mc, kk=kg * 8 + kc, f=first, l=last: e.matmul(
                                        a[0:mc, :], wp[:, kc, j * 128:j * 128 + mc], xs[:, kk, :], start=f, stop=l),
                                        reads=[wp, xs], writes=[accs[j]])
                                else:
                                    P.op("pe", lambda e, a=accs[j], wp=wp, xs=xs, kc=kc, j=j, nc_=ncols, kk=kg * 8 + kc, f=first, l=last: e.matmul(
                                        a[:, 0:nc_], xs[:, kk, j * 128:(j + 1) * 128], wp[:, kc, 0:nc_], start=f, stop=l),
                                        reads=[wp, xs], writes=[accs[j]])
                for j in range(nj):
                    epi(tb, nb, j, accs[j], ncols)
    C.gemm = gemm

    def epi_store_T(dst, dt, func=None):
        def epi(tb, nb, j, acc, ncols):
            mc = min(128, ncols - j * 128)
            sb = (C.st16 if dt == BF16 else C.st32).next()
            if func is None:
                P.op("dve", lambda e: e.tensor_copy(sb[0:mc, :], acc[0:mc, :]), reads=[acc], writes=[sb])
            else:
                P.op("act", lambda e: e.activation(out=sb[0:mc, :], in_=acc[0:mc, :], func=func), reads=[acc], writes=[sb])
            r0 = nb * 512 + j * 128
            store(sb, sb[0:mc, :], dst, dst.t[r0:r0 + mc, tb * 512:(tb + 1) * 512])
        return epi
    C.epi_store_T = epi_store_T

    def epi_tm(dst, func, dt=F32):
        def epi(tb, nb, j, acc, ncols):
            t = (C.st16 if dt == BF16 else C.st32).next()
            P.op("act", lambda e: e.activation(out=t[:, 0:ncols], in_=acc[:, 0:ncols], func=func), reads=[acc], writes=[t])
            r0 = tb * 512 + j * 128
            store(t, t[:, 0:ncols], dst, dst.t[r0:r0 + 128, nb * 512:nb * 512 + ncols])
        return epi
    C.epi_tm = epi_tm

    def rstd_op(ss, i, o, n):
        P.op("dve", lambda e: e.tensor_scalar(ss[:, o:o + 1], ss[:, i:i + 1], 1.0 / n, EPS, ALU.mult, ALU.add), reads=[ss], writes=[ss])
        P.op("act", lambda e: e.activation(out=ss[:, o:o + 1], in_=ss[:, o:o + 1], func=AF.Sqrt), reads=[ss], writes=[ss])
        P.op("dve", lambda e: e.reciprocal(ss[:, o:o + 1], ss[:, o:o + 1]), reads=[ss], writes=[ss])
    C.rstd_op = rstd_op

    def norm_T(src, srcT0, T, gain_ap, dstT):
        g = C.big[0]
        P.dma("sp", lambda e: e.dma_start(out=g[:], in_=bc_ap(gain_ap, 128)), P.ext(gain_ap), g)
        xr = Ring(C.big[1:3])
        hr = Ring(C.big[3:5])
        junk = C.junk
        KC = D // 128
        for tb in range(T // 512):
            hT = C.xring.next()
            for tt in range(4):
                r0 = srcT0 + tb * 512 + tt * 128
                xt = xr.next()
                P.dma("sp", lambda e, xt=xt, r0=r0: e.dma_start(out=xt[:], in_=src.t[r0:r0 + 128, :]), src, xt)
                ss = C.ssr.next()
                P.op("act", lambda e, xt=xt, ss=ss: e.activation(out=junk[:], in_=xt[:], func=AF.Square, accum_out=ss[:, 0:1]),
                     reads=[xt], writes=[junk, ss])
                rstd_op(ss, 0, 1, D)
                hb = hr.next()
                P.op("dve", lambda e, hb=hb, xt=xt, ss=ss: e.scalar_tensor_tensor(hb[:], xt[:], ss[:, 1:2], g[:], ALU.mult, ALU.mult),
                     reads=[xt, ss, g], writes=[hb])
                for k4 in range(KC // 4):
                    pt = C.ps.next()
                    for q in range(4):
                        kc = k4 * 4 + q
                        P.op("pe", lambda e, pt=pt, hb=hb, kc=kc, q=q: e.transpose(pt[:, q * 128:(q + 1) * 128], hb[:, kc * 128:(kc + 1) * 128], C.ident[:]),
                             reads=[hb, C.ident], writes=[pt])
                    if k4 % 2:
                        P.op("act", lambda e, pt=pt, hT=hT, k4=k4, tt=tt: e.activation(
                            out=hT[:, k4 * 4:k4 * 4 + 4, tt * 128:(tt + 1) * 128], in_=pt[:].rearrange("p (q t) -> p q t", q=4), func=AF.Copy),
                            reads=[pt], writes=[hT])
                    else:
                        P.op("dve", lambda e, pt=pt, hT=hT, k4=k4, tt=tt: e.tensor_copy(
                            hT[:, k4 * 4:k4 * 4 + 4, tt * 128:(tt + 1) * 128], pt[:].rearrange("p (q t) -> p q t", q=4)),
                            reads=[pt], writes=[hT])
            store(hT, hT[:, 0:KC, :], dstT, dstT.t.rearrange("(kc p) t -> p kc t", p=128)[:, :, tb * 512:(tb + 1) * 512])
    C.norm_T = norm_T
    return C


def attention(C, qaT, kaT, vtok, o_aT):
    P, I = C.P, C.I
    tiles, ktab = attn_tiles()
    P.begin_phase()
    amask = P.sbuf([128, 256], BF16)
    P.dma("pool", lambda e: e.dma_start(out=amask[:], in_=I.amask), P.ext(I.amask), amask)
    vcol = P.sbuf([128, len(ktab)], F32)
    P.dma("sp", lambda e: e.dma_start(out=vcol[:], in_=I.vcol), P.ext(I.vcol), vcol)
    ones = P.sbuf([128, 128], BF16)
    P.op("dve", lambda e: e.memset(ones[:], 1.0), writes=[ones])
    qTr = Ring([P.sbuf([128, TOWN], BF16) for _ in range(2)])
    kTr = Ring([P.sbuf([128, TA], BF16) for _ in range(2)])
    geo = {0: (1, 15, 9), 1: (4, 3, 3), 2: (16, 0, 2)}
    vts = {g: P.sbuf([128, geo[g][0], geo[g][2], 128], BF16) for g in range(3)}
    ULr = Ring([P.sbuf([128, 2, TOWN], F32) for _ in range(2)])
    Er = Ring([P.sbuf([128, 256], BF16) for _ in range(3)])
    Ewr = Ring([P.sbuf([128, 256], BF16) for _ in range(3)])
    ob = Ring([P.sbuf([128, TOWN], BF16) for _ in range(2)])
    rl = P.sbuf([128, TOWN], F32)
    scale = 128.0 ** -0.5
    for hh in range(8):
        UL = ULr.next()
        for g in range(3):
            d, bt0, nbt = geo[g]
            row0 = g * 1024 + hh * 128
            qT, kT, vt = qTr.next(), kTr.next(), vts[g]
            P.dma("sp", lambda e, qT=qT, row0=row0: e.dma_start(out=qT[:], in_=qaT.t[row0:row0 + 128, :]), qaT, qT)
            P.dma("sp", lambda e, kT=kT, row0=row0: e.dma_start(out=kT[:], in_=kaT.t[row0:row0 + 128, :]), kaT, kT)
            for r in range(d):
                if g < 2:
                    src = vtok.t[bass.ds(r + d * 128 * bt0, 128 * nbt, step=d), row0:row0 + 128].rearrange("(bt j) c -> j bt c", j=128)
                    P.dma("sp", lambda e, vt=vt, r=r, src=src: e.dma_start(out=vt[:, r, :, :], in_=src), vtok, vt)
                else:
                    src0 = vtok.t[bass.ds(r, 128, step=d), row0:row0 + 128]
                    src1 = vtok.t[bass.ds(r + d * 128, 64, step=d), row0:row0 + 128]
                    P.dma("sp", lambda e, vt=vt, r=r, src0=src0: e.dma_start(out=vt[:, r, 0, :], in_=src0), vtok, vt)
                    P.dma("sp", lambda e, vt=vt, r=r, src1=src1: e.dma_start(out=vt[0:64, r, 1, :], in_=src1), vtok, vt)
            for (tg, td, r, a, QW) in tiles:
                if tg != g:
                    continue
                k0 = ktab[(g, r, a - 128, 128)]
                k1 = ktab[(g, r, a, QW)]
                kc0 = bass.ds(r + d * (a - 128), 128, step=d)
                kc1 = bass.ds(r + d * a, QW, step=d)
                qc = bass.ds(r + d * a - HALO, QW, step=d)
                b0, b1 = (a - 128) // 128 - bt0, a // 128 - bt0
                ps_s, ps_u = C.ps.next(), C.ps.next()
                P.op("pe", lambda e, ps_s=ps_s, kT=kT, qT=qT, kc0=kc0, qc=qc, QW=QW: e.matmul(ps_s[:, 0:QW], kT[:, kc0], qT[:, qc], start=True, stop=True),
                     reads=[kT, qT], writes=[ps_s])
                P.op("pe", lambda e, ps_s=ps_s, kT=kT, qT=qT, kc1=kc1, qc=qc, QW=QW: e.matmul(ps_s[0:QW, 128:128 + QW], kT[:, kc1], qT[:, qc], start=True, stop=True),
                     reads=[kT, qT], writes=[ps_s])
                Er_, E = Er.next(), Ewr.next()
                P.op("act", lambda e, Er_=Er_, ps_s=ps_s, QW=QW: e.activation(out=Er_[:, 0:QW], in_=ps_s[:, 0:QW], func=AF.Exp, scale=scale), reads=[ps_s], writes=[Er_])
                P.op("act", lambda e, Er_=Er_, ps_s=ps_s, QW=QW: e.activation(out=Er_[0:QW, 128:128 + QW], in_=ps_s[0:QW, 128:128 + QW], func=AF.Exp, scale=scale), reads=[ps_s], writes=[Er_])
                P.op("dve", lambda e, E=E, Er_=Er_, k0=k0, QW=QW: e.scalar_tensor_tensor(E[:, 0:QW], Er_[:, 0:QW], vcol[:, k0:k0 + 1], amask[:, 0:QW], ALU.mult, ALU.mult),
                     reads=[Er_, vcol, amask], writes=[E])
                P.op("dve", lambda e, E=E, Er_=Er_, k1=k1, QW=QW: e.scalar_tensor_tensor(E[0:QW, 128:128 + QW], Er_[0:QW, 128:128 + QW], vcol[0:QW, k1:k1 + 1], amask[0:QW, 128:128 + QW], ALU.mult, ALU.mult),
                     reads=[Er_, vcol, amask], writes=[E])
                P.op("pe", lambda e, ps_u=ps_u, vt=vt, r=r, b0=b0, E=E, QW=QW: e.matmul(ps_u[:, 0:QW], vt[:, r, b0, :], E[:, 0:QW], start=True, stop=False), reads=[vt, E], writes=[ps_u])
                P.op("pe", lambda e, ps_u=ps_u, vt=vt, r=r, b1=b1, E=E, QW=QW: e.matmul(ps_u[:, 0:QW], vt[0:QW, r, b1, :], E[0:QW, 128:128 + QW], start=False, stop=True), reads=[vt, E], writes=[ps_u])
                P.op("pe", lambda e, ps_u=ps_u, E=E, QW=QW: e.matmul(ps_u[:, 128:128 + QW], ones[:, :], E[:, 0:QW], start=True, stop=False), reads=[ones, E], writes=[ps_u])
                P.op("pe", lambda e, ps_u=ps_u, E=E, QW=QW: e.matmul(ps_u[:, 128:128 + QW], ones[0:QW, :], E[0:QW, 128:128 + QW], start=False, stop=True), reads=[ones, E], writes=[ps_u])
                for w in range(2):
                    if g == 0:
                        P.op("dve", lambda e, UL=UL, ps_u=ps_u, w=w, qc=qc, QW=QW: e.tensor_copy(UL[:, w, qc], ps_u[:, w * 128:w * 128 + QW]), reads=[ps_u], writes=[UL])
                    else:
                        P.op("dve", lambda e, UL=UL, ps_u=ps_u, w=w, qc=qc, QW=QW: e.tensor_tensor(UL[:, w, qc], UL[:, w, qc], ps_u[:, w * 128:w * 128 + QW], ALU.add), reads=[ps_u, UL], writes=[UL])
        o = ob.next()
        P.op("dve", lambda e, UL=UL: e.reciprocal(rl[:], UL[:, 1, :]), reads=[UL], writes=[rl])
        P.op("dve", lambda e, UL=UL, o=o: e.tensor_tensor(o[:], UL[:, 0, :], rl[:], ALU.mult), reads=[UL, rl], writes=[o])
        C.store(o, o[:], o_aT, o_aT.t[hh * 128:(hh + 1) * 128, :])
    P.end_phase()


def v3(ap, h):
    return ap.rearrange("p (h j) -> p h j", h=h)


def bcl(ap2, n):
    p, h = ap2.shape
    return ap2.unsqueeze(2).broadcast_to([p, h, n])


def dn_prep(C, kvraw, qraw, khT, vT, qhT):
    P, I = C.P, C.I
    P.begin_phase()
    cw = P.sbuf([128, 48, 4], F32)
    P.dma("sp", lambda e: e.dma_start(out=cw[:], in_=I.conv_wT.rearrange("(c p) j -> p c j", p=128)), P.ext(I.conv_wT), cw)
    ones = P.sbuf([128, 128], F32)
    P.op("dve", lambda e: e.memset(ones[:], 1.0), writes=[ones])
    xr = Ring([P.sbuf([128, 515], F32) for _ in range(3)])
    yr = Ring([P.sbuf([128, 512], F32) for _ in range(3)])
    y2r = Ring([P.sbuf([128, 512], F32) for _ in range(3)])
    sqr = Ring([P.sbuf([128, 512], F32) for _ in range(2)])
    rr = Ring([P.sbuf([128, 512], F32) for _ in range(2)])
    y3r = Ring([P.sbuf([128, 512], F32) for _ in range(3)])
    TQ = qraw.t.shape[1]
    for fc in range(48):
        if fc < 16:
            src, row, nblk, dst, drow = qraw, fc * 128, TQ // 512, qhT, fc * 128
        elif fc < 32:
            src, row, nblk, dst, drow = kvraw, (fc - 16) * 128, S // 512, khT, (fc - 16) * 128
        else:
            src, row, nblk, dst, drow = kvraw, (fc - 16) * 128, S // 512, vT, (fc - 32) * 128
        for blk in range(nblk):
            x = xr.next()
            if blk == 0:
                P.op("dve", lambda e, x=x: e.memset(x[:, 0:3], 0.0), writes=[x])
                P.dma("sp", lambda e, x=x, src=src, row=row: e.dma_start(out=x[:, 3:515], in_=src.t[row:row + 128, 0:512]), src, x)
            else:
                P.dma("sp", lambda e, x=x, src=src, row=row, blk=blk: e.dma_start(out=x[:, 0:515], in_=src.t[row:row + 128, blk * 512 - 3:blk * 512 + 512]), src, x)
            y = yr.next()
            P.op("dve", lambda e, x=x, y=y, fc=fc: e.tensor_scalar_mul(y[:], x[:, 3:515], cw[:, fc, 3:4]), reads=[x, cw], writes=[y])
            for j in (2, 1, 0):
                P.op("dve", lambda e, x=x, y=y, fc=fc, j=j: e.scalar_tensor_tensor(y[:], x[:, j:j + 512], cw[:, fc, j:j + 1], y[:], ALU.mult, ALU.add),
                     reads=[x, cw, y], writes=[y])
            y2 = y2r.next()
            P.op("act", lambda e, y=y, y2=y2: e.activation(out=y2[:], in_=y[:], func=AF.Silu), reads=[y], writes=[y2])
            if fc < 32:
                sq, r, y3, ps = sqr.next(), rr.next(), y3r.next(), C.ps.next()
                P.op("dve", lambda e, sq=sq, y2=y2: e.tensor_tensor(sq[:], y2[:], y2[:], ALU.mult), reads=[y2], writes=[sq])
                P.op("pe", lambda e, ps=ps, sq=sq: e.matmul(ps[:, :], ones[:, :], sq[:, :], start=True, stop=True), reads=[ones, sq], writes=[ps])
                P.op("act", lambda e, r=r, ps=ps: e.activation(out=r[:], in_=ps[:], func=AF.Sqrt, bias=EPS), reads=[ps], writes=[r])
                P.op("dve", lambda e, r=r: e.reciprocal(r[:], r[:]), reads=[r], writes=[r])
                sc = 128.0 ** -0.5 if fc < 16 else 1.0
                P.op("dve", lambda e, y3=y3, y2=y2, r=r, sc=sc: e.scalar_tensor_tensor(y3[:], y2[:], sc, r[:], ALU.mult, ALU.mult), reads=[y2, r], writes=[y3])
                y2 = y3
            C.store(y2, y2[:], dst, dst.t[drow:drow + 128, blk * 512:(blk + 1) * 512])
    P.end_phase()


def dn_scan(C, ba, khT, vT, qhT, ztm, o_bT):
    P, I = C.P, C.I
    NCH = S // 64
    OWN0 = NCH - TOWN // 64
    TQ = qhT.t.shape[1]
    P.begin_phase()
    dnc = P.sbuf([128, 6 * 512], F32)
    P.dma("sp", lambda e: e.dma_start(out=dnc[:], in_=I.dnc), P.ext(I.dnc), dnc)
    Irep, strictrep, strictTrep = v3(dnc[0:64, 0:512], 8), v3(dnc[0:64, 512:1024], 8), v3(dnc[0:64, 1024:1536], 8)
    mneg, mnegT = dnc[0:64, 1536:2048], dnc[0:64, 2048:2560]
    Utri, ones, negones, negI64 = dnc[0:64, 2560:2624], dnc[:, 2624:2752], dnc[:, 2752:2880], dnc[0:64, 2880:2944]
    I64 = C.ident[0:64, 0:64]
    ident = C.ident
    gc_all = P.sbuf([64, NCH, 16], F32)
    beta_all = P.sbuf([64, NCH, 16], F32)
    egc_all = P.sbuf([64, NCH, 16], F32)
    kdecs_all = P.sbuf([64, NCH, 16], F32)
    egl_all = P.sbuf([128, NCH, 16], F32)
    alog = P.sbuf([64, 16], F32)
    dtb = P.sbuf([64, 16], F32)
    P.dma("sp", lambda e: e.dma_start(out=alog[:], in_=bc_ap(I.dn_a_log, 64)), P.ext(I.dn_a_log), alog)
    P.dma("sp", lambda e: e.dma_start(out=dtb[:], in_=bc_ap(I.dn_dt_bias, 64)), P.ext(I.dn_dt_bias), dtb)
    P.op("act", lambda e: e.activation(out=alog[:], in_=alog[:], func=AF.Exp), reads=[alog], writes=[alog])
    P.op("dve", lambda e: e.tensor_scalar_mul(alog[:], alog[:], -1.0), reads=[alog], writes=[alog])
    dnn = P.sbuf([64, 128], F32)
    P.dma("sp", lambda e: e.dma_start(out=dnn[:], in_=bc_ap(I.dn_norm, 64)), P.ext(I.dn_norm), dnn)
    QC = 32
    baq = P.sbuf([64, QC, 32], F32)
    spq = P.sbuf([64, QC, 16], F32)
    gq = P.sbuf([64, QC, 16], F32)
    tq = P.sbuf([64, QC, 16], F32)
    bav = ba.t.rearrange("(c t) n -> t c n", t=64)
    for q in range(NCH // QC if DN_SET >= 1 else 0):
        cs = slice(q * QC, (q + 1) * QC)
        P.dma("sp", lambda e, cs=cs: e.dma_start(out=baq[:], in_=bav[:, cs, :]), ba, baq)
        if DN_SET < 2:
            continue
        P.op("act", lambda e, cs=cs: e.activation(out=beta_all[:, cs, :], in_=baq[:, :, 0:16], func=AF.Sigmoid), reads=[baq], writes=[beta_all])
        if DN_SET < 3:
            continue
        P.op("dve", lambda e: e.tensor_tensor(spq[:], baq[:, :, 16:32], dtb[:].unsqueeze(1).broadcast_to([64, QC, 16]), ALU.add), reads=[baq, dtb], writes=[spq])
        P.op("act", lambda e: e.activation(out=spq[:], in_=spq[:], func=AF.Exp), reads=[spq], writes=[spq])
        P.op("act", lambda e: e.activation(out=spq[:], in_=spq[:], func=AF.Ln, bias=1.0), reads=[spq], writes=[spq])
        P.op("dve", lambda e: e.tensor_tensor(gq[:], spq[:], alog[:].unsqueeze(1).broadcast_to([64, QC, 16]), ALU.mult), reads=[spq, alog], writes=[gq])
        if DN_SET < 4:
            continue
        ps1, ps2 = C.ps.next(), C.ps.next()
        gflat = gq[:].rearrange("p c h -> p (c h)")
        P.op("pe", lambda e, ps1=ps1: e.matmul(ps1[0:64, :], Utri, gflat, start=True, stop=True), reads=[dnc, gq], writes=[ps1])
        P.op("pe", lambda e, ps2=ps2: e.matmul(ps2[:, :], ones[0:64, :], gflat, start=True, stop=True), reads=[dnc, gq], writes=[ps2])
        if DN_SET < 5:
            continue
        P.op("dve", lambda e, ps1=ps1, cs=cs: e.tensor_copy(gc_all[:, cs, :], v3(ps1[0:64, :], QC)), reads=[ps1], writes=[gc_all])
        if DN_SET < 6:
            continue
        P.op("act", lambda e, cs=cs: e.activation(out=egc_all[:, cs, :], in_=gc_all[:, cs, :], func=AF.Exp), reads=[gc_all], writes=[egc_all])
        if DN_SET < 7:
            continue
        P.op("act", lambda e, ps2=ps2, cs=cs: e.activation(out=egl_all[:, cs, :], in_=v3(ps2[:, :], QC), func=AF.Exp), reads=[ps2], writes=[egl_all])
        if DN_SET < 8:
            continue
        P.op("dve", lambda e, ps2=ps2, cs=cs: e.tensor_tensor(tq[:], v3(ps2[0:64, :], QC), gc_all[:, cs, :], ALU.subtract), reads=[ps2, gc_all], writes=[tq])
        if DN_SET < 9:
            continue
        P.op("act", lambda e, cs=cs: e.activation(out=kdecs_all[:, cs, :], in_=tq[:], func=AF.Exp), reads=[tq], writes=[kdecs_all])
    if hasattr(C, "dbg"):
        for nm, b_ in (("gc_all", gc_all), ("beta_all", beta_all), ("egc_all", egc_all), ("kdecs_all", kdecs_all), ("egl_all", egl_all)):
            C.dbg(nm, b_, b_[:].rearrange("p c h -> p (c h)"))
    S4 = [P.sbuf([128, 4, 128], F32) for _ in range(4)]
    for s_ in S4:
        P.op("dve", lambda e, s_=s_: e.memset(s_[:], 0.0), writes=[s_])
    kcr = Ring([P.sbuf([128, 16, 64], F32) for _ in range(2)])
    vcr = Ring([P.sbuf([128, 16, 64], F32) for _ in range(2)])
    qcr = Ring([P.sbuf([128, 16, 64], F32) for _ in range(2)])
    zc_ = P.sbuf([64, 2048], F32)
    obT = P.sbuf([128, 16, 64], BF16)
    nb_ = P.sbuf([64, 16], F32)
    kbgs = P.sbuf([64, 16], F32)
    T = lambda: P.sbuf([64, 8, 64], F32)
    diagG, diagB, diagE, Dm, DmT, DmS, DmTS, t1, t2, intraT = (T() for _ in range(10))
    sets = [(T(), T(), T()) for _ in range(2)]
    qdec = P.sbuf([128, 8, 64], F32)
    wT = P.sbuf([128, 8, 64], F32)
    T4 = lambda: P.sbuf([64, 8, 128], F32)
    kbg, kdec, vb, u, vnew = (T4() for _ in range(5))
    osb, on = P.sbuf([64, 4, 128], F32), P.sbuf([64, 4, 128], F32)
    ss = P.sbuf([64, 8], F32)
    F = lambda b: b[:].rearrange("p h j -> p (h j)")
    chunks = range(NCH) if DN_CHUNKS is None else list(range(DN_CHUNKS)) + ([OWN0] if DN_CUT >= 8 else [])
    for c in (chunks if DN_CUT >= 1 else []):
        own = c >= OWN0 and DN_CUT >= 8
        kc_, vc_ = kcr.next(), vcr.next()
        P.dma("sp", lambda e, kc_=kc_, c=c: e.dma_start(out=kc_[:], in_=khT.t.rearrange("(h d) t -> d h t", d=128)[:, :, c * 64:(c + 1) * 64]), khT, kc_)
        P.dma("sp", lambda e, vc_=vc_, c=c: e.dma_start(out=vc_[:], in_=vT.t.rearrange("(h d) t -> d h t", d=128)[:, :, c * 64:(c + 1) * 64]), vT, vc_)
        if own:
            qc_ = qcr.next()
            q0 = (c - OWN0) * 64 + (TQ - TOWN)
            P.dma("sp", lambda e, qc_=qc_, q0=q0: e.dma_start(out=qc_[:], in_=qhT.t.rearrange("(h d) t -> d h t", d=128)[:, :, q0:q0 + 64]), qhT, qc_)
            P.dma("sp", lambda e, c=c: e.dma_start(out=zc_[:], in_=ztm.t[(c - OWN0) * 64:(c - OWN0 + 1) * 64, :]), ztm, zc_)
        P.op("dve", lambda e, c=c: e.tensor_scalar_mul(nb_[:], beta_all[:, c, :], -1.0), reads=[beta_all], writes=[nb_])
        P.op("dve", lambda e, c=c: e.tensor_tensor(kbgs[:], beta_all[:, c, :], egc_all[:, c, :], ALU.mult), reads=[beta_all, egc_all], writes=[kbgs])
        for hg in range(2):
            h0 = hg * 8
            hs = slice(h0, h0 + 8)
            if DN_CUT < 2:
                continue
            gcs = bcl(gc_all[:, c, hs], 64)
            P.op("dve", lambda e, gcs=gcs: e.tensor_tensor(diagG[:], Irep, gcs, ALU.mult), reads=[dnc, gc_all], writes=[diagG])
            P.op("dve", lambda e, c=c, hs=hs: e.tensor_tensor(diagB[:], Irep, bcl(beta_all[:, c, hs], 64), ALU.mult), reads=[dnc, beta_all], writes=[diagB])
            p1, p3, p4, p5 = C.ps.next(), C.ps.next(), C.ps.next(), C.ps.next()
            for h in range(8):
                P.op("pe", lambda e, p1=p1, kc_=kc_, h=h, h0=h0: e.matmul(p1[0:64, h * 64:(h + 1) * 64], kc_[:, h0 + h, :], kc_[:, h0 + h, :], start=True, stop=True),
                     reads=[kc_], writes=[p1])
            P.op("pe", lambda e, p3=p3: e.matmul(p3[0:64, :], negones[0:64, 0:64], F(diagG), start=True, stop=False), reads=[dnc, diagG], writes=[p3])
            P.op("pe", lambda e, p3=p3, gcs=gcs: e.matmul(p3[0:64, :], I64, gcs, start=False, stop=False), reads=[ident, gc_all], writes=[p3])
            P.op("pe", lambda e, p3=p3: e.matmul(p3[0:64, :], I64, mneg, start=False, stop=True), reads=[ident, dnc], writes=[p3])
            P.op("pe", lambda e, p4=p4: e.matmul(p4[0:64, :], ones[0:64, 0:64], F(diagG), start=True, stop=False), reads=[dnc, diagG], writes=[p4])
            P.op("pe", lambda e, p4=p4, gcs=gcs: e.matmul(p4[0:64, :], negI64, gcs, start=False, stop=False), reads=[dnc, gc_all], writes=[p4])
            P.op("pe", lambda e, p4=p4: e.matmul(p4[0:64, :], I64, mnegT, start=False, stop=True), reads=[ident, dnc], writes=[p4])
            P.op("pe", lambda e, p5=p5: e.matmul(p5[0:64, :], negones[0:64, 0:64], F(diagB), start=True, stop=True), reads=[dnc, diagB], writes=[p5])
            P.op("act", lambda e, p3=p3: e.activation(out=F(Dm), in_=p3[0:64, :], func=AF.Exp), reads=[p3], writes=[Dm])
            P.op("act", lambda e, p4=p4: e.activation(out=F(DmT), in_=p4[0:64, :], func=AF.Exp), reads=[p4], writes=[DmT])
            P.op("dve", lambda e: e.tensor_tensor(DmS[:], Dm[:], strictrep, ALU.mult), reads=[Dm, dnc], writes=[DmS])
            P.op("dve", lambda e: e.tensor_tensor(DmTS[:], DmT[:], strictTrep, ALU.mult), reads=[DmT, dnc], writes=[DmTS])
            if DN_CUT < 3:
                continue
            M, MT, R = sets[0]
            P.op("dve", lambda e, p1=p1: e.tensor_tensor(t1[:], v3(p1[0:64, :], 8), DmS[:], ALU.mult), reads=[p1, DmS], writes=[t1])
            P.op("dve", lambda e, M=M, hs=hs: e.tensor_tensor(M[:], t1[:], bcl(nb_[:, hs], 64), ALU.mult), reads=[t1, nb_], writes=[M])
            P.op("dve", lambda e, p1=p1: e.tensor_tensor(t2[:], v3(p1[0:64, :], 8), DmTS[:], ALU.mult), reads=[p1, DmTS], writes=[t2])
            P.op("dve", lambda e, MT=MT, p5=p5: e.tensor_tensor(MT[:], t2[:], v3(p5[0:64, :], 8), ALU.mult), reads=[t2, p5], writes=[MT])
            P.op("dve", lambda e, R=R, MT=MT: e.tensor_tensor(R[:], MT[:], Irep, ALU.add), reads=[MT, dnc], writes=[R])
            cur = 0
            if DN_CUT < 4:
                continue
            for m in range(1, 6):
                M, MT, R = sets[cur]
                Mn, MTn, Rn = sets[1 - cur]
                pm, pmt, pr = C.ps.next(), C.ps.next(), C.ps.next()
                for h in range(8):
                    P.op("pe", lambda e, pm=pm, M=M, MT=MT, h=h: e.matmul(pm[0:64, h * 64:(h + 1) * 64], MT[:, h, :], M[:, h, :], start=True, stop=True), reads=[M, MT], writes=[pm])
                if m < 5:
                    for h in range(8):
                        P.op("pe", lambda e, pmt=pmt, M=M, MT=MT, h=h: e.matmul(pmt[0:64, h * 64:(h + 1) * 64], M[:, h, :], MT[:, h, :], start=True, stop=True), reads=[M, MT], writes=[pmt])
                P.op("act", lambda e, Mn=Mn, pm=pm: e.activation(out=F(Mn), in_=pm[0:64, :], func=AF.Copy), reads=[pm], writes=[Mn])
                if m < 5:
                    P.op("dve", lambda e, MTn=MTn, pmt=pmt: e.tensor_copy(F(MTn), pmt[0:64, :]), reads=[pmt], writes=[MTn])
                for h in range(8):
                    P.op("pe", lambda e, pr=pr, Mn=Mn, R=R, h=h: e.matmul(pr[0:64, h * 64:(h + 1) * 64], Mn[:, h, :], R[:, h, :], start=True, stop=True), reads=[Mn, R], writes=[pr])
                P.op("dve", lambda e, Rn=Rn, R=R, pr=pr: e.tensor_tensor(F(Rn), F(R), pr[0:64, :], ALU.add), reads=[R, pr], writes=[Rn])
                cur = 1 - cur
            R = sets[cur][2]
            if DN_CUT < 5:
                continue
            for half in range(2):
                pk, pv = C.ps.next(), C.ps.next()
                for hq in range(4):
                    h = h0 + half * 4 + hq
                    P.op("pe", lambda e, pk=pk, kc_=kc_, h=h, hq=hq: e.transpose(pk[0:64, hq * 128:(hq + 1) * 128], kc_[:, h, :], ident[:]), reads=[kc_, ident], writes=[pk])
                    P.op("pe", lambda e, pv=pv, vc_=vc_, h=h, hq=hq: e.transpose(pv[0:64, hq * 128:(hq + 1) * 128], vc_[:, h, :], ident[:]), reads=[vc_, ident], writes=[pv])
                a4 = slice(half * 4, half * 4 + 4)
                g4 = slice(h0 + half * 4, h0 + half * 4 + 4)
                P.op("dve", lambda e, pk=pk, a4=a4, g4=g4: e.tensor_tensor(kbg[:, a4, :], v3(pk[0:64, :], 4), bcl(kbgs[:, g4], 128), ALU.mult), reads=[pk, kbgs], writes=[kbg])
                P.op("dve", lambda e, pk=pk, a4=a4, g4=g4, c=c: e.tensor_tensor(kdec[:, a4, :], v3(pk[0:64, :], 4), bcl(kdecs_all[:, c, g4], 128), ALU.mult), reads=[pk, kdecs_all], writes=[kdec])
                P.op("dve", lambda e, pv=pv, a4=a4, g4=g4, c=c: e.tensor_tensor(vb[:, a4, :], v3(pv[0:64, :], 4), bcl(beta_all[:, c, g4], 128), ALU.mult), reads=[pv, beta_all], writes=[vb])
            if DN_CUT < 6:
                continue
            pw = C.ps.next()
            for h in range(8):
                P.op("pe", lambda e, pw=pw, R=R, h=h: e.matmul(pw[:, h * 64:(h + 1) * 64], kbg[:, h, :], R[:, h, :], start=True, stop=True), reads=[kbg, R], writes=[pw])
            P.op("dve", lambda e, pw=pw: e.tensor_copy(F(wT), pw[:, :]), reads=[pw], writes=[wT])
            for half in range(2):
                pu = C.ps.next()
                for hq in range(4):
                    h = half * 4 + hq
                    P.op("pe", lambda e, pu=pu, R=R, h=h, hq=hq: e.matmul(pu[0:64, hq * 128:(hq + 1) * 128], R[:, h, :], vb[:, h, :], start=True, stop=True), reads=[R, vb], writes=[pu])
                P.op("act", lambda e, pu=pu, half=half: e.activation(out=u[:, half * 4:half * 4 + 4, :], in_=v3(pu[0:64, :], 4), func=AF.Copy), reads=[pu], writes=[u])
            if own:
                p2, p6 = C.ps.next(), C.ps.next()
                for h in range(8):
                    P.op("pe", lambda e, p2=p2, kc_=kc_, qc_=qc_, h=h, h0=h0: e.matmul(p2[0:64, h * 64:(h + 1) * 64], kc_[:, h0 + h, :], qc_[:, h0 + h, :], start=True, stop=True),
                         reads=[kc_, qc_], writes=[p2])
                P.op("dve", lambda e, p2=p2: e.tensor_tensor(intraT[:], v3(p2[0:64, :], 8), DmT[:], ALU.mult), reads=[p2, DmT], writes=[intraT])
                P.op("dve", lambda e, c=c, hs=hs: e.tensor_tensor(diagE[:], Irep, bcl(egc_all[:, c, hs], 64), ALU.mult), reads=[dnc, egc_all], writes=[diagE])
                P.op("pe", lambda e, p6=p6: e.matmul(p6[:, :], ones[0:64, :], F(diagE), start=True, stop=True), reads=[dnc, diagE], writes=[p6])
                P.op("dve", lambda e, p6=p6, qc_=qc_, hs=hs: e.tensor_tensor(qdec[:], qc_[:, hs, :], v3(p6[:, :], 8), ALU.mult), reads=[qc_, p6], writes=[qdec])
            if DN_CUT < 7:
                continue
            for half in range(2):
                Sb = S4[hg * 2 + half]
                a4 = slice(half * 4, half * 4 + 4)
                pws, psu = C.ps.next(), C.ps.next()
                for hq in range(4):
                    h = half * 4 + hq
                    P.op("pe", lambda e, pws=pws, Sb=Sb, h=h, hq=hq: e.matmul(pws[0:64, hq * 128:(hq + 1) * 128], wT[:, h, :], Sb[:, hq, :], start=True, stop=True), reads=[wT, Sb], writes=[pws])
                P.op("dve", lambda e, pws=pws, a4=a4: e.tensor_tensor(vnew[:, a4, :], u[:, a4, :], v3(pws[0:64, :], 4), ALU.subtract), reads=[u, pws], writes=[vnew])
                if own:
                    po = C.ps.next()
                    for hq in range(4):
                        h = half * 4 + hq
                        P.op("pe", lambda e, po=po, Sb=Sb, h=h, hq=hq: e.matmul(po[0:64, hq * 128:(hq + 1) * 128], qdec[:, h, :], Sb[:, hq, :], start=True, stop=False), reads=[qdec, Sb], writes=[po])
                        P.op("pe", lambda e, po=po, h=h, hq=hq: e.matmul(po[0:64, hq * 128:(hq + 1) * 128], intraT[:, h, :], vnew[:, h, :], start=False, stop=True), reads=[intraT, vnew], writes=[po])
                for hq in range(4):
                    h = half * 4 + hq
                    P.op("pe", lambda e, psu=psu, h=h, hq=hq: e.matmul(psu[:, hq * 128:(hq + 1) * 128], kdec[:, h, :], vnew[:, h, :], start=True, stop=True), reads=[kdec, vnew], writes=[psu])
                g4 = slice(h0 + half * 4, h0 + half * 4 + 4)
                P.op("dve", lambda e, Sb=Sb, c=c, g4=g4: e.tensor_tensor(Sb[:], Sb[:], bcl(egl_all[:, c, g4], 128), ALU.mult), reads=[Sb, egl_all], writes=[Sb])
                P.op("dve", lambda e, Sb=Sb, psu=psu: e.tensor_tensor(Sb[:], Sb[:], v3(psu[:, :], 4), ALU.add), reads=[Sb, psu], writes=[Sb])
                if own:
                    P.op("act", lambda e, po=po: e.activation(out=osb[:], in_=v3(po[0:64, :], 4), func=AF.Copy), reads=[po], writes=[osb])
                    P.op("dve", lambda e: e.tensor_tensor(on[:], osb[:], osb[:], ALU.mult), reads=[osb], writes=[on])
                    P.op("dve", lambda e: e.tensor_reduce(ss[:, 0:4], on[:], AX.X, ALU.add), reads=[on], writes=[ss])
                    P.op("dve", lambda e: e.tensor_scalar(ss[:, 4:8], ss[:, 0:4], 1.0 / 128, EPS, ALU.mult, ALU.add), reads=[ss], writes=[ss])
                    P.op("act", lambda e: e.activation(out=ss[:, 4:8], in_=ss[:, 4:8], func=AF.Sqrt), reads=[ss], writes=[ss])
                    P.op("dve", lambda e: e.reciprocal(ss[:, 4:8], ss[:, 4:8]), reads=[ss], writes=[ss])
                    P.op("dve", lambda e: e.tensor_tensor(on[:], osb[:], bcl(ss[:, 4:8], 128), ALU.mult), reads=[osb, ss], writes=[on])
                    P.op("dve", lambda e: e.tensor_tensor(on[:], on[:], dnn[:].unsqueeze(1).broadcast_to([64, 4, 128]), ALU.mult), reads=[on, dnn], writes=[on])
                    z0 = (h0 + half * 4) * 128
                    P.op("dve", lambda e, z0=z0: e.tensor_tensor(on[:], on[:], v3(zc_[:, z0:z0 + 512], 4), ALU.mult), reads=[on, zc_], writes=[on])
                    pt = C.ps.next()
                    for hq in range(4):
                        P.op("pe", lambda e, pt=pt, hq=hq: e.transpose(pt[:, hq * 64:(hq + 1) * 64], on[:, hq, :], I64), reads=[on, ident], writes=[pt])
                    P.op("act", lambda e, pt=pt, g4=g4: e.activation(out=obT[:, g4, :], in_=v3(pt[:, 0:256], 4), func=AF.Copy), reads=[pt], writes=[obT])
        if own:
            C.store(obT, obT[:], o_bT, o_bT.t.rearrange("(h d) t -> d h t", d=128)[:, :, (c - OWN0) * 64:(c - OWN0 + 1) * 64])
    if hasattr(C, "dbg"):
        for nm, b_ in (("Dm", Dm), ("DmT", DmT), ("R0", sets[0][2]), ("R1", sets[1][2]), ("M0", sets[0][0]), ("intraT", intraT)):
            C.dbg(nm, b_, b_[:].rearrange("p h j -> p (h j)"))
        for nm, b_ in (("u", u), ("vnew", vnew), ("kbg", kbg), ("kdec", kdec), ("vb", vb)):
            C.dbg(nm, b_, b_[:].rearrange("p h j -> p (h j)"))
        C.dbg("wT", wT, wT[:].rearrange("p h j -> p (h j)"))
        for i_, s_ in enumerate(S4):
            C.dbg("S%d" % i_, s_, s_[:].rearrange("p h j -> p (h j)"))
    P.end_phase()


def build_all():
    C = build()
    P, I, nc, out = C.P, C.I, C.nc, C.out
    gemm, store, norm_T, dram = C.gemm, C.store, C.norm_T, C.dram
    xrB = P.ext(I.xr)
    T0 = S - TOWN
    TQ = TOWN + 512

    P.begin_phase()
    C.gemm_bufs()
    hT = dram("hT", [D, S], BF16)
    norm_T(xrB, 0, S, I.norm_mix, hT)
    gT = dram("gT", [2 * D, TOWN], BF16)
    gemm(hT, T0, TOWN, I.w_in, GA, 2 * D, D, "A", C.epi_store_T(gT, BF16, AF.Sigmoid))
    o_aT = dram("o_aT", [1024, TOWN], BF16)
    o_bT = dram("o_bT", [2048, TOWN], BF16)
    if "attn" in STAGES:
        qaT = dram("qaT", [3072, TOWN], BF16)
        kaT = dram("kaT", [3072, TA], BF16)
        vtok = dram("vtok", [TA, 3072], BF16)
        gemm(hT, T0, TOWN, I.w_in, QA, 3072, D, "A", C.epi_store_T(qaT, BF16))
        gemm(hT, S - TA, TA, I.w_in, KA, 3072, D, "A", C.epi_store_T(kaT, BF16))
        gemm(hT, S - TA, TA, I.w_in, VA, 3072, D, "B", C.epi_tm(vtok, AF.Copy, BF16))
    if "dn" in STAGES:
        kvraw = dram("kvraw", [4096, S], F32)
        qraw = dram("qraw", [2048, TQ], F32)
        ztm = dram("ztm", [TOWN, 2048], F32)
        ba = dram("ba", [S, 32], F32)
        gemm(hT, 0, S, I.w_in, KB, 4096, D, "A", C.epi_store_T(kvraw, F32))
        gemm(hT, S - TQ, TQ, I.w_in, QB, 2048, D, "A", C.epi_store_T(qraw, F32))
        gemm(hT, T0, TOWN, I.w_in, ZB, 2048, D, "B", C.epi_tm(ztm, AF.Silu))
        gemm(hT, 0, S, I.w_in, BETA, 32, D, "B", C.epi_tm(ba, AF.Copy))
    P.end_phase()

    if "attn" in STAGES:
        attention(C, qaT, kaT, vtok, o_aT)
    if "dn" in STAGES:
        khT = dram("khT", [2048, S], F32)
        vT = dram("vT", [2048, S], F32)
        qhT = dram("qhT", [2048, TQ], F32)
        dn_prep(C, kvraw, qraw, khT, vT, qhT)
        if "noscan" not in STAGES:
            dn_scan(C, ba, khT, vT, qhT, ztm, o_bT)

    P.begin_phase()
    C.gemm_bufs()
    e32, e16 = C.st32, C.st16
    AT = dram("AT", [D, TOWN], F32)
    BT = dram("BT", [D, TOWN], F32)
    gemm(o_aT, 0, TOWN, I.w_attn_up, 0, D, 1024, "A", C.epi_store_T(AT, F32))
    gemm(o_bT, 0, TOWN, I.w_dn_up, 0, D, 2048, "A", C.epi_store_T(BT, F32))
    mT = dram("mT", [D, TOWN], BF16)
    for r in range(D // 128):
        for tb in range(TOWN // 512):
            sl = (slice(r * 128, (r + 1) * 128), slice(tb * 512, (tb + 1) * 512))
            a, b, ga, gb, m = e32.next(), e32.next(), e16.next(), e16.next(), e16.next()
            P.dma("sp", lambda e, a=a, sl=sl: e.dma_start(out=a[:], in_=AT.t[sl]), AT, a)
            P.dma("sp", lambda e, b=b, sl=sl: e.dma_start(out=b[:], in_=BT.t[sl]), BT, b)
            P.dma("sp", lambda e, ga=ga, sl=sl: e.dma_start(out=ga[:], in_=gT.t[sl]), gT, ga)
            P.dma("sp", lambda e, gb=gb, sl=sl, r=r: e.dma_start(out=gb[:], in_=gT.t[D + r * 128:D + (r + 1) * 128, sl[1]]), gT, gb)
            P.op("dve", lambda e, a=a, ga=ga: e.tensor_tensor(a[:], a[:], ga[:], ALU.mult), reads=[a, ga], writes=[a])
            P.op("dve", lambda e, b=b, gb=gb: e.tensor_tensor(b[:], b[:], gb[:], ALU.mult), reads=[b, gb], writes=[b])
            P.op("dve", lambda e, a=a, b=b, m=m: e.tensor_tensor(m[:], a[:], b[:], ALU.add), reads=[a, b], writes=[m])
            store(m, m[:], mT, mT.t[sl])

    def epi_resid(src, src_r0, dst):
        def epi(tb, nb, j, acc, ncols):
            r0 = tb * 512 + j * 128
            xt = e32.next()
            P.dma("sp", lambda e: e.dma_start(out=xt[:, 0:ncols], in_=src.t[src_r0 + r0:src_r0 + r0 + 128, nb * 512:nb * 512 + ncols]), src, xt)
            P.op("dve", lambda e: e.tensor_tensor(xt[:, 0:ncols], xt[:, 0:ncols], acc[:, 0:ncols], ALU.add), reads=[xt, acc], writes=[xt])
            store(xt, xt[:, 0:ncols], dst, dst.t[r0:r0 + 128, nb * 512:nb * 512 + ncols])
        return epi

    x1 = dram("x1", [TOWN, D], F32)
    gemm(mT, 0, TOWN, I.w_out, 0, D, D, "B", epi_resid(xrB, T0, x1))
    h2T = dram("h2T", [D, TOWN], BF16)
    norm_T(x1, 0, TOWN, I.norm_mlp, h2T)
    hidT = dram("hidT", [DFF, TOWN], BF16)

    def epi_relu2(tb, nb, j, acc, ncols):
        t, sb = e32.next(), e16.next()
        P.op("act", lambda e: e.activation(out=t[:], in_=acc[:], func=AF.Relu), reads=[acc], writes=[t])
        P.op("dve", lambda e: e.tensor_tensor(sb[:], t[:], t[:], ALU.mult), reads=[t], writes=[sb])
        r0 = nb * 512 + j * 128
        store(sb, sb[:], hidT, hidT.t[r0:r0 + 128, tb * 512:(tb + 1) * 512])

    gemm(h2T, 0, TOWN, I.w_mlp_up, 0, DFF, D, "A", epi_relu2)
    x2 = dram("x2", [TOWN, D], F32)
    gemm(hidT, 0, TOWN, I.w_mlp_down, 0, D, DFF, "B", epi_resid(x1, 0, x2))
    h3T = dram("h3T", [D, TOWN], BF16)
    norm_T(x2, 0, TOWN, I.norm_ple, h3T)
    gate = dram("gate", [TOWN, D], F32)
    proj = dram("proj", [TOWN, D], F32)
    gemm(h3T, 0, TOWN, I.w_ple_gate, 0, D, D, "B", C.epi_tm(gate, AF.Sigmoid))
    pT = dram("pT", [256, TOWN], BF16)
    ppB = P.ext(I.pp)
    for tb in range(TOWN // 512):
        pTs = e16.next(), e16.next()
        for tt in range(4):
            pt_in = e32.next()
            r0 = tb * 512 + tt * 128
            P.dma("sp", lambda e, pt_in=pt_in, r0=r0: e.dma_start(out=pt_in[:, 0:256], in_=I.pp[r0:r0 + 128, :]), ppB, pt_in)
            ps = C.ps.next()
            for q in range(2):
                P.op("pe", lambda e, ps=ps, pt_in=pt_in, q=q: e.transpose(ps[:, q * 128:(q + 1) * 128], pt_in[:, q * 128:(q + 1) * 128], C.ident[:]),
                     reads=[pt_in, C.ident], writes=[ps])
            for q in range(2):
                P.op("dve", lambda e, ps=ps, q=q, tt=tt, pTs=pTs: e.tensor_copy(pTs[q][:, tt * 128:(tt + 1) * 128], ps[:, q * 128:(q + 1) * 128]),
                     reads=[ps], writes=[pTs[q]])
        for q in range(2):
            store(pTs[q], pTs[q][:], pT, pT.t[q * 128:(q + 1) * 128, tb * 512:(tb + 1) * 512])
    gemm(pT, 0, TOWN, I.w_ple_proj, 0, D, 256, "B", C.epi_tm(proj, AF.Copy))

    gpp, gfn = C.big[0], C.big[1]
    P.dma("sp", lambda e: e.dma_start(out=gpp[:], in_=bc_ap(I.ple_post, 128)), P.ext(I.ple_post), gpp)
    P.dma("sp", lambda e: e.dma_start(out=gfn[:], in_=bc_ap(I.final_norm, 128)), P.ext(I.final_norm), gfn)
    big = Ring(C.big[2:5])
    junk = C.junk
    outB = P.ext(out)
    for tt in range(TOWN // 128):
        rs = slice(tt * 128, (tt + 1) * 128)
        pr, gt, x2t = big.next(), big.next(), big.next()
        P.dma("sp", lambda e, pr=pr, rs=rs: e.dma_start(out=pr[:], in_=proj.t[rs, :]), proj, pr)
        P.dma("sp", lambda e, gt=gt, rs=rs: e.dma_start(out=gt[:], in_=gate.t[rs, :]), gate, gt)
        P.dma("sp", lambda e, x2t=x2t, rs=rs: e.dma_start(out=x2t[:], in_=x2.t[rs, :]), x2, x2t)
        ss = C.ssr.next()
        P.op("act", lambda e, pr=pr, ss=ss: e.activation(out=junk[:], in_=pr[:], func=AF.Square, accum_out=ss[:, 0:1]), reads=[pr], writes=[junk, ss])
        C.rstd_op(ss, 0, 1, D)
        P.op("dve", lambda e, pr=pr, ss=ss: e.scalar_tensor_tensor(pr[:], pr[:], ss[:, 1:2], gpp[:], ALU.mult, ALU.mult), reads=[pr, ss, gpp], writes=[pr])
        P.op("dve", lambda e, pr=pr, gt=gt: e.tensor_tensor(pr[:], pr[:], gt[:], ALU.mult), reads=[pr, gt], writes=[pr])
        P.op("dve", lambda e, pr=pr, x2t=x2t: e.tensor_tensor(x2t[:], x2t[:], pr[:], ALU.add), reads=[pr, x2t], writes=[x2t])
        P.op("act", lambda e, x2t=x2t, ss=ss: e.activation(out=junk[:], in_=x2t[:], func=AF.Square, accum_out=ss[:, 2:3]), reads=[x2t], writes=[junk, ss])
        C.rstd_op(ss, 2, 3, D)
        P.op("dve", lambda e, x2t=x2t, ss=ss, gt=gt: e.scalar_tensor_tensor(gt[:], x2t[:], ss[:, 3:4], gfn[:], ALU.mult, ALU.mult), reads=[x2t, ss, gfn], writes=[gt])
        store(gt, gt[:], outB, out[rs, :])
    fin = [outB]
    for name in DUMP:
        src = C.dr[name]
        o = nc.dram_tensor("dump_" + name, list(src.t.shape), src.t.dtype, kind="ExternalOutput").ap()
        ob_ = P.ext(o)
        P.dma("sp", lambda e, o=o, src=src: e.dma_start(out=o, in_=src.t), src, ob_)
        fin.append(ob_)
    P.wait_all("sp", fin)
    P.end_phase()
    P.emit()
    return nc


def host_consts(c):
    tiles, ktab = attn_tiles()
    m = {"ident": np.eye(128, dtype=np.float32)}
    j = np.arange(128)[:, None]
    i = np.arange(128)[None, :]
    m["amask"] = np.concatenate([(j >= i), (j <= i)], 1).astype(np.float32)
    vcol = np.zeros((128, len(ktab)), np.float32)
    first_valid = TA - (c + 1) * TOWN
    for (g, r, s0, n), col in ktab.items():
        d = (1, 4, 16)[g]
        u = r + d * (s0 + np.arange(n))
        vcol[:n, col] = (u >= first_valid)
    m["vcol"] = vcol
    dnc = np.zeros((128, 6 * 512), np.float32)
    a = np.arange(64)
    I64 = np.eye(64, dtype=np.float32)
    low_incl = (a[:, None] >= a[None, :]).astype(np.float32)
    low_strict = (a[:, None] > a[None, :]).astype(np.float32)
    dnc[:64, 0:512] = np.tile(I64, (1, 8))
    dnc[:64, 512:1024] = np.tile(low_strict, (1, 8))
    dnc[:64, 1024:1536] = np.tile(low_strict.T, (1, 8))
    dnc[:64, 1536:2048] = np.tile((1 - low_incl) * -30000.0, (1, 8))
    dnc[:64, 2048:2560] = np.tile((1 - low_incl.T) * -30000.0, (1, 8))
    dnc[:64, 2560:2624] = (a[:, None] <= a[None, :])
    dnc[:, 2624:2752] = 1.0
    dnc[:, 2752:2880] = -1.0
    dnc[:64, 2880:2944] = -I64
    m["dnc"] = dnc
    return m


def kernel(**inputs):
    f = lambda k: np.ascontiguousarray(np.asarray(inputs[k], np.float32)[0])
    x = f("x")
    p = np.asarray(inputs["p"], np.float32)[0, 0]
    shared = {k: f(k) for k in ("w_in", "w_attn_up", "w_dn_up", "w_out", "w_mlp_up", "w_mlp_down", "w_ple_gate",
                                "w_ple_proj", "norm_mix", "norm_mlp", "norm_ple", "ple_post_norm", "dn_a_log", "dn_dt_bias", "dn_norm")}
    shared["final_norm"] = np.ascontiguousarray(np.asarray(inputs["final_norm"], np.float32))
    shared["conv_wT"] = np.ascontiguousarray(f("conv_w").T)
    in_maps = []
    for c in range(NCORES):
        m = dict(shared)
        m.update(host_consts(c))
        xr = np.zeros((S, D), np.float32)
        xr[S - (c + 1) * TOWN:] = x[:(c + 1) * TOWN]
        m["xr"] = xr
        m["pp"] = np.ascontiguousarray(p[c * TOWN:(c + 1) * TOWN])
        in_maps.append(m)
    nc = build_all()
    res = run_bass_kernel_spmd(nc, in_maps, core_ids=list(range(NCORES)))
    kernel.last = res
    return np.concatenate([r["out"] for r in res.results], 0)[None].astype(np.float32)
```

```python
import numpy as np
from contextlib import ExitStack
import concourse.bass as bass
import concourse.mybir as mybir
from concourse.bass_utils import run_bass_kernel_spmd

F32 = mybir.dt.float32
BF16 = mybir.dt.bfloat16
AF = mybir.ActivationFunctionType
ALU = mybir.AluOpType
AX = mybir.AxisListType
ENGS = ["pe", "act", "dve", "pool", "sp"]

NCORES = 8
S = 8192
D = 4096
DFF = 16384
TOWN = 1024
HALO = 2048
TA = TOWN + HALO
EPS = 1e-6
QA, KA, VA, QB, KB, VB, ZB, BETA, ALPHA, GA = 0, 3072, 6144, 9216, 11264, 13312, 15360, 17408, 17424, 17440
DUMP = []
STAGES = {"attn", "dn", "tail"}
DN_CHUNKS = None
DN_CUT = 99
DN_SET = 99
TRACE = False


class Buf:
    def __init__(self, t, kind):
        self.t = t
        self.kind = kind
        self.lastw = None
        self.readers = []
        self.sem = None

    def __getitem__(self, k):
        return self.t[k]


class Prog:
    def __init__(self, nc):
        self.nc = nc
        self.es = ExitStack()
        self.q = {e: [] for e in ENGS}
        self.cnt = {e: 0 for e in ENGS}
        self.waited = {e: {} for e in ENGS}
        self.psem = {e: self.es.enter_context(nc.semaphore("prog_" + e)) for e in ENGS}
        self.dcnt = {}
        self._uid = 0
        self.allsems = []
        self.sem_pool = []
        self.phase_es = None
        self.phase_bufs = []

    def uid(self, p):
        self._uid += 1
        return f"{p}{self._uid}"

    def sbuf(self, shape, dt, name=None):
        es = self.phase_es if self.phase_es is not None else self.es
        b = Buf(es.enter_context(self.nc.sbuf_tensor(name or self.uid("sb"), list(shape), dt)), "sb")
        if self.phase_es is not None:
            self.phase_bufs.append(b)
        return b

    def begin_phase(self):
        assert self.phase_es is None
        self.phase_es = ExitStack()
        self.phase_bufs = []

    def barrier(self):
        deps = [(self.psem[x], self.cnt[x], x) for x in ENGS if self.cnt[x] > 0]
        deps += [v for x, v in getattr(self, "prev_final", {}).items() if self.cnt[x] == 0]
        deps += [(sem, self.dcnt[id(sem)], "dma") for sem in self.allsems if self.dcnt[id(sem)] > 0]
        for e in ENGS:
            w = self._waits(e, [d for d in deps if not (d[2] == e)] + [d for d in deps if d[2] == e and e != "pe"])
            if w:
                self.q[e].append((w, None, None))

    def end_phase(self):
        self.barrier()
        for b in self.phase_bufs:
            if b.sem is not None:
                self.sem_pool.append(b.sem)
        self.phase_es.close()
        self.phase_es = None
        self.phase_bufs = []

    def psum(self, shape, dt, name=None):
        return Buf(self.es.enter_context(self.nc.psum_tensor(name or self.uid("ps"), list(shape), dt)), "ps")

    def dram(self, shape, dt, name=None):
        return Buf(self.nc.dram_tensor(name or self.uid("dr"), list(shape), dt).ap(), "dr")

    def ext(self, ap):
        return Buf(ap, "dr")

    def _bsem(self, b):
        if b.sem is None:
            if self.sem_pool:
                b.sem = self.sem_pool.pop()
            else:
                b.sem = self.es.enter_context(self.nc.semaphore(self.uid("ds")))
                self.dcnt[id(b.sem)] = 0
                self.allsems.append(b.sem)
        return b.sem

    def _waits(self, eng, deps):
        w = []
        for d in deps:
            if d is None:
                continue
            sem, val, src = d
            if src == "pe" and eng == "pe":
                continue
            key = id(sem)
            if self.waited[eng].get(key, 0) >= val:
                continue
            self.waited[eng][key] = val
            w.append((sem, val))
        return w

    def _deps(self, reads, writes, waw=True):
        deps = []
        for b in reads:
            deps.append(b.lastw)
            if b.kind == "ps":
                deps += b.readers
        for b in writes:
            if waw:
                deps.append(b.lastw)
            deps += b.readers
        return deps

    def _commit(self, tok, reads, writes):
        for b in writes:
            b.lastw = tok
            b.readers = []
        for b in reads:
            b.readers.append(tok)

    EPOCH = 30000

    def op(self, eng, fn, reads=(), writes=(), signal=True):
        self.pend = getattr(self, "pend", {})
        if self.cnt[eng] >= self.EPOCH and not self.pend.get(eng):
            self.prev_final = getattr(self, "prev_final", {})
            self.prev_final[eng] = (self.psem[eng], self.cnt[eng], eng)
            self.psem[eng] = self.es.enter_context(self.nc.semaphore(self.uid("prog_" + eng)))
            self.cnt[eng] = 0
            self.total = getattr(self, "total", {})
            self.total[eng] = self.total.get(eng, 0) + self.EPOCH
        w = self._waits(eng, self._deps(reads, writes))
        if not signal:
            assert eng == "pe"
            self.pend[eng] = True
            tok = (self.psem[eng], self.cnt[eng] + 1, eng)
            self.q[eng].append((w, fn, None))
            self._commit(tok, reads, writes)
            return tok
        self.pend[eng] = False
        self.cnt[eng] += 1
        tok = (self.psem[eng], self.cnt[eng], eng)
        self.q[eng].append((w, fn, (self.psem[eng], 1)))
        self._commit(tok, reads, writes)
        return tok

    def dma(self, eng, fn, src, dst, waw=True):
        w = self._waits(eng, self._deps([src], [dst], waw=waw))
        sem = self._bsem(dst)
        if self.dcnt[id(sem)] + 16 > self.EPOCH:
            dst.sem = None
            sem = self._bsem(dst)
        self.dcnt[id(sem)] += 16
        tok = (sem, self.dcnt[id(sem)], "dma")
        self.q[eng].append((w, fn, (sem, 16)))
        if waw:
            self._commit(tok, [src], [dst])
        else:
            dst.lastw = tok
            src.readers.append(tok)
        return tok

    def wait_all(self, eng, bufs):
        w = self._waits(eng, [b.lastw for b in bufs])
        if w:
            self.q[eng].append((w, None, None))

    def emit(self):
        nc = self.nc
        print("ops per engine:", {e: len(self.q[e]) for e in ENGS}, "sems:", len(self.allsems))
        with nc.Block() as block:
            def run(e, engine):
                for w, fn, inc in self.q[e]:
                    for sem, val in w:
                        engine.wait_ge(sem, val)
                    if fn is None:
                        continue
                    ins = fn(engine)
                    if inc is not None:
                        ins.then_inc(inc[0], inc[1])

            @block.tensor
            def _(t):
                run("pe", t)

            @block.scalar
            def _(t):
                run("act", t)

            @block.vector
            def _(t):
                run("dve", t)

            @block.gpsimd
            def _(t):
                run("pool", t)

            @block.sync
            def _(t):
                run("sp", t)
        self.es.close()


class Ring:
    def __init__(self, bufs):
        self.bufs = bufs
        self.i = 0

    def next(self):
        b = self.bufs[self.i % len(self.bufs)]
        self.i += 1
        return b


def bc_ap(ap, nparts):
    n = ap.shape[-1]
    return bass.AP(ap.tensor, ap.offset, [[0, nparts], [1, n]])


def attn_tiles():
    tiles, kt = [], {}
    for g, d in enumerate((1, 4, 16)):
        n_own, s0 = TOWN // d, HALO // d
        QW = min(128, n_own)
        for r in range(d):
            for a in range(s0, s0 + n_own, QW):
                for key in ((g, r, a - 128, 128), (g, r, a, QW)):
                    if key not in kt:
                        kt[key] = len(kt)
                tiles.append((g, d, r, a, QW))
    return tiles, kt


class Ctx:
    pass


def build():
    nc = bass.Bass("TRN2", target_bir_lowering=False)
    IN_W = GA + 2 * D
    dt_in = lambda n, s: nc.dram_tensor(n, list(s), F32, kind="ExternalInput").ap()
    I = Ctx()
    I.xr = dt_in("xr", [S, D])
    I.pp = dt_in("pp", [TOWN, 256])
    I.w_in = dt_in("w_in", [D, IN_W])
    I.conv_wT = dt_in("conv_wT", [6144, 4])
    I.dn_a_log = dt_in("dn_a_log", [16])
    I.dn_dt_bias = dt_in("dn_dt_bias", [16])
    I.dn_norm = dt_in("dn_norm", [128])
    I.w_attn_up = dt_in("w_attn_up", [1024, D])
    I.w_dn_up = dt_in("w_dn_up", [2048, D])
    I.w_out = dt_in("w_out", [D, D])
    I.w_mlp_up = dt_in("w_mlp_up", [D, DFF])
    I.w_mlp_down = dt_in("w_mlp_down", [DFF, D])
    I.w_ple_gate = dt_in("w_ple_gate", [D, D])
    I.w_ple_proj = dt_in("w_ple_proj", [256, D])
    I.norm_mix = dt_in("norm_mix", [D])
    I.norm_mlp = dt_in("norm_mlp", [D])
    I.norm_ple = dt_in("norm_ple", [D])
    I.ple_post = dt_in("ple_post_norm", [D])
    I.final_norm = dt_in("final_norm", [D])
    I.ident = dt_in("ident", [128, 128])
    I.amask = dt_in("amask", [128, 256])
    ntile, ktab = attn_tiles()
    I.vcol = dt_in("vcol", [128, len(ktab)])
    I.dnc = dt_in("dnc", [128, 6 * 512])
    out = nc.dram_tensor("out", [TOWN, D], F32, kind="ExternalOutput").ap()

    P = Prog(nc)
    C = Ctx()
    C.I, C.P, C.nc, C.out = I, P, nc, out
    C.ps = Ring([P.psum([128, 512], F32) for _ in range(8)])
    C.ident = P.sbuf([128, 128], F32)
    P.dma("sp", lambda e: e.dma_start(out=C.ident[:], in_=I.ident), P.ext(I.ident), C.ident)
    C.dr = {}

    def dram(name, shape, dt):
        C.dr[name] = P.dram(shape, dt, name="scr_" + name)
        return C.dr[name]
    C.dram = dram

    def gemm_bufs():
        C.wring = Ring([P.sbuf([128, 8, 512], BF16) for _ in range(4)])
        C.xring = Ring([P.sbuf([128, 32, 512], BF16) for _ in range(2)])
        C.st32 = Ring([P.sbuf([128, 512], F32) for _ in range(4)])
        C.st16 = Ring([P.sbuf([128, 512], BF16) for _ in range(4)])
        C.big = [P.sbuf([128, D], F32) for _ in range(5)]
        C.junk = P.sbuf([128, D], BF16)
        C.ssr = Ring([P.sbuf([128, 4], F32) for _ in range(2)])
    C.gemm_bufs = gemm_bufs

    def store(sb, sb_ap, dr, dr_ap, q="sp"):
        P.dma(q, lambda e: e.dma_start(out=dr_ap, in_=sb_ap), sb, dr, waw=False)
    C.store = store

    def gemm(xT, t0, T, W, n0, N, K, form, epi):
        KC = K // 128
        wv = W.rearrange("(kc p) n -> p kc n", p=128)
        xv = xT.t.rearrange("(kc p) t -> p kc t", p=128)
        Wb = P.ext(W)
        nsup = max(1, KC // 32)
        kcs = min(KC, 32)
        for tb in range(T // 512):
            ts = t0 + tb * 512
            xs = None
            for nb in range((N + 511) // 512):
                ncols = min(512, N - nb * 512)
                accs = [C.ps.next() for _ in range(4)]
                nj = 4 if form == "B" else (ncols + 127) // 128
                for sup in range(nsup):
                    if xs is None or nsup > 1:
                        xs = C.xring.next()
                        P.dma("sp", lambda e, xs=xs, sup=sup, ts=ts: e.dma_start(
                            out=xs[:, 0:kcs, :], in_=xv[:, sup * 32:sup * 32 + kcs, ts:ts + 512]), xT, xs)
                    for kg in range((kcs + 7) // 8):
                        nk = min(8, kcs - kg * 8)
                        wp = C.wring.next()
                        k0 = sup * 32 + kg * 8
                        P.dma("pool", lambda e, wp=wp, k0=k0, nk=nk, nb=nb, ncols=ncols: e.dma_start(
                            out=wp[:, 0:nk, 0:ncols], in_=wv[:, k0:k0 + nk, n0 + nb * 512:n0 + nb * 512 + ncols]), Wb, wp)
                        for kc in range(nk):
                            first = (sup == 0 and kg == 0 and kc == 0)
                            last = (sup == nsup - 1 and kg * 8 + kc == kcs - 1)
                            for j in range(nj):
                                sig = last or (kc == nk - 1 and j == nj - 1)
                                if form == "A":
                                    mc = min(128, ncols - j * 128)
                                    P.op("pe", lambda e, a=accs[j], wp=wp, xs=xs, kc=kc, j=j, mc=mc, kk=kg * 8 + kc, f=first, l=last: e.matmul(
                                        a[0:mc, :], wp[:, kc, j * 128:j * 128 + mc], xs[:, kk, :], start=f, stop=l),
                                        reads=[wp, xs], writes=[accs[j]], signal=sig)
                                else:
                                    P.op("pe", lambda e, a=accs[j], wp=wp, xs=xs, kc=kc, j=j, nc_=ncols, kk=kg * 8 + kc, f=first, l=last: e.matmul(
                                        a[:, 0:nc_], xs[:, kk, j * 128:(j + 1) * 128], wp[:, kc, 0:nc_], start=f, stop=l),
                                        reads=[wp, xs], writes=[accs[j]], signal=sig)
                for j in range(nj):
                    epi(tb, nb, j, accs[j], ncols)
    C.gemm = gemm

    def epi_store_T(dst, dt, func=None):
        def epi(tb, nb, j, acc, ncols):
            mc = min(128, ncols - j * 128)
            sb = (C.st16 if dt == BF16 else C.st32).next()
            if func is None:
                P.op("dve", lambda e: e.tensor_copy(sb[0:mc, :], acc[0:mc, :]), reads=[acc], writes=[sb])
            else:
                P.op("act", lambda e: e.activation(out=sb[0:mc, :], in_=acc[0:mc, :], func=func), reads=[acc], writes=[sb])
            r0 = nb * 512 + j * 128
            store(sb, sb[0:mc, :], dst, dst.t[r0:r0 + mc, tb * 512:(tb + 1) * 512])
        return epi
    C.epi_store_T = epi_store_T

    def epi_tm(dst, func, dt=F32):
        def epi(tb, nb, j, acc, ncols):
            t = (C.st16 if dt == BF16 else C.st32).next()
            P.op("act", lambda e: e.activation(out=t[:, 0:ncols], in_=acc[:, 0:ncols], func=func), reads=[acc], writes=[t])
            r0 = tb * 512 + j * 128
            store(t, t[:, 0:ncols], dst, dst.t[r0:r0 + 128, nb * 512:nb * 512 + ncols])
        return epi
    C.epi_tm = epi_tm

    def rstd_op(ss, i, o, n):
        P.op("dve", lambda e: e.tensor_scalar(ss[:, o:o + 1], ss[:, i:i + 1], 1.0 / n, EPS, ALU.mult, ALU.add), reads=[ss], writes=[ss])
        P.op("act", lambda e: e.activation(out=ss[:, o:o + 1], in_=ss[:, o:o + 1], func=AF.Sqrt), reads=[ss], writes=[ss])
        P.op("dve", lambda e: e.reciprocal(ss[:, o:o + 1], ss[:, o:o + 1]), reads=[ss], writes=[ss])
    C.rstd_op = rstd_op

    def norm_T(src, srcT0, T, gain_ap, dstT):
        g = C.big[0]
        P.dma("sp", lambda e: e.dma_start(out=g[:], in_=bc_ap(gain_ap, 128)), P.ext(gain_ap), g)
        xr = Ring(C.big[1:3])
        hr = Ring(C.big[3:5])
        junk = C.junk
        KC = D // 128
        for tb in range(T // 512):
            hT = C.xring.next()
            for tt in range(4):
                r0 = srcT0 + tb * 512 + tt * 128
                xt = xr.next()
                P.dma("sp", lambda e, xt=xt, r0=r0: e.dma_start(out=xt[:], in_=src.t[r0:r0 + 128, :]), src, xt)
                ss = C.ssr.next()
                P.op("act", lambda e, xt=xt, ss=ss: e.activation(out=junk[:], in_=xt[:], func=AF.Square, accum_out=ss[:, 0:1]),
                     reads=[xt], writes=[junk, ss])
                rstd_op(ss, 0, 1, D)
                hb = hr.next()
                P.op("dve", lambda e, hb=hb, xt=xt, ss=ss: e.scalar_tensor_tensor(hb[:], xt[:], ss[:, 1:2], g[:], ALU.mult, ALU.mult),
                     reads=[xt, ss, g], writes=[hb])
                for k4 in range(KC // 4):
                    pt = C.ps.next()
                    for q in range(4):
                        kc = k4 * 4 + q
                        P.op("pe", lambda e, pt=pt, hb=hb, kc=kc, q=q: e.transpose(pt[:, q * 128:(q + 1) * 128], hb[:, kc * 128:(kc + 1) * 128], C.ident[:]),
                             reads=[hb, C.ident], writes=[pt])
                    if k4 % 2:
                        P.op("act", lambda e, pt=pt, hT=hT, k4=k4, tt=tt: e.activation(
                            out=hT[:, k4 * 4:k4 * 4 + 4, tt * 128:(tt + 1) * 128], in_=pt[:].rearrange("p (q t) -> p q t", q=4), func=AF.Copy),
                            reads=[pt], writes=[hT])
                    else:
                        P.op("dve", lambda e, pt=pt, hT=hT, k4=k4, tt=tt: e.tensor_copy(
                            hT[:, k4 * 4:k4 * 4 + 4, tt * 128:(tt + 1) * 128], pt[:].rearrange("p (q t) -> p q t", q=4)),
                            reads=[pt], writes=[hT])
            store(hT, hT[:, 0:KC, :], dstT, dstT.t.rearrange("(kc p) t -> p kc t", p=128)[:, :, tb * 512:(tb + 1) * 512])
    C.norm_T = norm_T
    return C


def attention(C, qaT, kaT, vtok, o_aT):
    P, I = C.P, C.I
    tiles, ktab = attn_tiles()
    P.begin_phase()
    amask = P.sbuf([128, 256], BF16)
    P.dma("pool", lambda e: e.dma_start(out=amask[:], in_=I.amask), P.ext(I.amask), amask)
    vcol = P.sbuf([128, len(ktab)], F32)
    P.dma("sp", lambda e: e.dma_start(out=vcol[:], in_=I.vcol), P.ext(I.vcol), vcol)
    ones = P.sbuf([128, 128], BF16)
    P.op("dve", lambda e: e.memset(ones[:], 1.0), writes=[ones])
    qTr = Ring([P.sbuf([128, TOWN], BF16) for _ in range(2)])
    kTr = Ring([P.sbuf([128, TA], BF16) for _ in range(2)])
    geo = {0: (1, 15, 9), 1: (4, 3, 3), 2: (16, 0, 2)}
    vts = {g: P.sbuf([128, geo[g][0], geo[g][2], 128], BF16) for g in range(3)}
    ULr = Ring([P.sbuf([128, 2, TOWN], F32) for _ in range(2)])
    Er = Ring([P.sbuf([128, 256], BF16) for _ in range(3)])
    Ewr = Ring([P.sbuf([128, 256], BF16) for _ in range(3)])
    ob = Ring([P.sbuf([128, TOWN], BF16) for _ in range(2)])
    rl = P.sbuf([128, TOWN], F32)
    scale = 128.0 ** -0.5
    for hh in range(8):
        UL = ULr.next()
        for g in range(3):
            d, bt0, nbt = geo[g]
            row0 = g * 1024 + hh * 128
            qT, kT, vt = qTr.next(), kTr.next(), vts[g]
            P.dma("sp", lambda e, qT=qT, row0=row0: e.dma_start(out=qT[:], in_=qaT.t[row0:row0 + 128, :]), qaT, qT)
            P.dma("sp", lambda e, kT=kT, row0=row0: e.dma_start(out=kT[:], in_=kaT.t[row0:row0 + 128, :]), kaT, kT)
            for r in range(d):
                if g < 2:
                    src = vtok.t[bass.ds(r + d * 128 * bt0, 128 * nbt, step=d), row0:row0 + 128].rearrange("(bt j) c -> j bt c", j=128)
                    P.dma("sp", lambda e, vt=vt, r=r, src=src: e.dma_start(out=vt[:, r, :, :], in_=src), vtok, vt)
                else:
                    src0 = vtok.t[bass.ds(r, 128, step=d), row0:row0 + 128]
                    src1 = vtok.t[bass.ds(r + d * 128, 64, step=d), row0:row0 + 128]
                    P.dma("sp", lambda e, vt=vt, r=r, src0=src0: e.dma_start(out=vt[:, r, 0, :], in_=src0), vtok, vt)
                    P.dma("sp", lambda e, vt=vt, r=r, src1=src1: e.dma_start(out=vt[0:64, r, 1, :], in_=src1), vtok, vt)
            for (tg, td, r, a, QW) in tiles:
                if tg != g:
                    continue
                k0 = ktab[(g, r, a - 128, 128)]
                k1 = ktab[(g, r, a, QW)]
                kc0 = bass.ds(r + d * (a - 128), 128, step=d)
                kc1 = bass.ds(r + d * a, QW, step=d)
                qc = bass.ds(r + d * a - HALO, QW, step=d)
                b0, b1 = (a - 128) // 128 - bt0, a // 128 - bt0
                ps_s, ps_u = C.ps.next(), C.ps.next()
                P.op("pe", lambda e, ps_s=ps_s, kT=kT, qT=qT, kc0=kc0, qc=qc, QW=QW: e.matmul(ps_s[:, 0:QW], kT[:, kc0], qT[:, qc], start=True, stop=True),
                     reads=[kT, qT], writes=[ps_s])
                P.op("pe", lambda e, ps_s=ps_s, kT=kT, qT=qT, kc1=kc1, qc=qc, QW=QW: e.matmul(ps_s[0:QW, 128:128 + QW], kT[:, kc1], qT[:, qc], start=True, stop=True),
                     reads=[kT, qT], writes=[ps_s])
                Er_, E = Er.next(), Ewr.next()
                P.op("act", lambda e, Er_=Er_, ps_s=ps_s, QW=QW: e.activation(out=Er_[:, 0:QW], in_=ps_s[:, 0:QW], func=AF.Exp, scale=scale), reads=[ps_s], writes=[Er_])
                P.op("act", lambda e, Er_=Er_, ps_s=ps_s, QW=QW: e.activation(out=Er_[0:QW, 128:128 + QW], in_=ps_s[0:QW, 128:128 + QW], func=AF.Exp, scale=scale), reads=[ps_s], writes=[Er_])
                P.op("dve", lambda e, E=E, Er_=Er_, k0=k0, QW=QW: e.scalar_tensor_tensor(E[:, 0:QW], Er_[:, 0:QW], vcol[:, k0:k0 + 1], amask[:, 0:QW], ALU.mult, ALU.mult),
                     reads=[Er_, vcol, amask], writes=[E])
                P.op("dve", lambda e, E=E, Er_=Er_, k1=k1, QW=QW: e.scalar_tensor_tensor(E[0:QW, 128:128 + QW], Er_[0:QW, 128:128 + QW], vcol[0:QW, k1:k1 + 1], amask[0:QW, 128:128 + QW], ALU.mult, ALU.mult),
                     reads=[Er_, vcol, amask], writes=[E])
                P.op("pe", lambda e, ps_u=ps_u, vt=vt, r=r, b0=b0, E=E, QW=QW: e.matmul(ps_u[:, 0:QW], vt[:, r, b0, :], E[:, 0:QW], start=True, stop=False), reads=[vt, E], writes=[ps_u])
                P.op("pe", lambda e, ps_u=ps_u, vt=vt, r=r, b1=b1, E=E, QW=QW: e.matmul(ps_u[:, 0:QW], vt[0:QW, r, b1, :], E[0:QW, 128:128 + QW], start=False, stop=True), reads=[vt, E], writes=[ps_u])
                P.op("pe", lambda e, ps_u=ps_u, E=E, QW=QW: e.matmul(ps_u[:, 128:128 + QW], ones[:, :], E[:, 0:QW], start=True, stop=False), reads=[ones, E], writes=[ps_u])
                P.op("pe", lambda e, ps_u=ps_u, E=E, QW=QW: e.matmul(ps_u[:, 128:128 + QW], ones[0:QW, :], E[0:QW, 128:128 + QW], start=False, stop=True), reads=[ones, E], writes=[ps_u])
                for w in range(2):
                    if g == 0:
                        P.op("dve", lambda e, UL=UL, ps_u=ps_u, w=w, qc=qc, QW=QW: e.tensor_copy(UL[:, w, qc], ps_u[:, w * 128:w * 128 + QW]), reads=[ps_u], writes=[UL])
                    else:
                        P.op("dve", lambda e, UL=UL, ps_u=ps_u, w=w, qc=qc, QW=QW: e.tensor_tensor(UL[:, w, qc], UL[:, w, qc], ps_u[:, w * 128:w * 128 + QW], ALU.add), reads=[ps_u, UL], writes=[UL])
        o = ob.next()
        P.op("dve", lambda e, UL=UL: e.reciprocal(rl[:], UL[:, 1, :]), reads=[UL], writes=[rl])
        P.op("dve", lambda e, UL=UL, o=o: e.tensor_tensor(o[:], UL[:, 0, :], rl[:], ALU.mult), reads=[UL, rl], writes=[o])
        C.store(o, o[:], o_aT, o_aT.t[hh * 128:(hh + 1) * 128, :])
    P.end_phase()


def v3(ap, h):
    return ap.rearrange("p (h j) -> p h j", h=h)


def bcl(ap2, n):
    p, h = ap2.shape
    return ap2.unsqueeze(2).broadcast_to([p, h, n])


def dn_prep(C, kvraw, qraw, khT, vT, qhT):
    P, I = C.P, C.I
    G = 4
    P.begin_phase()
    cw = P.sbuf([128, 48, 4], F32)
    P.dma("sp", lambda e: e.dma_start(out=cw[:], in_=I.conv_wT.rearrange("(c p) j -> p c j", p=128)), P.ext(I.conv_wT), cw)
    ones = P.sbuf([128, 128], F32)
    P.op("dve", lambda e: e.memset(ones[:], 1.0), writes=[ones])
    mk = lambda w: Ring([P.sbuf([128, w], F32) for _ in range(2 * G)])
    xr, yr, y2r, sqr, rr, y3r = mk(515), mk(512), mk(512), mk(512), mk(512), mk(512)
    TQ = qraw.t.shape[1]
    work = []
    for fc in range(48):
        if fc < 16:
            src, row, nblk, dst, drow = qraw, fc * 128, TQ // 512, qhT, fc * 128
        elif fc < 32:
            src, row, nblk, dst, drow = kvraw, (fc - 16) * 128, S // 512, khT, (fc - 16) * 128
        else:
            src, row, nblk, dst, drow = kvraw, (fc - 16) * 128, S // 512, vT, (fc - 32) * 128
        for blk in range(nblk):
            work.append((fc, src, row, dst, drow, blk))
    for g0 in range(0, len(work), G):
        grp = work[g0:g0 + G]
        st = []
        for (fc, src, row, dst, drow, blk) in grp:
            x = xr.next()
            if blk == 0:
                P.op("dve", lambda e, x=x: e.memset(x[:, 0:3], 0.0), writes=[x])
                P.dma("sp", lambda e, x=x, src=src, row=row: e.dma_start(out=x[:, 3:515], in_=src.t[row:row + 128, 0:512]), src, x)
            else:
                P.dma("sp", lambda e, x=x, src=src, row=row, blk=blk: e.dma_start(out=x[:, 0:515], in_=src.t[row:row + 128, blk * 512 - 3:blk * 512 + 512]), src, x)
            st.append(dict(fc=fc, x=x, y=yr.next(), y2=y2r.next(), dst=dst, drow=drow, blk=blk))
        for t in st:
            P.op("dve", lambda e, t=t: e.tensor_scalar_mul(t["y"][:], t["x"][:, 3:515], cw[:, t["fc"], 3:4]), reads=[t["x"], cw], writes=[t["y"]])
        for j in (2, 1, 0):
            for t in st:
                P.op("dve", lambda e, t=t, j=j: e.scalar_tensor_tensor(t["y"][:], t["x"][:, j:j + 512], cw[:, t["fc"], j:j + 1], t["y"][:], ALU.mult, ALU.add),
                     reads=[t["x"], cw, t["y"]], writes=[t["y"]])
        for t in st:
            P.op("act", lambda e, t=t: e.activation(out=t["y2"][:], in_=t["y"][:], func=AF.Silu), reads=[t["y"]], writes=[t["y2"]])
        nt = [t for t in st if t["fc"] < 32]
        for t in nt:
            t["sq"], t["r"], t["y3"], t["ps"] = sqr.next(), rr.next(), y3r.next(), C.ps.next()
            P.op("dve", lambda e, t=t: e.tensor_tensor(t["sq"][:], t["y2"][:], t["y2"][:], ALU.mult), reads=[t["y2"]], writes=[t["sq"]])
        for t in nt:
            P.op("pe", lambda e, t=t: e.matmul(t["ps"][:, :], ones[:, :], t["sq"][:, :], start=True, stop=True), reads=[ones, t["sq"]], writes=[t["ps"]])
        for t in nt:
            P.op("act", lambda e, t=t: e.activation(out=t["r"][:], in_=t["ps"][:], func=AF.Sqrt, bias=EPS), reads=[t["ps"]], writes=[t["r"]])
        for t in nt:
            P.op("dve", lambda e, t=t: e.reciprocal(t["r"][:], t["r"][:]), reads=[t["r"]], writes=[t["r"]])
        for t in nt:
            sc = 128.0 ** -0.5 if t["fc"] < 16 else 1.0
            P.op("dve", lambda e, t=t, sc=sc: e.scalar_tensor_tensor(t["y3"][:], t["y2"][:], sc, t["r"][:], ALU.mult, ALU.mult), reads=[t["y2"], t["r"]], writes=[t["y3"]])
            t["y2"] = t["y3"]
        for t in st:
            C.store(t["y2"], t["y2"][:], t["dst"], t["dst"].t[t["drow"]:t["drow"] + 128, t["blk"] * 512:(t["blk"] + 1) * 512])
    P.end_phase()


def dn_scan(C, ba, khT, vT, qhT, ztm, o_bT):
    P, I = C.P, C.I
    NCH = S // 64
    OWN0 = NCH - TOWN // 64
    TQ = qhT.t.shape[1]
    P.begin_phase()
    dnc = P.sbuf([128, 6 * 512], F32)
    P.dma("sp", lambda e: e.dma_start(out=dnc[:], in_=I.dnc), P.ext(I.dnc), dnc)
    Irep, strictrep, strictTrep = v3(dnc[0:64, 0:512], 8), v3(dnc[0:64, 512:1024], 8), v3(dnc[0:64, 1024:1536], 8)
    mneg, mnegT = dnc[0:64, 1536:2048], dnc[0:64, 2048:2560]
    Utri, ones, negones, negI64 = dnc[0:64, 2560:2624], dnc[:, 2624:2752], dnc[:, 2752:2880], dnc[0:64, 2880:2944]
    I64 = C.ident[0:64, 0:64]
    ident = C.ident
    gc_all = P.sbuf([64, NCH, 16], F32)
    beta_all = P.sbuf([64, NCH, 16], F32)
    kdecs_all = P.sbuf([64, NCH, 16], F32)
    egl_all = P.sbuf([128, NCH, 16], F32)
    alog = P.sbuf([64, 16], F32)
    dtb = P.sbuf([64, 16], F32)
    P.dma("sp", lambda e: e.dma_start(out=alog[:], in_=bc_ap(I.dn_a_log, 64)), P.ext(I.dn_a_log), alog)
    P.dma("sp", lambda e: e.dma_start(out=dtb[:], in_=bc_ap(I.dn_dt_bias, 64)), P.ext(I.dn_dt_bias), dtb)
    P.op("act", lambda e: e.activation(out=alog[:], in_=alog[:], func=AF.Exp), reads=[alog], writes=[alog])
    P.op("dve", lambda e: e.tensor_scalar_mul(alog[:], alog[:], -1.0), reads=[alog], writes=[alog])
    dnn = P.sbuf([64, 128], F32)
    P.dma("sp", lambda e: e.dma_start(out=dnn[:], in_=bc_ap(I.dn_norm, 64)), P.ext(I.dn_norm), dnn)
    QC = 16
    baq = P.sbuf([64, QC, 32], F32)
    spq = P.sbuf([64, QC, 16], F32)
    gq = P.sbuf([64, QC, 16], F32)
    tq = P.sbuf([64, QC, 16], F32)
    bav = ba.t.rearrange("(c t) n -> t c n", t=64)
    for q in range(NCH // QC if DN_SET >= 1 else 0):
        cs = slice(q * QC, (q + 1) * QC)
        P.dma("sp", lambda e, cs=cs: e.dma_start(out=baq[:], in_=bav[:, cs, :]), ba, baq)
        if DN_SET < 2:
            continue
        P.op("act", lambda e, cs=cs: e.activation(out=beta_all[:, cs, :], in_=baq[:, :, 0:16], func=AF.Sigmoid), reads=[baq], writes=[beta_all])
        if DN_SET < 3:
            continue
        P.op("dve", lambda e: e.tensor_tensor(spq[:], baq[:, :, 16:32], dtb[:].unsqueeze(1).broadcast_to([64, QC, 16]), ALU.add), reads=[baq, dtb], writes=[spq])
        P.op("act", lambda e: e.activation(out=spq[:], in_=spq[:], func=AF.Exp), reads=[spq], writes=[spq])
        P.op("act", lambda e: e.activation(out=spq[:], in_=spq[:], func=AF.Ln, bias=1.0), reads=[spq], writes=[spq])
        P.op("dve", lambda e: e.tensor_tensor(gq[:], spq[:], alog[:].unsqueeze(1).broadcast_to([64, QC, 16]), ALU.mult), reads=[spq, alog], writes=[gq])
        if DN_SET < 4:
            continue
        ps1, ps2 = C.ps.next(), C.ps.next()
        gflat = gq[:].rearrange("p c h -> p (c h)")
        P.op("pe", lambda e, ps1=ps1: e.matmul(ps1[0:64, 0:QC * 16], Utri, gflat, start=True, stop=True), reads=[dnc, gq], writes=[ps1])
        P.op("pe", lambda e, ps2=ps2: e.matmul(ps2[:, 0:QC * 16], ones[0:64, :], gflat, start=True, stop=True), reads=[dnc, gq], writes=[ps2])
        if DN_SET < 5:
            continue
        P.op("dve", lambda e, ps1=ps1, cs=cs: e.tensor_copy(gc_all[:, cs, :], v3(ps1[0:64, 0:QC * 16], QC)), reads=[ps1], writes=[gc_all])
        if DN_SET < 6:
            continue
        if DN_SET < 7:
            continue
        P.op("act", lambda e, ps2=ps2, cs=cs: e.activation(out=egl_all[:, cs, :], in_=v3(ps2[:, 0:QC * 16], QC), func=AF.Exp), reads=[ps2], writes=[egl_all])
        if DN_SET < 8:
            continue
        P.op("dve", lambda e, ps2=ps2, cs=cs: e.tensor_tensor(tq[:], v3(ps2[0:64, 0:QC * 16], QC), gc_all[:, cs, :], ALU.subtract), reads=[ps2, gc_all], writes=[tq])
        if DN_SET < 9:
            continue
        P.op("act", lambda e, cs=cs: e.activation(out=kdecs_all[:, cs, :], in_=tq[:], func=AF.Exp), reads=[tq], writes=[kdecs_all])
    if hasattr(C, "dbg"):
        for nm, b_ in (("gc_all", gc_all), ("beta_all", beta_all), ("kdecs_all", kdecs_all), ("egl_all", egl_all)):
            C.dbg(nm, b_, b_[:].rearrange("p c h -> p (c h)"))
    S4 = [P.sbuf([128, 4, 128], F32) for _ in range(4)]
    for s_ in S4:
        P.op("dve", lambda e, s_=s_: e.memset(s_[:], 0.0), writes=[s_])
    kcr = Ring([P.sbuf([128, 16, 64], F32) for _ in range(2)])
    vcr = Ring([P.sbuf([128, 16, 64], F32) for _ in range(2)])
    qc_ = P.sbuf([128, 16, 64], F32)
    zc_ = P.sbuf([64, 2048], F32)
    obT = P.sbuf([128, 16, 64], BF16)
    nb_ = P.sbuf([64, 16], F32)
    kbgs = P.sbuf([64, 16], F32)
    egc = P.sbuf([64, 16], F32)
    T = lambda: P.sbuf([64, 8, 64], F32)
    T4 = lambda: P.sbuf([64, 8, 128], F32)
    F = lambda b: b[:].rearrange("p h j -> p (h j)")

    def mk():
        X = Ctx()
        X.diagG, X.diagB, X.diagE, X.Dm, X.DmT, X.t1, X.t2, X.intraT = (T() for _ in range(8))
        X.sets = [(T(), T(), T()) for _ in range(2)]
        X.qdec, X.wT = P.sbuf([128, 8, 64], F32), P.sbuf([128, 8, 64], F32)
        X.kbg, X.kdec, X.vb, X.u, X.vnew = (T4() for _ in range(5))
        X.osb, X.on = P.sbuf([64, 4, 128], F32), P.sbuf([64, 4, 128], F32)
        X.ss = P.sbuf([64, 8], F32)
        return X
    XS = [mk(), mk()]
    chunks = range(NCH) if DN_CHUNKS is None else list(range(DN_CHUNKS)) + ([OWN0] if DN_CUT >= 8 else [])
    for c in (chunks if DN_CUT >= 1 else []):
        own = c >= OWN0 and DN_CUT >= 8
        kc_, vc_ = kcr.next(), vcr.next()
        P.dma("sp", lambda e, kc_=kc_, c=c: e.dma_start(out=kc_[:], in_=khT.t.rearrange("(h d) t -> d h t", d=128)[:, :, c * 64:(c + 1) * 64]), khT, kc_)
        P.dma("sp", lambda e, vc_=vc_, c=c: e.dma_start(out=vc_[:], in_=vT.t.rearrange("(h d) t -> d h t", d=128)[:, :, c * 64:(c + 1) * 64]), vT, vc_)
        if own:
            q0 = (c - OWN0) * 64 + (TQ - TOWN)
            P.dma("sp", lambda e, q0=q0: e.dma_start(out=qc_[:], in_=qhT.t.rearrange("(h d) t -> d h t", d=128)[:, :, q0:q0 + 64]), qhT, qc_)
            P.dma("sp", lambda e, c=c: e.dma_start(out=zc_[:], in_=ztm.t[(c - OWN0) * 64:(c - OWN0 + 1) * 64, :]), ztm, zc_)
        P.op("dve", lambda e, c=c: e.tensor_scalar_mul(nb_[:], beta_all[:, c, :], -1.0), reads=[beta_all], writes=[nb_])
        P.op("act", lambda e, c=c: e.activation(out=egc[:], in_=gc_all[:, c, :], func=AF.Exp), reads=[gc_all], writes=[egc])
        P.op("dve", lambda e, c=c: e.tensor_tensor(kbgs[:], beta_all[:, c, :], egc[:], ALU.mult), reads=[beta_all, egc], writes=[kbgs])
        HG = (0, 1)
        hsl = lambda hg: slice(hg * 8, hg * 8 + 8)
        for hg in HG:
            X = XS[hg]
            X.gcs = bcl(gc_all[:, c, hsl(hg)], 64)
            P.op("dve", lambda e, X=X, g_=X.gcs: e.tensor_tensor(X.diagG[:], Irep, g_, ALU.mult), reads=[dnc, gc_all], writes=[X.diagG])
            P.op("dve", lambda e, X=X, c=c, hg=hg: e.tensor_tensor(X.diagB[:], Irep, bcl(beta_all[:, c, hsl(hg)], 64), ALU.mult), reads=[dnc, beta_all], writes=[X.diagB])
        for hg in HG:
            X = XS[hg]
            h0 = hg * 8
            X.p1, X.p3, X.p4, X.p5 = C.ps.next(), C.ps.next(), C.ps.next(), C.ps.next()
            for h in range(8):
                P.op("pe", lambda e, p1=X.p1, kc_=kc_, h=h, h0=h0: e.matmul(p1[0:64, h * 64:(h + 1) * 64], kc_[:, h0 + h, :], kc_[:, h0 + h, :], start=True, stop=True),
                     reads=[kc_], writes=[X.p1], signal=(h == 7))
            P.op("pe", lambda e, X=X, p3=X.p3: e.matmul(p3[0:64, :], negones[0:64, 0:64], F(X.diagG), start=True, stop=False), reads=[dnc, X.diagG], writes=[X.p3])
            P.op("pe", lambda e, X=X, g_=X.gcs, p3=X.p3: e.matmul(p3[0:64, :], I64, g_, start=False, stop=False), reads=[ident, gc_all], writes=[X.p3])
            P.op("pe", lambda e, X=X, p3=X.p3: e.matmul(p3[0:64, :], I64, mneg, start=False, stop=True), reads=[ident, dnc], writes=[X.p3])
            P.op("pe", lambda e, X=X, p4=X.p4: e.matmul(p4[0:64, :], ones[0:64, 0:64], F(X.diagG), start=True, stop=False), reads=[dnc, X.diagG], writes=[X.p4])
            P.op("pe", lambda e, X=X, g_=X.gcs, p4=X.p4: e.matmul(p4[0:64, :], negI64, g_, start=False, stop=False), reads=[dnc, gc_all], writes=[X.p4])
            P.op("pe", lambda e, X=X, p4=X.p4: e.matmul(p4[0:64, :], I64, mnegT, start=False, stop=True), reads=[ident, dnc], writes=[X.p4])
            P.op("pe", lambda e, X=X, p5=X.p5: e.matmul(p5[0:64, :], negones[0:64, 0:64], F(X.diagB), start=True, stop=True), reads=[dnc, X.diagB], writes=[X.p5])
        for hg in HG:
            X = XS[hg]
            P.op("act", lambda e, X=X, p3=X.p3: e.activation(out=F(X.Dm), in_=p3[0:64, :], func=AF.Exp), reads=[X.p3], writes=[X.Dm])
            P.op("act", lambda e, X=X, p4=X.p4: e.activation(out=F(X.DmT), in_=p4[0:64, :], func=AF.Exp), reads=[X.p4], writes=[X.DmT])
        if DN_CUT < 3:
            continue
        for hg in HG:
            X = XS[hg]
            P.op("dve", lambda e, X=X, p1=X.p1: e.tensor_tensor(X.t1[:], v3(p1[0:64, :], 8), strictrep, ALU.mult), reads=[X.p1, dnc], writes=[X.t1])
            P.op("dve", lambda e, X=X, p1=X.p1: e.tensor_tensor(X.t2[:], v3(p1[0:64, :], 8), strictTrep, ALU.mult), reads=[X.p1, dnc], writes=[X.t2])
        for hg in HG:
            X = XS[hg]
            P.op("dve", lambda e, X=X: e.tensor_tensor(X.t1[:], X.t1[:], X.Dm[:], ALU.mult), reads=[X.t1, X.Dm], writes=[X.t1])
            P.op("dve", lambda e, X=X: e.tensor_tensor(X.t2[:], X.t2[:], X.DmT[:], ALU.mult), reads=[X.t2, X.DmT], writes=[X.t2])
        for hg in HG:
            X = XS[hg]
            M, MT, R = X.sets[0]
            P.op("dve", lambda e, X=X, M=M, hg=hg: e.tensor_tensor(M[:], X.t1[:], bcl(nb_[:, hsl(hg)], 64), ALU.mult), reads=[X.t1, nb_], writes=[M])
            P.op("dve", lambda e, X=X, MT=MT, p5=X.p5: e.tensor_tensor(MT[:], X.t2[:], v3(p5[0:64, :], 8), ALU.mult), reads=[X.t2, X.p5], writes=[MT])
        for hg in HG:
            M, MT, R = XS[hg].sets[0]
            P.op("dve", lambda e, R=R, MT=MT: e.tensor_tensor(R[:], MT[:], Irep, ALU.add), reads=[MT, dnc], writes=[R])
        if DN_CUT < 4:
            continue
        cur = 0
        for m in range(1, 6):
            for hg in HG:
                X = XS[hg]
                M, MT, R = X.sets[cur]
                X.pm, X.pmt = C.ps.next(), C.ps.next()
                for h in range(8):
                    P.op("pe", lambda e, pm=X.pm, M=M, MT=MT, h=h: e.matmul(pm[0:64, h * 64:(h + 1) * 64], MT[:, h, :], M[:, h, :], start=True, stop=True), reads=[M, MT], writes=[X.pm], signal=(h == 7))
                if m < 5:
                    for h in range(8):
                        P.op("pe", lambda e, pmt=X.pmt, M=M, MT=MT, h=h: e.matmul(pmt[0:64, h * 64:(h + 1) * 64], M[:, h, :], MT[:, h, :], start=True, stop=True), reads=[M, MT], writes=[X.pmt], signal=(h == 7))
            for hg in HG:
                X = XS[hg]
                Mn, MTn, Rn = X.sets[1 - cur]
                P.op("act", lambda e, Mn=Mn, pm=X.pm: e.activation(out=F(Mn), in_=pm[0:64, :], func=AF.Copy), reads=[X.pm], writes=[Mn])
                if m < 5:
                    P.op("dve", lambda e, MTn=MTn, pmt=X.pmt: e.tensor_copy(F(MTn), pmt[0:64, :]), reads=[X.pmt], writes=[MTn])
            for hg in HG:
                X = XS[hg]
                M, MT, R = X.sets[cur]
                Mn, MTn, Rn = X.sets[1 - cur]
                X.pr = C.ps.next()
                for h in range(8):
                    P.op("pe", lambda e, pr=X.pr, Mn=Mn, R=R, h=h: e.matmul(pr[0:64, h * 64:(h + 1) * 64], Mn[:, h, :], R[:, h, :], start=True, stop=True), reads=[Mn, R], writes=[X.pr], signal=(h == 7))
            for hg in HG:
                X = XS[hg]
                M, MT, R = X.sets[cur]
                Mn, MTn, Rn = X.sets[1 - cur]
                P.op("dve", lambda e, Rn=Rn, R=R, pr=X.pr: e.tensor_tensor(F(Rn), F(R), pr[0:64, :], ALU.add), reads=[R, X.pr], writes=[Rn])
            cur = 1 - cur
        if DN_CUT < 5:
            continue
        for half in range(2):
            for hg in HG:
                X = XS[hg]
                h0 = hg * 8
                X.pk, X.pv = C.ps.next(), C.ps.next()
                for hq in range(4):
                    h = h0 + half * 4 + hq
                    P.op("pe", lambda e, pk=X.pk, kc_=kc_, h=h, hq=hq: e.transpose(pk[0:64, hq * 128:(hq + 1) * 128], kc_[:, h, :], ident[:]), reads=[kc_, ident], writes=[X.pk])
                    P.op("pe", lambda e, pv=X.pv, vc_=vc_, h=h, hq=hq: e.transpose(pv[0:64, hq * 128:(hq + 1) * 128], vc_[:, h, :], ident[:]), reads=[vc_, ident], writes=[X.pv])
            for hg in HG:
                X = XS[hg]
                h0 = hg * 8
                a4 = slice(half * 4, half * 4 + 4)
                g4 = slice(h0 + half * 4, h0 + half * 4 + 4)
                P.op("dve", lambda e, X=X, pk=X.pk, a4=a4, g4=g4: e.tensor_tensor(X.kbg[:, a4, :], v3(pk[0:64, :], 4), bcl(kbgs[:, g4], 128), ALU.mult), reads=[X.pk, kbgs], writes=[X.kbg])
                P.op("dve", lambda e, X=X, pk=X.pk, a4=a4, g4=g4, c=c: e.tensor_tensor(X.kdec[:, a4, :], v3(pk[0:64, :], 4), bcl(kdecs_all[:, c, g4], 128), ALU.mult), reads=[X.pk, kdecs_all], writes=[X.kdec])
                P.op("dve", lambda e, X=X, pv=X.pv, a4=a4, g4=g4, c=c: e.tensor_tensor(X.vb[:, a4, :], v3(pv[0:64, :], 4), bcl(beta_all[:, c, g4], 128), ALU.mult), reads=[X.pv, beta_all], writes=[X.vb])
        if DN_CUT < 6:
            continue
        for hg in HG:
            X = XS[hg]
            R = X.sets[cur][2]
            X.pw = C.ps.next()
            for h in range(8):
                P.op("pe", lambda e, X=X, pw=X.pw, R=R, h=h: e.matmul(pw[:, h * 64:(h + 1) * 64], X.kbg[:, h, :], R[:, h, :], start=True, stop=True), reads=[X.kbg, R], writes=[X.pw], signal=(h == 7))
        for hg in HG:
            X = XS[hg]
            P.op("dve", lambda e, X=X, pw=X.pw: e.tensor_copy(F(X.wT), pw[:, :]), reads=[X.pw], writes=[X.wT])
        for half in range(2):
            for hg in HG:
                X = XS[hg]
                R = X.sets[cur][2]
                X.pu = C.ps.next()
                for hq in range(4):
                    h = half * 4 + hq
                    P.op("pe", lambda e, X=X, pu=X.pu, R=R, h=h, hq=hq: e.matmul(pu[0:64, hq * 128:(hq + 1) * 128], R[:, h, :], X.vb[:, h, :], start=True, stop=True), reads=[R, X.vb], writes=[X.pu], signal=(hq == 3))
            for hg in HG:
                X = XS[hg]
                P.op("act", lambda e, X=X, pu=X.pu, half=half: e.activation(out=X.u[:, half * 4:half * 4 + 4, :], in_=v3(pu[0:64, :], 4), func=AF.Copy), reads=[X.pu], writes=[X.u])
        if own:
            for hg in HG:
                X = XS[hg]
                h0 = hg * 8
                X.p2, X.p6 = C.ps.next(), C.ps.next()
                for h in range(8):
                    P.op("pe", lambda e, p2=X.p2, kc_=kc_, h=h, h0=h0: e.matmul(p2[0:64, h * 64:(h + 1) * 64], kc_[:, h0 + h, :], qc_[:, h0 + h, :], start=True, stop=True),
                         reads=[kc_, qc_], writes=[X.p2], signal=(h == 7))
                P.op("dve", lambda e, X=X, hg=hg: e.tensor_tensor(X.diagE[:], Irep, bcl(egc[:, hsl(hg)], 64), ALU.mult), reads=[dnc, egc], writes=[X.diagE])
            for hg in HG:
                X = XS[hg]
                P.op("dve", lambda e, X=X, p2=X.p2: e.tensor_tensor(X.intraT[:], v3(p2[0:64, :], 8), X.DmT[:], ALU.mult), reads=[X.p2, X.DmT], writes=[X.intraT])
                P.op("pe", lambda e, X=X, p6=X.p6: e.matmul(p6[:, :], ones[0:64, :], F(X.diagE), start=True, stop=True), reads=[dnc, X.diagE], writes=[X.p6])
            for hg in HG:
                X = XS[hg]
                P.op("dve", lambda e, X=X, p6=X.p6, hg=hg: e.tensor_tensor(X.qdec[:], qc_[:, hsl(hg), :], v3(p6[:, :], 8), ALU.mult), reads=[qc_, X.p6], writes=[X.qdec])
        if DN_CUT < 7:
            continue
        for half in range(2):
            a4 = slice(half * 4, half * 4 + 4)
            for hg in HG:
                X = XS[hg]
                Sb = S4[hg * 2 + half]
                X.pws = C.ps.next()
                for hq in range(4):
                    h = half * 4 + hq
                    P.op("pe", lambda e, X=X, pws=X.pws, Sb=Sb, h=h, hq=hq: e.matmul(pws[0:64, hq * 128:(hq + 1) * 128], X.wT[:, h, :], Sb[:, hq, :], start=True, stop=True), reads=[X.wT, Sb], writes=[X.pws], signal=(hq == 3))
            for hg in HG:
                X = XS[hg]
                P.op("dve", lambda e, X=X, pws=X.pws, a4=a4: e.tensor_tensor(X.vnew[:, a4, :], X.u[:, a4, :], v3(pws[0:64, :], 4), ALU.subtract), reads=[X.u, X.pws], writes=[X.vnew])
            for hg in HG:
                X = XS[hg]
                Sb = S4[hg * 2 + half]
                if own:
                    X.po = C.ps.next()
                    for hq in range(4):
                        h = half * 4 + hq
                        P.op("pe", lambda e, X=X, po=X.po, Sb=Sb, h=h, hq=hq: e.matmul(po[0:64, hq * 128:(hq + 1) * 128], X.qdec[:, h, :], Sb[:, hq, :], start=True, stop=False), reads=[X.qdec, Sb], writes=[X.po])
                        P.op("pe", lambda e, X=X, po=X.po, h=h, hq=hq: e.matmul(po[0:64, hq * 128:(hq + 1) * 128], X.intraT[:, h, :], X.vnew[:, h, :], start=False, stop=True), reads=[X.intraT, X.vnew], writes=[X.po])
                X.psu = C.ps.next()
                for hq in range(4):
                    h = half * 4 + hq
                    P.op("pe", lambda e, X=X, psu=X.psu, h=h, hq=hq: e.matmul(psu[:, hq * 128:(hq + 1) * 128], X.kdec[:, h, :], X.vnew[:, h, :], start=True, stop=True), reads=[X.kdec, X.vnew], writes=[X.psu], signal=(hq == 3))
            for hg in HG:
                Sb = S4[hg * 2 + half]
                g4 = slice(hg * 8 + half * 4, hg * 8 + half * 4 + 4)
                P.op("dve", lambda e, Sb=Sb, c=c, g4=g4: e.tensor_tensor(Sb[:], Sb[:], bcl(egl_all[:, c, g4], 128), ALU.mult), reads=[Sb, egl_all], writes=[Sb])
            for hg in HG:
                X = XS[hg]
                Sb = S4[hg * 2 + half]
                P.op("dve", lambda e, Sb=Sb, psu=X.psu: e.tensor_tensor(Sb[:], Sb[:], v3(psu[:, :], 4), ALU.add), reads=[Sb, X.psu], writes=[Sb])
            if own:
                for hg in HG:
                    X = XS[hg]
                    P.op("act", lambda e, X=X, po=X.po: e.activation(out=X.osb[:], in_=v3(po[0:64, :], 4), func=AF.Copy), reads=[X.po], writes=[X.osb])
                for hg in HG:
                    X = XS[hg]
                    P.op("dve", lambda e, X=X: e.tensor_tensor(X.on[:], X.osb[:], X.osb[:], ALU.mult), reads=[X.osb], writes=[X.on])
                for hg in HG:
                    X = XS[hg]
                    P.op("dve", lambda e, X=X: e.tensor_reduce(X.ss[:, 0:4], X.on[:], AX.X, ALU.add), reads=[X.on], writes=[X.ss])
                for hg in HG:
                    X = XS[hg]
                    P.op("dve", lambda e, X=X: e.tensor_scalar(X.ss[:, 4:8], X.ss[:, 0:4], 1.0 / 128, EPS, ALU.mult, ALU.add), reads=[X.ss], writes=[X.ss])
                for hg in HG:
                    X = XS[hg]
                    P.op("act", lambda e, X=X: e.activation(out=X.ss[:, 4:8], in_=X.ss[:, 4:8], func=AF.Sqrt), reads=[X.ss], writes=[X.ss])
                for hg in HG:
                    X = XS[hg]
                    P.op("dve", lambda e, X=X: e.reciprocal(X.ss[:, 4:8], X.ss[:, 4:8]), reads=[X.ss], writes=[X.ss])
                for hg in HG:
                    X = XS[hg]
                    P.op("dve", lambda e, X=X: e.tensor_tensor(X.on[:], X.osb[:], bcl(X.ss[:, 4:8], 128), ALU.mult), reads=[X.osb, X.ss], writes=[X.on])
                for hg in HG:
                    X = XS[hg]
                    P.op("dve", lambda e, X=X: e.tensor_tensor(X.on[:], X.on[:], dnn[:].unsqueeze(1).broadcast_to([64, 4, 128]), ALU.mult), reads=[X.on, dnn], writes=[X.on])
                for hg in HG:
                    X = XS[hg]
                    z0 = (hg * 8 + half * 4) * 128
                    P.op("dve", lambda e, X=X, z0=z0: e.tensor_tensor(X.on[:], X.on[:], v3(zc_[:, z0:z0 + 512], 4), ALU.mult), reads=[X.on, zc_], writes=[X.on])
                for hg in HG:
                    X = XS[hg]
                    X.pt = C.ps.next()
                    for hq in range(4):
                        P.op("pe", lambda e, X=X, pt=X.pt, hq=hq: e.transpose(pt[:, hq * 64:(hq + 1) * 64], X.on[:, hq, :], I64), reads=[X.on, ident], writes=[X.pt], signal=(hq == 3))
                for hg in HG:
                    X = XS[hg]
                    g4 = slice(hg * 8 + half * 4, hg * 8 + half * 4 + 4)
                    P.op("act", lambda e, pt=X.pt, g4=g4: e.activation(out=obT[:, g4, :], in_=v3(pt[:, 0:256], 4), func=AF.Copy), reads=[X.pt], writes=[obT])
        if own:
            C.store(obT, obT[:], o_bT, o_bT.t.rearrange("(h d) t -> d h t", d=128)[:, :, (c - OWN0) * 64:(c - OWN0 + 1) * 64])
    if hasattr(C, "dbg"):
        X = XS[1]
        for nm, b_ in (("Dm", X.Dm), ("DmT", X.DmT), ("R0", X.sets[0][2]), ("R1", X.sets[1][2]), ("M0", X.sets[0][0]), ("intraT", X.intraT)):
            C.dbg(nm, b_, b_[:].rearrange("p h j -> p (h j)"))
        for nm, b_ in (("u", X.u), ("vnew", X.vnew), ("kbg", X.kbg), ("kdec", X.kdec), ("vb", X.vb)):
            C.dbg(nm, b_, b_[:].rearrange("p h j -> p (h j)"))
        C.dbg("wT", X.wT, X.wT[:].rearrange("p h j -> p (h j)"))
        for i_, s_ in enumerate(S4):
            C.dbg("S%d" % i_, s_, s_[:].rearrange("p h j -> p (h j)"))
    P.end_phase()


def build_all():
    C = build()
    P, I, nc, out = C.P, C.I, C.nc, C.out
    gemm, store, norm_T, dram = C.gemm, C.store, C.norm_T, C.dram
    xrB = P.ext(I.xr)
    T0 = S - TOWN
    TQ = TOWN + 512

    P.begin_phase()
    C.gemm_bufs()
    hT = dram("hT", [D, S], BF16)
    norm_T(xrB, 0, S, I.norm_mix, hT)
    gT = dram("gT", [2 * D, TOWN], BF16)
    gemm(hT, T0, TOWN, I.w_in, GA, 2 * D, D, "A", C.epi_store_T(gT, BF16, AF.Sigmoid))
    o_aT = dram("o_aT", [1024, TOWN], BF16)
    o_bT = dram("o_bT", [2048, TOWN], BF16)
    if "attn" in STAGES:
        qaT = dram("qaT", [3072, TOWN], BF16)
        kaT = dram("kaT", [3072, TA], BF16)
        vtok = dram("vtok", [TA, 3072], BF16)
        gemm(hT, T0, TOWN, I.w_in, QA, 3072, D, "A", C.epi_store_T(qaT, BF16))
        gemm(hT, S - TA, TA, I.w_in, KA, 3072, D, "A", C.epi_store_T(kaT, BF16))
        gemm(hT, S - TA, TA, I.w_in, VA, 3072, D, "B", C.epi_tm(vtok, AF.Copy, BF16))
    if "dn" in STAGES:
        kvraw = dram("kvraw", [4096, S], F32)
        qraw = dram("qraw", [2048, TQ], F32)
        ztm = dram("ztm", [TOWN, 2048], F32)
        ba = dram("ba", [S, 32], F32)
        gemm(hT, 0, S, I.w_in, KB, 4096, D, "A", C.epi_store_T(kvraw, F32))
        gemm(hT, S - TQ, TQ, I.w_in, QB, 2048, D, "A", C.epi_store_T(qraw, F32))
        gemm(hT, T0, TOWN, I.w_in, ZB, 2048, D, "B", C.epi_tm(ztm, AF.Silu))
        gemm(hT, 0, S, I.w_in, BETA, 32, D, "B", C.epi_tm(ba, AF.Copy))
    P.end_phase()

    if "attn" in STAGES:
        attention(C, qaT, kaT, vtok, o_aT)
    if "dn" in STAGES:
        khT = dram("khT", [2048, S], F32)
        vT = dram("vT", [2048, S], F32)
        qhT = dram("qhT", [2048, TQ], F32)
        dn_prep(C, kvraw, qraw, khT, vT, qhT)
        if "noscan" not in STAGES:
            dn_scan(C, ba, khT, vT, qhT, ztm, o_bT)

    P.begin_phase()
    C.gemm_bufs()
    e32, e16 = C.st32, C.st16
    AT = dram("AT", [D, TOWN], F32)
    BT = dram("BT", [D, TOWN], F32)
    gemm(o_aT, 0, TOWN, I.w_attn_up, 0, D, 1024, "A", C.epi_store_T(AT, F32))
    gemm(o_bT, 0, TOWN, I.w_dn_up, 0, D, 2048, "A", C.epi_store_T(BT, F32))
    mT = dram("mT", [D, TOWN], BF16)
    for r in range(D // 128):
        for tb in range(TOWN // 512):
            sl = (slice(r * 128, (r + 1) * 128), slice(tb * 512, (tb + 1) * 512))
            a, b, ga, gb, m = e32.next(), e32.next(), e16.next(), e16.next(), e16.next()
            P.dma("sp", lambda e, a=a, sl=sl: e.dma_start(out=a[:], in_=AT.t[sl]), AT, a)
            P.dma("sp", lambda e, b=b, sl=sl: e.dma_start(out=b[:], in_=BT.t[sl]), BT, b)
            P.dma("sp", lambda e, ga=ga, sl=sl: e.dma_start(out=ga[:], in_=gT.t[sl]), gT, ga)
            P.dma("sp", lambda e, gb=gb, sl=sl, r=r: e.dma_start(out=gb[:], in_=gT.t[D + r * 128:D + (r + 1) * 128, sl[1]]), gT, gb)
            P.op("dve", lambda e, a=a, ga=ga: e.tensor_tensor(a[:], a[:], ga[:], ALU.mult), reads=[a, ga], writes=[a])
            P.op("dve", lambda e, b=b, gb=gb: e.tensor_tensor(b[:], b[:], gb[:], ALU.mult), reads=[b, gb], writes=[b])
            P.op("dve", lambda e, a=a, b=b, m=m: e.tensor_tensor(m[:], a[:], b[:], ALU.add), reads=[a, b], writes=[m])
            store(m, m[:], mT, mT.t[sl])

    def epi_resid(src, src_r0, dst):
        def epi(tb, nb, j, acc, ncols):
            r0 = tb * 512 + j * 128
            xt = e32.next()
            P.dma("sp", lambda e: e.dma_start(out=xt[:, 0:ncols], in_=src.t[src_r0 + r0:src_r0 + r0 + 128, nb * 512:nb * 512 + ncols]), src, xt)
            P.op("dve", lambda e: e.tensor_tensor(xt[:, 0:ncols], xt[:, 0:ncols], acc[:, 0:ncols], ALU.add), reads=[xt, acc], writes=[xt])
            store(xt, xt[:, 0:ncols], dst, dst.t[r0:r0 + 128, nb * 512:nb * 512 + ncols])
        return epi

    x1 = dram("x1", [TOWN, D], F32)
    gemm(mT, 0, TOWN, I.w_out, 0, D, D, "B", epi_resid(xrB, T0, x1))
    h2T = dram("h2T", [D, TOWN], BF16)
    norm_T(x1, 0, TOWN, I.norm_mlp, h2T)
    hidT = dram("hidT", [DFF, TOWN], BF16)

    def epi_relu2(tb, nb, j, acc, ncols):
        t, sb = e32.next(), e16.next()
        P.op("act", lambda e: e.activation(out=t[:], in_=acc[:], func=AF.Relu), reads=[acc], writes=[t])
        P.op("dve", lambda e: e.tensor_tensor(sb[:], t[:], t[:], ALU.mult), reads=[t], writes=[sb])
        r0 = nb * 512 + j * 128
        store(sb, sb[:], hidT, hidT.t[r0:r0 + 128, tb * 512:(tb + 1) * 512])

    gemm(h2T, 0, TOWN, I.w_mlp_up, 0, DFF, D, "A", epi_relu2)
    x2 = dram("x2", [TOWN, D], F32)
    gemm(hidT, 0, TOWN, I.w_mlp_down, 0, D, DFF, "B", epi_resid(x1, 0, x2))
    h3T = dram("h3T", [D, TOWN], BF16)
    norm_T(x2, 0, TOWN, I.norm_ple, h3T)
    gate = dram("gate", [TOWN, D], F32)
    proj = dram("proj", [TOWN, D], F32)
    gemm(h3T, 0, TOWN, I.w_ple_gate, 0, D, D, "B", C.epi_tm(gate, AF.Sigmoid))
    pT = dram("pT", [256, TOWN], BF16)
    ppB = P.ext(I.pp)
    for tb in range(TOWN // 512):
        pTs = e16.next(), e16.next()
        for tt in range(4):
            pt_in = e32.next()
            r0 = tb * 512 + tt * 128
            P.dma("sp", lambda e, pt_in=pt_in, r0=r0: e.dma_start(out=pt_in[:, 0:256], in_=I.pp[r0:r0 + 128, :]), ppB, pt_in)
            ps = C.ps.next()
            for q in range(2):
                P.op("pe", lambda e, ps=ps, pt_in=pt_in, q=q: e.transpose(ps[:, q * 128:(q + 1) * 128], pt_in[:, q * 128:(q + 1) * 128], C.ident[:]),
                     reads=[pt_in, C.ident], writes=[ps])
            for q in range(2):
                P.op("dve", lambda e, ps=ps, q=q, tt=tt, pTs=pTs: e.tensor_copy(pTs[q][:, tt * 128:(tt + 1) * 128], ps[:, q * 128:(q + 1) * 128]),
                     reads=[ps], writes=[pTs[q]])
        for q in range(2):
            store(pTs[q], pTs[q][:], pT, pT.t[q * 128:(q + 1) * 128, tb * 512:(tb + 1) * 512])
    gemm(pT, 0, TOWN, I.w_ple_proj, 0, D, 256, "B", C.epi_tm(proj, AF.Copy))

    gpp, gfn = C.big[0], C.big[1]
    P.dma("sp", lambda e: e.dma_start(out=gpp[:], in_=bc_ap(I.ple_post, 128)), P.ext(I.ple_post), gpp)
    P.dma("sp", lambda e: e.dma_start(out=gfn[:], in_=bc_ap(I.final_norm, 128)), P.ext(I.final_norm), gfn)
    big = Ring(C.big[2:5])
    junk = C.junk
    outB = P.ext(out)
    for tt in range(TOWN // 128):
        rs = slice(tt * 128, (tt + 1) * 128)
        pr, gt, x2t = big.next(), big.next(), big.next()
        P.dma("sp", lambda e, pr=pr, rs=rs: e.dma_start(out=pr[:], in_=proj.t[rs, :]), proj, pr)
        P.dma("sp", lambda e, gt=gt, rs=rs: e.dma_start(out=gt[:], in_=gate.t[rs, :]), gate, gt)
        P.dma("sp", lambda e, x2t=x2t, rs=rs: e.dma_start(out=x2t[:], in_=x2.t[rs, :]), x2, x2t)
        ss = C.ssr.next()
        P.op("act", lambda e, pr=pr, ss=ss: e.activation(out=junk[:], in_=pr[:], func=AF.Square, accum_out=ss[:, 0:1]), reads=[pr], writes=[junk, ss])
        C.rstd_op(ss, 0, 1, D)
        P.op("dve", lambda e, pr=pr, ss=ss: e.scalar_tensor_tensor(pr[:], pr[:], ss[:, 1:2], gpp[:], ALU.mult, ALU.mult), reads=[pr, ss, gpp], writes=[pr])
        P.op("dve", lambda e, pr=pr, gt=gt: e.tensor_tensor(pr[:], pr[:], gt[:], ALU.mult), reads=[pr, gt], writes=[pr])
        P.op("dve", lambda e, pr=pr, x2t=x2t: e.tensor_tensor(x2t[:], x2t[:], pr[:], ALU.add), reads=[pr, x2t], writes=[x2t])
        P.op("act", lambda e, x2t=x2t, ss=ss: e.activation(out=junk[:], in_=x2t[:], func=AF.Square, accum_out=ss[:, 2:3]), reads=[x2t], writes=[junk, ss])
        C.rstd_op(ss, 2, 3, D)
        P.op("dve", lambda e, x2t=x2t, ss=ss, gt=gt: e.scalar_tensor_tensor(gt[:], x2t[:], ss[:, 3:4], gfn[:], ALU.mult, ALU.mult), reads=[x2t, ss, gfn], writes=[gt])
        store(gt, gt[:], outB, out[rs, :])
    fin = [outB]
    for name in DUMP:
        src = C.dr[name]
        o = nc.dram_tensor("dump_" + name, list(src.t.shape), src.t.dtype, kind="ExternalOutput").ap()
        ob_ = P.ext(o)
        P.dma("sp", lambda e, o=o, src=src: e.dma_start(out=o, in_=src.t), src, ob_)
        fin.append(ob_)
    P.wait_all("sp", fin)
    P.end_phase()
    P.emit()
    return nc


def host_consts(c):
    tiles, ktab = attn_tiles()
    m = {"ident": np.eye(128, dtype=np.float32)}
    j = np.arange(128)[:, None]
    i = np.arange(128)[None, :]
    m["amask"] = np.concatenate([(j >= i), (j <= i)], 1).astype(np.float32)
    vcol = np.zeros((128, len(ktab)), np.float32)
    first_valid = TA - (c + 1) * TOWN
    for (g, r, s0, n), col in ktab.items():
        d = (1, 4, 16)[g]
        u = r + d * (s0 + np.arange(n))
        vcol[:n, col] = (u >= first_valid)
    m["vcol"] = vcol
    dnc = np.zeros((128, 6 * 512), np.float32)
    a = np.arange(64)
    I64 = np.eye(64, dtype=np.float32)
    low_incl = (a[:, None] >= a[None, :]).astype(np.float32)
    low_strict = (a[:, None] > a[None, :]).astype(np.float32)
    dnc[:64, 0:512] = np.tile(I64, (1, 8))
    dnc[:64, 512:1024] = np.tile(low_strict, (1, 8))
    dnc[:64, 1024:1536] = np.tile(low_strict.T, (1, 8))
    dnc[:64, 1536:2048] = np.tile((1 - low_incl) * -30000.0, (1, 8))
    dnc[:64, 2048:2560] = np.tile((1 - low_incl.T) * -30000.0, (1, 8))
    dnc[:64, 2560:2624] = (a[:, None] <= a[None, :])
    dnc[:, 2624:2752] = 1.0
    dnc[:, 2752:2880] = -1.0
    dnc[:64, 2880:2944] = -I64
    m["dnc"] = dnc
    return m


def kernel(**inputs):
    f = lambda k: np.ascontiguousarray(np.asarray(inputs[k], np.float32)[0])
    x = f("x")
    p = np.asarray(inputs["p"], np.float32)[0, 0]
    shared = {k: f(k) for k in ("w_in", "w_attn_up", "w_dn_up", "w_out", "w_mlp_up", "w_mlp_down", "w_ple_gate",
                                "w_ple_proj", "norm_mix", "norm_mlp", "norm_ple", "ple_post_norm", "dn_a_log", "dn_dt_bias", "dn_norm")}
    shared["final_norm"] = np.ascontiguousarray(np.asarray(inputs["final_norm"], np.float32))
    shared["conv_wT"] = np.ascontiguousarray(f("conv_w").T)
    in_maps = []
    for c in range(NCORES):
        m = dict(shared)
        m.update(host_consts(c))
        xr = np.zeros((S, D), np.float32)
        xr[S - (c + 1) * TOWN:] = x[:(c + 1) * TOWN]
        m["xr"] = xr
        m["pp"] = np.ascontiguousarray(p[c * TOWN:(c + 1) * TOWN])
        in_maps.append(m)
    nc = build_all()
    res = run_bass_kernel_spmd(nc, in_maps, core_ids=list(range(NCORES)), **({"trace": True} if TRACE else {}))
    kernel.last = res
    return np.concatenate([r["out"] for r in res.results], 0)[None].astype(np.float32)
```

```python
import numpy as np
from contextlib import ExitStack
import concourse.bass as bass
import concourse.mybir as mybir
from concourse.bass_utils import run_bass_kernel_spmd

F32 = mybir.dt.float32
BF16 = mybir.dt.bfloat16
AF = mybir.ActivationFunctionType
ALU = mybir.AluOpType
AX = mybir.AxisListType
ENGS = ["pe", "act", "dve", "pool", "sp"]

NCORES = 8
S = 8192
D = 4096
DFF = 16384
TOWN = 1024
HALO = 2048
TA = TOWN + HALO
EPS = 1e-6
QA, KA, VA, QB, KB, VB, ZB, BETA, ALPHA, GA = 0, 3072, 6144, 9216, 11264, 13312, 15360, 17408, 17424, 17440
DUMP = []
STAGES = {"attn", "dn", "tail"}
DN_CHUNKS = None
DN_CUT = 99
DN_SET = 99
TRACE = False


class Buf:
    def __init__(self, t, kind):
        self.t = t
        self.kind = kind
        self.lastw = None
        self.readers = []
        self.sem = None

    def __getitem__(self, k):
        return self.t[k]


class Prog:
    def __init__(self, nc):
        self.nc = nc
        self.es = ExitStack()
        self.q = {e: [] for e in ENGS}
        self.cnt = {e: 0 for e in ENGS}
        self.waited = {e: {} for e in ENGS}
        self.psem = {e: self.es.enter_context(nc.semaphore("prog_" + e)) for e in ENGS}
        self.dcnt = {}
        self._uid = 0
        self.allsems = []
        self.sem_pool = []
        self.phase_es = None
        self.phase_bufs = []

    def uid(self, p):
        self._uid += 1
        return f"{p}{self._uid}"

    def sbuf(self, shape, dt, name=None):
        es = self.phase_es if self.phase_es is not None else self.es
        b = Buf(es.enter_context(self.nc.sbuf_tensor(name or self.uid("sb"), list(shape), dt)), "sb")
        if self.phase_es is not None:
            self.phase_bufs.append(b)
        return b

    def begin_phase(self):
        assert self.phase_es is None
        self.phase_es = ExitStack()
        self.phase_bufs = []

    def barrier(self):
        deps = [(self.psem[x], self.cnt[x], x) for x in ENGS if self.cnt[x] > 0]
        deps += [v for x, v in getattr(self, "prev_final", {}).items() if self.cnt[x] == 0]
        deps += [(sem, self.dcnt[id(sem)], "dma") for sem in self.allsems if self.dcnt[id(sem)] > 0]
        for e in ENGS:
            w = self._waits(e, [d for d in deps if not (d[2] == e)] + [d for d in deps if d[2] == e and e != "pe"])
            if w:
                self.q[e].append((w, None, None))

    def end_phase(self):
        self.barrier()
        for b in self.phase_bufs:
            if b.sem is not None:
                self.sem_pool.append(b.sem)
        self.phase_es.close()
        self.phase_es = None
        self.phase_bufs = []

    def psum(self, shape, dt, name=None):
        return Buf(self.es.enter_context(self.nc.psum_tensor(name or self.uid("ps"), list(shape), dt)), "ps")

    def dram(self, shape, dt, name=None):
        return Buf(self.nc.dram_tensor(name or self.uid("dr"), list(shape), dt).ap(), "dr")

    def ext(self, ap):
        return Buf(ap, "dr")

    def _bsem(self, b):
        if b.sem is None:
            if self.sem_pool:
                b.sem = self.sem_pool.pop()
            else:
                b.sem = self.es.enter_context(self.nc.semaphore(self.uid("ds")))
                self.dcnt[id(b.sem)] = 0
                self.allsems.append(b.sem)
        return b.sem

    def _waits(self, eng, deps):
        w = []
        for d in deps:
            if d is None:
                continue
            sem, val, src = d
            if src == "pe" and eng == "pe":
                continue
            key = id(sem)
            if self.waited[eng].get(key, 0) >= val:
                continue
            self.waited[eng][key] = val
            w.append((sem, val))
        return w

    def _deps(self, reads, writes, waw=True):
        deps = []
        for b in reads:
            deps.append(b.lastw)
            if b.kind == "ps":
                deps += b.readers
        for b in writes:
            if waw:
                deps.append(b.lastw)
            deps += b.readers
        return deps

    def _commit(self, tok, reads, writes):
        for b in writes:
            b.lastw = tok
            b.readers = []
        for b in reads:
            b.readers.append(tok)

    EPOCH = 30000

    def op(self, eng, fn, reads=(), writes=(), signal=True):
        self.pend = getattr(self, "pend", {})
        if self.cnt[eng] >= self.EPOCH and not self.pend.get(eng):
            self.prev_final = getattr(self, "prev_final", {})
            self.prev_final[eng] = (self.psem[eng], self.cnt[eng], eng)
            self.psem[eng] = self.es.enter_context(self.nc.semaphore(self.uid("prog_" + eng)))
            self.cnt[eng] = 0
            self.total = getattr(self, "total", {})
            self.total[eng] = self.total.get(eng, 0) + self.EPOCH
        w = self._waits(eng, self._deps(reads, writes))
        if not signal:
            assert eng == "pe"
            self.pend[eng] = True
            tok = (self.psem[eng], self.cnt[eng] + 1, eng)
            self.q[eng].append((w, fn, None))
            self._commit(tok, reads, writes)
            return tok
        self.pend[eng] = False
        self.cnt[eng] += 1
        tok = (self.psem[eng], self.cnt[eng], eng)
        self.q[eng].append((w, fn, (self.psem[eng], 1)))
        self._commit(tok, reads, writes)
        return tok

    def dma(self, eng, fn, src, dst, waw=True):
        w = self._waits(eng, self._deps([src], [dst], waw=waw))
        sem = self._bsem(dst)
        if self.dcnt[id(sem)] + 16 > self.EPOCH:
            dst.sem = None
            sem = self._bsem(dst)
        self.dcnt[id(sem)] += 16
        tok = (sem, self.dcnt[id(sem)], "dma")
        self.q[eng].append((w, fn, (sem, 16)))
        if waw:
            self._commit(tok, [src], [dst])
        else:
            dst.lastw = tok
            src.readers.append(tok)
        return tok

    def wait_all(self, eng, bufs):
        w = self._waits(eng, [b.lastw for b in bufs])
        if w:
            self.q[eng].append((w, None, None))

    def emit(self):
        nc = self.nc
        print("ops per engine:", {e: len(self.q[e]) for e in ENGS}, "sems:", len(self.allsems))
        with nc.Block() as block:
            def run(e, engine):
                for w, fn, inc in self.q[e]:
                    for sem, val in w:
                        engine.wait_ge(sem, val)
                    if fn is None:
                        continue
                    ins = fn(engine)
                    if inc is not None:
                        ins.then_inc(inc[0], inc[1])

            @block.tensor
            def _(t):
                run("pe", t)

            @block.scalar
            def _(t):
                run("act", t)

            @block.vector
            def _(t):
                run("dve", t)

            @block.gpsimd
            def _(t):
                run("pool", t)

            @block.sync
            def _(t):
                run("sp", t)
        self.es.close()


class Ring:
    def __init__(self, bufs):
        self.bufs = bufs
        self.i = 0

    def next(self):
        b = self.bufs[self.i % len(self.bufs)]
        self.i += 1
        return b


def bc_ap(ap, nparts):
    n = ap.shape[-1]
    return bass.AP(ap.tensor, ap.offset, [[0, nparts], [1, n]])


def attn_tiles():
    tiles, kt = [], {}
    for g, d in enumerate((1, 4, 16)):
        n_own, s0 = TOWN // d, HALO // d
        QW = min(128, n_own)
        for r in range(d):
            for a in range(s0, s0 + n_own, QW):
                for key in ((g, r, a - 128, 128), (g, r, a, QW)):
                    if key not in kt:
                        kt[key] = len(kt)
                tiles.append((g, d, r, a, QW))
    return tiles, kt


class Ctx:
    pass


def build():
    nc = bass.Bass("TRN2", target_bir_lowering=False)
    IN_W = GA + 2 * D
    dt_in = lambda n, s: nc.dram_tensor(n, list(s), F32, kind="ExternalInput").ap()
    I = Ctx()
    I.xr = dt_in("xr", [S, D])
    I.pp = dt_in("pp", [TOWN, 256])
    I.w_in = dt_in("w_in", [D, IN_W])
    I.conv_wT = dt_in("conv_wT", [6144, 4])
    I.dn_a_log = dt_in("dn_a_log", [16])
    I.dn_dt_bias = dt_in("dn_dt_bias", [16])
    I.dn_norm = dt_in("dn_norm", [128])
    I.w_attn_up = dt_in("w_attn_up", [1024, D])
    I.w_dn_up = dt_in("w_dn_up", [2048, D])
    I.w_out = dt_in("w_out", [D, D])
    I.w_mlp_up = dt_in("w_mlp_up", [D, DFF])
    I.w_mlp_down = dt_in("w_mlp_down", [DFF, D])
    I.w_ple_gate = dt_in("w_ple_gate", [D, D])
    I.w_ple_proj = dt_in("w_ple_proj", [256, D])
    I.norm_mix = dt_in("norm_mix", [D])
    I.norm_mlp = dt_in("norm_mlp", [D])
    I.norm_ple = dt_in("norm_ple", [D])
    I.ple_post = dt_in("ple_post_norm", [D])
    I.final_norm = dt_in("final_norm", [D])
    I.ident = dt_in("ident", [128, 128])
    I.amask = dt_in("amask", [128, 256])
    ntile, ktab = attn_tiles()
    I.vcol = dt_in("vcol", [128, len(ktab)])
    I.dnc = dt_in("dnc", [128, 6 * 512])
    out = nc.dram_tensor("out", [TOWN, D], F32, kind="ExternalOutput").ap()

    P = Prog(nc)
    C = Ctx()
    C.I, C.P, C.nc, C.out = I, P, nc, out
    C.ps = Ring([P.psum([128, 512], F32) for _ in range(8)])
    C.ident = P.sbuf([128, 128], F32)
    P.dma("sp", lambda e: e.dma_start(out=C.ident[:], in_=I.ident), P.ext(I.ident), C.ident)
    C.dr = {}

    def dram(name, shape, dt):
        C.dr[name] = P.dram(shape, dt, name="scr_" + name)
        return C.dr[name]
    C.dram = dram

    def gemm_bufs(norm=True):
        C.wring = Ring([P.sbuf([128, 8, 512], BF16) for _ in range(4)])
        C.xring = Ring([P.sbuf([128, 32, 512], BF16) for _ in range(2)])
        C.st32 = Ring([P.sbuf([128, 512], F32) for _ in range(4)])
        C.st16 = Ring([P.sbuf([128, 512], BF16) for _ in range(4)])
        if norm:
            C.big = [P.sbuf([128, D], F32) for _ in range(5)]
            C.junk = P.sbuf([128, D], BF16)
            C.ssr = Ring([P.sbuf([128, 4], F32) for _ in range(2)])
    C.gemm_bufs = gemm_bufs

    def store(sb, sb_ap, dr, dr_ap, q="sp"):
        P.dma(q, lambda e: e.dma_start(out=dr_ap, in_=sb_ap), sb, dr, waw=False)
    C.store = store

    def gemm(xT, t0, T, W, n0, N, K, form, epi, epi_grp=None):
        KC = K // 128
        wv = W.rearrange("(kc p) n -> p kc n", p=128)
        xv = xT.t.rearrange("(kc p) t -> p kc t", p=128)
        Wb = P.ext(W)
        nsup = max(1, KC // 32)
        kcs = min(KC, 32)
        for tb in range(T // 512):
            ts = t0 + tb * 512
            xs = None
            for nb in range((N + 511) // 512):
                ncols = min(512, N - nb * 512)
                accs = [C.ps.next() for _ in range(4)]
                nj = 4 if form == "B" else (ncols + 127) // 128
                for sup in range(nsup):
                    if xs is None or nsup > 1:
                        xs = C.xring.next()
                        P.dma("sp", lambda e, xs=xs, sup=sup, ts=ts: e.dma_start(
                            out=xs[:, 0:kcs, :], in_=xv[:, sup * 32:sup * 32 + kcs, ts:ts + 512]), xT, xs)
                    for kg in range((kcs + 7) // 8):
                        nk = min(8, kcs - kg * 8)
                        wp = C.wring.next()
                        k0 = sup * 32 + kg * 8
                        P.dma("pool", lambda e, wp=wp, k0=k0, nk=nk, nb=nb, ncols=ncols: e.dma_start(
                            out=wp[:, 0:nk, 0:ncols], in_=wv[:, k0:k0 + nk, n0 + nb * 512:n0 + nb * 512 + ncols]), Wb, wp)
                        for kc in range(nk):
                            first = (sup == 0 and kg == 0 and kc == 0)
                            last = (sup == nsup - 1 and kg * 8 + kc == kcs - 1)
                            for j in range(nj):
                                sig = last or (kc == nk - 1 and j == nj - 1)
                                if form == "A":
                                    mc = min(128, ncols - j * 128)
                                    P.op("pe", lambda e, a=accs[j], wp=wp, xs=xs, kc=kc, j=j, mc=mc, kk=kg * 8 + kc, f=first, l=last: e.matmul(
                                        a[0:mc, :], wp[:, kc, j * 128:j * 128 + mc], xs[:, kk, :], start=f, stop=l),
                                        reads=[wp, xs], writes=[accs[j]], signal=sig)
                                else:
                                    P.op("pe", lambda e, a=accs[j], wp=wp, xs=xs, kc=kc, j=j, nc_=ncols, kk=kg * 8 + kc, f=first, l=last: e.matmul(
                                        a[:, 0:nc_], xs[:, kk, j * 128:(j + 1) * 128], wp[:, kc, 0:nc_], start=f, stop=l),
                                        reads=[wp, xs], writes=[accs[j]], signal=sig)
                if epi_grp is not None:
                    epi_grp(tb, nb, accs[:nj], ncols)
                else:
                    for j in range(nj):
                        epi(tb, nb, j, accs[j], ncols)
    C.gemm = gemm

    def epi_store_T(dst, dt, func=None):
        def epi(tb, nb, j, acc, ncols):
            mc = min(128, ncols - j * 128)
            sb = (C.st16 if dt == BF16 else C.st32).next()
            if func is None:
                P.op("dve", lambda e: e.tensor_copy(sb[0:mc, :], acc[0:mc, :]), reads=[acc], writes=[sb])
            else:
                P.op("act", lambda e: e.activation(out=sb[0:mc, :], in_=acc[0:mc, :], func=func), reads=[acc], writes=[sb])
            r0 = nb * 512 + j * 128
            store(sb, sb[0:mc, :], dst, dst.t[r0:r0 + mc, tb * 512:(tb + 1) * 512])
        return epi
    C.epi_store_T = epi_store_T

    def epi_tm(dst, func, dt=F32):
        def epi(tb, nb, j, acc, ncols):
            t = (C.st16 if dt == BF16 else C.st32).next()
            P.op("act", lambda e: e.activation(out=t[:, 0:ncols], in_=acc[:, 0:ncols], func=func), reads=[acc], writes=[t])
            r0 = tb * 512 + j * 128
            store(t, t[:, 0:ncols], dst, dst.t[r0:r0 + 128, nb * 512:nb * 512 + ncols])
        return epi
    C.epi_tm = epi_tm

    def rstd_op(ss, i, o, n):
        P.op("dve", lambda e: e.tensor_scalar(ss[:, o:o + 1], ss[:, i:i + 1], 1.0 / n, EPS, ALU.mult, ALU.add), reads=[ss], writes=[ss])
        P.op("act", lambda e: e.activation(out=ss[:, o:o + 1], in_=ss[:, o:o + 1], func=AF.Sqrt), reads=[ss], writes=[ss])
        P.op("dve", lambda e: e.reciprocal(ss[:, o:o + 1], ss[:, o:o + 1]), reads=[ss], writes=[ss])
    C.rstd_op = rstd_op

    def norm_T(src, srcT0, T, gain_ap, dstT):
        g = C.big[0]
        P.dma("sp", lambda e: e.dma_start(out=g[:], in_=bc_ap(gain_ap, 128)), P.ext(gain_ap), g)
        xr = Ring(C.big[1:3])
        hr = Ring(C.big[3:5])
        junk = C.junk
        KC = D // 128
        for tb in range(T // 512):
            hT = C.xring.next()
            for tt in range(4):
                r0 = srcT0 + tb * 512 + tt * 128
                xt = xr.next()
                P.dma("sp", lambda e, xt=xt, r0=r0: e.dma_start(out=xt[:], in_=src.t[r0:r0 + 128, :]), src, xt)
                ss = C.ssr.next()
                P.op("act", lambda e, xt=xt, ss=ss: e.activation(out=junk[:], in_=xt[:], func=AF.Square, accum_out=ss[:, 0:1]),
                     reads=[xt], writes=[junk, ss])
                rstd_op(ss, 0, 1, D)
                hb = hr.next()
                P.op("dve", lambda e, hb=hb, xt=xt, ss=ss: e.scalar_tensor_tensor(hb[:], xt[:], ss[:, 1:2], g[:], ALU.mult, ALU.mult),
                     reads=[xt, ss, g], writes=[hb])
                for k4 in range(KC // 4):
                    pt = C.ps.next()
                    for q in range(4):
                        kc = k4 * 4 + q
                        P.op("pe", lambda e, pt=pt, hb=hb, kc=kc, q=q: e.transpose(pt[:, q * 128:(q + 1) * 128], hb[:, kc * 128:(kc + 1) * 128], C.ident[:]),
                             reads=[hb, C.ident], writes=[pt])
                    if k4 % 2:
                        P.op("act", lambda e, pt=pt, hT=hT, k4=k4, tt=tt: e.activation(
                            out=hT[:, k4 * 4:k4 * 4 + 4, tt * 128:(tt + 1) * 128], in_=pt[:].rearrange("p (q t) -> p q t", q=4), func=AF.Copy),
                            reads=[pt], writes=[hT])
                    else:
                        P.op("dve", lambda e, pt=pt, hT=hT, k4=k4, tt=tt: e.tensor_copy(
                            hT[:, k4 * 4:k4 * 4 + 4, tt * 128:(tt + 1) * 128], pt[:].rearrange("p (q t) -> p q t", q=4)),
                            reads=[pt], writes=[hT])
            store(hT, hT[:, 0:KC, :], dstT, dstT.t.rearrange("(kc p) t -> p kc t", p=128)[:, :, tb * 512:(tb + 1) * 512])
    C.norm_T = norm_T
    return C


def attention(C, qaT, kaT, vtok, o_aT):
    P, I = C.P, C.I
    tiles, ktab = attn_tiles()
    P.begin_phase()
    amask = P.sbuf([128, 256], BF16)
    P.dma("pool", lambda e: e.dma_start(out=amask[:], in_=I.amask), P.ext(I.amask), amask)
    vcol = P.sbuf([128, len(ktab)], F32)
    P.dma("sp", lambda e: e.dma_start(out=vcol[:], in_=I.vcol), P.ext(I.vcol), vcol)
    ones = P.sbuf([128, 128], BF16)
    P.op("dve", lambda e: e.memset(ones[:], 1.0), writes=[ones])
    qTr = Ring([P.sbuf([128, TOWN], BF16) for _ in range(2)])
    kTr = Ring([P.sbuf([128, TA], BF16) for _ in range(2)])
    geo = {0: (1, 15, 9), 1: (4, 3, 3), 2: (16, 0, 2)}
    vts = {g: P.sbuf([128, geo[g][0], geo[g][2], 128], BF16) for g in range(3)}
    ULr = Ring([P.sbuf([128, 2, TOWN], F32) for _ in range(2)])
    Er = Ring([P.sbuf([128, 256], BF16) for _ in range(3)])
    Ewr = Ring([P.sbuf([128, 256], BF16) for _ in range(3)])
    ob = Ring([P.sbuf([128, TOWN], BF16) for _ in range(2)])
    rl = P.sbuf([128, TOWN], F32)
    scale = 128.0 ** -0.5
    for hh in range(8):
        UL = ULr.next()
        for g in range(3):
            d, bt0, nbt = geo[g]
            row0 = g * 1024 + hh * 128
            qT, kT, vt = qTr.next(), kTr.next(), vts[g]
            P.dma("sp", lambda e, qT=qT, row0=row0: e.dma_start(out=qT[:], in_=qaT.t[row0:row0 + 128, :]), qaT, qT)
            P.dma("sp", lambda e, kT=kT, row0=row0: e.dma_start(out=kT[:], in_=kaT.t[row0:row0 + 128, :]), kaT, kT)
            for r in range(d):
                if g < 2:
                    src = vtok.t[bass.ds(r + d * 128 * bt0, 128 * nbt, step=d), row0:row0 + 128].rearrange("(bt j) c -> j bt c", j=128)
                    P.dma("sp", lambda e, vt=vt, r=r, src=src: e.dma_start(out=vt[:, r, :, :], in_=src), vtok, vt)
                else:
                    src0 = vtok.t[bass.ds(r, 128, step=d), row0:row0 + 128]
                    src1 = vtok.t[bass.ds(r + d * 128, 64, step=d), row0:row0 + 128]
                    P.dma("sp", lambda e, vt=vt, r=r, src0=src0: e.dma_start(out=vt[:, r, 0, :], in_=src0), vtok, vt)
                    P.dma("sp", lambda e, vt=vt, r=r, src1=src1: e.dma_start(out=vt[0:64, r, 1, :], in_=src1), vtok, vt)
            for (tg, td, r, a, QW) in tiles:
                if tg != g:
                    continue
                k0 = ktab[(g, r, a - 128, 128)]
                k1 = ktab[(g, r, a, QW)]
                kc0 = bass.ds(r + d * (a - 128), 128, step=d)
                kc1 = bass.ds(r + d * a, QW, step=d)
                qc = bass.ds(r + d * a - HALO, QW, step=d)
                b0, b1 = (a - 128) // 128 - bt0, a // 128 - bt0
                ps_s, ps_u = C.ps.next(), C.ps.next()
                P.op("pe", lambda e, ps_s=ps_s, kT=kT, qT=qT, kc0=kc0, qc=qc, QW=QW: e.matmul(ps_s[:, 0:QW], kT[:, kc0], qT[:, qc], start=True, stop=True),
                     reads=[kT, qT], writes=[ps_s])
                P.op("pe", lambda e, ps_s=ps_s, kT=kT, qT=qT, kc1=kc1, qc=qc, QW=QW: e.matmul(ps_s[0:QW, 128:128 + QW], kT[:, kc1], qT[:, qc], start=True, stop=True),
                     reads=[kT, qT], writes=[ps_s])
                Er_, E = Er.next(), Ewr.next()
                P.op("act", lambda e, Er_=Er_, ps_s=ps_s, QW=QW: e.activation(out=Er_[:, 0:QW], in_=ps_s[:, 0:QW], func=AF.Exp, scale=scale), reads=[ps_s], writes=[Er_])
                P.op("act", lambda e, Er_=Er_, ps_s=ps_s, QW=QW: e.activation(out=Er_[0:QW, 128:128 + QW], in_=ps_s[0:QW, 128:128 + QW], func=AF.Exp, scale=scale), reads=[ps_s], writes=[Er_])
                P.op("dve", lambda e, E=E, Er_=Er_, k0=k0, QW=QW: e.scalar_tensor_tensor(E[:, 0:QW], Er_[:, 0:QW], vcol[:, k0:k0 + 1], amask[:, 0:QW], ALU.mult, ALU.mult),
                     reads=[Er_, vcol, amask], writes=[E])
                P.op("dve", lambda e, E=E, Er_=Er_, k1=k1, QW=QW: e.scalar_tensor_tensor(E[0:QW, 128:128 + QW], Er_[0:QW, 128:128 + QW], vcol[0:QW, k1:k1 + 1], amask[0:QW, 128:128 + QW], ALU.mult, ALU.mult),
                     reads=[Er_, vcol, amask], writes=[E])
                P.op("pe", lambda e, ps_u=ps_u, vt=vt, r=r, b0=b0, E=E, QW=QW: e.matmul(ps_u[:, 0:QW], vt[:, r, b0, :], E[:, 0:QW], start=True, stop=False), reads=[vt, E], writes=[ps_u])
                P.op("pe", lambda e, ps_u=ps_u, vt=vt, r=r, b1=b1, E=E, QW=QW: e.matmul(ps_u[:, 0:QW], vt[0:QW, r, b1, :], E[0:QW, 128:128 + QW], start=False, stop=True), reads=[vt, E], writes=[ps_u])
                P.op("pe", lambda e, ps_u=ps_u, E=E, QW=QW: e.matmul(ps_u[:, 128:128 + QW], ones[:, :], E[:, 0:QW], start=True, stop=False), reads=[ones, E], writes=[ps_u])
                P.op("pe", lambda e, ps_u=ps_u, E=E, QW=QW: e.matmul(ps_u[:, 128:128 + QW], ones[0:QW, :], E[0:QW, 128:128 + QW], start=False, stop=True), reads=[ones, E], writes=[ps_u])
                for w in range(2):
                    if g == 0:
                        P.op("dve", lambda e, UL=UL, ps_u=ps_u, w=w, qc=qc, QW=QW: e.tensor_copy(UL[:, w, qc], ps_u[:, w * 128:w * 128 + QW]), reads=[ps_u], writes=[UL])
                    else:
                        P.op("dve", lambda e, UL=UL, ps_u=ps_u, w=w, qc=qc, QW=QW: e.tensor_tensor(UL[:, w, qc], UL[:, w, qc], ps_u[:, w * 128:w * 128 + QW], ALU.add), reads=[ps_u, UL], writes=[UL])
        o = ob.next()
        P.op("dve", lambda e, UL=UL: e.reciprocal(rl[:], UL[:, 1, :]), reads=[UL], writes=[rl])
        P.op("dve", lambda e, UL=UL, o=o: e.tensor_tensor(o[:], UL[:, 0, :], rl[:], ALU.mult), reads=[UL, rl], writes=[o])
        C.store(o, o[:], o_aT, o_aT.t[hh * 128:(hh + 1) * 128, :])
    P.end_phase()


def v3(ap, h):
    return ap.rearrange("p (h j) -> p h j", h=h)


def bcl(ap2, n):
    p, h = ap2.shape
    return ap2.unsqueeze(2).broadcast_to([p, h, n])


def dn_prep_bufs(C):
    P, I = C.P, C.I
    Z = Ctx()
    Z.cw = P.sbuf([128, 48, 4], F32)
    P.dma("sp", lambda e: e.dma_start(out=Z.cw[:], in_=I.conv_wT.rearrange("(c p) j -> p c j", p=128)), P.ext(I.conv_wT), Z.cw)
    Z.ones = P.sbuf([128, 128], F32)
    P.op("dve", lambda e: e.memset(Z.ones[:], 1.0), writes=[Z.ones])
    Z.halo = P.sbuf([128, 48, 3], F32)
    P.op("dve", lambda e: e.memset(Z.halo[:], 0.0), writes=[Z.halo])
    mk = lambda w: Ring([P.sbuf([128, w], F32) for _ in range(4)])
    Z.xr, Z.yr, Z.y2r, Z.sqr, Z.rr, Z.y3r = mk(515), mk(512), mk(512), mk(512), mk(512), mk(512)
    return Z


def dn_prep_epi(C, Z, fc0, dst_of):
    P = C.P
    cw, ones, halo = Z.cw, Z.ones, Z.halo

    def epi_grp(tb, nb, accs, ncols):
        st = []
        for j, acc in enumerate(accs):
            fc = fc0 + nb * 4 + j
            dst, drow = dst_of(fc)
            x = Z.xr.next()
            P.op("act", lambda e, x=x, acc=acc: e.activation(out=x[:, 3:515], in_=acc[:, :], func=AF.Copy), reads=[acc], writes=[x])
            st.append(dict(fc=fc, x=x, y=Z.yr.next(), y2=Z.y2r.next(), dst=dst, drow=drow, blk=tb))
        for t in st:
            P.op("dve", lambda e, t=t: e.tensor_copy(t["x"][:, 0:3], halo[:, t["fc"], :]), reads=[halo], writes=[t["x"]])
        for t in st:
            P.op("dve", lambda e, t=t: e.tensor_copy(halo[:, t["fc"], :], t["x"][:, 512:515]), reads=[t["x"]], writes=[halo])
        for t in st:
            P.op("dve", lambda e, t=t: e.tensor_scalar_mul(t["y"][:], t["x"][:, 3:515], cw[:, t["fc"], 3:4]), reads=[t["x"], cw], writes=[t["y"]])
        for j in (2, 1, 0):
            for t in st:
                P.op("dve", lambda e, t=t, j=j: e.scalar_tensor_tensor(t["y"][:], t["x"][:, j:j + 512], cw[:, t["fc"], j:j + 1], t["y"][:], ALU.mult, ALU.add),
                     reads=[t["x"], cw, t["y"]], writes=[t["y"]])
        for t in st:
            P.op("act", lambda e, t=t: e.activation(out=t["y2"][:], in_=t["y"][:], func=AF.Silu), reads=[t["y"]], writes=[t["y2"]])
        nt = [t for t in st if t["fc"] < 32]
        for t in nt:
            t["sq"], t["r"], t["y3"], t["ps"] = Z.sqr.next(), Z.rr.next(), Z.y3r.next(), C.ps.next()
            P.op("dve", lambda e, t=t: e.tensor_tensor(t["sq"][:], t["y2"][:], t["y2"][:], ALU.mult), reads=[t["y2"]], writes=[t["sq"]])
        for t in nt:
            P.op("pe", lambda e, t=t: e.matmul(t["ps"][:, :], ones[:, :], t["sq"][:, :], start=True, stop=True), reads=[ones, t["sq"]], writes=[t["ps"]])
        for t in nt:
            P.op("act", lambda e, t=t: e.activation(out=t["r"][:], in_=t["ps"][:], func=AF.Sqrt, bias=EPS), reads=[t["ps"]], writes=[t["r"]])
        for t in nt:
            P.op("dve", lambda e, t=t: e.reciprocal(t["r"][:], t["r"][:]), reads=[t["r"]], writes=[t["r"]])
        for t in nt:
            sc = 128.0 ** -0.5 if t["fc"] < 16 else 1.0
            P.op("dve", lambda e, t=t, sc=sc: e.scalar_tensor_tensor(t["y3"][:], t["y2"][:], sc, t["r"][:], ALU.mult, ALU.mult), reads=[t["y2"], t["r"]], writes=[t["y3"]])
            t["y2"] = t["y3"]
        for t in st:
            C.store(t["y2"], t["y2"][:], t["dst"], t["dst"].t[t["drow"]:t["drow"] + 128, t["blk"] * 512:(t["blk"] + 1) * 512])
    return epi_grp


def dn_scan(C, ba, khT, vT, qhT, ztm, o_bT):
    P, I = C.P, C.I
    NCH = S // 64
    OWN0 = NCH - TOWN // 64
    TQ = qhT.t.shape[1]
    P.begin_phase()
    dnc = P.sbuf([128, 6 * 512], F32)
    P.dma("sp", lambda e: e.dma_start(out=dnc[:], in_=I.dnc), P.ext(I.dnc), dnc)
    Irep, strictrep, strictTrep = v3(dnc[0:64, 0:512], 8), v3(dnc[0:64, 512:1024], 8), v3(dnc[0:64, 1024:1536], 8)
    mneg, mnegT = dnc[0:64, 1536:2048], dnc[0:64, 2048:2560]
    Utri, ones, negones, negI64 = dnc[0:64, 2560:2624], dnc[:, 2624:2752], dnc[:, 2752:2880], dnc[0:64, 2880:2944]
    I64 = C.ident[0:64, 0:64]
    ident = C.ident
    gc_all = P.sbuf([64, NCH, 16], F32)
    beta_all = P.sbuf([64, NCH, 16], F32)
    kdecs_all = P.sbuf([64, NCH, 16], F32)
    egl_all = P.sbuf([128, NCH, 16], F32)
    alog = P.sbuf([64, 16], F32)
    dtb = P.sbuf([64, 16], F32)
    P.dma("sp", lambda e: e.dma_start(out=alog[:], in_=bc_ap(I.dn_a_log, 64)), P.ext(I.dn_a_log), alog)
    P.dma("sp", lambda e: e.dma_start(out=dtb[:], in_=bc_ap(I.dn_dt_bias, 64)), P.ext(I.dn_dt_bias), dtb)
    P.op("act", lambda e: e.activation(out=alog[:], in_=alog[:], func=AF.Exp), reads=[alog], writes=[alog])
    P.op("dve", lambda e: e.tensor_scalar_mul(alog[:], alog[:], -1.0), reads=[alog], writes=[alog])
    dnn = P.sbuf([64, 128], F32)
    P.dma("sp", lambda e: e.dma_start(out=dnn[:], in_=bc_ap(I.dn_norm, 64)), P.ext(I.dn_norm), dnn)
    QC = 16
    baq = P.sbuf([64, QC, 32], F32)
    spq = P.sbuf([64, QC, 16], F32)
    gq = P.sbuf([64, QC, 16], F32)
    tq = P.sbuf([64, QC, 16], F32)
    bav = ba.t.rearrange("(c t) n -> t c n", t=64)
    for q in range(NCH // QC if DN_SET >= 1 else 0):
        cs = slice(q * QC, (q + 1) * QC)
        P.dma("sp", lambda e, cs=cs: e.dma_start(out=baq[:], in_=bav[:, cs, :]), ba, baq)
        if DN_SET < 2:
            continue
        P.op("act", lambda e, cs=cs: e.activation(out=beta_all[:, cs, :], in_=baq[:, :, 0:16], func=AF.Sigmoid), reads=[baq], writes=[beta_all])
        if DN_SET < 3:
            continue
        P.op("dve", lambda e: e.tensor_tensor(spq[:], baq[:, :, 16:32], dtb[:].unsqueeze(1).broadcast_to([64, QC, 16]), ALU.add), reads=[baq, dtb], writes=[spq])
        P.op("act", lambda e: e.activation(out=spq[:], in_=spq[:], func=AF.Exp), reads=[spq], writes=[spq])
        P.op("act", lambda e: e.activation(out=spq[:], in_=spq[:], func=AF.Ln, bias=1.0), reads=[spq], writes=[spq])
        P.op("dve", lambda e: e.tensor_tensor(gq[:], spq[:], alog[:].unsqueeze(1).broadcast_to([64, QC, 16]), ALU.mult), reads=[spq, alog], writes=[gq])
        if DN_SET < 4:
            continue
        ps1, ps2 = C.ps.next(), C.ps.next()
        gflat = gq[:].rearrange("p c h -> p (c h)")
        P.op("pe", lambda e, ps1=ps1: e.matmul(ps1[0:64, 0:QC * 16], Utri, gflat, start=True, stop=True), reads=[dnc, gq], writes=[ps1])
        P.op("pe", lambda e, ps2=ps2: e.matmul(ps2[:, 0:QC * 16], ones[0:64, :], gflat, start=True, stop=True), reads=[dnc, gq], writes=[ps2])
        if DN_SET < 5:
            continue
        P.op("dve", lambda e, ps1=ps1, cs=cs: e.tensor_copy(gc_all[:, cs, :], v3(ps1[0:64, 0:QC * 16], QC)), reads=[ps1], writes=[gc_all])
        if DN_SET < 6:
            continue
        if DN_SET < 7:
            continue
        P.op("act", lambda e, ps2=ps2, cs=cs: e.activation(out=egl_all[:, cs, :], in_=v3(ps2[:, 0:QC * 16], QC), func=AF.Exp), reads=[ps2], writes=[egl_all])
        if DN_SET < 8:
            continue
        P.op("dve", lambda e, ps2=ps2, cs=cs: e.tensor_tensor(tq[:], v3(ps2[0:64, 0:QC * 16], QC), gc_all[:, cs, :], ALU.subtract), reads=[ps2, gc_all], writes=[tq])
        if DN_SET < 9:
            continue
        P.op("act", lambda e, cs=cs: e.activation(out=kdecs_all[:, cs, :], in_=tq[:], func=AF.Exp), reads=[tq], writes=[kdecs_all])
    if hasattr(C, "dbg"):
        for nm, b_ in (("gc_all", gc_all), ("beta_all", beta_all), ("kdecs_all", kdecs_all), ("egl_all", egl_all)):
            C.dbg(nm, b_, b_[:].rearrange("p c h -> p (c h)"))
    S4 = [P.sbuf([128, 4, 128], F32) for _ in range(4)]
    for s_ in S4:
        P.op("dve", lambda e, s_=s_: e.memset(s_[:], 0.0), writes=[s_])
    kcr = Ring([P.sbuf([128, 16, 64], F32) for _ in range(2)])
    vcr = Ring([P.sbuf([128, 16, 64], F32) for _ in range(2)])
    qc_ = P.sbuf([128, 16, 64], F32)
    zc_ = P.sbuf([64, 2048], F32)
    obT = P.sbuf([128, 16, 64], BF16)
    nb_ = P.sbuf([64, 16], F32)
    kbgs = P.sbuf([64, 16], F32)
    egc = P.sbuf([64, 16], F32)
    T = lambda: P.sbuf([64, 8, 64], F32)
    T4 = lambda: P.sbuf([64, 8, 128], F32)
    F = lambda b: b[:].rearrange("p h j -> p (h j)")

    def mk():
        X = Ctx()
        X.diagG, X.diagB, X.diagE, X.Dm, X.DmT, X.t1, X.t2, X.intraT = (T() for _ in range(8))
        X.sets = [(T(), T(), T()) for _ in range(2)]
        X.qdec, X.wT = P.sbuf([128, 8, 64], F32), P.sbuf([128, 8, 64], F32)
        X.kbg, X.kdec, X.vb, X.u, X.vnew = (T4() for _ in range(5))
        X.osb, X.on = P.sbuf([64, 4, 128], F32), P.sbuf([64, 4, 128], F32)
        X.ss = P.sbuf([64, 8], F32)
        return X
    XS = [mk(), mk()]
    chunks = range(NCH) if DN_CHUNKS is None else list(range(DN_CHUNKS)) + ([OWN0] if DN_CUT >= 8 else [])
    for c in (chunks if DN_CUT >= 1 else []):
        own = c >= OWN0 and DN_CUT >= 8
        kc_, vc_ = kcr.next(), vcr.next()
        P.dma("sp", lambda e, kc_=kc_, c=c: e.dma_start(out=kc_[:], in_=khT.t.rearrange("(h d) t -> d h t", d=128)[:, :, c * 64:(c + 1) * 64]), khT, kc_)
        P.dma("sp", lambda e, vc_=vc_, c=c: e.dma_start(out=vc_[:], in_=vT.t.rearrange("(h d) t -> d h t", d=128)[:, :, c * 64:(c + 1) * 64]), vT, vc_)
        if own:
            q0 = (c - OWN0) * 64 + (TQ - TOWN)
            P.dma("sp", lambda e, q0=q0: e.dma_start(out=qc_[:], in_=qhT.t.rearrange("(h d) t -> d h t", d=128)[:, :, q0:q0 + 64]), qhT, qc_)
            P.dma("sp", lambda e, c=c: e.dma_start(out=zc_[:], in_=ztm.t[(c - OWN0) * 64:(c - OWN0 + 1) * 64, :]), ztm, zc_)
        P.op("dve", lambda e, c=c: e.tensor_scalar_mul(nb_[:], beta_all[:, c, :], -1.0), reads=[beta_all], writes=[nb_])
        P.op("act", lambda e, c=c: e.activation(out=egc[:], in_=gc_all[:, c, :], func=AF.Exp), reads=[gc_all], writes=[egc])
        P.op("dve", lambda e, c=c: e.tensor_tensor(kbgs[:], beta_all[:, c, :], egc[:], ALU.mult), reads=[beta_all, egc], writes=[kbgs])
        HG = (0, 1)
        hsl = lambda hg: slice(hg * 8, hg * 8 + 8)
        for hg in HG:
            X = XS[hg]
            X.gcs = bcl(gc_all[:, c, hsl(hg)], 64)
            P.op("dve", lambda e, X=X, g_=X.gcs: e.tensor_tensor(X.diagG[:], Irep, g_, ALU.mult), reads=[dnc, gc_all], writes=[X.diagG])
            P.op("dve", lambda e, X=X, c=c, hg=hg: e.tensor_tensor(X.diagB[:], Irep, bcl(beta_all[:, c, hsl(hg)], 64), ALU.mult), reads=[dnc, beta_all], writes=[X.diagB])
        for hg in HG:
            X = XS[hg]
            h0 = hg * 8
            X.p1, X.p3, X.p4, X.p5 = C.ps.next(), C.ps.next(), C.ps.next(), C.ps.next()
            for h in range(8):
                P.op("pe", lambda e, p1=X.p1, kc_=kc_, h=h, h0=h0: e.matmul(p1[0:64, h * 64:(h + 1) * 64], kc_[:, h0 + h, :], kc_[:, h0 + h, :], start=True, stop=True),
                     reads=[kc_], writes=[X.p1], signal=(h == 7))
            P.op("pe", lambda e, X=X, p3=X.p3: e.matmul(p3[0:64, :], negones[0:64, 0:64], F(X.diagG), start=True, stop=False), reads=[dnc, X.diagG], writes=[X.p3])
            P.op("pe", lambda e, X=X, g_=X.gcs, p3=X.p3: e.matmul(p3[0:64, :], I64, g_, start=False, stop=False), reads=[ident, gc_all], writes=[X.p3])
            P.op("pe", lambda e, X=X, p3=X.p3: e.matmul(p3[0:64, :], I64, mneg, start=False, stop=True), reads=[ident, dnc], writes=[X.p3])
            P.op("pe", lambda e, X=X, p4=X.p4: e.matmul(p4[0:64, :], ones[0:64, 0:64], F(X.diagG), start=True, stop=False), reads=[dnc, X.diagG], writes=[X.p4])
            P.op("pe", lambda e, X=X, g_=X.gcs, p4=X.p4: e.matmul(p4[0:64, :], negI64, g_, start=False, stop=False), reads=[dnc, gc_all], writes=[X.p4])
            P.op("pe", lambda e, X=X, p4=X.p4: e.matmul(p4[0:64, :], I64, mnegT, start=False, stop=True), reads=[ident, dnc], writes=[X.p4])
            P.op("pe", lambda e, X=X, p5=X.p5: e.matmul(p5[0:64, :], negones[0:64, 0:64], F(X.diagB), start=True, stop=True), reads=[dnc, X.diagB], writes=[X.p5])
        for hg in HG:
            X = XS[hg]
            P.op("act", lambda e, X=X, p3=X.p3: e.activation(out=F(X.Dm), in_=p3[0:64, :], func=AF.Exp), reads=[X.p3], writes=[X.Dm])
            P.op("act", lambda e, X=X, p4=X.p4: e.activation(out=F(X.DmT), in_=p4[0:64, :], func=AF.Exp), reads=[X.p4], writes=[X.DmT])
        if DN_CUT < 3:
            continue
        for hg in HG:
            X = XS[hg]
            P.op("dve", lambda e, X=X, p1=X.p1: e.tensor_tensor(X.t1[:], v3(p1[0:64, :], 8), strictrep, ALU.mult), reads=[X.p1, dnc], writes=[X.t1])
            P.op("dve", lambda e, X=X, p1=X.p1: e.tensor_tensor(X.t2[:], v3(p1[0:64, :], 8), strictTrep, ALU.mult), reads=[X.p1, dnc], writes=[X.t2])
        for hg in HG:
            X = XS[hg]
            P.op("dve", lambda e, X=X: e.tensor_tensor(X.t1[:], X.t1[:], X.Dm[:], ALU.mult), reads=[X.t1, X.Dm], writes=[X.t1])
            P.op("dve", lambda e, X=X: e.tensor_tensor(X.t2[:], X.t2[:], X.DmT[:], ALU.mult), reads=[X.t2, X.DmT], writes=[X.t2])
        for hg in HG:
            X = XS[hg]
            M, MT, R = X.sets[0]
            P.op("dve", lambda e, X=X, M=M, hg=hg: e.tensor_tensor(M[:], X.t1[:], bcl(nb_[:, hsl(hg)], 64), ALU.mult), reads=[X.t1, nb_], writes=[M])
            P.op("dve", lambda e, X=X, MT=MT, p5=X.p5: e.tensor_tensor(MT[:], X.t2[:], v3(p5[0:64, :], 8), ALU.mult), reads=[X.t2, X.p5], writes=[MT])
        for hg in HG:
            M, MT, R = XS[hg].sets[0]
            P.op("dve", lambda e, R=R, MT=MT: e.tensor_tensor(R[:], MT[:], Irep, ALU.add), reads=[MT, dnc], writes=[R])
        if DN_CUT < 4:
            continue
        cur = 0
        for m in range(1, 6):
            for hg in HG:
                X = XS[hg]
                M, MT, R = X.sets[cur]
                X.pm, X.pmt = C.ps.next(), C.ps.next()
                for h in range(8):
                    P.op("pe", lambda e, pm=X.pm, M=M, MT=MT, h=h: e.matmul(pm[0:64, h * 64:(h + 1) * 64], MT[:, h, :], M[:, h, :], start=True, stop=True), reads=[M, MT], writes=[X.pm], signal=(h == 7))
                if m < 5:
                    for h in range(8):
                        P.op("pe", lambda e, pmt=X.pmt, M=M, MT=MT, h=h: e.matmul(pmt[0:64, h * 64:(h + 1) * 64], M[:, h, :], MT[:, h, :], start=True, stop=True), reads=[M, MT], writes=[X.pmt], signal=(h == 7))
            for hg in HG:
                X = XS[hg]
                Mn, MTn, Rn = X.sets[1 - cur]
                P.op("act", lambda e, Mn=Mn, pm=X.pm: e.activation(out=F(Mn), in_=pm[0:64, :], func=AF.Copy), reads=[X.pm], writes=[Mn])
                if m < 5:
                    P.op("dve", lambda e, MTn=MTn, pmt=X.pmt: e.tensor_copy(F(MTn), pmt[0:64, :]), reads=[X.pmt], writes=[MTn])
            for hg in HG:
                X = XS[hg]
                M, MT, R = X.sets[cur]
                Mn, MTn, Rn = X.sets[1 - cur]
                X.pr = C.ps.next()
                for h in range(8):
                    P.op("pe", lambda e, pr=X.pr, Mn=Mn, R=R, h=h: e.matmul(pr[0:64, h * 64:(h + 1) * 64], Mn[:, h, :], R[:, h, :], start=True, stop=True), reads=[Mn, R], writes=[X.pr], signal=(h == 7))
            for hg in HG:
                X = XS[hg]
                M, MT, R = X.sets[cur]
                Mn, MTn, Rn = X.sets[1 - cur]
                P.op("dve", lambda e, Rn=Rn, R=R, pr=X.pr: e.tensor_tensor(F(Rn), F(R), pr[0:64, :], ALU.add), reads=[R, X.pr], writes=[Rn])
            cur = 1 - cur
        if DN_CUT < 5:
            continue
        for half in range(2):
            for hg in HG:
                X = XS[hg]
                h0 = hg * 8
                X.pk, X.pv = C.ps.next(), C.ps.next()
                for hq in range(4):
                    h = h0 + half * 4 + hq
                    P.op("pe", lambda e, pk=X.pk, kc_=kc_, h=h, hq=hq: e.transpose(pk[0:64, hq * 128:(hq + 1) * 128], kc_[:, h, :], ident[:]), reads=[kc_, ident], writes=[X.pk])
                    P.op("pe", lambda e, pv=X.pv, vc_=vc_, h=h, hq=hq: e.transpose(pv[0:64, hq * 128:(hq + 1) * 128], vc_[:, h, :], ident[:]), reads=[vc_, ident], writes=[X.pv])
            for hg in HG:
                X = XS[hg]
                h0 = hg * 8
                a4 = slice(half * 4, half * 4 + 4)
                g4 = slice(h0 + half * 4, h0 + half * 4 + 4)
                P.op("dve", lambda e, X=X, pk=X.pk, a4=a4, g4=g4: e.tensor_tensor(X.kbg[:, a4, :], v3(pk[0:64, :], 4), bcl(kbgs[:, g4], 128), ALU.mult), reads=[X.pk, kbgs], writes=[X.kbg])
                P.op("dve", lambda e, X=X, pk=X.pk, a4=a4, g4=g4, c=c: e.tensor_tensor(X.kdec[:, a4, :], v3(pk[0:64, :], 4), bcl(kdecs_all[:, c, g4], 128), ALU.mult), reads=[X.pk, kdecs_all], writes=[X.kdec])
                P.op("dve", lambda e, X=X, pv=X.pv, a4=a4, g4=g4, c=c: e.tensor_tensor(X.vb[:, a4, :], v3(pv[0:64, :], 4), bcl(beta_all[:, c, g4], 128), ALU.mult), reads=[X.pv, beta_all], writes=[X.vb])
        if DN_CUT < 6:
            continue
        for hg in HG:
            X = XS[hg]
            R = X.sets[cur][2]
            X.pw = C.ps.next()
            for h in range(8):
                P.op("pe", lambda e, X=X, pw=X.pw, R=R, h=h: e.matmul(pw[:, h * 64:(h + 1) * 64], X.kbg[:, h, :], R[:, h, :], start=True, stop=True), reads=[X.kbg, R], writes=[X.pw], signal=(h == 7))
        for hg in HG:
            X = XS[hg]
            P.op("dve", lambda e, X=X, pw=X.pw: e.tensor_copy(F(X.wT), pw[:, :]), reads=[X.pw], writes=[X.wT])
        for half in range(2):
            for hg in HG:
                X = XS[hg]
                R = X.sets[cur][2]
                X.pu = C.ps.next()
                for hq in range(4):
                    h = half * 4 + hq
                    P.op("pe", lambda e, X=X, pu=X.pu, R=R, h=h, hq=hq: e.matmul(pu[0:64, hq * 128:(hq + 1) * 128], R[:, h, :], X.vb[:, h, :], start=True, stop=True), reads=[R, X.vb], writes=[X.pu], signal=(hq == 3))
            for hg in HG:
                X = XS[hg]
                P.op("act", lambda e, X=X, pu=X.pu, half=half: e.activation(out=X.u[:, half * 4:half * 4 + 4, :], in_=v3(pu[0:64, :], 4), func=AF.Copy), reads=[X.pu], writes=[X.u])
        if own:
            for hg in HG:
                X = XS[hg]
                h0 = hg * 8
                X.p2, X.p6 = C.ps.next(), C.ps.next()
                for h in range(8):
                    P.op("pe", lambda e, p2=X.p2, kc_=kc_, h=h, h0=h0: e.matmul(p2[0:64, h * 64:(h + 1) * 64], kc_[:, h0 + h, :], qc_[:, h0 + h, :], start=True, stop=True),
                         reads=[kc_, qc_], writes=[X.p2], signal=(h == 7))
                P.op("dve", lambda e, X=X, hg=hg: e.tensor_tensor(X.diagE[:], Irep, bcl(egc[:, hsl(hg)], 64), ALU.mult), reads=[dnc, egc], writes=[X.diagE])
            for hg in HG:
                X = XS[hg]
                P.op("dve", lambda e, X=X, p2=X.p2: e.tensor_tensor(X.intraT[:], v3(p2[0:64, :], 8), X.DmT[:], ALU.mult), reads=[X.p2, X.DmT], writes=[X.intraT])
                P.op("pe", lambda e, X=X, p6=X.p6: e.matmul(p6[:, :], ones[0:64, :], F(X.diagE), start=True, stop=True), reads=[dnc, X.diagE], writes=[X.p6])
            for hg in HG:
                X = XS[hg]
                P.op("dve", lambda e, X=X, p6=X.p6, hg=hg: e.tensor_tensor(X.qdec[:], qc_[:, hsl(hg), :], v3(p6[:, :], 8), ALU.mult), reads=[qc_, X.p6], writes=[X.qdec])
        if DN_CUT < 7:
            continue
        for half in range(2):
            a4 = slice(half * 4, half * 4 + 4)
            for hg in HG:
                X = XS[hg]
                Sb = S4[hg * 2 + half]
                X.pws = C.ps.next()
                for hq in range(4):
                    h = half * 4 + hq
                    P.op("pe", lambda e, X=X, pws=X.pws, Sb=Sb, h=h, hq=hq: e.matmul(pws[0:64, hq * 128:(hq + 1) * 128], X.wT[:, h, :], Sb[:, hq, :], start=True, stop=True), reads=[X.wT, Sb], writes=[X.pws], signal=(hq == 3))
            for hg in HG:
                X = XS[hg]
                P.op("dve", lambda e, X=X, pws=X.pws, a4=a4: e.tensor_tensor(X.vnew[:, a4, :], X.u[:, a4, :], v3(pws[0:64, :], 4), ALU.subtract), reads=[X.u, X.pws], writes=[X.vnew])
            for hg in HG:
                X = XS[hg]
                Sb = S4[hg * 2 + half]
                if own:
                    X.po = C.ps.next()
                    for hq in range(4):
                        h = half * 4 + hq
                        P.op("pe", lambda e, X=X, po=X.po, Sb=Sb, h=h, hq=hq: e.matmul(po[0:64, hq * 128:(hq + 1) * 128], X.qdec[:, h, :], Sb[:, hq, :], start=True, stop=False), reads=[X.qdec, Sb], writes=[X.po])
                        P.op("pe", lambda e, X=X, po=X.po, h=h, hq=hq: e.matmul(po[0:64, hq * 128:(hq + 1) * 128], X.intraT[:, h, :], X.vnew[:, h, :], start=False, stop=True), reads=[X.intraT, X.vnew], writes=[X.po])
                X.psu = C.ps.next()
                for hq in range(4):
                    h = half * 4 + hq
                    P.op("pe", lambda e, X=X, psu=X.psu, h=h, hq=hq: e.matmul(psu[:, hq * 128:(hq + 1) * 128], X.kdec[:, h, :], X.vnew[:, h, :], start=True, stop=True), reads=[X.kdec, X.vnew], writes=[X.psu], signal=(hq == 3))
            for hg in HG:
                Sb = S4[hg * 2 + half]
                g4 = slice(hg * 8 + half * 4, hg * 8 + half * 4 + 4)
                P.op("dve", lambda e, Sb=Sb, c=c, g4=g4: e.tensor_tensor(Sb[:], Sb[:], bcl(egl_all[:, c, g4], 128), ALU.mult), reads=[Sb, egl_all], writes=[Sb])
            for hg in HG:
                X = XS[hg]
                Sb = S4[hg * 2 + half]
                P.op("dve", lambda e, Sb=Sb, psu=X.psu: e.tensor_tensor(Sb[:], Sb[:], v3(psu[:, :], 4), ALU.add), reads=[Sb, X.psu], writes=[Sb])
            if own:
                for hg in HG:
                    X = XS[hg]
                    P.op("act", lambda e, X=X, po=X.po: e.activation(out=X.osb[:], in_=v3(po[0:64, :], 4), func=AF.Copy), reads=[X.po], writes=[X.osb])
                for hg in HG:
                    X = XS[hg]
                    P.op("dve", lambda e, X=X: e.tensor_tensor(X.on[:], X.osb[:], X.osb[:], ALU.mult), reads=[X.osb], writes=[X.on])
                for hg in HG:
                    X = XS[hg]
                    P.op("dve", lambda e, X=X: e.tensor_reduce(X.ss[:, 0:4], X.on[:], AX.X, ALU.add), reads=[X.on], writes=[X.ss])
                for hg in HG:
                    X = XS[hg]
                    P.op("dve", lambda e, X=X: e.tensor_scalar(X.ss[:, 4:8], X.ss[:, 0:4], 1.0 / 128, EPS, ALU.mult, ALU.add), reads=[X.ss], writes=[X.ss])
                for hg in HG:
                    X = XS[hg]
                    P.op("act", lambda e, X=X: e.activation(out=X.ss[:, 4:8], in_=X.ss[:, 4:8], func=AF.Sqrt), reads=[X.ss], writes=[X.ss])
                for hg in HG:
                    X = XS[hg]
                    P.op("dve", lambda e, X=X: e.reciprocal(X.ss[:, 4:8], X.ss[:, 4:8]), reads=[X.ss], writes=[X.ss])
                for hg in HG:
                    X = XS[hg]
                    P.op("dve", lambda e, X=X: e.tensor_tensor(X.on[:], X.osb[:], bcl(X.ss[:, 4:8], 128), ALU.mult), reads=[X.osb, X.ss], writes=[X.on])
                for hg in HG:
                    X = XS[hg]
                    P.op("dve", lambda e, X=X: e.tensor_tensor(X.on[:], X.on[:], dnn[:].unsqueeze(1).broadcast_to([64, 4, 128]), ALU.mult), reads=[X.on, dnn], writes=[X.on])
                for hg in HG:
                    X = XS[hg]
                    z0 = (hg * 8 + half * 4) * 128
                    P.op("dve", lambda e, X=X, z0=z0: e.tensor_tensor(X.on[:], X.on[:], v3(zc_[:, z0:z0 + 512], 4), ALU.mult), reads=[X.on, zc_], writes=[X.on])
                for hg in HG:
                    X = XS[hg]
                    X.pt = C.ps.next()
                    for hq in range(4):
                        P.op("pe", lambda e, X=X, pt=X.pt, hq=hq: e.transpose(pt[:, hq * 64:(hq + 1) * 64], X.on[:, hq, :], I64), reads=[X.on, ident], writes=[X.pt], signal=(hq == 3))
                for hg in HG:
                    X = XS[hg]
                    g4 = slice(hg * 8 + half * 4, hg * 8 + half * 4 + 4)
                    P.op("act", lambda e, pt=X.pt, g4=g4: e.activation(out=obT[:, g4, :], in_=v3(pt[:, 0:256], 4), func=AF.Copy), reads=[X.pt], writes=[obT])
        if own:
            C.store(obT, obT[:], o_bT, o_bT.t.rearrange("(h d) t -> d h t", d=128)[:, :, (c - OWN0) * 64:(c - OWN0 + 1) * 64])
    if hasattr(C, "dbg"):
        X = XS[1]
        for nm, b_ in (("Dm", X.Dm), ("DmT", X.DmT), ("R0", X.sets[0][2]), ("R1", X.sets[1][2]), ("M0", X.sets[0][0]), ("intraT", X.intraT)):
            C.dbg(nm, b_, b_[:].rearrange("p h j -> p (h j)"))
        for nm, b_ in (("u", X.u), ("vnew", X.vnew), ("kbg", X.kbg), ("kdec", X.kdec), ("vb", X.vb)):
            C.dbg(nm, b_, b_[:].rearrange("p h j -> p (h j)"))
        C.dbg("wT", X.wT, X.wT[:].rearrange("p h j -> p (h j)"))
        for i_, s_ in enumerate(S4):
            C.dbg("S%d" % i_, s_, s_[:].rearrange("p h j -> p (h j)"))
    P.end_phase()


def build_all():
    C = build()
    P, I, nc, out = C.P, C.I, C.nc, C.out
    gemm, store, norm_T, dram = C.gemm, C.store, C.norm_T, C.dram
    xrB = P.ext(I.xr)
    T0 = S - TOWN
    TQ = TOWN + 512

    P.begin_phase()
    C.gemm_bufs()
    hT = dram("hT", [D, S], BF16)
    norm_T(xrB, 0, S, I.norm_mix, hT)
    P.end_phase()
    P.begin_phase()
    C.gemm_bufs(norm=False)
    gT = dram("gT", [2 * D, TOWN], BF16)
    gemm(hT, T0, TOWN, I.w_in, GA, 2 * D, D, "A", C.epi_store_T(gT, BF16, AF.Sigmoid))
    o_aT = dram("o_aT", [1024, TOWN], BF16)
    o_bT = dram("o_bT", [2048, TOWN], BF16)
    if "attn" in STAGES:
        qaT = dram("qaT", [3072, TOWN], BF16)
        kaT = dram("kaT", [3072, TA], BF16)
        vtok = dram("vtok", [TA, 3072], BF16)
        gemm(hT, T0, TOWN, I.w_in, QA, 3072, D, "A", C.epi_store_T(qaT, BF16))
        gemm(hT, S - TA, TA, I.w_in, KA, 3072, D, "A", C.epi_store_T(kaT, BF16))
        gemm(hT, S - TA, TA, I.w_in, VA, 3072, D, "B", C.epi_tm(vtok, AF.Copy, BF16))
    if "dn" in STAGES:
        ztm = dram("ztm", [TOWN, 2048], F32)
        ba = dram("ba", [S, 32], F32)
        khT = dram("khT", [2048, S], F32)
        vT = dram("vT", [2048, S], F32)
        qhT = dram("qhT", [2048, TQ], F32)
        Z = dn_prep_bufs(C)
        kv_dst = lambda fc: (khT, (fc - 16) * 128) if fc < 32 else (vT, (fc - 32) * 128)
        gemm(hT, 0, S, I.w_in, KB, 4096, D, "A", None, epi_grp=dn_prep_epi(C, Z, 16, kv_dst))
        gemm(hT, S - TQ, TQ, I.w_in, QB, 2048, D, "A", None, epi_grp=dn_prep_epi(C, Z, 0, lambda fc: (qhT, fc * 128)))
        gemm(hT, T0, TOWN, I.w_in, ZB, 2048, D, "B", C.epi_tm(ztm, AF.Silu))
        gemm(hT, 0, S, I.w_in, BETA, 32, D, "B", C.epi_tm(ba, AF.Copy))
    P.end_phase()

    if "attn" in STAGES:
        attention(C, qaT, kaT, vtok, o_aT)
    if "dn" in STAGES:
        if "noscan" not in STAGES:
            dn_scan(C, ba, khT, vT, qhT, ztm, o_bT)

    P.begin_phase()
    C.gemm_bufs()
    e32, e16 = C.st32, C.st16
    AT = dram("AT", [D, TOWN], F32)
    BT = dram("BT", [D, TOWN], F32)
    gemm(o_aT, 0, TOWN, I.w_attn_up, 0, D, 1024, "A", C.epi_store_T(AT, F32))
    gemm(o_bT, 0, TOWN, I.w_dn_up, 0, D, 2048, "A", C.epi_store_T(BT, F32))
    mT = dram("mT", [D, TOWN], BF16)
    for r in range(D // 128):
        for tb in range(TOWN // 512):
            sl = (slice(r * 128, (r + 1) * 128), slice(tb * 512, (tb + 1) * 512))
            a, b, ga, gb, m = e32.next(), e32.next(), e16.next(), e16.next(), e16.next()
            P.dma("sp", lambda e, a=a, sl=sl: e.dma_start(out=a[:], in_=AT.t[sl]), AT, a)
            P.dma("sp", lambda e, b=b, sl=sl: e.dma_start(out=b[:], in_=BT.t[sl]), BT, b)
            P.dma("sp", lambda e, ga=ga, sl=sl: e.dma_start(out=ga[:], in_=gT.t[sl]), gT, ga)
            P.dma("sp", lambda e, gb=gb, sl=sl, r=r: e.dma_start(out=gb[:], in_=gT.t[D + r * 128:D + (r + 1) * 128, sl[1]]), gT, gb)
            P.op("dve", lambda e, a=a, ga=ga: e.tensor_tensor(a[:], a[:], ga[:], ALU.mult), reads=[a, ga], writes=[a])
            P.op("dve", lambda e, b=b, gb=gb: e.tensor_tensor(b[:], b[:], gb[:], ALU.mult), reads=[b, gb], writes=[b])
            P.op("dve", lambda e, a=a, b=b, m=m: e.tensor_tensor(m[:], a[:], b[:], ALU.add), reads=[a, b], writes=[m])
            store(m, m[:], mT, mT.t[sl])

    def epi_resid(src, src_r0, dst):
        def epi(tb, nb, j, acc, ncols):
            r0 = tb * 512 + j * 128
            xt = e32.next()
            P.dma("sp", lambda e: e.dma_start(out=xt[:, 0:ncols], in_=src.t[src_r0 + r0:src_r0 + r0 + 128, nb * 512:nb * 512 + ncols]), src, xt)
            P.op("dve", lambda e: e.tensor_tensor(xt[:, 0:ncols], xt[:, 0:ncols], acc[:, 0:ncols], ALU.add), reads=[xt, acc], writes=[xt])
            store(xt, xt[:, 0:ncols], dst, dst.t[r0:r0 + 128, nb * 512:nb * 512 + ncols])
        return epi

    x1 = dram("x1", [TOWN, D], F32)
    gemm(mT, 0, TOWN, I.w_out, 0, D, D, "B", epi_resid(xrB, T0, x1))
    h2T = dram("h2T", [D, TOWN], BF16)
    norm_T(x1, 0, TOWN, I.norm_mlp, h2T)
    hidT = dram("hidT", [DFF, TOWN], BF16)

    def epi_relu2(tb, nb, j, acc, ncols):
        t, sb = e32.next(), e16.next()
        P.op("act", lambda e: e.activation(out=t[:], in_=acc[:], func=AF.Relu), reads=[acc], writes=[t])
        P.op("dve", lambda e: e.tensor_tensor(sb[:], t[:], t[:], ALU.mult), reads=[t], writes=[sb])
        r0 = nb * 512 + j * 128
        store(sb, sb[:], hidT, hidT.t[r0:r0 + 128, tb * 512:(tb + 1) * 512])

    gemm(h2T, 0, TOWN, I.w_mlp_up, 0, DFF, D, "A", epi_relu2)
    x2 = dram("x2", [TOWN, D], F32)
    gemm(hidT, 0, TOWN, I.w_mlp_down, 0, D, DFF, "B", epi_resid(x1, 0, x2))
    h3T = dram("h3T", [D, TOWN], BF16)
    norm_T(x2, 0, TOWN, I.norm_ple, h3T)
    gate = dram("gate", [TOWN, D], F32)
    proj = dram("proj", [TOWN, D], F32)
    gemm(h3T, 0, TOWN, I.w_ple_gate, 0, D, D, "B", C.epi_tm(gate, AF.Sigmoid))
    pT = dram("pT", [256, TOWN], BF16)
    ppB = P.ext(I.pp)
    for tb in range(TOWN // 512):
        pTs = e16.next(), e16.next()
        for tt in range(4):
            pt_in = e32.next()
            r0 = tb * 512 + tt * 128
            P.dma("sp", lambda e, pt_in=pt_in, r0=r0: e.dma_start(out=pt_in[:, 0:256], in_=I.pp[r0:r0 + 128, :]), ppB, pt_in)
            ps = C.ps.next()
            for q in range(2):
                P.op("pe", lambda e, ps=ps, pt_in=pt_in, q=q: e.transpose(ps[:, q * 128:(q + 1) * 128], pt_in[:, q * 128:(q + 1) * 128], C.ident[:]),
                     reads=[pt_in, C.ident], writes=[ps])
            for q in range(2):
                P.op("dve", lambda e, ps=ps, q=q, tt=tt, pTs=pTs: e.tensor_copy(pTs[q][:, tt * 128:(tt + 1) * 128], ps[:, q * 128:(q + 1) * 128]),
                     reads=[ps], writes=[pTs[q]])
        for q in range(2):
            store(pTs[q], pTs[q][:], pT, pT.t[q * 128:(q + 1) * 128, tb * 512:(tb + 1) * 512])
    gemm(pT, 0, TOWN, I.w_ple_proj, 0, D, 256, "B", C.epi_tm(proj, AF.Copy))

    gpp, gfn = C.big[0], C.big[1]
    P.dma("sp", lambda e: e.dma_start(out=gpp[:], in_=bc_ap(I.ple_post, 128)), P.ext(I.ple_post), gpp)
    P.dma("sp", lambda e: e.dma_start(out=gfn[:], in_=bc_ap(I.final_norm, 128)), P.ext(I.final_norm), gfn)
    big = Ring(C.big[2:5])
    junk = C.junk
    outB = P.ext(out)
    for tt in range(TOWN // 128):
        rs = slice(tt * 128, (tt + 1) * 128)
        pr, gt, x2t = big.next(), big.next(), big.next()
        P.dma("sp", lambda e, pr=pr, rs=rs: e.dma_start(out=pr[:], in_=proj.t[rs, :]), proj, pr)
        P.dma("sp", lambda e, gt=gt, rs=rs: e.dma_start(out=gt[:], in_=gate.t[rs, :]), gate, gt)
        P.dma("sp", lambda e, x2t=x2t, rs=rs: e.dma_start(out=x2t[:], in_=x2.t[rs, :]), x2, x2t)
        ss = C.ssr.next()
        P.op("act", lambda e, pr=pr, ss=ss: e.activation(out=junk[:], in_=pr[:], func=AF.Square, accum_out=ss[:, 0:1]), reads=[pr], writes=[junk, ss])
        C.rstd_op(ss, 0, 1, D)
        P.op("dve", lambda e, pr=pr, ss=ss: e.scalar_tensor_tensor(pr[:], pr[:], ss[:, 1:2], gpp[:], ALU.mult, ALU.mult), reads=[pr, ss, gpp], writes=[pr])
        P.op("dve", lambda e, pr=pr, gt=gt: e.tensor_tensor(pr[:], pr[:], gt[:], ALU.mult), reads=[pr, gt], writes=[pr])
        P.op("dve", lambda e, pr=pr, x2t=x2t: e.tensor_tensor(x2t[:], x2t[:], pr[:], ALU.add), reads=[pr, x2t], writes=[x2t])
        P.op("act", lambda e, x2t=x2t, ss=ss: e.activation(out=junk[:], in_=x2t[:], func=AF.Square, accum_out=ss[:, 2:3]), reads=[x2t], writes=[junk, ss])
        C.rstd_op(ss, 2, 3, D)
        P.op("dve", lambda e, x2t=x2t, ss=ss, gt=gt: e.scalar_tensor_tensor(gt[:], x2t[:], ss[:, 3:4], gfn[:], ALU.mult, ALU.mult), reads=[x2t, ss, gfn], writes=[gt])
        store(gt, gt[:], outB, out[rs, :])
    fin = [outB]
    for name in DUMP:
        src = C.dr[name]
        o = nc.dram_tensor("dump_" + name, list(src.t.shape), src.t.dtype, kind="ExternalOutput").ap()
        ob_ = P.ext(o)
        P.dma("sp", lambda e, o=o, src=src: e.dma_start(out=o, in_=src.t), src, ob_)
        fin.append(ob_)
    P.wait_all("sp", fin)
    P.end_phase()
    P.emit()
    return nc


def host_consts(c):
    tiles, ktab = attn_tiles()
    m = {"ident": np.eye(128, dtype=np.float32)}
    j = np.arange(128)[:, None]
    i = np.arange(128)[None, :]
    m["amask"] = np.concatenate([(j >= i), (j <= i)], 1).astype(np.float32)
    vcol = np.zeros((128, len(ktab)), np.float32)
    first_valid = TA - (c + 1) * TOWN
    for (g, r, s0, n), col in ktab.items():
        d = (1, 4, 16)[g]
        u = r + d * (s0 + np.arange(n))
        vcol[:n, col] = (u >= first_valid)
    m["vcol"] = vcol
    dnc = np.zeros((128, 6 * 512), np.float32)
    a = np.arange(64)
    I64 = np.eye(64, dtype=np.float32)
    low_incl = (a[:, None] >= a[None, :]).astype(np.float32)
    low_strict = (a[:, None] > a[None, :]).astype(np.float32)
    dnc[:64, 0:512] = np.tile(I64, (1, 8))
    dnc[:64, 512:1024] = np.tile(low_strict, (1, 8))
    dnc[:64, 1024:1536] = np.tile(low_strict.T, (1, 8))
    dnc[:64, 1536:2048] = np.tile((1 - low_incl) * -30000.0, (1, 8))
    dnc[:64, 2048:2560] = np.tile((1 - low_incl.T) * -30000.0, (1, 8))
    dnc[:64, 2560:2624] = (a[:, None] <= a[None, :])
    dnc[:, 2624:2752] = 1.0
    dnc[:, 2752:2880] = -1.0
    dnc[:64, 2880:2944] = -I64
    m["dnc"] = dnc
    return m


def kernel(**inputs):
    f = lambda k: np.ascontiguousarray(np.asarray(inputs[k], np.float32)[0])
    x = f("x")
    p = np.asarray(inputs["p"], np.float32)[0, 0]
    shared = {k: f(k) for k in ("w_in", "w_attn_up", "w_dn_up", "w_out", "w_mlp_up", "w_mlp_down", "w_ple_gate",
                                "w_ple_proj", "norm_mix", "norm_mlp", "norm_ple", "ple_post_norm", "dn_a_log", "dn_dt_bias", "dn_norm")}
    shared["final_norm"] = np.ascontiguousarray(np.asarray(inputs["final_norm"], np.float32))
    shared["conv_wT"] = np.ascontiguousarray(f("conv_w").T)
    in_maps = []
    for c in range(NCORES):
        m = dict(shared)
        m.update(host_consts(c))
        xr = np.zeros((S, D), np.float32)
        xr[S - (c + 1) * TOWN:] = x[:(c + 1) * TOWN]
        m["xr"] = xr
        m["pp"] = np.ascontiguousarray(p[c * TOWN:(c + 1) * TOWN])
        in_maps.append(m)
    nc = build_all()
    res = run_bass_kernel_spmd(nc, in_maps, core_ids=list(range(NCORES)), **({"trace": True} if TRACE else {}))
    kernel.last = res
    return np.concatenate([r["out"] for r in res.results], 0)[None].astype(np.float32)
```

```python
import numpy as np
from contextlib import ExitStack
import concourse.bass as bass
import concourse.mybir as mybir
from concourse.bass_utils import run_bass_kernel_spmd

F32 = mybir.dt.float32
BF16 = mybir.dt.bfloat16
AF = mybir.ActivationFunctionType
ALU = mybir.AluOpType
AX = mybir.AxisListType
ENGS = ["pe", "act", "dve", "pool", "sp"]

NCORES = 8
S = 8192
D = 4096
DFF = 16384
TOWN = 1024
HALO = 2048
TA = TOWN + HALO
EPS = 1e-6
QA, KA, VA, QB, KB, VB, ZB, BETA, ALPHA, GA = 0, 3072, 6144, 9216, 11264, 13312, 15360, 17408, 17424, 17440
DUMP = []
STAGES = {"attn", "dn", "tail"}
DN_CHUNKS = None
DN_CUT = 99
DN_SET = 99
TRACE = False


class Buf:
    def __init__(self, t, kind):
        self.t = t
        self.kind = kind
        self.lastw = None
        self.readers = []
        self.sem = None

    def __getitem__(self, k):
        return self.t[k]


class Prog:
    def __init__(self, nc):
        self.nc = nc
        self.es = ExitStack()
        self.q = {e: [] for e in ENGS}
        self.cnt = {e: 0 for e in ENGS}
        self.waited = {e: {} for e in ENGS}
        self.psem = {e: self.es.enter_context(nc.semaphore("prog_" + e)) for e in ENGS}
        self.dcnt = {}
        self._uid = 0
        self.allsems = []
        self.sem_pool = []
        self.phase_es = None
        self.phase_bufs = []

    def uid(self, p):
        self._uid += 1
        return f"{p}{self._uid}"

    def sbuf(self, shape, dt, name=None):
        es = self.phase_es if self.phase_es is not None else self.es
        b = Buf(es.enter_context(self.nc.sbuf_tensor(name or self.uid("sb"), list(shape), dt)), "sb")
        if self.phase_es is not None:
            self.phase_bufs.append(b)
        return b

    def begin_phase(self):
        assert self.phase_es is None
        self.phase_es = ExitStack()
        self.phase_bufs = []

    def barrier(self):
        deps = [(self.psem[x], self.cnt[x], x) for x in ENGS if self.cnt[x] > 0]
        deps += [v for x, v in getattr(self, "prev_final", {}).items() if self.cnt[x] == 0]
        deps += [(sem, self.dcnt[id(sem)], "dma") for sem in self.allsems if self.dcnt[id(sem)] > 0]
        for e in ENGS:
            w = self._waits(e, [d for d in deps if not (d[2] == e)] + [d for d in deps if d[2] == e and e != "pe"])
            if w:
                self.q[e].append((w, None, None))

    def end_phase(self):
        self.barrier()
        for b in self.phase_bufs:
            if b.sem is not None:
                self.sem_pool.append(b.sem)
        self.phase_es.close()
        self.phase_es = None
        self.phase_bufs = []

    def psum(self, shape, dt, name=None):
        return Buf(self.es.enter_context(self.nc.psum_tensor(name or self.uid("ps"), list(shape), dt)), "ps")

    def dram(self, shape, dt, name=None):
        return Buf(self.nc.dram_tensor(name or self.uid("dr"), list(shape), dt).ap(), "dr")

    def ext(self, ap):
        return Buf(ap, "dr")

    def _bsem(self, b):
        if b.sem is None:
            if self.sem_pool:
                b.sem = self.sem_pool.pop()
            else:
                b.sem = self.es.enter_context(self.nc.semaphore(self.uid("ds")))
                self.dcnt[id(b.sem)] = 0
                self.allsems.append(b.sem)
        return b.sem

    def _waits(self, eng, deps):
        w = []
        for d in deps:
            if d is None:
                continue
            sem, val, src = d
            if src == "pe" and eng == "pe":
                continue
            key = id(sem)
            if self.waited[eng].get(key, 0) >= val:
                continue
            self.waited[eng][key] = val
            w.append((sem, val))
        return w

    def _deps(self, reads, writes, waw=True):
        deps = []
        for b in reads:
            deps.append(b.lastw)
            if b.kind == "ps":
                deps += b.readers
        for b in writes:
            if waw:
                deps.append(b.lastw)
            deps += b.readers
        return deps

    def _commit(self, tok, reads, writes):
        for b in writes:
            b.lastw = tok
            b.readers = []
        for b in reads:
            b.readers.append(tok)

    EPOCH = 30000

    def op(self, eng, fn, reads=(), writes=(), signal=True):
        self.pend = getattr(self, "pend", {})
        if self.cnt[eng] >= self.EPOCH and not self.pend.get(eng):
            self.prev_final = getattr(self, "prev_final", {})
            self.prev_final[eng] = (self.psem[eng], self.cnt[eng], eng)
            self.psem[eng] = self.es.enter_context(self.nc.semaphore(self.uid("prog_" + eng)))
            self.cnt[eng] = 0
            self.total = getattr(self, "total", {})
            self.total[eng] = self.total.get(eng, 0) + self.EPOCH
        w = self._waits(eng, self._deps(reads, writes))
        if not signal:
            assert eng == "pe"
            self.pend[eng] = True
            tok = (self.psem[eng], self.cnt[eng] + 1, eng)
            self.q[eng].append((w, fn, None))
            self._commit(tok, reads, writes)
            return tok
        self.pend[eng] = False
        self.cnt[eng] += 1
        tok = (self.psem[eng], self.cnt[eng], eng)
        self.q[eng].append((w, fn, (self.psem[eng], 1)))
        self._commit(tok, reads, writes)
        return tok

    def dma(self, eng, fn, src, dst, waw=True):
        w = self._waits(eng, self._deps([src], [dst], waw=waw))
        sem = self._bsem(dst)
        if self.dcnt[id(sem)] + 16 > self.EPOCH:
            dst.sem = None
            sem = self._bsem(dst)
        self.dcnt[id(sem)] += 16
        tok = (sem, self.dcnt[id(sem)], "dma")
        self.q[eng].append((w, fn, (sem, 16)))
        if waw:
            self._commit(tok, [src], [dst])
        else:
            dst.lastw = tok
            src.readers.append(tok)
        return tok

    def wait_all(self, eng, bufs):
        w = self._waits(eng, [b.lastw for b in bufs])
        if w:
            self.q[eng].append((w, None, None))

    def emit(self):
        nc = self.nc
        print("ops per engine:", {e: len(self.q[e]) for e in ENGS}, "sems:", len(self.allsems))
        with nc.Block() as block:
            def run(e, engine):
                for w, fn, inc in self.q[e]:
                    for sem, val in w:
                        engine.wait_ge(sem, val)
                    if fn is None:
                        continue
                    ins = fn(engine)
                    if inc is not None:
                        ins.then_inc(inc[0], inc[1])

            @block.tensor
            def _(t):
                run("pe", t)

            @block.scalar
            def _(t):
                run("act", t)

            @block.vector
            def _(t):
                run("dve", t)

            @block.gpsimd
            def _(t):
                run("pool", t)

            @block.sync
            def _(t):
                run("sp", t)
        self.es.close()


class Ring:
    def __init__(self, bufs):
        self.bufs = bufs
        self.i = 0

    def next(self):
        b = self.bufs[self.i % len(self.bufs)]
        self.i += 1
        return b


def bc_ap(ap, nparts):
    n = ap.shape[-1]
    return bass.AP(ap.tensor, ap.offset, [[0, nparts], [1, n]])


def attn_tiles():
    tiles, kt = [], {}
    for g, d in enumerate((1, 4, 16)):
        n_own, s0 = TOWN // d, HALO // d
        QW = min(128, n_own)
        for r in range(d):
            for a in range(s0, s0 + n_own, QW):
                for key in ((g, r, a - 128, 128), (g, r, a, QW)):
                    if key not in kt:
                        kt[key] = len(kt)
                tiles.append((g, d, r, a, QW))
    return tiles, kt


class Ctx:
    pass


def build():
    nc = bass.Bass("TRN2", target_bir_lowering=False)
    IN_W = GA + 2 * D
    dt_in = lambda n, s: nc.dram_tensor(n, list(s), F32, kind="ExternalInput").ap()
    I = Ctx()
    I.xr = dt_in("xr", [S, D])
    I.pp = dt_in("pp", [TOWN, 256])
    I.w_in = dt_in("w_in", [D, IN_W])
    I.conv_wT = dt_in("conv_wT", [6144, 4])
    I.dn_a_log = dt_in("dn_a_log", [16])
    I.dn_dt_bias = dt_in("dn_dt_bias", [16])
    I.dn_norm = dt_in("dn_norm", [128])
    I.w_attn_up = dt_in("w_attn_up", [1024, D])
    I.w_dn_up = dt_in("w_dn_up", [2048, D])
    I.w_out = dt_in("w_out", [D, D])
    I.w_mlp_up = dt_in("w_mlp_up", [D, DFF])
    I.w_mlp_down = dt_in("w_mlp_down", [DFF, D])
    I.w_ple_gate = dt_in("w_ple_gate", [D, D])
    I.w_ple_proj = dt_in("w_ple_proj", [256, D])
    I.norm_mix = dt_in("norm_mix", [D])
    I.norm_mlp = dt_in("norm_mlp", [D])
    I.norm_ple = dt_in("norm_ple", [D])
    I.ple_post = dt_in("ple_post_norm", [D])
    I.final_norm = dt_in("final_norm", [D])
    I.ident = dt_in("ident", [128, 128])
    I.amask = dt_in("amask", [128, 256])
    ntile, ktab = attn_tiles()
    I.vcol = dt_in("vcol", [128, len(ktab)])
    I.dnc = dt_in("dnc", [128, 6 * 512])
    out = nc.dram_tensor("out", [TOWN, D], F32, kind="ExternalOutput").ap()

    P = Prog(nc)
    C = Ctx()
    C.I, C.P, C.nc, C.out = I, P, nc, out
    C.ps = Ring([P.psum([128, 512], F32) for _ in range(8)])
    C.ident = P.sbuf([128, 128], F32)
    P.dma("sp", lambda e: e.dma_start(out=C.ident[:], in_=I.ident), P.ext(I.ident), C.ident)
    C.dr = {}

    def dram(name, shape, dt):
        C.dr[name] = P.dram(shape, dt, name="scr_" + name)
        return C.dr[name]
    C.dram = dram

    def gemm_bufs(norm=True):
        C.wring = Ring([P.sbuf([128, 8, 512], BF16) for _ in range(4)])
        C.xring = Ring([P.sbuf([128, 32, 512], BF16) for _ in range(2)])
        C.st32 = Ring([P.sbuf([128, 512], F32) for _ in range(4)])
        C.st16 = Ring([P.sbuf([128, 512], BF16) for _ in range(4)])
        if norm:
            C.big = [P.sbuf([128, D], F32) for _ in range(5)]
            C.junk = P.sbuf([128, D], BF16)
            C.ssr = Ring([P.sbuf([128, 4], F32) for _ in range(2)])
    C.gemm_bufs = gemm_bufs

    def store(sb, sb_ap, dr, dr_ap, q="sp"):
        P.dma(q, lambda e: e.dma_start(out=dr_ap, in_=sb_ap), sb, dr, waw=False)
    C.store = store

    def gemm(xT, t0, T, W, n0, N, K, form, epi, epi_grp=None):
        KC = K // 128
        wv = W.rearrange("(kc p) n -> p kc n", p=128)
        xv = xT.t.rearrange("(kc p) t -> p kc t", p=128)
        Wb = P.ext(W)
        nsup = max(1, KC // 32)
        kcs = min(KC, 32)
        for tb in range(T // 512):
            ts = t0 + tb * 512
            xs = None
            for nb in range((N + 511) // 512):
                ncols = min(512, N - nb * 512)
                accs = [C.ps.next() for _ in range(4)]
                nj = 4 if form == "B" else (ncols + 127) // 128
                for sup in range(nsup):
                    if xs is None or nsup > 1:
                        xs = C.xring.next()
                        P.dma("sp", lambda e, xs=xs, sup=sup, ts=ts: e.dma_start(
                            out=xs[:, 0:kcs, :], in_=xv[:, sup * 32:sup * 32 + kcs, ts:ts + 512]), xT, xs)
                    for kg in range((kcs + 7) // 8):
                        nk = min(8, kcs - kg * 8)
                        wp = C.wring.next()
                        k0 = sup * 32 + kg * 8
                        P.dma("pool", lambda e, wp=wp, k0=k0, nk=nk, nb=nb, ncols=ncols: e.dma_start(
                            out=wp[:, 0:nk, 0:ncols], in_=wv[:, k0:k0 + nk, n0 + nb * 512:n0 + nb * 512 + ncols]), Wb, wp)
                        for kc in range(nk):
                            first = (sup == 0 and kg == 0 and kc == 0)
                            last = (sup == nsup - 1 and kg * 8 + kc == kcs - 1)
                            for j in range(nj):
                                sig = last or (kc == nk - 1 and j == nj - 1)
                                if form == "A":
                                    mc = min(128, ncols - j * 128)
                                    P.op("pe", lambda e, a=accs[j], wp=wp, xs=xs, kc=kc, j=j, mc=mc, kk=kg * 8 + kc, f=first, l=last: e.matmul(
                                        a[0:mc, :], wp[:, kc, j * 128:j * 128 + mc], xs[:, kk, :], start=f, stop=l),
                                        reads=[wp, xs], writes=[accs[j]], signal=sig)
                                else:
                                    P.op("pe", lambda e, a=accs[j], wp=wp, xs=xs, kc=kc, j=j, nc_=ncols, kk=kg * 8 + kc, f=first, l=last: e.matmul(
                                        a[:, 0:nc_], xs[:, kk, j * 128:(j + 1) * 128], wp[:, kc, 0:nc_], start=f, stop=l),
                                        reads=[wp, xs], writes=[accs[j]], signal=sig)
                if epi_grp is not None:
                    epi_grp(tb, nb, accs[:nj], ncols)
                else:
                    for j in range(nj):
                        epi(tb, nb, j, accs[j], ncols)
    C.gemm = gemm

    def epi_store_T(dst, dt, func=None, roff=0, coff=0):
        def epi(tb, nb, j, acc, ncols):
            mc = min(128, ncols - j * 128)
            sb = (C.st16 if dt == BF16 else C.st32).next()
            if func is None:
                P.op("dve", lambda e: e.tensor_copy(sb[0:mc, :], acc[0:mc, :]), reads=[acc], writes=[sb])
            else:
                P.op("act", lambda e: e.activation(out=sb[0:mc, :], in_=acc[0:mc, :], func=func), reads=[acc], writes=[sb])
            r0 = roff + nb * 512 + j * 128
            store(sb, sb[0:mc, :], dst, dst.t[r0:r0 + mc, coff + tb * 512:coff + (tb + 1) * 512])
        return epi
    C.epi_store_T = epi_store_T

    def epi_tm(dst, func, dt=F32, roff=0, coff=0):
        def epi(tb, nb, j, acc, ncols):
            t = (C.st16 if dt == BF16 else C.st32).next()
            P.op("act", lambda e: e.activation(out=t[:, 0:ncols], in_=acc[:, 0:ncols], func=func), reads=[acc], writes=[t])
            r0 = roff + tb * 512 + j * 128
            store(t, t[:, 0:ncols], dst, dst.t[r0:r0 + 128, coff + nb * 512:coff + nb * 512 + ncols])
        return epi
    C.epi_tm = epi_tm

    def rstd_op(ss, i, o, n):
        P.op("dve", lambda e: e.tensor_scalar(ss[:, o:o + 1], ss[:, i:i + 1], 1.0 / n, EPS, ALU.mult, ALU.add), reads=[ss], writes=[ss])
        P.op("act", lambda e: e.activation(out=ss[:, o:o + 1], in_=ss[:, o:o + 1], func=AF.Sqrt), reads=[ss], writes=[ss])
        P.op("dve", lambda e: e.reciprocal(ss[:, o:o + 1], ss[:, o:o + 1]), reads=[ss], writes=[ss])
    C.rstd_op = rstd_op

    def norm_T(src, srcT0, T, gain_ap, dstT):
        g = C.big[0]
        P.dma("sp", lambda e: e.dma_start(out=g[:], in_=bc_ap(gain_ap, 128)), P.ext(gain_ap), g)
        xr = Ring(C.big[1:3])
        hr = Ring(C.big[3:5])
        junk = C.junk
        KC = D // 128
        for tb in range(T // 512):
            hT = C.xring.next()
            for tt in range(4):
                r0 = srcT0 + tb * 512 + tt * 128
                xt = xr.next()
                P.dma("sp", lambda e, xt=xt, r0=r0: e.dma_start(out=xt[:], in_=src.t[r0:r0 + 128, :]), src, xt)
                ss = C.ssr.next()
                P.op("act", lambda e, xt=xt, ss=ss: e.activation(out=junk[:], in_=xt[:], func=AF.Square, accum_out=ss[:, 0:1]),
                     reads=[xt], writes=[junk, ss])
                rstd_op(ss, 0, 1, D)
                hb = hr.next()
                P.op("dve", lambda e, hb=hb, xt=xt, ss=ss: e.scalar_tensor_tensor(hb[:], xt[:], ss[:, 1:2], g[:], ALU.mult, ALU.mult),
                     reads=[xt, ss, g], writes=[hb])
                for k4 in range(KC // 4):
                    pt = C.ps.next()
                    for q in range(4):
                        kc = k4 * 4 + q
                        P.op("pe", lambda e, pt=pt, hb=hb, kc=kc, q=q: e.transpose(pt[:, q * 128:(q + 1) * 128], hb[:, kc * 128:(kc + 1) * 128], C.ident[:]),
                             reads=[hb, C.ident], writes=[pt])
                    if k4 % 2:
                        P.op("act", lambda e, pt=pt, hT=hT, k4=k4, tt=tt: e.activation(
                            out=hT[:, k4 * 4:k4 * 4 + 4, tt * 128:(tt + 1) * 128], in_=pt[:].rearrange("p (q t) -> p q t", q=4), func=AF.Copy),
                            reads=[pt], writes=[hT])
                    else:
                        P.op("dve", lambda e, pt=pt, hT=hT, k4=k4, tt=tt: e.tensor_copy(
                            hT[:, k4 * 4:k4 * 4 + 4, tt * 128:(tt + 1) * 128], pt[:].rearrange("p (q t) -> p q t", q=4)),
                            reads=[pt], writes=[hT])
            store(hT, hT[:, 0:KC, :], dstT, dstT.t.rearrange("(kc p) t -> p kc t", p=128)[:, :, tb * 512:(tb + 1) * 512])
    C.norm_T = norm_T
    return C


def attention(C, qaT, kaT, vtok, o_aT):
    P, I = C.P, C.I
    tiles, ktab = attn_tiles()
    P.begin_phase()
    amask = P.sbuf([128, 256], BF16)
    P.dma("pool", lambda e: e.dma_start(out=amask[:], in_=I.amask), P.ext(I.amask), amask)
    vcol = P.sbuf([128, len(ktab)], F32)
    P.dma("sp", lambda e: e.dma_start(out=vcol[:], in_=I.vcol), P.ext(I.vcol), vcol)
    ones = P.sbuf([128, 128], BF16)
    P.op("dve", lambda e: e.memset(ones[:], 1.0), writes=[ones])
    qTr = Ring([P.sbuf([128, TOWN], BF16) for _ in range(2)])
    kTr = Ring([P.sbuf([128, TA], BF16) for _ in range(2)])
    geo = {0: (1, 15, 9), 1: (4, 3, 3), 2: (16, 0, 2)}
    vts = {g: P.sbuf([128, geo[g][0], geo[g][2], 128], BF16) for g in range(3)}
    ULr = Ring([P.sbuf([128, 2, TOWN], F32) for _ in range(2)])
    Er = Ring([P.sbuf([128, 256], BF16) for _ in range(3)])
    Ewr = Ring([P.sbuf([128, 256], BF16) for _ in range(3)])
    ob = Ring([P.sbuf([128, TOWN], BF16) for _ in range(2)])
    rl = P.sbuf([128, TOWN], F32)
    scale = 128.0 ** -0.5
    for hh in range(8):
        UL = ULr.next()
        for g in range(3):
            d, bt0, nbt = geo[g]
            row0 = g * 1024 + hh * 128
            qT, kT, vt = qTr.next(), kTr.next(), vts[g]
            P.dma("sp", lambda e, qT=qT, row0=row0: e.dma_start(out=qT[:], in_=qaT.t[row0:row0 + 128, :]), qaT, qT)
            P.dma("sp", lambda e, kT=kT, row0=row0: e.dma_start(out=kT[:], in_=kaT.t[row0:row0 + 128, :]), kaT, kT)
            for r in range(d):
                if g < 2:
                    src = vtok.t[bass.ds(r + d * 128 * bt0, 128 * nbt, step=d), row0:row0 + 128].rearrange("(bt j) c -> j bt c", j=128)
                    P.dma("sp", lambda e, vt=vt, r=r, src=src: e.dma_start(out=vt[:, r, :, :], in_=src), vtok, vt)
                else:
                    src0 = vtok.t[bass.ds(r, 128, step=d), row0:row0 + 128]
                    src1 = vtok.t[bass.ds(r + d * 128, 64, step=d), row0:row0 + 128]
                    P.dma("sp", lambda e, vt=vt, r=r, src0=src0: e.dma_start(out=vt[:, r, 0, :], in_=src0), vtok, vt)
                    P.dma("sp", lambda e, vt=vt, r=r, src1=src1: e.dma_start(out=vt[0:64, r, 1, :], in_=src1), vtok, vt)
            for (tg, td, r, a, QW) in tiles:
                if tg != g:
                    continue
                k0 = ktab[(g, r, a - 128, 128)]
                k1 = ktab[(g, r, a, QW)]
                kc0 = bass.ds(r + d * (a - 128), 128, step=d)
                kc1 = bass.ds(r + d * a, QW, step=d)
                qc = bass.ds(r + d * a - HALO, QW, step=d)
                b0, b1 = (a - 128) // 128 - bt0, a // 128 - bt0
                ps_s, ps_u = C.ps.next(), C.ps.next()
                P.op("pe", lambda e, ps_s=ps_s, kT=kT, qT=qT, kc0=kc0, qc=qc, QW=QW: e.matmul(ps_s[:, 0:QW], kT[:, kc0], qT[:, qc], start=True, stop=True),
                     reads=[kT, qT], writes=[ps_s])
                P.op("pe", lambda e, ps_s=ps_s, kT=kT, qT=qT, kc1=kc1, qc=qc, QW=QW: e.matmul(ps_s[0:QW, 128:128 + QW], kT[:, kc1], qT[:, qc], start=True, stop=True),
                     reads=[kT, qT], writes=[ps_s])
                Er_, E = Er.next(), Ewr.next()
                P.op("act", lambda e, Er_=Er_, ps_s=ps_s, QW=QW: e.activation(out=Er_[:, 0:QW], in_=ps_s[:, 0:QW], func=AF.Exp, scale=scale), reads=[ps_s], writes=[Er_])
                P.op("act", lambda e, Er_=Er_, ps_s=ps_s, QW=QW: e.activation(out=Er_[0:QW, 128:128 + QW], in_=ps_s[0:QW, 128:128 + QW], func=AF.Exp, scale=scale), reads=[ps_s], writes=[Er_])
                P.op("dve", lambda e, E=E, Er_=Er_, k0=k0, QW=QW: e.scalar_tensor_tensor(E[:, 0:QW], Er_[:, 0:QW], vcol[:, k0:k0 + 1], amask[:, 0:QW], ALU.mult, ALU.mult),
                     reads=[Er_, vcol, amask], writes=[E])
                P.op("dve", lambda e, E=E, Er_=Er_, k1=k1, QW=QW: e.scalar_tensor_tensor(E[0:QW, 128:128 + QW], Er_[0:QW, 128:128 + QW], vcol[0:QW, k1:k1 + 1], amask[0:QW, 128:128 + QW], ALU.mult, ALU.mult),
                     reads=[Er_, vcol, amask], writes=[E])
                P.op("pe", lambda e, ps_u=ps_u, vt=vt, r=r, b0=b0, E=E, QW=QW: e.matmul(ps_u[:, 0:QW], vt[:, r, b0, :], E[:, 0:QW], start=True, stop=False), reads=[vt, E], writes=[ps_u])
                P.op("pe", lambda e, ps_u=ps_u, vt=vt, r=r, b1=b1, E=E, QW=QW: e.matmul(ps_u[:, 0:QW], vt[0:QW, r, b1, :], E[0:QW, 128:128 + QW], start=False, stop=True), reads=[vt, E], writes=[ps_u])
                P.op("pe", lambda e, ps_u=ps_u, E=E, QW=QW: e.matmul(ps_u[:, 128:128 + QW], ones[:, :], E[:, 0:QW], start=True, stop=False), reads=[ones, E], writes=[ps_u])
                P.op("pe", lambda e, ps_u=ps_u, E=E, QW=QW: e.matmul(ps_u[:, 128:128 + QW], ones[0:QW, :], E[0:QW, 128:128 + QW], start=False, stop=True), reads=[ones, E], writes=[ps_u])
                for w in range(2):
                    if g == 0:
                        P.op("dve", lambda e, UL=UL, ps_u=ps_u, w=w, qc=qc, QW=QW: e.tensor_copy(UL[:, w, qc], ps_u[:, w * 128:w * 128 + QW]), reads=[ps_u], writes=[UL])
                    else:
                        P.op("dve", lambda e, UL=UL, ps_u=ps_u, w=w, qc=qc, QW=QW: e.tensor_tensor(UL[:, w, qc], UL[:, w, qc], ps_u[:, w * 128:w * 128 + QW], ALU.add), reads=[ps_u, UL], writes=[UL])
        o = ob.next()
        P.op("dve", lambda e, UL=UL: e.reciprocal(rl[:], UL[:, 1, :]), reads=[UL], writes=[rl])
        P.op("dve", lambda e, UL=UL, o=o: e.tensor_tensor(o[:], UL[:, 0, :], rl[:], ALU.mult), reads=[UL, rl], writes=[o])
        C.store(o, o[:], o_aT, o_aT.t[hh * 128:(hh + 1) * 128, :])
    P.end_phase()


def v3(ap, h):
    return ap.rearrange("p (h j) -> p h j", h=h)


def bcl(ap2, n):
    p, h = ap2.shape
    return ap2.unsqueeze(2).broadcast_to([p, h, n])


def dn_prep_bufs(C):
    P, I = C.P, C.I
    Z = Ctx()
    Z.cw = P.sbuf([128, 48, 4], F32)
    P.dma("sp", lambda e: e.dma_start(out=Z.cw[:], in_=I.conv_wT.rearrange("(c p) j -> p c j", p=128)), P.ext(I.conv_wT), Z.cw)
    Z.ones = P.sbuf([128, 128], F32)
    P.op("dve", lambda e: e.memset(Z.ones[:], 1.0), writes=[Z.ones])
    Z.halo = P.sbuf([128, 48, 3], F32)
    P.op("dve", lambda e: e.memset(Z.halo[:], 0.0), writes=[Z.halo])
    mk = lambda w: Ring([P.sbuf([128, w], F32) for _ in range(4)])
    Z.xr, Z.yr, Z.y2r, Z.sqr, Z.rr, Z.y3r = mk(515), mk(512), mk(512), mk(512), mk(512), mk(512)
    return Z


def dn_prep_epi(C, Z, fc0, dst_of):
    P = C.P
    cw, ones, halo = Z.cw, Z.ones, Z.halo

    def epi_grp(tb, nb, accs, ncols):
        st = []
        for j, acc in enumerate(accs):
            fc = fc0 + nb * 4 + j
            dst, drow = dst_of(fc)
            x = Z.xr.next()
            P.op("act", lambda e, x=x, acc=acc: e.activation(out=x[:, 3:515], in_=acc[:, :], func=AF.Copy), reads=[acc], writes=[x])
            st.append(dict(fc=fc, x=x, y=Z.yr.next(), y2=Z.y2r.next(), dst=dst, drow=drow, blk=tb))
        for t in st:
            P.op("dve", lambda e, t=t: e.tensor_copy(t["x"][:, 0:3], halo[:, t["fc"], :]), reads=[halo], writes=[t["x"]])
        for t in st:
            P.op("dve", lambda e, t=t: e.tensor_copy(halo[:, t["fc"], :], t["x"][:, 512:515]), reads=[t["x"]], writes=[halo])
        for t in st:
            P.op("dve", lambda e, t=t: e.tensor_scalar_mul(t["y"][:], t["x"][:, 3:515], cw[:, t["fc"], 3:4]), reads=[t["x"], cw], writes=[t["y"]])
        for j in (2, 1, 0):
            for t in st:
                P.op("dve", lambda e, t=t, j=j: e.scalar_tensor_tensor(t["y"][:], t["x"][:, j:j + 512], cw[:, t["fc"], j:j + 1], t["y"][:], ALU.mult, ALU.add),
                     reads=[t["x"], cw, t["y"]], writes=[t["y"]])
        for t in st:
            P.op("act", lambda e, t=t: e.activation(out=t["y2"][:], in_=t["y"][:], func=AF.Silu), reads=[t["y"]], writes=[t["y2"]])
        nt = [t for t in st if t["fc"] < 32]
        for t in nt:
            t["sq"], t["r"], t["y3"], t["ps"] = Z.sqr.next(), Z.rr.next(), Z.y3r.next(), C.ps.next()
            P.op("dve", lambda e, t=t: e.tensor_tensor(t["sq"][:], t["y2"][:], t["y2"][:], ALU.mult), reads=[t["y2"]], writes=[t["sq"]])
        for t in nt:
            P.op("pe", lambda e, t=t: e.matmul(t["ps"][:, :], ones[:, :], t["sq"][:, :], start=True, stop=True), reads=[ones, t["sq"]], writes=[t["ps"]])
        for t in nt:
            P.op("act", lambda e, t=t: e.activation(out=t["r"][:], in_=t["ps"][:], func=AF.Sqrt, bias=EPS), reads=[t["ps"]], writes=[t["r"]])
        for t in nt:
            P.op("dve", lambda e, t=t: e.reciprocal(t["r"][:], t["r"][:]), reads=[t["r"]], writes=[t["r"]])
        for t in nt:
            sc = 128.0 ** -0.5 if t["fc"] < 16 else 1.0
            P.op("dve", lambda e, t=t, sc=sc: e.scalar_tensor_tensor(t["y3"][:], t["y2"][:], sc, t["r"][:], ALU.mult, ALU.mult), reads=[t["y2"], t["r"]], writes=[t["y3"]])
            t["y2"] = t["y3"]
        for t in st:
            C.store(t["y2"], t["y2"][:], t["dst"], t["dst"].t[t["drow"]:t["drow"] + 128, t["blk"] * 512:(t["blk"] + 1) * 512])
    return epi_grp


def dn_scan(C, ba, khT, vT, qhT, ztm, o_bT):
    P, I = C.P, C.I
    NCH = S // 64
    OWN0 = NCH - TOWN // 64
    TQ = qhT.t.shape[1]
    P.begin_phase()
    dnc = P.sbuf([128, 6 * 512], F32)
    P.dma("sp", lambda e: e.dma_start(out=dnc[:], in_=I.dnc), P.ext(I.dnc), dnc)
    Irep, strictrep, strictTrep = v3(dnc[0:64, 0:512], 8), v3(dnc[0:64, 512:1024], 8), v3(dnc[0:64, 1024:1536], 8)
    mneg, mnegT = dnc[0:64, 1536:2048], dnc[0:64, 2048:2560]
    Utri, ones, negones, negI64 = dnc[0:64, 2560:2624], dnc[:, 2624:2752], dnc[:, 2752:2880], dnc[0:64, 2880:2944]
    I64 = C.ident[0:64, 0:64]
    ident = C.ident
    gc_all = P.sbuf([64, NCH, 16], F32)
    beta_all = P.sbuf([64, NCH, 16], F32)
    kdecs_all = P.sbuf([64, NCH, 16], F32)
    egl_all = P.sbuf([128, NCH, 16], F32)
    alog = P.sbuf([64, 16], F32)
    dtb = P.sbuf([64, 16], F32)
    P.dma("sp", lambda e: e.dma_start(out=alog[:], in_=bc_ap(I.dn_a_log, 64)), P.ext(I.dn_a_log), alog)
    P.dma("sp", lambda e: e.dma_start(out=dtb[:], in_=bc_ap(I.dn_dt_bias, 64)), P.ext(I.dn_dt_bias), dtb)
    P.op("act", lambda e: e.activation(out=alog[:], in_=alog[:], func=AF.Exp), reads=[alog], writes=[alog])
    P.op("dve", lambda e: e.tensor_scalar_mul(alog[:], alog[:], -1.0), reads=[alog], writes=[alog])
    dnn = P.sbuf([64, 128], F32)
    P.dma("sp", lambda e: e.dma_start(out=dnn[:], in_=bc_ap(I.dn_norm, 64)), P.ext(I.dn_norm), dnn)
    QC = 16
    baq = P.sbuf([64, QC, 32], F32)
    spq = P.sbuf([64, QC, 16], F32)
    gq = P.sbuf([64, QC, 16], F32)
    tq = P.sbuf([64, QC, 16], F32)
    bav = ba.t.rearrange("(c t) n -> t c n", t=64)
    for q in range(NCH // QC if DN_SET >= 1 else 0):
        cs = slice(q * QC, (q + 1) * QC)
        P.dma("sp", lambda e, cs=cs: e.dma_start(out=baq[:], in_=bav[:, cs, :]), ba, baq)
        if DN_SET < 2:
            continue
        P.op("act", lambda e, cs=cs: e.activation(out=beta_all[:, cs, :], in_=baq[:, :, 0:16], func=AF.Sigmoid), reads=[baq], writes=[beta_all])
        if DN_SET < 3:
            continue
        P.op("dve", lambda e: e.tensor_tensor(spq[:], baq[:, :, 16:32], dtb[:].unsqueeze(1).broadcast_to([64, QC, 16]), ALU.add), reads=[baq, dtb], writes=[spq])
        P.op("act", lambda e: e.activation(out=spq[:], in_=spq[:], func=AF.Exp), reads=[spq], writes=[spq])
        P.op("act", lambda e: e.activation(out=spq[:], in_=spq[:], func=AF.Ln, bias=1.0), reads=[spq], writes=[spq])
        P.op("dve", lambda e: e.tensor_tensor(gq[:], spq[:], alog[:].unsqueeze(1).broadcast_to([64, QC, 16]), ALU.mult), reads=[spq, alog], writes=[gq])
        if DN_SET < 4:
            continue
        ps1, ps2 = C.ps.next(), C.ps.next()
        gflat = gq[:].rearrange("p c h -> p (c h)")
        P.op("pe", lambda e, ps1=ps1: e.matmul(ps1[0:64, 0:QC * 16], Utri, gflat, start=True, stop=True), reads=[dnc, gq], writes=[ps1])
        P.op("pe", lambda e, ps2=ps2: e.matmul(ps2[:, 0:QC * 16], ones[0:64, :], gflat, start=True, stop=True), reads=[dnc, gq], writes=[ps2])
        if DN_SET < 5:
            continue
        P.op("dve", lambda e, ps1=ps1, cs=cs: e.tensor_copy(gc_all[:, cs, :], v3(ps1[0:64, 0:QC * 16], QC)), reads=[ps1], writes=[gc_all])
        if DN_SET < 6:
            continue
        if DN_SET < 7:
            continue
        P.op("act", lambda e, ps2=ps2, cs=cs: e.activation(out=egl_all[:, cs, :], in_=v3(ps2[:, 0:QC * 16], QC), func=AF.Exp), reads=[ps2], writes=[egl_all])
        if DN_SET < 8:
            continue
        P.op("dve", lambda e, ps2=ps2, cs=cs: e.tensor_tensor(tq[:], v3(ps2[0:64, 0:QC * 16], QC), gc_all[:, cs, :], ALU.subtract), reads=[ps2, gc_all], writes=[tq])
        if DN_SET < 9:
            continue
        P.op("act", lambda e, cs=cs: e.activation(out=kdecs_all[:, cs, :], in_=tq[:], func=AF.Exp), reads=[tq], writes=[kdecs_all])
    if hasattr(C, "dbg"):
        for nm, b_ in (("gc_all", gc_all), ("beta_all", beta_all), ("kdecs_all", kdecs_all), ("egl_all", egl_all)):
            C.dbg(nm, b_, b_[:].rearrange("p c h -> p (c h)"))
    S4 = [P.sbuf([128, 4, 128], F32) for _ in range(4)]
    for s_ in S4:
        P.op("dve", lambda e, s_=s_: e.memset(s_[:], 0.0), writes=[s_])
    kcr = Ring([P.sbuf([128, 16, 64], F32) for _ in range(2)])
    vcr = Ring([P.sbuf([128, 16, 64], F32) for _ in range(2)])
    qc_ = P.sbuf([128, 16, 64], F32)
    zc_ = P.sbuf([64, 2048], F32)
    obT = P.sbuf([128, 16, 64], BF16)
    nb_ = P.sbuf([64, 16], F32)
    kbgs = P.sbuf([64, 16], F32)
    egc = P.sbuf([64, 16], F32)
    T = lambda: P.sbuf([64, 8, 64], F32)
    T4 = lambda: P.sbuf([64, 8, 128], F32)
    F = lambda b: b[:].rearrange("p h j -> p (h j)")

    def mk():
        X = Ctx()
        X.diagG, X.diagB, X.diagE, X.Dm, X.DmT, X.t1, X.t2, X.intraT = (T() for _ in range(8))
        X.sets = [(T(), T(), T()) for _ in range(2)]
        X.qdec, X.wT = P.sbuf([128, 8, 64], F32), P.sbuf([128, 8, 64], F32)
        X.kbg, X.kdec, X.vb, X.u, X.vnew = (T4() for _ in range(5))
        X.osb, X.on = P.sbuf([64, 4, 128], F32), P.sbuf([64, 4, 128], F32)
        X.ss = P.sbuf([64, 8], F32)
        return X
    XS = [mk(), mk()]
    chunks = range(NCH) if DN_CHUNKS is None else list(range(DN_CHUNKS)) + ([OWN0] if DN_CUT >= 8 else [])
    for c in (chunks if DN_CUT >= 1 else []):
        own = c >= OWN0 and DN_CUT >= 8
        kc_, vc_ = kcr.next(), vcr.next()
        P.dma("sp", lambda e, kc_=kc_, c=c: e.dma_start(out=kc_[:], in_=khT.t.rearrange("(h d) t -> d h t", d=128)[:, :, c * 64:(c + 1) * 64]), khT, kc_)
        P.dma("sp", lambda e, vc_=vc_, c=c: e.dma_start(out=vc_[:], in_=vT.t.rearrange("(h d) t -> d h t", d=128)[:, :, c * 64:(c + 1) * 64]), vT, vc_)
        if own:
            q0 = (c - OWN0) * 64 + (TQ - TOWN)
            P.dma("sp", lambda e, q0=q0: e.dma_start(out=qc_[:], in_=qhT.t.rearrange("(h d) t -> d h t", d=128)[:, :, q0:q0 + 64]), qhT, qc_)
            P.dma("sp", lambda e, c=c: e.dma_start(out=zc_[:], in_=ztm.t[(c - OWN0) * 64:(c - OWN0 + 1) * 64, :]), ztm, zc_)
        P.op("dve", lambda e, c=c: e.tensor_scalar_mul(nb_[:], beta_all[:, c, :], -1.0), reads=[beta_all], writes=[nb_])
        P.op("act", lambda e, c=c: e.activation(out=egc[:], in_=gc_all[:, c, :], func=AF.Exp), reads=[gc_all], writes=[egc])
        P.op("dve", lambda e, c=c: e.tensor_tensor(kbgs[:], beta_all[:, c, :], egc[:], ALU.mult), reads=[beta_all, egc], writes=[kbgs])
        HG = (0, 1)
        hsl = lambda hg: slice(hg * 8, hg * 8 + 8)
        for hg in HG:
            X = XS[hg]
            X.gcs = bcl(gc_all[:, c, hsl(hg)], 64)
            P.op("dve", lambda e, X=X, g_=X.gcs: e.tensor_tensor(X.diagG[:], Irep, g_, ALU.mult), reads=[dnc, gc_all], writes=[X.diagG])
            P.op("dve", lambda e, X=X, c=c, hg=hg: e.tensor_tensor(X.diagB[:], Irep, bcl(beta_all[:, c, hsl(hg)], 64), ALU.mult), reads=[dnc, beta_all], writes=[X.diagB])
        for hg in HG:
            X = XS[hg]
            h0 = hg * 8
            X.p1, X.p3, X.p4, X.p5 = C.ps.next(), C.ps.next(), C.ps.next(), C.ps.next()
            for h in range(8):
                P.op("pe", lambda e, p1=X.p1, kc_=kc_, h=h, h0=h0: e.matmul(p1[0:64, h * 64:(h + 1) * 64], kc_[:, h0 + h, :], kc_[:, h0 + h, :], start=True, stop=True),
                     reads=[kc_], writes=[X.p1], signal=(h == 7))
            P.op("pe", lambda e, X=X, p3=X.p3: e.matmul(p3[0:64, :], negones[0:64, 0:64], F(X.diagG), start=True, stop=False), reads=[dnc, X.diagG], writes=[X.p3])
            P.op("pe", lambda e, X=X, g_=X.gcs, p3=X.p3: e.matmul(p3[0:64, :], I64, g_, start=False, stop=False), reads=[ident, gc_all], writes=[X.p3])
            P.op("pe", lambda e, X=X, p3=X.p3: e.matmul(p3[0:64, :], I64, mneg, start=False, stop=True), reads=[ident, dnc], writes=[X.p3])
            P.op("pe", lambda e, X=X, p4=X.p4: e.matmul(p4[0:64, :], ones[0:64, 0:64], F(X.diagG), start=True, stop=False), reads=[dnc, X.diagG], writes=[X.p4])
            P.op("pe", lambda e, X=X, g_=X.gcs, p4=X.p4: e.matmul(p4[0:64, :], negI64, g_, start=False, stop=False), reads=[dnc, gc_all], writes=[X.p4])
            P.op("pe", lambda e, X=X, p4=X.p4: e.matmul(p4[0:64, :], I64, mnegT, start=False, stop=True), reads=[ident, dnc], writes=[X.p4])
            P.op("pe", lambda e, X=X, p5=X.p5: e.matmul(p5[0:64, :], negones[0:64, 0:64], F(X.diagB), start=True, stop=True), reads=[dnc, X.diagB], writes=[X.p5])
        for hg in HG:
            X = XS[hg]
            P.op("act", lambda e, X=X, p3=X.p3: e.activation(out=F(X.Dm), in_=p3[0:64, :], func=AF.Exp), reads=[X.p3], writes=[X.Dm])
            P.op("act", lambda e, X=X, p4=X.p4: e.activation(out=F(X.DmT), in_=p4[0:64, :], func=AF.Exp), reads=[X.p4], writes=[X.DmT])
        if DN_CUT < 3:
            continue
        for hg in HG:
            X = XS[hg]
            P.op("dve", lambda e, X=X, p1=X.p1: e.tensor_tensor(X.t1[:], v3(p1[0:64, :], 8), strictrep, ALU.mult), reads=[X.p1, dnc], writes=[X.t1])
            P.op("dve", lambda e, X=X, p1=X.p1: e.tensor_tensor(X.t2[:], v3(p1[0:64, :], 8), strictTrep, ALU.mult), reads=[X.p1, dnc], writes=[X.t2])
        for hg in HG:
            X = XS[hg]
            P.op("dve", lambda e, X=X: e.tensor_tensor(X.t1[:], X.t1[:], X.Dm[:], ALU.mult), reads=[X.t1, X.Dm], writes=[X.t1])
            P.op("dve", lambda e, X=X: e.tensor_tensor(X.t2[:], X.t2[:], X.DmT[:], ALU.mult), reads=[X.t2, X.DmT], writes=[X.t2])
        for hg in HG:
            X = XS[hg]
            M, MT, R = X.sets[0]
            P.op("dve", lambda e, X=X, M=M, hg=hg: e.tensor_tensor(M[:], X.t1[:], bcl(nb_[:, hsl(hg)], 64), ALU.mult), reads=[X.t1, nb_], writes=[M])
            P.op("dve", lambda e, X=X, MT=MT, p5=X.p5: e.tensor_tensor(MT[:], X.t2[:], v3(p5[0:64, :], 8), ALU.mult), reads=[X.t2, X.p5], writes=[MT])
        for hg in HG:
            M, MT, R = XS[hg].sets[0]
            P.op("dve", lambda e, R=R, MT=MT: e.tensor_tensor(R[:], MT[:], Irep, ALU.add), reads=[MT, dnc], writes=[R])
        if DN_CUT < 4:
            continue
        cur = 0
        for m in range(1, 6):
            for hg in HG:
                X = XS[hg]
                M, MT, R = X.sets[cur]
                X.pm, X.pmt = C.ps.next(), C.ps.next()
                for h in range(8):
                    P.op("pe", lambda e, pm=X.pm, M=M, MT=MT, h=h: e.matmul(pm[0:64, h * 64:(h + 1) * 64], MT[:, h, :], M[:, h, :], start=True, stop=True), reads=[M, MT], writes=[X.pm], signal=(h == 7))
                if m < 5:
                    for h in range(8):
                        P.op("pe", lambda e, pmt=X.pmt, M=M, MT=MT, h=h: e.matmul(pmt[0:64, h * 64:(h + 1) * 64], M[:, h, :], MT[:, h, :], start=True, stop=True), reads=[M, MT], writes=[X.pmt], signal=(h == 7))
            for hg in HG:
                X = XS[hg]
                Mn, MTn, Rn = X.sets[1 - cur]
                P.op("act", lambda e, Mn=Mn, pm=X.pm: e.activation(out=F(Mn), in_=pm[0:64, :], func=AF.Copy), reads=[X.pm], writes=[Mn])
                if m < 5:
                    P.op("act", lambda e, MTn=MTn, pmt=X.pmt: e.activation(out=F(MTn), in_=pmt[0:64, :], func=AF.Copy), reads=[X.pmt], writes=[MTn])
            for hg in HG:
                X = XS[hg]
                M, MT, R = X.sets[cur]
                Mn, MTn, Rn = X.sets[1 - cur]
                X.pr = C.ps.next()
                for h in range(8):
                    P.op("pe", lambda e, pr=X.pr, Mn=Mn, R=R, h=h: e.matmul(pr[0:64, h * 64:(h + 1) * 64], Mn[:, h, :], R[:, h, :], start=True, stop=True), reads=[Mn, R], writes=[X.pr], signal=(h == 7))
            for hg in HG:
                X = XS[hg]
                M, MT, R = X.sets[cur]
                Mn, MTn, Rn = X.sets[1 - cur]
                P.op("dve", lambda e, Rn=Rn, R=R, pr=X.pr: e.tensor_tensor(F(Rn), F(R), pr[0:64, :], ALU.add), reads=[R, X.pr], writes=[Rn])
            cur = 1 - cur
        if DN_CUT < 5:
            continue
        for half in range(2):
            for hg in HG:
                X = XS[hg]
                h0 = hg * 8
                X.pk, X.pv = C.ps.next(), C.ps.next()
                for hq in range(4):
                    h = h0 + half * 4 + hq
                    P.op("pe", lambda e, pk=X.pk, kc_=kc_, h=h, hq=hq: e.transpose(pk[0:64, hq * 128:(hq + 1) * 128], kc_[:, h, :], ident[:]), reads=[kc_, ident], writes=[X.pk])
                    P.op("pe", lambda e, pv=X.pv, vc_=vc_, h=h, hq=hq: e.transpose(pv[0:64, hq * 128:(hq + 1) * 128], vc_[:, h, :], ident[:]), reads=[vc_, ident], writes=[X.pv])
            for hg in HG:
                X = XS[hg]
                h0 = hg * 8
                a4 = slice(half * 4, half * 4 + 4)
                g4 = slice(h0 + half * 4, h0 + half * 4 + 4)
                P.op("dve", lambda e, X=X, pk=X.pk, a4=a4, g4=g4: e.tensor_tensor(X.kbg[:, a4, :], v3(pk[0:64, :], 4), bcl(kbgs[:, g4], 128), ALU.mult), reads=[X.pk, kbgs], writes=[X.kbg])
                P.op("dve", lambda e, X=X, pk=X.pk, a4=a4, g4=g4, c=c: e.tensor_tensor(X.kdec[:, a4, :], v3(pk[0:64, :], 4), bcl(kdecs_all[:, c, g4], 128), ALU.mult), reads=[X.pk, kdecs_all], writes=[X.kdec])
                P.op("dve", lambda e, X=X, pv=X.pv, a4=a4, g4=g4, c=c: e.tensor_tensor(X.vb[:, a4, :], v3(pv[0:64, :], 4), bcl(beta_all[:, c, g4], 128), ALU.mult), reads=[X.pv, beta_all], writes=[X.vb])
        if DN_CUT < 6:
            continue
        for hg in HG:
            X = XS[hg]
            R = X.sets[cur][2]
            X.pw = C.ps.next()
            for h in range(8):
                P.op("pe", lambda e, X=X, pw=X.pw, R=R, h=h: e.matmul(pw[:, h * 64:(h + 1) * 64], X.kbg[:, h, :], R[:, h, :], start=True, stop=True), reads=[X.kbg, R], writes=[X.pw], signal=(h == 7))
        for hg in HG:
            X = XS[hg]
            P.op("act", lambda e, X=X, pw=X.pw: e.activation(out=F(X.wT), in_=pw[:, :], func=AF.Copy), reads=[X.pw], writes=[X.wT])
        for half in range(2):
            for hg in HG:
                X = XS[hg]
                R = X.sets[cur][2]
                X.pu = C.ps.next()
                for hq in range(4):
                    h = half * 4 + hq
                    P.op("pe", lambda e, X=X, pu=X.pu, R=R, h=h, hq=hq: e.matmul(pu[0:64, hq * 128:(hq + 1) * 128], R[:, h, :], X.vb[:, h, :], start=True, stop=True), reads=[R, X.vb], writes=[X.pu], signal=(hq == 3))
            for hg in HG:
                X = XS[hg]
                P.op("act", lambda e, X=X, pu=X.pu, half=half: e.activation(out=X.u[:, half * 4:half * 4 + 4, :], in_=v3(pu[0:64, :], 4), func=AF.Copy), reads=[X.pu], writes=[X.u])
        if own:
            for hg in HG:
                X = XS[hg]
                h0 = hg * 8
                X.p2, X.p6 = C.ps.next(), C.ps.next()
                for h in range(8):
                    P.op("pe", lambda e, p2=X.p2, kc_=kc_, h=h, h0=h0: e.matmul(p2[0:64, h * 64:(h + 1) * 64], kc_[:, h0 + h, :], qc_[:, h0 + h, :], start=True, stop=True),
                         reads=[kc_, qc_], writes=[X.p2], signal=(h == 7))
                P.op("dve", lambda e, X=X, hg=hg: e.tensor_tensor(X.diagE[:], Irep, bcl(egc[:, hsl(hg)], 64), ALU.mult), reads=[dnc, egc], writes=[X.diagE])
            for hg in HG:
                X = XS[hg]
                P.op("dve", lambda e, X=X, p2=X.p2: e.tensor_tensor(X.intraT[:], v3(p2[0:64, :], 8), X.DmT[:], ALU.mult), reads=[X.p2, X.DmT], writes=[X.intraT])
                P.op("pe", lambda e, X=X, p6=X.p6: e.matmul(p6[:, :], ones[0:64, :], F(X.diagE), start=True, stop=True), reads=[dnc, X.diagE], writes=[X.p6])
            for hg in HG:
                X = XS[hg]
                P.op("dve", lambda e, X=X, p6=X.p6, hg=hg: e.tensor_tensor(X.qdec[:], qc_[:, hsl(hg), :], v3(p6[:, :], 8), ALU.mult), reads=[qc_, X.p6], writes=[X.qdec])
        if DN_CUT < 7:
            continue
        for half in range(2):
            a4 = slice(half * 4, half * 4 + 4)
            for hg in HG:
                X = XS[hg]
                Sb = S4[hg * 2 + half]
                X.pws = C.ps.next()
                for hq in range(4):
                    h = half * 4 + hq
                    P.op("pe", lambda e, X=X, pws=X.pws, Sb=Sb, h=h, hq=hq: e.matmul(pws[0:64, hq * 128:(hq + 1) * 128], X.wT[:, h, :], Sb[:, hq, :], start=True, stop=True), reads=[X.wT, Sb], writes=[X.pws], signal=(hq == 3))
            for hg in HG:
                X = XS[hg]
                P.op("dve", lambda e, X=X, pws=X.pws, a4=a4: e.tensor_tensor(X.vnew[:, a4, :], X.u[:, a4, :], v3(pws[0:64, :], 4), ALU.subtract), reads=[X.u, X.pws], writes=[X.vnew])
            for hg in HG:
                X = XS[hg]
                Sb = S4[hg * 2 + half]
                if own:
                    X.po = C.ps.next()
                    for hq in range(4):
                        h = half * 4 + hq
                        P.op("pe", lambda e, X=X, po=X.po, Sb=Sb, h=h, hq=hq: e.matmul(po[0:64, hq * 128:(hq + 1) * 128], X.qdec[:, h, :], Sb[:, hq, :], start=True, stop=False), reads=[X.qdec, Sb], writes=[X.po])
                        P.op("pe", lambda e, X=X, po=X.po, h=h, hq=hq: e.matmul(po[0:64, hq * 128:(hq + 1) * 128], X.intraT[:, h, :], X.vnew[:, h, :], start=False, stop=True), reads=[X.intraT, X.vnew], writes=[X.po])
                X.psu = C.ps.next()
                for hq in range(4):
                    h = half * 4 + hq
                    P.op("pe", lambda e, X=X, psu=X.psu, h=h, hq=hq: e.matmul(psu[:, hq * 128:(hq + 1) * 128], X.kdec[:, h, :], X.vnew[:, h, :], start=True, stop=True), reads=[X.kdec, X.vnew], writes=[X.psu], signal=(hq == 3))
            for hg in HG:
                Sb = S4[hg * 2 + half]
                g4 = slice(hg * 8 + half * 4, hg * 8 + half * 4 + 4)
                P.op("dve", lambda e, Sb=Sb, c=c, g4=g4: e.tensor_tensor(Sb[:], Sb[:], bcl(egl_all[:, c, g4], 128), ALU.mult), reads=[Sb, egl_all], writes=[Sb])
            for hg in HG:
                X = XS[hg]
                Sb = S4[hg * 2 + half]
                P.op("dve", lambda e, Sb=Sb, psu=X.psu: e.tensor_tensor(Sb[:], Sb[:], v3(psu[:, :], 4), ALU.add), reads=[Sb, X.psu], writes=[Sb])
            if own:
                for hg in HG:
                    X = XS[hg]
                    P.op("act", lambda e, X=X, po=X.po: e.activation(out=X.osb[:], in_=v3(po[0:64, :], 4), func=AF.Copy), reads=[X.po], writes=[X.osb])
                for hg in HG:
                    X = XS[hg]
                    P.op("dve", lambda e, X=X: e.tensor_tensor(X.on[:], X.osb[:], X.osb[:], ALU.mult), reads=[X.osb], writes=[X.on])
                for hg in HG:
                    X = XS[hg]
                    P.op("dve", lambda e, X=X: e.tensor_reduce(X.ss[:, 0:4], X.on[:], AX.X, ALU.add), reads=[X.on], writes=[X.ss])
                for hg in HG:
                    X = XS[hg]
                    P.op("dve", lambda e, X=X: e.tensor_scalar(X.ss[:, 4:8], X.ss[:, 0:4], 1.0 / 128, EPS, ALU.mult, ALU.add), reads=[X.ss], writes=[X.ss])
                for hg in HG:
                    X = XS[hg]
                    P.op("act", lambda e, X=X: e.activation(out=X.ss[:, 4:8], in_=X.ss[:, 4:8], func=AF.Sqrt), reads=[X.ss], writes=[X.ss])
                for hg in HG:
                    X = XS[hg]
                    P.op("dve", lambda e, X=X: e.reciprocal(X.ss[:, 4:8], X.ss[:, 4:8]), reads=[X.ss], writes=[X.ss])
                for hg in HG:
                    X = XS[hg]
                    P.op("dve", lambda e, X=X: e.tensor_tensor(X.on[:], X.osb[:], bcl(X.ss[:, 4:8], 128), ALU.mult), reads=[X.osb, X.ss], writes=[X.on])
                for hg in HG:
                    X = XS[hg]
                    P.op("dve", lambda e, X=X: e.tensor_tensor(X.on[:], X.on[:], dnn[:].unsqueeze(1).broadcast_to([64, 4, 128]), ALU.mult), reads=[X.on, dnn], writes=[X.on])
                for hg in HG:
                    X = XS[hg]
                    z0 = (hg * 8 + half * 4) * 128
                    P.op("dve", lambda e, X=X, z0=z0: e.tensor_tensor(X.on[:], X.on[:], v3(zc_[:, z0:z0 + 512], 4), ALU.mult), reads=[X.on, zc_], writes=[X.on])
                for hg in HG:
                    X = XS[hg]
                    X.pt = C.ps.next()
                    for hq in range(4):
                        P.op("pe", lambda e, X=X, pt=X.pt, hq=hq: e.transpose(pt[:, hq * 64:(hq + 1) * 64], X.on[:, hq, :], I64), reads=[X.on, ident], writes=[X.pt], signal=(hq == 3))
                for hg in HG:
                    X = XS[hg]
                    g4 = slice(hg * 8 + half * 4, hg * 8 + half * 4 + 4)
                    P.op("act", lambda e, pt=X.pt, g4=g4: e.activation(out=obT[:, g4, :], in_=v3(pt[:, 0:256], 4), func=AF.Copy), reads=[X.pt], writes=[obT])
        if own:
            C.store(obT, obT[:], o_bT, o_bT.t.rearrange("(h d) t -> d h t", d=128)[:, :, (c - OWN0) * 64:(c - OWN0 + 1) * 64])
    if hasattr(C, "dbg"):
        X = XS[1]
        for nm, b_ in (("Dm", X.Dm), ("DmT", X.DmT), ("R0", X.sets[0][2]), ("R1", X.sets[1][2]), ("M0", X.sets[0][0]), ("intraT", X.intraT)):
            C.dbg(nm, b_, b_[:].rearrange("p h j -> p (h j)"))
        for nm, b_ in (("u", X.u), ("vnew", X.vnew), ("kbg", X.kbg), ("kdec", X.kdec), ("vb", X.vb)):
            C.dbg(nm, b_, b_[:].rearrange("p h j -> p (h j)"))
        C.dbg("wT", X.wT, X.wT[:].rearrange("p h j -> p (h j)"))
        for i_, s_ in enumerate(S4):
            C.dbg("S%d" % i_, s_, s_[:].rearrange("p h j -> p (h j)"))
    P.end_phase()


def build_all():
    C = build()
    P, I, nc, out = C.P, C.I, C.nc, C.out
    gemm, store, norm_T, dram = C.gemm, C.store, C.norm_T, C.dram
    xrB = P.ext(I.xr)
    T0 = S - TOWN
    TQ = TOWN + 512

    P.begin_phase()
    C.gemm_bufs()
    hT = dram("hT", [D, S], BF16)
    norm_T(xrB, 0, S, I.norm_mix, hT)
    P.end_phase()
    P.begin_phase()
    C.gemm_bufs(norm=False)
    gT = dram("gT", [2 * D, TOWN], BF16)
    gemm(hT, T0, TOWN, I.w_in, GA, 2 * D, D, "A", C.epi_store_T(gT, BF16, AF.Sigmoid))
    o_aT = dram("o_aT", [1024, TOWN], BF16)
    o_bT = dram("o_bT", [2048, TOWN], BF16)
    if "attn" in STAGES:
        qaT = dram("qaT", [3072, TOWN], BF16)
        kaT = dram("kaT", [3072, TA], BF16)
        vtok = dram("vtok", [TA, 3072], BF16)
        gemm(hT, T0, TOWN, I.w_in, QA, 3072, D, "A", C.epi_store_T(qaT, BF16))
        TS = TOWN + 512
        gemm(hT, S - TS, TS, I.w_in, KA, 2048, D, "A", C.epi_store_T(kaT, BF16, coff=TA - TS))
        gemm(hT, S - TA, TA, I.w_in, KA + 2048, 1024, D, "A", C.epi_store_T(kaT, BF16, roff=2048))
        gemm(hT, S - TS, TS, I.w_in, VA, 2048, D, "B", C.epi_tm(vtok, AF.Copy, BF16, roff=TA - TS))
        gemm(hT, S - TA, TA, I.w_in, VA + 2048, 1024, D, "B", C.epi_tm(vtok, AF.Copy, BF16, coff=2048))
    if "dn" in STAGES:
        ztm = dram("ztm", [TOWN, 2048], F32)
        ba = dram("ba", [S, 32], F32)
        khT = dram("khT", [2048, S], F32)
        vT = dram("vT", [2048, S], F32)
        qhT = dram("qhT", [2048, TQ], F32)
        Z = dn_prep_bufs(C)
        kv_dst = lambda fc: (khT, (fc - 16) * 128) if fc < 32 else (vT, (fc - 32) * 128)
        gemm(hT, 0, S, I.w_in, KB, 4096, D, "A", None, epi_grp=dn_prep_epi(C, Z, 16, kv_dst))
        gemm(hT, S - TQ, TQ, I.w_in, QB, 2048, D, "A", None, epi_grp=dn_prep_epi(C, Z, 0, lambda fc: (qhT, fc * 128)))
        gemm(hT, T0, TOWN, I.w_in, ZB, 2048, D, "B", C.epi_tm(ztm, AF.Silu))
        gemm(hT, 0, S, I.w_in, BETA, 32, D, "B", C.epi_tm(ba, AF.Copy))
    P.end_phase()

    if "attn" in STAGES:
        attention(C, qaT, kaT, vtok, o_aT)
    if "dn" in STAGES:
        if "noscan" not in STAGES:
            dn_scan(C, ba, khT, vT, qhT, ztm, o_bT)

    P.begin_phase()
    C.gemm_bufs()
    e32, e16 = C.st32, C.st16
    AT = dram("AT", [D, TOWN], F32)
    BT = dram("BT", [D, TOWN], F32)
    gemm(o_aT, 0, TOWN, I.w_attn_up, 0, D, 1024, "A", C.epi_store_T(AT, F32))
    gemm(o_bT, 0, TOWN, I.w_dn_up, 0, D, 2048, "A", C.epi_store_T(BT, F32))
    mT = dram("mT", [D, TOWN], BF16)
    for r in range(D // 128):
        for tb in range(TOWN // 512):
            sl = (slice(r * 128, (r + 1) * 128), slice(tb * 512, (tb + 1) * 512))
            a, b, ga, gb, m = e32.next(), e32.next(), e16.next(), e16.next(), e16.next()
            P.dma("sp", lambda e, a=a, sl=sl: e.dma_start(out=a[:], in_=AT.t[sl]), AT, a)
            P.dma("sp", lambda e, b=b, sl=sl: e.dma_start(out=b[:], in_=BT.t[sl]), BT, b)
            P.dma("sp", lambda e, ga=ga, sl=sl: e.dma_start(out=ga[:], in_=gT.t[sl]), gT, ga)
            P.dma("sp", lambda e, gb=gb, sl=sl, r=r: e.dma_start(out=gb[:], in_=gT.t[D + r * 128:D + (r + 1) * 128, sl[1]]), gT, gb)
            P.op("dve", lambda e, a=a, ga=ga: e.tensor_tensor(a[:], a[:], ga[:], ALU.mult), reads=[a, ga], writes=[a])
            P.op("dve", lambda e, b=b, gb=gb: e.tensor_tensor(b[:], b[:], gb[:], ALU.mult), reads=[b, gb], writes=[b])
            P.op("dve", lambda e, a=a, b=b, m=m: e.tensor_tensor(m[:], a[:], b[:], ALU.add), reads=[a, b], writes=[m])
            store(m, m[:], mT, mT.t[sl])

    def epi_resid(src, src_r0, dst):
        def epi(tb, nb, j, acc, ncols):
            r0 = tb * 512 + j * 128
            xt = e32.next()
            P.dma("sp", lambda e: e.dma_start(out=xt[:, 0:ncols], in_=src.t[src_r0 + r0:src_r0 + r0 + 128, nb * 512:nb * 512 + ncols]), src, xt)
            P.op("dve", lambda e: e.tensor_tensor(xt[:, 0:ncols], xt[:, 0:ncols], acc[:, 0:ncols], ALU.add), reads=[xt, acc], writes=[xt])
            store(xt, xt[:, 0:ncols], dst, dst.t[r0:r0 + 128, nb * 512:nb * 512 + ncols])
        return epi

    x1 = dram("x1", [TOWN, D], F32)
    gemm(mT, 0, TOWN, I.w_out, 0, D, D, "B", epi_resid(xrB, T0, x1))
    h2T = dram("h2T", [D, TOWN], BF16)
    norm_T(x1, 0, TOWN, I.norm_mlp, h2T)
    hidT = dram("hidT", [DFF, TOWN], BF16)

    def epi_relu2(tb, nb, j, acc, ncols):
        t, sb = e32.next(), e16.next()
        P.op("act", lambda e: e.activation(out=t[:], in_=acc[:], func=AF.Relu), reads=[acc], writes=[t])
        P.op("dve", lambda e: e.tensor_tensor(sb[:], t[:], t[:], ALU.mult), reads=[t], writes=[sb])
        r0 = nb * 512 + j * 128
        store(sb, sb[:], hidT, hidT.t[r0:r0 + 128, tb * 512:(tb + 1) * 512])

    gemm(h2T, 0, TOWN, I.w_mlp_up, 0, DFF, D, "A", epi_relu2)
    x2 = dram("x2", [TOWN, D], F32)
    gemm(hidT, 0, TOWN, I.w_mlp_down, 0, D, DFF, "B", epi_resid(x1, 0, x2))
    h3T = dram("h3T", [D, TOWN], BF16)
    norm_T(x2, 0, TOWN, I.norm_ple, h3T)
    gate = dram("gate", [TOWN, D], F32)
    proj = dram("proj", [TOWN, D], F32)
    gemm(h3T, 0, TOWN, I.w_ple_gate, 0, D, D, "B", C.epi_tm(gate, AF.Sigmoid))
    pT = dram("pT", [256, TOWN], BF16)
    ppB = P.ext(I.pp)
    for tb in range(TOWN // 512):
        pTs = e16.next(), e16.next()
        for tt in range(4):
            pt_in = e32.next()
            r0 = tb * 512 + tt * 128
            P.dma("sp", lambda e, pt_in=pt_in, r0=r0: e.dma_start(out=pt_in[:, 0:256], in_=I.pp[r0:r0 + 128, :]), ppB, pt_in)
            ps = C.ps.next()
            for q in range(2):
                P.op("pe", lambda e, ps=ps, pt_in=pt_in, q=q: e.transpose(ps[:, q * 128:(q + 1) * 128], pt_in[:, q * 128:(q + 1) * 128], C.ident[:]),
                     reads=[pt_in, C.ident], writes=[ps])
            for q in range(2):
                P.op("dve", lambda e, ps=ps, q=q, tt=tt, pTs=pTs: e.tensor_copy(pTs[q][:, tt * 128:(tt + 1) * 128], ps[:, q * 128:(q + 1) * 128]),
                     reads=[ps], writes=[pTs[q]])
        for q in range(2):
            store(pTs[q], pTs[q][:], pT, pT.t[q * 128:(q + 1) * 128, tb * 512:(tb + 1) * 512])
    gemm(pT, 0, TOWN, I.w_ple_proj, 0, D, 256, "B", C.epi_tm(proj, AF.Copy))

    gpp, gfn = C.big[0], C.big[1]
    P.dma("sp", lambda e: e.dma_start(out=gpp[:], in_=bc_ap(I.ple_post, 128)), P.ext(I.ple_post), gpp)
    P.dma("sp", lambda e: e.dma_start(out=gfn[:], in_=bc_ap(I.final_norm, 128)), P.ext(I.final_norm), gfn)
    big = Ring(C.big[2:5])
    junk = C.junk
    outB = P.ext(out)
    for tt in range(TOWN // 128):
        rs = slice(tt * 128, (tt + 1) * 128)
        pr, gt, x2t = big.next(), big.next(), big.next()
        P.dma("sp", lambda e, pr=pr, rs=rs: e.dma_start(out=pr[:], in_=proj.t[rs, :]), proj, pr)
        P.dma("sp", lambda e, gt=gt, rs=rs: e.dma_start(out=gt[:], in_=gate.t[rs, :]), gate, gt)
        P.dma("sp", lambda e, x2t=x2t, rs=rs: e.dma_start(out=x2t[:], in_=x2.t[rs, :]), x2, x2t)
        ss = C.ssr.next()
        P.op("act", lambda e, pr=pr, ss=ss: e.activation(out=junk[:], in_=pr[:], func=AF.Square, accum_out=ss[:, 0:1]), reads=[pr], writes=[junk, ss])
        C.rstd_op(ss, 0, 1, D)
        P.op("dve", lambda e, pr=pr, ss=ss: e.scalar_tensor_tensor(pr[:], pr[:], ss[:, 1:2], gpp[:], ALU.mult, ALU.mult), reads=[pr, ss, gpp], writes=[pr])
        P.op("dve", lambda e, pr=pr, gt=gt: e.tensor_tensor(pr[:], pr[:], gt[:], ALU.mult), reads=[pr, gt], writes=[pr])
        P.op("dve", lambda e, pr=pr, x2t=x2t: e.tensor_tensor(x2t[:], x2t[:], pr[:], ALU.add), reads=[pr, x2t], writes=[x2t])
        P.op("act", lambda e, x2t=x2t, ss=ss: e.activation(out=junk[:], in_=x2t[:], func=AF.Square, accum_out=ss[:, 2:3]), reads=[x2t], writes=[junk, ss])
        C.rstd_op(ss, 2, 3, D)
        P.op("dve", lambda e, x2t=x2t, ss=ss, gt=gt: e.scalar_tensor_tensor(gt[:], x2t[:], ss[:, 3:4], gfn[:], ALU.mult, ALU.mult), reads=[x2t, ss, gfn], writes=[gt])
        store(gt, gt[:], outB, out[rs, :])
    fin = [outB]
    for name in DUMP:
        src = C.dr[name]
        o = nc.dram_tensor("dump_" + name, list(src.t.shape), src.t.dtype, kind="ExternalOutput").ap()
        ob_ = P.ext(o)
        P.dma("sp", lambda e, o=o, src=src: e.dma_start(out=o, in_=src.t), src, ob_)
        fin.append(ob_)
    P.wait_all("sp", fin)
    P.end_phase()
    P.emit()
    return nc


def host_consts(c):
    tiles, ktab = attn_tiles()
    m = {"ident": np.eye(128, dtype=np.float32)}
    j = np.arange(128)[:, None]
    i = np.arange(128)[None, :]
    m["amask"] = np.concatenate([(j >= i), (j <= i)], 1).astype(np.float32)
    vcol = np.zeros((128, len(ktab)), np.float32)
    first_valid = TA - (c + 1) * TOWN
    for (g, r, s0, n), col in ktab.items():
        d = (1, 4, 16)[g]
        u = r + d * (s0 + np.arange(n))
        vcol[:n, col] = (u >= first_valid)
    m["vcol"] = vcol
    dnc = np.zeros((128, 6 * 512), np.float32)
    a = np.arange(64)
    I64 = np.eye(64, dtype=np.float32)
    low_incl = (a[:, None] >= a[None, :]).astype(np.float32)
    low_strict = (a[:, None] > a[None, :]).astype(np.float32)
    dnc[:64, 0:512] = np.tile(I64, (1, 8))
    dnc[:64, 512:1024] = np.tile(low_strict, (1, 8))
    dnc[:64, 1024:1536] = np.tile(low_strict.T, (1, 8))
    dnc[:64, 1536:2048] = np.tile((1 - low_incl) * -30000.0, (1, 8))
    dnc[:64, 2048:2560] = np.tile((1 - low_incl.T) * -30000.0, (1, 8))
    dnc[:64, 2560:2624] = (a[:, None] <= a[None, :])
    dnc[:, 2624:2752] = 1.0
    dnc[:, 2752:2880] = -1.0
    dnc[:64, 2880:2944] = -I64
    m["dnc"] = dnc
    return m


def kernel(**inputs):
    f = lambda k: np.ascontiguousarray(np.asarray(inputs[k], np.float32)[0])
    x = f("x")
    p = np.asarray(inputs["p"], np.float32)[0, 0]
    shared = {k: f(k) for k in ("w_in", "w_attn_up", "w_dn_up", "w_out", "w_mlp_up", "w_mlp_down", "w_ple_gate",
                                "w_ple_proj", "norm_mix", "norm_mlp", "norm_ple", "ple_post_norm", "dn_a_log", "dn_dt_bias", "dn_norm")}
    shared["final_norm"] = np.ascontiguousarray(np.asarray(inputs["final_norm"], np.float32))
    shared["conv_wT"] = np.ascontiguousarray(f("conv_w").T)
    in_maps = []
    for c in range(NCORES):
        m = dict(shared)
        m.update(host_consts(c))
        xr = np.zeros((S, D), np.float32)
        xr[S - (c + 1) * TOWN:] = x[:(c + 1) * TOWN]
        m["xr"] = xr
        m["pp"] = np.ascontiguousarray(p[c * TOWN:(c + 1) * TOWN])
        in_maps.append(m)
    nc = build_all()
    res = run_bass_kernel_spmd(nc, in_maps, core_ids=list(range(NCORES)), **({"trace": True} if TRACE else {}))
    kernel.last = res
    return np.concatenate([r["out"] for r in res.results], 0)[None].astype(np.float32)
```

```python
import numpy as np
from contextlib import ExitStack
import concourse.bass as bass
import concourse.mybir as mybir
from concourse.bass_utils import run_bass_kernel_spmd

F32 = mybir.dt.float32
BF16 = mybir.dt.bfloat16
AF = mybir.ActivationFunctionType
ALU = mybir.AluOpType
AX = mybir.AxisListType
ENGS = ["pe", "act", "dve", "pool", "sp"]

NCORES = 8
S = 8192
D = 4096
DFF = 16384
TOWN = 1024
HALO = 2048
TA = TOWN + HALO
EPS = 1e-6
QA, KA, VA, QB, KB, VB, ZB, BETA, ALPHA, GA = 0, 3072, 6144, 9216, 11264, 13312, 15360, 17408, 17424, 17440
DUMP = []
STAGES = {"attn", "dn", "tail"}
DN_CHUNKS = None
DN_CUT = 99
DN_SET = 99
TRACE = False
WIDE_FORCE = False


class Buf:
    def __init__(self, t, kind):
        self.t = t
        self.kind = kind
        self.lastw = None
        self.readers = []
        self.sem = None

    def __getitem__(self, k):
        return self.t[k]


class Prog:
    def __init__(self, nc):
        self.nc = nc
        self.es = ExitStack()
        self.q = {e: [] for e in ENGS}
        self.cnt = {e: 0 for e in ENGS}
        self.waited = {e: {} for e in ENGS}
        self.psem = {e: self.es.enter_context(nc.semaphore("prog_" + e)) for e in ENGS}
        self.dcnt = {}
        self._uid = 0
        self.allsems = []
        self.sem_pool = []
        self.phase_es = None
        self.phase_bufs = []

    def uid(self, p):
        self._uid += 1
        return f"{p}{self._uid}"

    def sbuf(self, shape, dt, name=None):
        es = self.phase_es if self.phase_es is not None else self.es
        b = Buf(es.enter_context(self.nc.sbuf_tensor(name or self.uid("sb"), list(shape), dt)), "sb")
        if self.phase_es is not None:
            self.phase_bufs.append(b)
        return b

    def begin_phase(self):
        assert self.phase_es is None
        self.phase_es = ExitStack()
        self.phase_bufs = []

    def barrier(self):
        deps = [(self.psem[x], self.cnt[x], x) for x in ENGS if self.cnt[x] > 0]
        deps += [v for x, v in getattr(self, "prev_final", {}).items() if self.cnt[x] == 0]
        deps += [(sem, self.dcnt[id(sem)], "dma") for sem in self.allsems if self.dcnt[id(sem)] > 0]
        for e in ENGS:
            w = self._waits(e, [d for d in deps if not (d[2] == e)] + [d for d in deps if d[2] == e and e != "pe"])
            if w:
                self.q[e].append((w, None, None))

    def end_phase(self):
        self.barrier()
        for b in self.phase_bufs:
            if b.sem is not None:
                self.sem_pool.append(b.sem)
        self.phase_es.close()
        self.phase_es = None
        self.phase_bufs = []

    def psum(self, shape, dt, name=None):
        return Buf(self.es.enter_context(self.nc.psum_tensor(name or self.uid("ps"), list(shape), dt)), "ps")

    def dram(self, shape, dt, name=None):
        return Buf(self.nc.dram_tensor(name or self.uid("dr"), list(shape), dt).ap(), "dr")

    def ext(self, ap):
        return Buf(ap, "dr")

    def _bsem(self, b):
        if b.sem is None:
            if self.sem_pool:
                b.sem = self.sem_pool.pop()
            else:
                b.sem = self.es.enter_context(self.nc.semaphore(self.uid("ds")))
                self.dcnt[id(b.sem)] = 0
                self.allsems.append(b.sem)
        return b.sem

    def _waits(self, eng, deps):
        w = []
        for d in deps:
            if d is None:
                continue
            sem, val, src = d
            if src == "pe" and eng == "pe":
                continue
            key = id(sem)
            if self.waited[eng].get(key, 0) >= val:
                continue
            self.waited[eng][key] = val
            w.append((sem, val))
        return w

    def _deps(self, reads, writes, waw=True):
        deps = []
        for b in reads:
            deps.append(b.lastw)
            if b.kind == "ps":
                deps += b.readers
        for b in writes:
            if waw:
                deps.append(b.lastw)
            deps += b.readers
        return deps

    def _commit(self, tok, reads, writes):
        for b in writes:
            b.lastw = tok
            b.readers = []
        for b in reads:
            b.readers.append(tok)

    EPOCH = 30000

    def op(self, eng, fn, reads=(), writes=(), signal=True):
        self.pend = getattr(self, "pend", {})
        if self.cnt[eng] >= self.EPOCH and not self.pend.get(eng):
            self.prev_final = getattr(self, "prev_final", {})
            self.prev_final[eng] = (self.psem[eng], self.cnt[eng], eng)
            self.psem[eng] = self.es.enter_context(self.nc.semaphore(self.uid("prog_" + eng)))
            self.cnt[eng] = 0
            self.total = getattr(self, "total", {})
            self.total[eng] = self.total.get(eng, 0) + self.EPOCH
        w = self._waits(eng, self._deps(reads, writes))
        if not signal:
            assert eng == "pe"
            self.pend[eng] = True
            tok = (self.psem[eng], self.cnt[eng] + 1, eng)
            self.q[eng].append((w, fn, None))
            self._commit(tok, reads, writes)
            return tok
        self.pend[eng] = False
        self.cnt[eng] += 1
        tok = (self.psem[eng], self.cnt[eng], eng)
        self.q[eng].append((w, fn, (self.psem[eng], 1)))
        self._commit(tok, reads, writes)
        return tok

    def dma(self, eng, fn, src, dst, waw=True):
        w = self._waits(eng, self._deps([src], [dst], waw=waw))
        sem = self._bsem(dst)
        if self.dcnt[id(sem)] + 16 > self.EPOCH:
            dst.sem = None
            sem = self._bsem(dst)
        self.dcnt[id(sem)] += 16
        tok = (sem, self.dcnt[id(sem)], "dma")
        self.q[eng].append((w, fn, (sem, 16)))
        if waw:
            self._commit(tok, [src], [dst])
        else:
            dst.lastw = tok
            src.readers.append(tok)
        return tok

    def wait_all(self, eng, bufs):
        w = self._waits(eng, [b.lastw for b in bufs])
        if w:
            self.q[eng].append((w, None, None))

    def emit(self):
        nc = self.nc
        print("ops per engine:", {e: len(self.q[e]) for e in ENGS}, "sems:", len(self.allsems))
        with nc.Block() as block:
            def run(e, engine):
                for w, fn, inc in self.q[e]:
                    for sem, val in w:
                        engine.wait_ge(sem, val)
                    if fn is None:
                        continue
                    ins = fn(engine)
                    if inc is not None:
                        ins.then_inc(inc[0], inc[1])

            @block.tensor
            def _(t):
                run("pe", t)

            @block.scalar
            def _(t):
                run("act", t)

            @block.vector
            def _(t):
                run("dve", t)

            @block.gpsimd
            def _(t):
                run("pool", t)

            @block.sync
            def _(t):
                run("sp", t)
        self.es.close()


class Ring:
    def __init__(self, bufs):
        self.bufs = bufs
        self.i = 0

    def next(self):
        b = self.bufs[self.i % len(self.bufs)]
        self.i += 1
        return b


def bc_ap(ap, nparts):
    n = ap.shape[-1]
    return bass.AP(ap.tensor, ap.offset, [[0, nparts], [1, n]])


def attn_tiles():
    tiles, kt = [], {}
    for g, d in enumerate((1, 4, 16)):
        n_own, s0 = TOWN // d, HALO // d
        QW = min(128, n_own)
        for r in range(d):
            for a in range(s0, s0 + n_own, QW):
                for key in ((g, r, a - 128, 128), (g, r, a, QW)):
                    if key not in kt:
                        kt[key] = len(kt)
                tiles.append((g, d, r, a, QW))
    return tiles, kt


class Ctx:
    pass


def build():
    nc = bass.Bass("TRN2", target_bir_lowering=False)
    IN_W = GA + 2 * D
    dt_in = lambda n, s: nc.dram_tensor(n, list(s), F32, kind="ExternalInput").ap()
    I = Ctx()
    I.xr = dt_in("xr", [S, D])
    I.pp = dt_in("pp", [TOWN, 256])
    I.w_in = dt_in("w_in", [D, IN_W])
    I.conv_wT = dt_in("conv_wT", [6144, 4])
    I.dn_a_log = dt_in("dn_a_log", [16])
    I.dn_dt_bias = dt_in("dn_dt_bias", [16])
    I.dn_norm = dt_in("dn_norm", [128])
    I.w_attn_up = dt_in("w_attn_up", [1024, D])
    I.w_dn_up = dt_in("w_dn_up", [2048, D])
    I.w_out = dt_in("w_out", [D, D])
    I.w_mlp_up = dt_in("w_mlp_up", [D, DFF])
    I.w_mlp_down = dt_in("w_mlp_down", [DFF, D])
    I.w_ple_gate = dt_in("w_ple_gate", [D, D])
    I.w_ple_proj = dt_in("w_ple_proj", [256, D])
    I.norm_mix = dt_in("norm_mix", [D])
    I.norm_mlp = dt_in("norm_mlp", [D])
    I.norm_ple = dt_in("norm_ple", [D])
    I.ple_post = dt_in("ple_post_norm", [D])
    I.final_norm = dt_in("final_norm", [D])
    I.ident = dt_in("ident", [128, 128])
    I.amask = dt_in("amask", [128, 256])
    ntile, ktab = attn_tiles()
    I.vcol = dt_in("vcol", [128, len(ktab)])
    I.dnc = dt_in("dnc", [128, 6 * 512])
    out = nc.dram_tensor("out", [TOWN, D], F32, kind="ExternalOutput").ap()

    P = Prog(nc)
    C = Ctx()
    C.I, C.P, C.nc, C.out = I, P, nc, out
    C.ps = Ring([P.psum([128, 512], F32) for _ in range(8)])
    C.ident = P.sbuf([128, 128], F32)
    P.dma("sp", lambda e: e.dma_start(out=C.ident[:], in_=I.ident), P.ext(I.ident), C.ident)
    C.dr = {}

    def dram(name, shape, dt):
        C.dr[name] = P.dram(shape, dt, name="scr_" + name)
        return C.dr[name]
    C.dram = dram

    def gemm_bufs(norm=True):
        C.wring = Ring([P.sbuf([128, 8, 512], BF16) for _ in range(4)])
        C.xring = Ring([P.sbuf([128, 32, 512], BF16) for _ in range(2)])
        C.st32 = Ring([P.sbuf([128, 512], F32) for _ in range(4)])
        C.st16 = Ring([P.sbuf([128, 512], BF16) for _ in range(4)])
        if norm:
            C.big = [P.sbuf([128, D], F32) for _ in range(5)]
            C.junk = P.sbuf([128, D], BF16)
            C.ssr = Ring([P.sbuf([128, 4], F32) for _ in range(2)])
    C.gemm_bufs = gemm_bufs

    def store(sb, sb_ap, dr, dr_ap, q="sp"):
        P.dma(q, lambda e: e.dma_start(out=dr_ap, in_=sb_ap), sb, dr, waw=False)
    C.store = store

    def gemm(xT, t0, T, W, n0, N, K, form, epi, epi_grp=None):
        KC = K // 128
        wv = W.rearrange("(kc p) n -> p kc n", p=128)
        xv = xT.t.rearrange("(kc p) t -> p kc t", p=128)
        Wb = P.ext(W)
        nsup = max(1, KC // 32)
        kcs = min(KC, 32)
        if form == "B" and (nsup > 1 or (WIDE_FORCE and K >= 1024)) and N % 1024 == 0 and epi_grp is None and kcs % 8 == 0:
            for tb in range(T // 512):
                ts = t0 + tb * 512
                for nb2 in range(N // 1024):
                    accs = [C.ps.next() for _ in range(8)]
                    for sup in range(nsup):
                        xs = C.xring.next()
                        P.dma("sp", lambda e, xs=xs, sup=sup, ts=ts: e.dma_start(
                            out=xs[:, 0:kcs, :], in_=xv[:, sup * 32:sup * 32 + kcs, ts:ts + 512]), xT, xs)
                        for kg in range(kcs // 8):
                            k0 = sup * 32 + kg * 8
                            wps = []
                            for cb in range(2):
                                wp = C.wring.next()
                                c0 = n0 + nb2 * 1024 + cb * 512
                                P.dma("pool", lambda e, wp=wp, k0=k0, c0=c0: e.dma_start(
                                    out=wp[:, 0:8, 0:512], in_=wv[:, k0:k0 + 8, c0:c0 + 512]), Wb, wp)
                                wps.append(wp)
                            for kc in range(8):
                                first = (sup == 0 and kg == 0 and kc == 0)
                                last = (sup == nsup - 1 and kg == kcs // 8 - 1 and kc == 7)
                                for cb in range(2):
                                    for j in range(4):
                                        sig = last or (kc == 7 and j == 3)
                                        P.op("pe", lambda e, a=accs[cb * 4 + j], wp=wps[cb], xs=xs, kc=kc, j=j, kk=kg * 8 + kc, f=first, l=last: e.matmul(
                                            a[:, 0:512], xs[:, kk, j * 128:(j + 1) * 128], wp[:, kc, 0:512], start=f, stop=l),
                                            reads=[wps[cb], xs], writes=[accs[cb * 4 + j]], signal=sig)
                    for cb in range(2):
                        for j in range(4):
                            epi(tb, nb2 * 2 + cb, j, accs[cb * 4 + j], 512)
            return
        for tb in range(T // 512):
            ts = t0 + tb * 512
            xs = None
            for nb in range((N + 511) // 512):
                ncols = min(512, N - nb * 512)
                accs = [C.ps.next() for _ in range(4)]
                nj = 4 if form == "B" else (ncols + 127) // 128
                for sup in range(nsup):
                    if xs is None or nsup > 1:
                        xs = C.xring.next()
                        P.dma("sp", lambda e, xs=xs, sup=sup, ts=ts: e.dma_start(
                            out=xs[:, 0:kcs, :], in_=xv[:, sup * 32:sup * 32 + kcs, ts:ts + 512]), xT, xs)
                    for kg in range((kcs + 7) // 8):
                        nk = min(8, kcs - kg * 8)
                        wp = C.wring.next()
                        k0 = sup * 32 + kg * 8
                        P.dma("pool", lambda e, wp=wp, k0=k0, nk=nk, nb=nb, ncols=ncols: e.dma_start(
                            out=wp[:, 0:nk, 0:ncols], in_=wv[:, k0:k0 + nk, n0 + nb * 512:n0 + nb * 512 + ncols]), Wb, wp)
                        for kc in range(nk):
                            first = (sup == 0 and kg == 0 and kc == 0)
                            last = (sup == nsup - 1 and kg * 8 + kc == kcs - 1)
                            for j in range(nj):
                                sig = last or (kc == nk - 1 and j == nj - 1)
                                if form == "A":
                                    mc = min(128, ncols - j * 128)
                                    P.op("pe", lambda e, a=accs[j], wp=wp, xs=xs, kc=kc, j=j, mc=mc, kk=kg * 8 + kc, f=first, l=last: e.matmul(
                                        a[0:mc, :], wp[:, kc, j * 128:j * 128 + mc], xs[:, kk, :], start=f, stop=l),
                                        reads=[wp, xs], writes=[accs[j]], signal=sig)
                                else:
                                    P.op("pe", lambda e, a=accs[j], wp=wp, xs=xs, kc=kc, j=j, nc_=ncols, kk=kg * 8 + kc, f=first, l=last: e.matmul(
                                        a[:, 0:nc_], xs[:, kk, j * 128:(j + 1) * 128], wp[:, kc, 0:nc_], start=f, stop=l),
                                        reads=[wp, xs], writes=[accs[j]], signal=sig)
                if epi_grp is not None:
                    epi_grp(tb, nb, accs[:nj], ncols)
                else:
                    for j in range(nj):
                        epi(tb, nb, j, accs[j], ncols)
    C.gemm = gemm

    def epi_store_T(dst, dt, func=None, roff=0, coff=0):
        def epi(tb, nb, j, acc, ncols):
            mc = min(128, ncols - j * 128)
            sb = (C.st16 if dt == BF16 else C.st32).next()
            if func is None:
                P.op("dve", lambda e: e.tensor_copy(sb[0:mc, :], acc[0:mc, :]), reads=[acc], writes=[sb])
            else:
                P.op("act", lambda e: e.activation(out=sb[0:mc, :], in_=acc[0:mc, :], func=func), reads=[acc], writes=[sb])
            r0 = roff + nb * 512 + j * 128
            store(sb, sb[0:mc, :], dst, dst.t[r0:r0 + mc, coff + tb * 512:coff + (tb + 1) * 512])
        return epi
    C.epi_store_T = epi_store_T

    def epi_tm(dst, func, dt=F32, roff=0, coff=0):
        def epi(tb, nb, j, acc, ncols):
            t = (C.st16 if dt == BF16 else C.st32).next()
            P.op("act", lambda e: e.activation(out=t[:, 0:ncols], in_=acc[:, 0:ncols], func=func), reads=[acc], writes=[t])
            r0 = roff + tb * 512 + j * 128
            store(t, t[:, 0:ncols], dst, dst.t[r0:r0 + 128, coff + nb * 512:coff + nb * 512 + ncols])
        return epi
    C.epi_tm = epi_tm

    def rstd_op(ss, i, o, n):
        P.op("dve", lambda e: e.tensor_scalar(ss[:, o:o + 1], ss[:, i:i + 1], 1.0 / n, EPS, ALU.mult, ALU.add), reads=[ss], writes=[ss])
        P.op("act", lambda e: e.activation(out=ss[:, o:o + 1], in_=ss[:, o:o + 1], func=AF.Sqrt), reads=[ss], writes=[ss])
        P.op("dve", lambda e: e.reciprocal(ss[:, o:o + 1], ss[:, o:o + 1]), reads=[ss], writes=[ss])
    C.rstd_op = rstd_op

    def norm_T(src, srcT0, T, gain_ap, dstT):
        g = C.big[0]
        P.dma("sp", lambda e: e.dma_start(out=g[:], in_=bc_ap(gain_ap, 128)), P.ext(gain_ap), g)
        xr = Ring(C.big[1:3])
        hr = Ring(C.big[3:5])
        junk = C.junk
        KC = D // 128
        for tb in range(T // 512):
            hT = C.xring.next()
            for tt in range(4):
                r0 = srcT0 + tb * 512 + tt * 128
                xt = xr.next()
                P.dma("sp", lambda e, xt=xt, r0=r0: e.dma_start(out=xt[:], in_=src.t[r0:r0 + 128, :]), src, xt)
                ss = C.ssr.next()
                P.op("act", lambda e, xt=xt, ss=ss: e.activation(out=junk[:], in_=xt[:], func=AF.Square, accum_out=ss[:, 0:1]),
                     reads=[xt], writes=[junk, ss])
                rstd_op(ss, 0, 1, D)
                hb = hr.next()
                P.op("dve", lambda e, hb=hb, xt=xt, ss=ss: e.scalar_tensor_tensor(hb[:], xt[:], ss[:, 1:2], g[:], ALU.mult, ALU.mult),
                     reads=[xt, ss, g], writes=[hb])
                for k4 in range(KC // 4):
                    pt = C.ps.next()
                    for q in range(4):
                        kc = k4 * 4 + q
                        P.op("pe", lambda e, pt=pt, hb=hb, kc=kc, q=q: e.transpose(pt[:, q * 128:(q + 1) * 128], hb[:, kc * 128:(kc + 1) * 128], C.ident[:]),
                             reads=[hb, C.ident], writes=[pt])
                    if k4 % 2:
                        P.op("act", lambda e, pt=pt, hT=hT, k4=k4, tt=tt: e.activation(
                            out=hT[:, k4 * 4:k4 * 4 + 4, tt * 128:(tt + 1) * 128], in_=pt[:].rearrange("p (q t) -> p q t", q=4), func=AF.Copy),
                            reads=[pt], writes=[hT])
                    else:
                        P.op("dve", lambda e, pt=pt, hT=hT, k4=k4, tt=tt: e.tensor_copy(
                            hT[:, k4 * 4:k4 * 4 + 4, tt * 128:(tt + 1) * 128], pt[:].rearrange("p (q t) -> p q t", q=4)),
                            reads=[pt], writes=[hT])
            store(hT, hT[:, 0:KC, :], dstT, dstT.t.rearrange("(kc p) t -> p kc t", p=128)[:, :, tb * 512:(tb + 1) * 512])
    C.norm_T = norm_T
    return C


def attention(C, qaT, kaT, vtok, o_aT):
    P, I = C.P, C.I
    tiles, ktab = attn_tiles()
    P.begin_phase()
    amask = P.sbuf([128, 256], BF16)
    P.dma("pool", lambda e: e.dma_start(out=amask[:], in_=I.amask), P.ext(I.amask), amask)
    vcol = P.sbuf([128, len(ktab)], F32)
    P.dma("sp", lambda e: e.dma_start(out=vcol[:], in_=I.vcol), P.ext(I.vcol), vcol)
    ones = P.sbuf([128, 128], BF16)
    P.op("dve", lambda e: e.memset(ones[:], 1.0), writes=[ones])
    qTr = Ring([P.sbuf([128, TOWN], BF16) for _ in range(2)])
    kTr = Ring([P.sbuf([128, TA], BF16) for _ in range(2)])
    geo = {0: (1, 15, 9), 1: (4, 3, 3), 2: (16, 0, 2)}
    vts = {g: P.sbuf([128, geo[g][0], geo[g][2], 128], BF16) for g in range(3)}
    ULr = Ring([P.sbuf([128, 2, TOWN], F32) for _ in range(2)])
    Er = Ring([P.sbuf([128, 256], BF16) for _ in range(3)])
    Ewr = Ring([P.sbuf([128, 256], BF16) for _ in range(3)])
    ob = Ring([P.sbuf([128, TOWN], BF16) for _ in range(2)])
    rl = P.sbuf([128, TOWN], F32)
    scale = 128.0 ** -0.5
    for hh in range(8):
        UL = ULr.next()
        for g in range(3):
            d, bt0, nbt = geo[g]
            row0 = g * 1024 + hh * 128
            qT, kT, vt = qTr.next(), kTr.next(), vts[g]
            P.dma("sp", lambda e, qT=qT, row0=row0: e.dma_start(out=qT[:], in_=qaT.t[row0:row0 + 128, :]), qaT, qT)
            P.dma("sp", lambda e, kT=kT, row0=row0: e.dma_start(out=kT[:], in_=kaT.t[row0:row0 + 128, :]), kaT, kT)
            for r in range(d):
                if g < 2:
                    src = vtok.t[bass.ds(r + d * 128 * bt0, 128 * nbt, step=d), row0:row0 + 128].rearrange("(bt j) c -> j bt c", j=128)
                    P.dma("sp", lambda e, vt=vt, r=r, src=src: e.dma_start(out=vt[:, r, :, :], in_=src), vtok, vt)
                else:
                    src0 = vtok.t[bass.ds(r, 128, step=d), row0:row0 + 128]
                    src1 = vtok.t[bass.ds(r + d * 128, 64, step=d), row0:row0 + 128]
                    P.dma("sp", lambda e, vt=vt, r=r, src0=src0: e.dma_start(out=vt[:, r, 0, :], in_=src0), vtok, vt)
                    P.dma("sp", lambda e, vt=vt, r=r, src1=src1: e.dma_start(out=vt[0:64, r, 1, :], in_=src1), vtok, vt)
            for (tg, td, r, a, QW) in tiles:
                if tg != g:
                    continue
                k0 = ktab[(g, r, a - 128, 128)]
                k1 = ktab[(g, r, a, QW)]
                kc0 = bass.ds(r + d * (a - 128), 128, step=d)
                kc1 = bass.ds(r + d * a, QW, step=d)
                qc = bass.ds(r + d * a - HALO, QW, step=d)
                b0, b1 = (a - 128) // 128 - bt0, a // 128 - bt0
                ps_s, ps_u = C.ps.next(), C.ps.next()
                P.op("pe", lambda e, ps_s=ps_s, kT=kT, qT=qT, kc0=kc0, qc=qc, QW=QW: e.matmul(ps_s[:, 0:QW], kT[:, kc0], qT[:, qc], start=True, stop=True),
                     reads=[kT, qT], writes=[ps_s])
                P.op("pe", lambda e, ps_s=ps_s, kT=kT, qT=qT, kc1=kc1, qc=qc, QW=QW: e.matmul(ps_s[0:QW, 128:128 + QW], kT[:, kc1], qT[:, qc], start=True, stop=True),
                     reads=[kT, qT], writes=[ps_s])
                Er_, E = Er.next(), Ewr.next()
                P.op("act", lambda e, Er_=Er_, ps_s=ps_s, QW=QW: e.activation(out=Er_[:, 0:QW], in_=ps_s[:, 0:QW], func=AF.Exp, scale=scale), reads=[ps_s], writes=[Er_])
                P.op("act", lambda e, Er_=Er_, ps_s=ps_s, QW=QW: e.activation(out=Er_[0:QW, 128:128 + QW], in_=ps_s[0:QW, 128:128 + QW], func=AF.Exp, scale=scale), reads=[ps_s], writes=[Er_])
                P.op("dve", lambda e, E=E, Er_=Er_, k0=k0, QW=QW: e.scalar_tensor_tensor(E[:, 0:QW], Er_[:, 0:QW], vcol[:, k0:k0 + 1], amask[:, 0:QW], ALU.mult, ALU.mult),
                     reads=[Er_, vcol, amask], writes=[E])
                P.op("dve", lambda e, E=E, Er_=Er_, k1=k1, QW=QW: e.scalar_tensor_tensor(E[0:QW, 128:128 + QW], Er_[0:QW, 128:128 + QW], vcol[0:QW, k1:k1 + 1], amask[0:QW, 128:128 + QW], ALU.mult, ALU.mult),
                     reads=[Er_, vcol, amask], writes=[E])
                P.op("pe", lambda e, ps_u=ps_u, vt=vt, r=r, b0=b0, E=E, QW=QW: e.matmul(ps_u[:, 0:QW], vt[:, r, b0, :], E[:, 0:QW], start=True, stop=False), reads=[vt, E], writes=[ps_u])
                P.op("pe", lambda e, ps_u=ps_u, vt=vt, r=r, b1=b1, E=E, QW=QW: e.matmul(ps_u[:, 0:QW], vt[0:QW, r, b1, :], E[0:QW, 128:128 + QW], start=False, stop=True), reads=[vt, E], writes=[ps_u])
                P.op("pe", lambda e, ps_u=ps_u, E=E, QW=QW: e.matmul(ps_u[:, 128:128 + QW], ones[:, :], E[:, 0:QW], start=True, stop=False), reads=[ones, E], writes=[ps_u])
                P.op("pe", lambda e, ps_u=ps_u, E=E, QW=QW: e.matmul(ps_u[:, 128:128 + QW], ones[0:QW, :], E[0:QW, 128:128 + QW], start=False, stop=True), reads=[ones, E], writes=[ps_u])
                for w in range(2):
                    if g == 0:
                        P.op("dve", lambda e, UL=UL, ps_u=ps_u, w=w, qc=qc, QW=QW: e.tensor_copy(UL[:, w, qc], ps_u[:, w * 128:w * 128 + QW]), reads=[ps_u], writes=[UL])
                    else:
                        P.op("dve", lambda e, UL=UL, ps_u=ps_u, w=w, qc=qc, QW=QW: e.tensor_tensor(UL[:, w, qc], UL[:, w, qc], ps_u[:, w * 128:w * 128 + QW], ALU.add), reads=[ps_u, UL], writes=[UL])
        o = ob.next()
        P.op("dve", lambda e, UL=UL: e.reciprocal(rl[:], UL[:, 1, :]), reads=[UL], writes=[rl])
        P.op("dve", lambda e, UL=UL, o=o: e.tensor_tensor(o[:], UL[:, 0, :], rl[:], ALU.mult), reads=[UL, rl], writes=[o])
        C.store(o, o[:], o_aT, o_aT.t[hh * 128:(hh + 1) * 128, :])
    P.end_phase()


def v3(ap, h):
    return ap.rearrange("p (h j) -> p h j", h=h)


def bcl(ap2, n):
    p, h = ap2.shape
    return ap2.unsqueeze(2).broadcast_to([p, h, n])


def dn_prep_bufs(C):
    P, I = C.P, C.I
    Z = Ctx()
    Z.cw = P.sbuf([128, 48, 4], F32)
    P.dma("sp", lambda e: e.dma_start(out=Z.cw[:], in_=I.conv_wT.rearrange("(c p) j -> p c j", p=128)), P.ext(I.conv_wT), Z.cw)
    Z.ones = P.sbuf([128, 128], F32)
    P.op("dve", lambda e: e.memset(Z.ones[:], 1.0), writes=[Z.ones])
    Z.halo = P.sbuf([128, 48, 3], F32)
    P.op("dve", lambda e: e.memset(Z.halo[:], 0.0), writes=[Z.halo])
    mk = lambda w, n=4: Ring([P.sbuf([128, w], F32) for _ in range(n)])
    Z.xr, Z.yr, Z.y2r, Z.sqr, Z.rr, Z.y3r = mk(515), mk(512), mk(512, 8), mk(512, 8), mk(512), mk(512)
    Z.pending = None
    return Z


def dn_prep_epi(C, Z, fc0, dst_of):
    P = C.P
    cw, ones, halo = Z.cw, Z.ones, Z.halo

    def epi_grp(tb, nb, accs, ncols):
        st = []
        for j, acc in enumerate(accs):
            fc = fc0 + nb * 4 + j
            dst, drow = dst_of(fc)
            x = Z.xr.next()
            P.op("act", lambda e, x=x, acc=acc: e.activation(out=x[:, 3:515], in_=acc[:, :], func=AF.Copy), reads=[acc], writes=[x])
            st.append(dict(fc=fc, x=x, y=Z.yr.next(), y2=Z.y2r.next(), dst=dst, drow=drow, blk=tb))
        for t in st:
            P.op("dve", lambda e, t=t: e.tensor_copy(t["x"][:, 0:3], halo[:, t["fc"], :]), reads=[halo], writes=[t["x"]])
        for t in st:
            P.op("dve", lambda e, t=t: e.tensor_copy(halo[:, t["fc"], :], t["x"][:, 512:515]), reads=[t["x"]], writes=[halo])
        for t in st:
            P.op("dve", lambda e, t=t: e.tensor_scalar_mul(t["y"][:], t["x"][:, 3:515], cw[:, t["fc"], 3:4]), reads=[t["x"], cw], writes=[t["y"]])
        for j in (2, 1, 0):
            for t in st:
                P.op("dve", lambda e, t=t, j=j: e.scalar_tensor_tensor(t["y"][:], t["x"][:, j:j + 512], cw[:, t["fc"], j:j + 1], t["y"][:], ALU.mult, ALU.add),
                     reads=[t["x"], cw, t["y"]], writes=[t["y"]])
        for t in st:
            P.op("act", lambda e, t=t: e.activation(out=t["y2"][:], in_=t["y"][:], func=AF.Silu), reads=[t["y"]], writes=[t["y2"]])
        nt = [t for t in st if t["fc"] < 32]
        for t in nt:
            t["sq"] = Z.sqr.next()
            P.op("dve", lambda e, t=t: e.tensor_tensor(t["sq"][:], t["y2"][:], t["y2"][:], ALU.mult), reads=[t["y2"]], writes=[t["sq"]])
        prev, Z.pending = Z.pending, st
        if prev is not None:
            tail(prev)

    def tail(st):
        nt = [t for t in st if t["fc"] < 32]
        for t in nt:
            t["r"], t["y3"], t["ps"] = Z.rr.next(), Z.y3r.next(), C.ps.next()
            P.op("pe", lambda e, t=t: e.matmul(t["ps"][:, :], ones[:, :], t["sq"][:, :], start=True, stop=True), reads=[ones, t["sq"]], writes=[t["ps"]])
        for t in nt:
            P.op("act", lambda e, t=t: e.activation(out=t["r"][:], in_=t["ps"][:], func=AF.Sqrt, bias=EPS), reads=[t["ps"]], writes=[t["r"]])
        for t in nt:
            P.op("dve", lambda e, t=t: e.reciprocal(t["r"][:], t["r"][:]), reads=[t["r"]], writes=[t["r"]])
        for t in nt:
            sc = 128.0 ** -0.5 if t["fc"] < 16 else 1.0
            P.op("dve", lambda e, t=t, sc=sc: e.scalar_tensor_tensor(t["y3"][:], t["y2"][:], sc, t["r"][:], ALU.mult, ALU.mult), reads=[t["y2"], t["r"]], writes=[t["y3"]])
        for t in st:
            yo = t["y3"] if t["fc"] < 32 else t["y2"]
            C.store(yo, yo[:], t["dst"], t["dst"].t[t["drow"]:t["drow"] + 128, t["blk"] * 512:(t["blk"] + 1) * 512])

    def flush():
        prev, Z.pending = Z.pending, None
        if prev is not None:
            tail(prev)
    epi_grp.flush = flush
    return epi_grp


def dn_scan(C, ba, khT, vT, qhT, ztm, o_bT):
    P, I = C.P, C.I
    NCH = S // 64
    OWN0 = NCH - TOWN // 64
    TQ = qhT.t.shape[1]
    P.begin_phase()
    dnc = P.sbuf([128, 6 * 512], F32)
    P.dma("sp", lambda e: e.dma_start(out=dnc[:], in_=I.dnc), P.ext(I.dnc), dnc)
    Irep, strictrep, strictTrep = v3(dnc[0:64, 0:512], 8), v3(dnc[0:64, 512:1024], 8), v3(dnc[0:64, 1024:1536], 8)
    mneg, mnegT = dnc[0:64, 1536:2048], dnc[0:64, 2048:2560]
    Utri, ones, negones, negI64 = dnc[0:64, 2560:2624], dnc[:, 2624:2752], dnc[:, 2752:2880], dnc[0:64, 2880:2944]
    I64 = C.ident[0:64, 0:64]
    ident = C.ident
    gc_all = P.sbuf([64, NCH, 16], F32)
    beta_all = P.sbuf([64, NCH, 16], F32)
    kdecs_all = P.sbuf([64, NCH, 16], F32)
    egl_all = P.sbuf([128, NCH, 16], F32)
    alog = P.sbuf([64, 16], F32)
    dtb = P.sbuf([64, 16], F32)
    P.dma("sp", lambda e: e.dma_start(out=alog[:], in_=bc_ap(I.dn_a_log, 64)), P.ext(I.dn_a_log), alog)
    P.dma("sp", lambda e: e.dma_start(out=dtb[:], in_=bc_ap(I.dn_dt_bias, 64)), P.ext(I.dn_dt_bias), dtb)
    P.op("act", lambda e: e.activation(out=alog[:], in_=alog[:], func=AF.Exp), reads=[alog], writes=[alog])
    P.op("dve", lambda e: e.tensor_scalar_mul(alog[:], alog[:], -1.0), reads=[alog], writes=[alog])
    dnn = P.sbuf([64, 128], F32)
    P.dma("sp", lambda e: e.dma_start(out=dnn[:], in_=bc_ap(I.dn_norm, 64)), P.ext(I.dn_norm), dnn)
    QC = 16
    baq = P.sbuf([64, QC, 32], F32)
    spq = P.sbuf([64, QC, 16], F32)
    gq = P.sbuf([64, QC, 16], F32)
    tq = P.sbuf([64, QC, 16], F32)
    bav = ba.t.rearrange("(c t) n -> t c n", t=64)
    for q in range(NCH // QC if DN_SET >= 1 else 0):
        cs = slice(q * QC, (q + 1) * QC)
        P.dma("sp", lambda e, cs=cs: e.dma_start(out=baq[:], in_=bav[:, cs, :]), ba, baq)
        if DN_SET < 2:
            continue
        P.op("act", lambda e, cs=cs: e.activation(out=beta_all[:, cs, :], in_=baq[:, :, 0:16], func=AF.Sigmoid), reads=[baq], writes=[beta_all])
        if DN_SET < 3:
            continue
        P.op("dve", lambda e: e.tensor_tensor(spq[:], baq[:, :, 16:32], dtb[:].unsqueeze(1).broadcast_to([64, QC, 16]), ALU.add), reads=[baq, dtb], writes=[spq])
        P.op("act", lambda e: e.activation(out=spq[:], in_=spq[:], func=AF.Exp), reads=[spq], writes=[spq])
        P.op("act", lambda e: e.activation(out=spq[:], in_=spq[:], func=AF.Ln, bias=1.0), reads=[spq], writes=[spq])
        P.op("dve", lambda e: e.tensor_tensor(gq[:], spq[:], alog[:].unsqueeze(1).broadcast_to([64, QC, 16]), ALU.mult), reads=[spq, alog], writes=[gq])
        if DN_SET < 4:
            continue
        ps1, ps2 = C.ps.next(), C.ps.next()
        gflat = gq[:].rearrange("p c h -> p (c h)")
        P.op("pe", lambda e, ps1=ps1: e.matmul(ps1[0:64, 0:QC * 16], Utri, gflat, start=True, stop=True), reads=[dnc, gq], writes=[ps1])
        P.op("pe", lambda e, ps2=ps2: e.matmul(ps2[:, 0:QC * 16], ones[0:64, :], gflat, start=True, stop=True), reads=[dnc, gq], writes=[ps2])
        if DN_SET < 5:
            continue
        P.op("dve", lambda e, ps1=ps1, cs=cs: e.tensor_copy(gc_all[:, cs, :], v3(ps1[0:64, 0:QC * 16], QC)), reads=[ps1], writes=[gc_all])
        if DN_SET < 6:
            continue
        if DN_SET < 7:
            continue
        P.op("act", lambda e, ps2=ps2, cs=cs: e.activation(out=egl_all[:, cs, :], in_=v3(ps2[:, 0:QC * 16], QC), func=AF.Exp), reads=[ps2], writes=[egl_all])
        if DN_SET < 8:
            continue
        P.op("dve", lambda e, ps2=ps2, cs=cs: e.tensor_tensor(tq[:], v3(ps2[0:64, 0:QC * 16], QC), gc_all[:, cs, :], ALU.subtract), reads=[ps2, gc_all], writes=[tq])
        if DN_SET < 9:
            continue
        P.op("act", lambda e, cs=cs: e.activation(out=kdecs_all[:, cs, :], in_=tq[:], func=AF.Exp), reads=[tq], writes=[kdecs_all])
    if hasattr(C, "dbg"):
        for nm, b_ in (("gc_all", gc_all), ("beta_all", beta_all), ("kdecs_all", kdecs_all), ("egl_all", egl_all)):
            C.dbg(nm, b_, b_[:].rearrange("p c h -> p (c h)"))
    S4 = [P.sbuf([128, 4, 128], F32) for _ in range(4)]
    for s_ in S4:
        P.op("dve", lambda e, s_=s_: e.memset(s_[:], 0.0), writes=[s_])
    kcr = Ring([P.sbuf([128, 16, 64], F32) for _ in range(2)])
    vcr = Ring([P.sbuf([128, 16, 64], F32) for _ in range(2)])
    qc_ = P.sbuf([128, 16, 64], F32)
    zc_ = P.sbuf([64, 2048], F32)
    obT = P.sbuf([128, 16, 64], BF16)
    nb_ = P.sbuf([64, 16], F32)
    kbgs = P.sbuf([64, 16], F32)
    egc = P.sbuf([64, 16], F32)
    T = lambda: P.sbuf([64, 8, 64], F32)
    T4 = lambda: P.sbuf([64, 8, 128], F32)
    F = lambda b: b[:].rearrange("p h j -> p (h j)")

    def mk():
        X = Ctx()
        X.diagG, X.diagB, X.diagE, X.Dm, X.DmT, X.t1, X.t2, X.intraT = (T() for _ in range(8))
        X.sets = [(T(), T(), T()) for _ in range(2)]
        X.qdec, X.wT = P.sbuf([128, 8, 64], F32), P.sbuf([128, 8, 64], F32)
        X.kbg, X.kdec, X.vb, X.u, X.vnew = (T4() for _ in range(5))
        X.osb, X.on = P.sbuf([64, 4, 128], F32), P.sbuf([64, 4, 128], F32)
        X.ss = P.sbuf([64, 8], F32)
        return X
    XS = [mk(), mk()]
    chunks = range(NCH) if DN_CHUNKS is None else list(range(DN_CHUNKS)) + ([OWN0] if DN_CUT >= 8 else [])
    for c in (chunks if DN_CUT >= 1 else []):
        own = c >= OWN0 and DN_CUT >= 8
        kc_, vc_ = kcr.next(), vcr.next()
        P.dma("sp", lambda e, kc_=kc_, c=c: e.dma_start(out=kc_[:], in_=khT.t.rearrange("(h d) t -> d h t", d=128)[:, :, c * 64:(c + 1) * 64]), khT, kc_)
        P.dma("sp", lambda e, vc_=vc_, c=c: e.dma_start(out=vc_[:], in_=vT.t.rearrange("(h d) t -> d h t", d=128)[:, :, c * 64:(c + 1) * 64]), vT, vc_)
        if own:
            q0 = (c - OWN0) * 64 + (TQ - TOWN)
            P.dma("sp", lambda e, q0=q0: e.dma_start(out=qc_[:], in_=qhT.t.rearrange("(h d) t -> d h t", d=128)[:, :, q0:q0 + 64]), qhT, qc_)
            P.dma("sp", lambda e, c=c: e.dma_start(out=zc_[:], in_=ztm.t[(c - OWN0) * 64:(c - OWN0 + 1) * 64, :]), ztm, zc_)
        P.op("dve", lambda e, c=c: e.tensor_scalar_mul(nb_[:], beta_all[:, c, :], -1.0), reads=[beta_all], writes=[nb_])
        P.op("act", lambda e, c=c: e.activation(out=egc[:], in_=gc_all[:, c, :], func=AF.Exp), reads=[gc_all], writes=[egc])
        P.op("dve", lambda e, c=c: e.tensor_tensor(kbgs[:], beta_all[:, c, :], egc[:], ALU.mult), reads=[beta_all, egc], writes=[kbgs])
        HG = (0, 1)
        hsl = lambda hg: slice(hg * 8, hg * 8 + 8)
        for hg in HG:
            X = XS[hg]
            X.gcs = bcl(gc_all[:, c, hsl(hg)], 64)
            P.op("dve", lambda e, X=X, g_=X.gcs: e.tensor_tensor(X.diagG[:], Irep, g_, ALU.mult), reads=[dnc, gc_all], writes=[X.diagG])
            P.op("dve", lambda e, X=X, c=c, hg=hg: e.tensor_tensor(X.diagB[:], Irep, bcl(beta_all[:, c, hsl(hg)], 64), ALU.mult), reads=[dnc, beta_all], writes=[X.diagB])
        for hg in HG:
            X = XS[hg]
            h0 = hg * 8
            X.p1, X.p3, X.p4, X.p5 = C.ps.next(), C.ps.next(), C.ps.next(), C.ps.next()
            for h in range(8):
                P.op("pe", lambda e, p1=X.p1, kc_=kc_, h=h, h0=h0: e.matmul(p1[0:64, h * 64:(h + 1) * 64], kc_[:, h0 + h, :], kc_[:, h0 + h, :], start=True, stop=True),
                     reads=[kc_], writes=[X.p1], signal=(h == 7))
            P.op("pe", lambda e, X=X, p3=X.p3: e.matmul(p3[0:64, :], negones[0:64, 0:64], F(X.diagG), start=True, stop=False), reads=[dnc, X.diagG], writes=[X.p3])
            P.op("pe", lambda e, X=X, g_=X.gcs, p3=X.p3: e.matmul(p3[0:64, :], I64, g_, start=False, stop=False), reads=[ident, gc_all], writes=[X.p3])
            P.op("pe", lambda e, X=X, p3=X.p3: e.matmul(p3[0:64, :], I64, mneg, start=False, stop=True), reads=[ident, dnc], writes=[X.p3])
            P.op("pe", lambda e, X=X, p4=X.p4: e.matmul(p4[0:64, :], ones[0:64, 0:64], F(X.diagG), start=True, stop=False), reads=[dnc, X.diagG], writes=[X.p4])
            P.op("pe", lambda e, X=X, g_=X.gcs, p4=X.p4: e.matmul(p4[0:64, :], negI64, g_, start=False, stop=False), reads=[dnc, gc_all], writes=[X.p4])
            P.op("pe", lambda e, X=X, p4=X.p4: e.matmul(p4[0:64, :], I64, mnegT, start=False, stop=True), reads=[ident, dnc], writes=[X.p4])
            P.op("pe", lambda e, X=X, p5=X.p5: e.matmul(p5[0:64, :], negones[0:64, 0:64], F(X.diagB), start=True, stop=True), reads=[dnc, X.diagB], writes=[X.p5])
        for hg in HG:
            X = XS[hg]
            P.op("act", lambda e, X=X, p3=X.p3: e.activation(out=F(X.Dm), in_=p3[0:64, :], func=AF.Exp), reads=[X.p3], writes=[X.Dm])
            P.op("act", lambda e, X=X, p4=X.p4: e.activation(out=F(X.DmT), in_=p4[0:64, :], func=AF.Exp), reads=[X.p4], writes=[X.DmT])
        if DN_CUT < 3:
            continue
        for hg in HG:
            X = XS[hg]
            P.op("dve", lambda e, X=X, p1=X.p1: e.tensor_tensor(X.t1[:], v3(p1[0:64, :], 8), strictrep, ALU.mult), reads=[X.p1, dnc], writes=[X.t1])
            P.op("dve", lambda e, X=X, p1=X.p1: e.tensor_tensor(X.t2[:], v3(p1[0:64, :], 8), strictTrep, ALU.mult), reads=[X.p1, dnc], writes=[X.t2])
        for hg in HG:
            X = XS[hg]
            P.op("dve", lambda e, X=X: e.tensor_tensor(X.t1[:], X.t1[:], X.Dm[:], ALU.mult), reads=[X.t1, X.Dm], writes=[X.t1])
            P.op("dve", lambda e, X=X: e.tensor_tensor(X.t2[:], X.t2[:], X.DmT[:], ALU.mult), reads=[X.t2, X.DmT], writes=[X.t2])
        for hg in HG:
            X = XS[hg]
            M, MT, R = X.sets[0]
            P.op("dve", lambda e, X=X, M=M, hg=hg: e.tensor_tensor(M[:], X.t1[:], bcl(nb_[:, hsl(hg)], 64), ALU.mult), reads=[X.t1, nb_], writes=[M])
            P.op("dve", lambda e, X=X, MT=MT, p5=X.p5: e.tensor_tensor(MT[:], X.t2[:], v3(p5[0:64, :], 8), ALU.mult), reads=[X.t2, X.p5], writes=[MT])
        for hg in HG:
            M, MT, R = XS[hg].sets[0]
            P.op("dve", lambda e, R=R, MT=MT: e.tensor_tensor(R[:], MT[:], Irep, ALU.add), reads=[MT, dnc], writes=[R])
        if DN_CUT < 4:
            continue
        cur = 0
        for m in range(1, 6):
            for hg in HG:
                X = XS[hg]
                M, MT, R = X.sets[cur]
                X.pm, X.pmt = C.ps.next(), C.ps.next()
                for h in range(8):
                    P.op("pe", lambda e, pm=X.pm, M=M, MT=MT, h=h: e.matmul(pm[0:64, h * 64:(h + 1) * 64], MT[:, h, :], M[:, h, :], start=True, stop=True), reads=[M, MT], writes=[X.pm], signal=(h == 7))
                if m < 5:
                    for h in range(8):
                        P.op("pe", lambda e, pmt=X.pmt, M=M, MT=MT, h=h: e.matmul(pmt[0:64, h * 64:(h + 1) * 64], M[:, h, :], MT[:, h, :], start=True, stop=True), reads=[M, MT], writes=[X.pmt], signal=(h == 7))
            for hg in HG:
                X = XS[hg]
                Mn, MTn, Rn = X.sets[1 - cur]
                P.op("act", lambda e, Mn=Mn, pm=X.pm: e.activation(out=F(Mn), in_=pm[0:64, :], func=AF.Copy), reads=[X.pm], writes=[Mn])
                if m < 5:
                    P.op("act", lambda e, MTn=MTn, pmt=X.pmt: e.activation(out=F(MTn), in_=pmt[0:64, :], func=AF.Copy), reads=[X.pmt], writes=[MTn])
            for hg in HG:
                X = XS[hg]
                M, MT, R = X.sets[cur]
                Mn, MTn, Rn = X.sets[1 - cur]
                X.pr = C.ps.next()
                for h in range(8):
                    P.op("pe", lambda e, pr=X.pr, Mn=Mn, R=R, h=h: e.matmul(pr[0:64, h * 64:(h + 1) * 64], Mn[:, h, :], R[:, h, :], start=True, stop=True), reads=[Mn, R], writes=[X.pr], signal=(h == 7))
            for hg in HG:
                X = XS[hg]
                M, MT, R = X.sets[cur]
                Mn, MTn, Rn = X.sets[1 - cur]
                P.op("dve", lambda e, Rn=Rn, R=R, pr=X.pr: e.tensor_tensor(F(Rn), F(R), pr[0:64, :], ALU.add), reads=[R, X.pr], writes=[Rn])
            cur = 1 - cur
        if DN_CUT < 5:
            continue
        for half in range(2):
            for hg in HG:
                X = XS[hg]
                h0 = hg * 8
                X.pk, X.pv = C.ps.next(), C.ps.next()
                for hq in range(4):
                    h = h0 + half * 4 + hq
                    P.op("pe", lambda e, pk=X.pk, kc_=kc_, h=h, hq=hq: e.transpose(pk[0:64, hq * 128:(hq + 1) * 128], kc_[:, h, :], ident[:]), reads=[kc_, ident], writes=[X.pk])
                    P.op("pe", lambda e, pv=X.pv, vc_=vc_, h=h, hq=hq: e.transpose(pv[0:64, hq * 128:(hq + 1) * 128], vc_[:, h, :], ident[:]), reads=[vc_, ident], writes=[X.pv])
            for hg in HG:
                X = XS[hg]
                h0 = hg * 8
                a4 = slice(half * 4, half * 4 + 4)
                g4 = slice(h0 + half * 4, h0 + half * 4 + 4)
                P.op("dve", lambda e, X=X, pk=X.pk, a4=a4, g4=g4: e.tensor_tensor(X.kbg[:, a4, :], v3(pk[0:64, :], 4), bcl(kbgs[:, g4], 128), ALU.mult), reads=[X.pk, kbgs], writes=[X.kbg])
                P.op("dve", lambda e, X=X, pk=X.pk, a4=a4, g4=g4, c=c: e.tensor_tensor(X.kdec[:, a4, :], v3(pk[0:64, :], 4), bcl(kdecs_all[:, c, g4], 128), ALU.mult), reads=[X.pk, kdecs_all], writes=[X.kdec])
                P.op("dve", lambda e, X=X, pv=X.pv, a4=a4, g4=g4, c=c: e.tensor_tensor(X.vb[:, a4, :], v3(pv[0:64, :], 4), bcl(beta_all[:, c, g4], 128), ALU.mult), reads=[X.pv, beta_all], writes=[X.vb])
        if DN_CUT < 6:
            continue
        for hg in HG:
            X = XS[hg]
            R = X.sets[cur][2]
            X.pw = C.ps.next()
            for h in range(8):
                P.op("pe", lambda e, X=X, pw=X.pw, R=R, h=h: e.matmul(pw[:, h * 64:(h + 1) * 64], X.kbg[:, h, :], R[:, h, :], start=True, stop=True), reads=[X.kbg, R], writes=[X.pw], signal=(h == 7))
        for hg in HG:
            X = XS[hg]
            P.op("act", lambda e, X=X, pw=X.pw: e.activation(out=F(X.wT), in_=pw[:, :], func=AF.Copy), reads=[X.pw], writes=[X.wT])
        for half in range(2):
            for hg in HG:
                X = XS[hg]
                R = X.sets[cur][2]
                X.pu = C.ps.next()
                for hq in range(4):
                    h = half * 4 + hq
                    P.op("pe", lambda e, X=X, pu=X.pu, R=R, h=h, hq=hq: e.matmul(pu[0:64, hq * 128:(hq + 1) * 128], R[:, h, :], X.vb[:, h, :], start=True, stop=True), reads=[R, X.vb], writes=[X.pu], signal=(hq == 3))
            for hg in HG:
                X = XS[hg]
                P.op("act", lambda e, X=X, pu=X.pu, half=half: e.activation(out=X.u[:, half * 4:half * 4 + 4, :], in_=v3(pu[0:64, :], 4), func=AF.Copy), reads=[X.pu], writes=[X.u])
        if own:
            for hg in HG:
                X = XS[hg]
                h0 = hg * 8
                X.p2, X.p6 = C.ps.next(), C.ps.next()
                for h in range(8):
                    P.op("pe", lambda e, p2=X.p2, kc_=kc_, h=h, h0=h0: e.matmul(p2[0:64, h * 64:(h + 1) * 64], kc_[:, h0 + h, :], qc_[:, h0 + h, :], start=True, stop=True),
                         reads=[kc_, qc_], writes=[X.p2], signal=(h == 7))
                P.op("dve", lambda e, X=X, hg=hg: e.tensor_tensor(X.diagE[:], Irep, bcl(egc[:, hsl(hg)], 64), ALU.mult), reads=[dnc, egc], writes=[X.diagE])
            for hg in HG:
                X = XS[hg]
                P.op("dve", lambda e, X=X, p2=X.p2: e.tensor_tensor(X.intraT[:], v3(p2[0:64, :], 8), X.DmT[:], ALU.mult), reads=[X.p2, X.DmT], writes=[X.intraT])
                P.op("pe", lambda e, X=X, p6=X.p6: e.matmul(p6[:, :], ones[0:64, :], F(X.diagE), start=True, stop=True), reads=[dnc, X.diagE], writes=[X.p6])
            for hg in HG:
                X = XS[hg]
                P.op("dve", lambda e, X=X, p6=X.p6, hg=hg: e.tensor_tensor(X.qdec[:], qc_[:, hsl(hg), :], v3(p6[:, :], 8), ALU.mult), reads=[qc_, X.p6], writes=[X.qdec])
        if DN_CUT < 7:
            continue
        for half in range(2):
            a4 = slice(half * 4, half * 4 + 4)
            for hg in HG:
                X = XS[hg]
                Sb = S4[hg * 2 + half]
                X.pws = C.ps.next()
                for hq in range(4):
                    h = half * 4 + hq
                    P.op("pe", lambda e, X=X, pws=X.pws, Sb=Sb, h=h, hq=hq: e.matmul(pws[0:64, hq * 128:(hq + 1) * 128], X.wT[:, h, :], Sb[:, hq, :], start=True, stop=True), reads=[X.wT, Sb], writes=[X.pws], signal=(hq == 3))
            for hg in HG:
                X = XS[hg]
                P.op("dve", lambda e, X=X, pws=X.pws, a4=a4: e.tensor_tensor(X.vnew[:, a4, :], X.u[:, a4, :], v3(pws[0:64, :], 4), ALU.subtract), reads=[X.u, X.pws], writes=[X.vnew])
            for hg in HG:
                X = XS[hg]
                Sb = S4[hg * 2 + half]
                if own:
                    X.po = C.ps.next()
                    for hq in range(4):
                        h = half * 4 + hq
                        P.op("pe", lambda e, X=X, po=X.po, Sb=Sb, h=h, hq=hq: e.matmul(po[0:64, hq * 128:(hq + 1) * 128], X.qdec[:, h, :], Sb[:, hq, :], start=True, stop=False), reads=[X.qdec, Sb], writes=[X.po])
                        P.op("pe", lambda e, X=X, po=X.po, h=h, hq=hq: e.matmul(po[0:64, hq * 128:(hq + 1) * 128], X.intraT[:, h, :], X.vnew[:, h, :], start=False, stop=True), reads=[X.intraT, X.vnew], writes=[X.po])
                X.psu = C.ps.next()
                for hq in range(4):
                    h = half * 4 + hq
                    P.op("pe", lambda e, X=X, psu=X.psu, h=h, hq=hq: e.matmul(psu[:, hq * 128:(hq + 1) * 128], X.kdec[:, h, :], X.vnew[:, h, :], start=True, stop=True), reads=[X.kdec, X.vnew], writes=[X.psu], signal=(hq == 3))
            for hg in HG:
                Sb = S4[hg * 2 + half]
                g4 = slice(hg * 8 + half * 4, hg * 8 + half * 4 + 4)
                P.op("dve", lambda e, Sb=Sb, c=c, g4=g4: e.tensor_tensor(Sb[:], Sb[:], bcl(egl_all[:, c, g4], 128), ALU.mult), reads=[Sb, egl_all], writes=[Sb])
            for hg in HG:
                X = XS[hg]
                Sb = S4[hg * 2 + half]
                P.op("dve", lambda e, Sb=Sb, psu=X.psu: e.tensor_tensor(Sb[:], Sb[:], v3(psu[:, :], 4), ALU.add), reads=[Sb, X.psu], writes=[Sb])
            if own:
                for hg in HG:
                    X = XS[hg]
                    P.op("act", lambda e, X=X, po=X.po: e.activation(out=X.osb[:], in_=v3(po[0:64, :], 4), func=AF.Copy), reads=[X.po], writes=[X.osb])
                for hg in HG:
                    X = XS[hg]
                    P.op("dve", lambda e, X=X: e.tensor_tensor(X.on[:], X.osb[:], X.osb[:], ALU.mult), reads=[X.osb], writes=[X.on])
                for hg in HG:
                    X = XS[hg]
                    P.op("dve", lambda e, X=X: e.tensor_reduce(X.ss[:, 0:4], X.on[:], AX.X, ALU.add), reads=[X.on], writes=[X.ss])
                for hg in HG:
                    X = XS[hg]
                    P.op("dve", lambda e, X=X: e.tensor_scalar(X.ss[:, 4:8], X.ss[:, 0:4], 1.0 / 128, EPS, ALU.mult, ALU.add), reads=[X.ss], writes=[X.ss])
                for hg in HG:
                    X = XS[hg]
                    P.op("act", lambda e, X=X: e.activation(out=X.ss[:, 4:8], in_=X.ss[:, 4:8], func=AF.Sqrt), reads=[X.ss], writes=[X.ss])
                for hg in HG:
                    X = XS[hg]
                    P.op("dve", lambda e, X=X: e.reciprocal(X.ss[:, 4:8], X.ss[:, 4:8]), reads=[X.ss], writes=[X.ss])
                for hg in HG:
                    X = XS[hg]
                    P.op("dve", lambda e, X=X: e.tensor_tensor(X.on[:], X.osb[:], bcl(X.ss[:, 4:8], 128), ALU.mult), reads=[X.osb, X.ss], writes=[X.on])
                for hg in HG:
                    X = XS[hg]
                    P.op("dve", lambda e, X=X: e.tensor_tensor(X.on[:], X.on[:], dnn[:].unsqueeze(1).broadcast_to([64, 4, 128]), ALU.mult), reads=[X.on, dnn], writes=[X.on])
                for hg in HG:
                    X = XS[hg]
                    z0 = (hg * 8 + half * 4) * 128
                    P.op("dve", lambda e, X=X, z0=z0: e.tensor_tensor(X.on[:], X.on[:], v3(zc_[:, z0:z0 + 512], 4), ALU.mult), reads=[X.on, zc_], writes=[X.on])
                for hg in HG:
                    X = XS[hg]
                    X.pt = C.ps.next()
                    for hq in range(4):
                        P.op("pe", lambda e, X=X, pt=X.pt, hq=hq: e.transpose(pt[:, hq * 64:(hq + 1) * 64], X.on[:, hq, :], I64), reads=[X.on, ident], writes=[X.pt], signal=(hq == 3))
                for hg in HG:
                    X = XS[hg]
                    g4 = slice(hg * 8 + half * 4, hg * 8 + half * 4 + 4)
                    P.op("act", lambda e, pt=X.pt, g4=g4: e.activation(out=obT[:, g4, :], in_=v3(pt[:, 0:256], 4), func=AF.Copy), reads=[X.pt], writes=[obT])
        if own:
            C.store(obT, obT[:], o_bT, o_bT.t.rearrange("(h d) t -> d h t", d=128)[:, :, (c - OWN0) * 64:(c - OWN0 + 1) * 64])
    if hasattr(C, "dbg"):
        X = XS[1]
        for nm, b_ in (("Dm", X.Dm), ("DmT", X.DmT), ("R0", X.sets[0][2]), ("R1", X.sets[1][2]), ("M0", X.sets[0][0]), ("intraT", X.intraT)):
            C.dbg(nm, b_, b_[:].rearrange("p h j -> p (h j)"))
        for nm, b_ in (("u", X.u), ("vnew", X.vnew), ("kbg", X.kbg), ("kdec", X.kdec), ("vb", X.vb)):
            C.dbg(nm, b_, b_[:].rearrange("p h j -> p (h j)"))
        C.dbg("wT", X.wT, X.wT[:].rearrange("p h j -> p (h j)"))
        for i_, s_ in enumerate(S4):
            C.dbg("S%d" % i_, s_, s_[:].rearrange("p h j -> p (h j)"))
    P.end_phase()


def build_all():
    C = build()
    P, I, nc, out = C.P, C.I, C.nc, C.out
    gemm, store, norm_T, dram = C.gemm, C.store, C.norm_T, C.dram
    xrB = P.ext(I.xr)
    T0 = S - TOWN
    TQ = TOWN + 512

    P.begin_phase()
    C.gemm_bufs()
    hT = dram("hT", [D, S], BF16)
    norm_T(xrB, 0, S, I.norm_mix, hT)
    P.end_phase()
    P.begin_phase()
    C.gemm_bufs(norm=False)
    gT = dram("gT", [2 * D, TOWN], BF16)
    gemm(hT, T0, TOWN, I.w_in, GA, 2 * D, D, "A", C.epi_store_T(gT, BF16, AF.Sigmoid))
    o_aT = dram("o_aT", [1024, TOWN], BF16)
    o_bT = dram("o_bT", [2048, TOWN], BF16)
    if "attn" in STAGES:
        qaT = dram("qaT", [3072, TOWN], BF16)
        kaT = dram("kaT", [3072, TA], BF16)
        vtok = dram("vtok", [TA, 3072], BF16)
        gemm(hT, T0, TOWN, I.w_in, QA, 3072, D, "A", C.epi_store_T(qaT, BF16))
        TS = TOWN + 512
        gemm(hT, S - TS, TS, I.w_in, KA, 2048, D, "A", C.epi_store_T(kaT, BF16, coff=TA - TS))
        gemm(hT, S - TA, TA, I.w_in, KA + 2048, 1024, D, "A", C.epi_store_T(kaT, BF16, roff=2048))
        gemm(hT, S - TS, TS, I.w_in, VA, 2048, D, "B", C.epi_tm(vtok, AF.Copy, BF16, roff=TA - TS))
        gemm(hT, S - TA, TA, I.w_in, VA + 2048, 1024, D, "B", C.epi_tm(vtok, AF.Copy, BF16, coff=2048))
    if "dn" in STAGES:
        ztm = dram("ztm", [TOWN, 2048], F32)
        ba = dram("ba", [S, 32], F32)
        khT = dram("khT", [2048, S], F32)
        vT = dram("vT", [2048, S], F32)
        qhT = dram("qhT", [2048, TQ], F32)
        Z = dn_prep_bufs(C)
        kv_dst = lambda fc: (khT, (fc - 16) * 128) if fc < 32 else (vT, (fc - 32) * 128)
        eg = dn_prep_epi(C, Z, 16, kv_dst)
        gemm(hT, 0, S, I.w_in, KB, 4096, D, "A", None, epi_grp=eg)
        eg.flush()
        eg = dn_prep_epi(C, Z, 0, lambda fc: (qhT, fc * 128))
        gemm(hT, S - TQ, TQ, I.w_in, QB, 2048, D, "A", None, epi_grp=eg)
        eg.flush()
        gemm(hT, T0, TOWN, I.w_in, ZB, 2048, D, "B", C.epi_tm(ztm, AF.Silu))
        gemm(hT, 0, S, I.w_in, BETA, 32, D, "B", C.epi_tm(ba, AF.Copy))
    P.end_phase()

    if "attn" in STAGES:
        attention(C, qaT, kaT, vtok, o_aT)
    if "dn" in STAGES:
        if "noscan" not in STAGES:
            dn_scan(C, ba, khT, vT, qhT, ztm, o_bT)

    P.begin_phase()
    C.gemm_bufs()
    e32, e16 = C.st32, C.st16
    AT = dram("AT", [D, TOWN], F32)
    BT = dram("BT", [D, TOWN], F32)
    gemm(o_aT, 0, TOWN, I.w_attn_up, 0, D, 1024, "A", C.epi_store_T(AT, F32))
    gemm(o_bT, 0, TOWN, I.w_dn_up, 0, D, 2048, "A", C.epi_store_T(BT, F32))
    mT = dram("mT", [D, TOWN], BF16)
    for r in range(D // 128):
        for tb in range(TOWN // 512):
            sl = (slice(r * 128, (r + 1) * 128), slice(tb * 512, (tb + 1) * 512))
            a, b, ga, gb, m = e32.next(), e32.next(), e16.next(), e16.next(), e16.next()
            P.dma("sp", lambda e, a=a, sl=sl: e.dma_start(out=a[:], in_=AT.t[sl]), AT, a)
            P.dma("sp", lambda e, b=b, sl=sl: e.dma_start(out=b[:], in_=BT.t[sl]), BT, b)
            P.dma("sp", lambda e, ga=ga, sl=sl: e.dma_start(out=ga[:], in_=gT.t[sl]), gT, ga)
            P.dma("sp", lambda e, gb=gb, sl=sl, r=r: e.dma_start(out=gb[:], in_=gT.t[D + r * 128:D + (r + 1) * 128, sl[1]]), gT, gb)
            P.op("dve", lambda e, a=a, ga=ga: e.tensor_tensor(a[:], a[:], ga[:], ALU.mult), reads=[a, ga], writes=[a])
            P.op("dve", lambda e, b=b, gb=gb: e.tensor_tensor(b[:], b[:], gb[:], ALU.mult), reads=[b, gb], writes=[b])
            P.op("dve", lambda e, a=a, b=b, m=m: e.tensor_tensor(m[:], a[:], b[:], ALU.add), reads=[a, b], writes=[m])
            store(m, m[:], mT, mT.t[sl])

    def epi_resid(src, src_r0, dst):
        def epi(tb, nb, j, acc, ncols):
            r0 = tb * 512 + j * 128
            xt = e32.next()
            P.dma("sp", lambda e: e.dma_start(out=xt[:, 0:ncols], in_=src.t[src_r0 + r0:src_r0 + r0 + 128, nb * 512:nb * 512 + ncols]), src, xt)
            P.op("dve", lambda e: e.tensor_tensor(xt[:, 0:ncols], xt[:, 0:ncols], acc[:, 0:ncols], ALU.add), reads=[xt, acc], writes=[xt])
            store(xt, xt[:, 0:ncols], dst, dst.t[r0:r0 + 128, nb * 512:nb * 512 + ncols])
        return epi

    x1 = dram("x1", [TOWN, D], F32)
    gemm(mT, 0, TOWN, I.w_out, 0, D, D, "B", epi_resid(xrB, T0, x1))
    h2T = dram("h2T", [D, TOWN], BF16)
    norm_T(x1, 0, TOWN, I.norm_mlp, h2T)
    hidT = dram("hidT", [DFF, TOWN], BF16)

    def epi_relu2(tb, nb, j, acc, ncols):
        t, sb = e32.next(), e16.next()
        P.op("act", lambda e: e.activation(out=t[:], in_=acc[:], func=AF.Relu), reads=[acc], writes=[t])
        P.op("dve", lambda e: e.tensor_tensor(sb[:], t[:], t[:], ALU.mult), reads=[t], writes=[sb])
        r0 = nb * 512 + j * 128
        store(sb, sb[:], hidT, hidT.t[r0:r0 + 128, tb * 512:(tb + 1) * 512])

    gemm(h2T, 0, TOWN, I.w_mlp_up, 0, DFF, D, "A", epi_relu2)
    x2 = dram("x2", [TOWN, D], F32)
    gemm(hidT, 0, TOWN, I.w_mlp_down, 0, D, DFF, "B", epi_resid(x1, 0, x2))
    h3T = dram("h3T", [D, TOWN], BF16)
    norm_T(x2, 0, TOWN, I.norm_ple, h3T)
    gate = dram("gate", [TOWN, D], F32)
    proj = dram("proj", [TOWN, D], F32)
    gemm(h3T, 0, TOWN, I.w_ple_gate, 0, D, D, "B", C.epi_tm(gate, AF.Sigmoid))
    pT = dram("pT", [256, TOWN], BF16)
    ppB = P.ext(I.pp)
    for tb in range(TOWN // 512):
        pTs = e16.next(), e16.next()
        for tt in range(4):
            pt_in = e32.next()
            r0 = tb * 512 + tt * 128
            P.dma("sp", lambda e, pt_in=pt_in, r0=r0: e.dma_start(out=pt_in[:, 0:256], in_=I.pp[r0:r0 + 128, :]), ppB, pt_in)
            ps = C.ps.next()
            for q in range(2):
                P.op("pe", lambda e, ps=ps, pt_in=pt_in, q=q: e.transpose(ps[:, q * 128:(q + 1) * 128], pt_in[:, q * 128:(q + 1) * 128], C.ident[:]),
                     reads=[pt_in, C.ident], writes=[ps])
            for q in range(2):
                P.op("dve", lambda e, ps=ps, q=q, tt=tt, pTs=pTs: e.tensor_copy(pTs[q][:, tt * 128:(tt + 1) * 128], ps[:, q * 128:(q + 1) * 128]),
                     reads=[ps], writes=[pTs[q]])
        for q in range(2):
            store(pTs[q], pTs[q][:], pT, pT.t[q * 128:(q + 1) * 128, tb * 512:(tb + 1) * 512])
    gemm(pT, 0, TOWN, I.w_ple_proj, 0, D, 256, "B", C.epi_tm(proj, AF.Copy))

    gpp, gfn = C.big[0], C.big[1]
    P.dma("sp", lambda e: e.dma_start(out=gpp[:], in_=bc_ap(I.ple_post, 128)), P.ext(I.ple_post), gpp)
    P.dma("sp", lambda e: e.dma_start(out=gfn[:], in_=bc_ap(I.final_norm, 128)), P.ext(I.final_norm), gfn)
    big = Ring(C.big[2:5])
    junk = C.junk
    outB = P.ext(out)
    for tt in range(TOWN // 128):
        rs = slice(tt * 128, (tt + 1) * 128)
        pr, gt, x2t = big.next(), big.next(), big.next()
        P.dma("sp", lambda e, pr=pr, rs=rs: e.dma_start(out=pr[:], in_=proj.t[rs, :]), proj, pr)
        P.dma("sp", lambda e, gt=gt, rs=rs: e.dma_start(out=gt[:], in_=gate.t[rs, :]), gate, gt)
        P.dma("sp", lambda e, x2t=x2t, rs=rs: e.dma_start(out=x2t[:], in_=x2.t[rs, :]), x2, x2t)
        ss = C.ssr.next()
        P.op("act", lambda e, pr=pr, ss=ss: e.activation(out=junk[:], in_=pr[:], func=AF.Square, accum_out=ss[:, 0:1]), reads=[pr], writes=[junk, ss])
        C.rstd_op(ss, 0, 1, D)
        P.op("dve", lambda e, pr=pr, ss=ss: e.scalar_tensor_tensor(pr[:], pr[:], ss[:, 1:2], gpp[:], ALU.mult, ALU.mult), reads=[pr, ss, gpp], writes=[pr])
        P.op("dve", lambda e, pr=pr, gt=gt: e.tensor_tensor(pr[:], pr[:], gt[:], ALU.mult), reads=[pr, gt], writes=[pr])
        P.op("dve", lambda e, pr=pr, x2t=x2t: e.tensor_tensor(x2t[:], x2t[:], pr[:], ALU.add), reads=[pr, x2t], writes=[x2t])
        P.op("act", lambda e, x2t=x2t, ss=ss: e.activation(out=junk[:], in_=x2t[:], func=AF.Square, accum_out=ss[:, 2:3]), reads=[x2t], writes=[junk, ss])
        C.rstd_op(ss, 2, 3, D)
        P.op("dve", lambda e, x2t=x2t, ss=ss, gt=gt: e.scalar_tensor_tensor(gt[:], x2t[:], ss[:, 3:4], gfn[:], ALU.mult, ALU.mult), reads=[x2t, ss, gfn], writes=[gt])
        store(gt, gt[:], outB, out[rs, :])
    fin = [outB]
    for name in DUMP:
        src = C.dr[name]
        o = nc.dram_tensor("dump_" + name, list(src.t.shape), src.t.dtype, kind="ExternalOutput").ap()
        ob_ = P.ext(o)
        P.dma("sp", lambda e, o=o, src=src: e.dma_start(out=o, in_=src.t), src, ob_)
        fin.append(ob_)
    P.wait_all("sp", fin)
    P.end_phase()
    P.emit()
    return nc


def host_consts(c):
    tiles, ktab = attn_tiles()
    m = {"ident": np.eye(128, dtype=np.float32)}
    j = np.arange(128)[:, None]
    i = np.arange(128)[None, :]
    m["amask"] = np.concatenate([(j >= i), (j <= i)], 1).astype(np.float32)
    vcol = np.zeros((128, len(ktab)), np.float32)
    first_valid = TA - (c + 1) * TOWN
    for (g, r, s0, n), col in ktab.items():
        d = (1, 4, 16)[g]
        u = r + d * (s0 + np.arange(n))
        vcol[:n, col] = (u >= first_valid)
    m["vcol"] = vcol
    dnc = np.zeros((128, 6 * 512), np.float32)
    a = np.arange(64)
    I64 = np.eye(64, dtype=np.float32)
    low_incl = (a[:, None] >= a[None, :]).astype(np.float32)
    low_strict = (a[:, None] > a[None, :]).astype(np.float32)
    dnc[:64, 0:512] = np.tile(I64, (1, 8))
    dnc[:64, 512:1024] = np.tile(low_strict, (1, 8))
    dnc[:64, 1024:1536] = np.tile(low_strict.T, (1, 8))
    dnc[:64, 1536:2048] = np.tile((1 - low_incl) * -30000.0, (1, 8))
    dnc[:64, 2048:2560] = np.tile((1 - low_incl.T) * -30000.0, (1, 8))
    dnc[:64, 2560:2624] = (a[:, None] <= a[None, :])
    dnc[:, 2624:2752] = 1.0
    dnc[:, 2752:2880] = -1.0
    dnc[:64, 2880:2944] = -I64
    m["dnc"] = dnc
    return m


def kernel(**inputs):
    f = lambda k: np.ascontiguousarray(np.asarray(inputs[k], np.float32)[0])
    x = f("x")
    p = np.asarray(inputs["p"], np.float32)[0, 0]
    shared = {k: f(k) for k in ("w_in", "w_attn_up", "w_dn_up", "w_out", "w_mlp_up", "w_mlp_down", "w_ple_gate",
                                "w_ple_proj", "norm_mix", "norm_mlp", "norm_ple", "ple_post_norm", "dn_a_log", "dn_dt_bias", "dn_norm")}
    shared["final_norm"] = np.ascontiguousarray(np.asarray(inputs["final_norm"], np.float32))
    shared["conv_wT"] = np.ascontiguousarray(f("conv_w").T)
    in_maps = []
    for c in range(NCORES):
        m = dict(shared)
        m.update(host_consts(c))
        xr = np.zeros((S, D), np.float32)
        xr[S - (c + 1) * TOWN:] = x[:(c + 1) * TOWN]
        m["xr"] = xr
        m["pp"] = np.ascontiguousarray(p[c * TOWN:(c + 1) * TOWN])
        in_maps.append(m)
    nc = build_all()
    res = run_bass_kernel_spmd(nc, in_maps, core_ids=list(range(NCORES)), **({"trace": True} if TRACE else {}))
    kernel.last = res
    return np.concatenate([r["out"] for r in res.results], 0)[None].astype(np.float32)
```

```python
import numpy as np
from contextlib import ExitStack
import concourse.bass as bass
import concourse.mybir as mybir
from concourse.bass_utils import run_bass_kernel_spmd

F32 = mybir.dt.float32
BF16 = mybir.dt.bfloat16
AF = mybir.ActivationFunctionType
ALU = mybir.AluOpType
AX = mybir.AxisListType
ENGS = ["pe", "act", "dve", "pool", "sp"]

NCORES = 8
S = 8192
D = 4096
DFF = 16384
TOWN = 1024
HALO = 2048
TA = TOWN + HALO
EPS = 1e-6
QA, KA, VA, QB, KB, VB, ZB, BETA, ALPHA, GA = 0, 3072, 6144, 9216, 11264, 13312, 15360, 17408, 17424, 17440
DUMP = []
STAGES = {"attn", "dn", "tail"}
DN_CHUNKS = None
DN_CUT = 99
DN_SET = 99
TRACE = False
WIDE_FORCE = False


class Buf:
    def __init__(self, t, kind):
        self.t = t
        self.kind = kind
        self.lastw = None
        self.readers = []
        self.sem = None

    def __getitem__(self, k):
        return self.t[k]


class Prog:
    def __init__(self, nc):
        self.nc = nc
        self.es = ExitStack()
        self.q = {e: [] for e in ENGS}
        self.cnt = {e: 0 for e in ENGS}
        self.waited = {e: {} for e in ENGS}
        self.psem = {e: self.es.enter_context(nc.semaphore("prog_" + e)) for e in ENGS}
        self.dcnt = {}
        self._uid = 0
        self.allsems = []
        self.sem_pool = []
        self.phase_es = None
        self.phase_bufs = []

    def uid(self, p):
        self._uid += 1
        return f"{p}{self._uid}"

    def sbuf(self, shape, dt, name=None):
        es = self.phase_es if self.phase_es is not None else self.es
        b = Buf(es.enter_context(self.nc.sbuf_tensor(name or self.uid("sb"), list(shape), dt)), "sb")
        if self.phase_es is not None:
            self.phase_bufs.append(b)
        return b

    def begin_phase(self):
        assert self.phase_es is None
        self.phase_es = ExitStack()
        self.phase_bufs = []

    def barrier(self):
        deps = [(self.psem[x], self.cnt[x], x) for x in ENGS if self.cnt[x] > 0]
        deps += [v for x, v in getattr(self, "prev_final", {}).items() if self.cnt[x] == 0]
        deps += [(sem, self.dcnt[id(sem)], "dma") for sem in self.allsems if self.dcnt[id(sem)] > 0]
        for e in ENGS:
            w = self._waits(e, [d for d in deps if not (d[2] == e)] + [d for d in deps if d[2] == e and e != "pe"])
            if w:
                self.q[e].append((w, None, None))

    def end_phase(self):
        self.barrier()
        for b in self.phase_bufs:
            if b.sem is not None:
                self.sem_pool.append(b.sem)
        self.phase_es.close()
        self.phase_es = None
        self.phase_bufs = []

    def psum(self, shape, dt, name=None):
        return Buf(self.es.enter_context(self.nc.psum_tensor(name or self.uid("ps"), list(shape), dt)), "ps")

    def dram(self, shape, dt, name=None):
        return Buf(self.nc.dram_tensor(name or self.uid("dr"), list(shape), dt).ap(), "dr")

    def ext(self, ap):
        return Buf(ap, "dr")

    def _bsem(self, b):
        if b.sem is None:
            if self.sem_pool:
                b.sem = self.sem_pool.pop()
            else:
                b.sem = self.es.enter_context(self.nc.semaphore(self.uid("ds")))
                self.dcnt[id(b.sem)] = 0
                self.allsems.append(b.sem)
        return b.sem

    def _waits(self, eng, deps):
        w = []
        for d in deps:
            if d is None:
                continue
            sem, val, src = d
            if src == "pe" and eng == "pe":
                continue
            key = id(sem)
            if self.waited[eng].get(key, 0) >= val:
                continue
            self.waited[eng][key] = val
            w.append((sem, val))
        return w

    def _deps(self, reads, writes, waw=True):
        deps = []
        for b in reads:
            deps.append(b.lastw)
            if b.kind == "ps":
                deps += b.readers
        for b in writes:
            if waw:
                deps.append(b.lastw)
            deps += b.readers
        return deps

    def _commit(self, tok, reads, writes):
        for b in writes:
            b.lastw = tok
            b.readers = []
        for b in reads:
            b.readers.append(tok)

    EPOCH = 30000

    def op(self, eng, fn, reads=(), writes=(), signal=True):
        self.pend = getattr(self, "pend", {})
        if self.cnt[eng] >= self.EPOCH and not self.pend.get(eng):
            self.prev_final = getattr(self, "prev_final", {})
            self.prev_final[eng] = (self.psem[eng], self.cnt[eng], eng)
            self.psem[eng] = self.es.enter_context(self.nc.semaphore(self.uid("prog_" + eng)))
            self.cnt[eng] = 0
            self.total = getattr(self, "total", {})
            self.total[eng] = self.total.get(eng, 0) + self.EPOCH
        w = self._waits(eng, self._deps(reads, writes))
        if not signal:
            assert eng == "pe"
            self.pend[eng] = True
            tok = (self.psem[eng], self.cnt[eng] + 1, eng)
            self.q[eng].append((w, fn, None))
            self._commit(tok, reads, writes)
            return tok
        self.pend[eng] = False
        self.cnt[eng] += 1
        tok = (self.psem[eng], self.cnt[eng], eng)
        self.q[eng].append((w, fn, (self.psem[eng], 1)))
        self._commit(tok, reads, writes)
        return tok

    def dma(self, eng, fn, src, dst, waw=True):
        w = self._waits(eng, self._deps([src], [dst], waw=waw))
        sem = self._bsem(dst)
        if self.dcnt[id(sem)] + 16 > self.EPOCH:
            dst.sem = None
            sem = self._bsem(dst)
        self.dcnt[id(sem)] += 16
        tok = (sem, self.dcnt[id(sem)], "dma")
        self.q[eng].append((w, fn, (sem, 16)))
        if waw:
            self._commit(tok, [src], [dst])
        else:
            dst.lastw = tok
            src.readers.append(tok)
        return tok

    def wait_all(self, eng, bufs):
        w = self._waits(eng, [b.lastw for b in bufs])
        if w:
            self.q[eng].append((w, None, None))

    def emit(self):
        nc = self.nc
        print("ops per engine:", {e: len(self.q[e]) for e in ENGS}, "sems:", len(self.allsems))
        with nc.Block() as block:
            def run(e, engine):
                for w, fn, inc in self.q[e]:
                    for sem, val in w:
                        engine.wait_ge(sem, val)
                    if fn is None:
                        continue
                    ins = fn(engine)
                    if inc is not None:
                        ins.then_inc(inc[0], inc[1])

            @block.tensor
            def _(t):
                run("pe", t)

            @block.scalar
            def _(t):
                run("act", t)

            @block.vector
            def _(t):
                run("dve", t)

            @block.gpsimd
            def _(t):
                run("pool", t)

            @block.sync
            def _(t):
                run("sp", t)
        self.es.close()


class Ring:
    def __init__(self, bufs):
        self.bufs = bufs
        self.i = 0

    def next(self):
        b = self.bufs[self.i % len(self.bufs)]
        self.i += 1
        return b


def bc_ap(ap, nparts):
    n = ap.shape[-1]
    return bass.AP(ap.tensor, ap.offset, [[0, nparts], [1, n]])


def attn_tiles():
    tiles, kt = [], {}
    for g, d in enumerate((1, 4, 16)):
        n_own, s0 = TOWN // d, HALO // d
        QW = min(128, n_own)
        for r in range(d):
            for a in range(s0, s0 + n_own, QW):
                for key in ((g, r, a - 128, 128), (g, r, a, QW)):
                    if key not in kt:
                        kt[key] = len(kt)
                tiles.append((g, d, r, a, QW))
    return tiles, kt


class Ctx:
    pass


def build():
    nc = bass.Bass("TRN2", target_bir_lowering=False)
    IN_W = GA + 2 * D
    dt_in = lambda n, s: nc.dram_tensor(n, list(s), F32, kind="ExternalInput").ap()
    I = Ctx()
    I.xr = dt_in("xr", [S, D])
    I.pp = dt_in("pp", [TOWN, 256])
    I.w_in = dt_in("w_in", [D, IN_W])
    I.conv_wT = dt_in("conv_wT", [6144, 4])
    I.dn_a_log = dt_in("dn_a_log", [16])
    I.dn_dt_bias = dt_in("dn_dt_bias", [16])
    I.dn_norm = dt_in("dn_norm", [128])
    I.w_attn_up = dt_in("w_attn_up", [1024, D])
    I.w_dn_up = dt_in("w_dn_up", [2048, D])
    I.w_out = dt_in("w_out", [D, D])
    I.w_mlp_up = dt_in("w_mlp_up", [D, DFF])
    I.w_mlp_down = dt_in("w_mlp_down", [DFF, D])
    I.w_ple_gate = dt_in("w_ple_gate", [D, D])
    I.w_ple_proj = dt_in("w_ple_proj", [256, D])
    I.norm_mix = dt_in("norm_mix", [D])
    I.norm_mlp = dt_in("norm_mlp", [D])
    I.norm_ple = dt_in("norm_ple", [D])
    I.ple_post = dt_in("ple_post_norm", [D])
    I.final_norm = dt_in("final_norm", [D])
    I.ident = dt_in("ident", [128, 128])
    I.amask = dt_in("amask", [128, 256])
    ntile, ktab = attn_tiles()
    I.vcol = dt_in("vcol", [128, len(ktab)])
    I.dnc = dt_in("dnc", [128, 6 * 512])
    out = nc.dram_tensor("out", [TOWN, D], F32, kind="ExternalOutput").ap()

    P = Prog(nc)
    C = Ctx()
    C.I, C.P, C.nc, C.out = I, P, nc, out
    C.ps = Ring([P.psum([128, 512], F32) for _ in range(8)])
    C.ident = P.sbuf([128, 128], F32)
    P.dma("sp", lambda e: e.dma_start(out=C.ident[:], in_=I.ident), P.ext(I.ident), C.ident)
    C.dr = {}

    def dram(name, shape, dt):
        C.dr[name] = P.dram(shape, dt, name="scr_" + name)
        return C.dr[name]
    C.dram = dram

    def gemm_bufs(norm=True):
        C.wring = Ring([P.sbuf([128, 8, 512], BF16) for _ in range(4)])
        C.xring = Ring([P.sbuf([128, 32, 512], BF16) for _ in range(2)])
        C.st32 = Ring([P.sbuf([128, 512], F32) for _ in range(4)])
        C.st16 = Ring([P.sbuf([128, 512], BF16) for _ in range(4)])
        if norm:
            C.big = [P.sbuf([128, D], F32) for _ in range(5)]
            C.junk = P.sbuf([128, D], BF16)
            C.ssr = Ring([P.sbuf([128, 4], F32) for _ in range(2)])
    C.gemm_bufs = gemm_bufs

    def store(sb, sb_ap, dr, dr_ap, q="sp"):
        P.dma(q, lambda e: e.dma_start(out=dr_ap, in_=sb_ap), sb, dr, waw=False)
    C.store = store

    def gemm(xT, t0, T, W, n0, N, K, form, epi, epi_grp=None):
        KC = K // 128
        wv = W.rearrange("(kc p) n -> p kc n", p=128)
        xv = xT.t.rearrange("(kc p) t -> p kc t", p=128)
        Wb = P.ext(W)
        nsup = max(1, KC // 32)
        kcs = min(KC, 32)
        if form == "B" and (nsup > 1 or (WIDE_FORCE and K >= 1024)) and N % 1024 == 0 and epi_grp is None and kcs % 8 == 0:
            for tb in range(T // 512):
                ts = t0 + tb * 512
                for nb2 in range(N // 1024):
                    accs = [C.ps.next() for _ in range(8)]
                    for sup in range(nsup):
                        xs = C.xring.next()
                        P.dma("sp", lambda e, xs=xs, sup=sup, ts=ts: e.dma_start(
                            out=xs[:, 0:kcs, :], in_=xv[:, sup * 32:sup * 32 + kcs, ts:ts + 512]), xT, xs)
                        for kg in range(kcs // 8):
                            k0 = sup * 32 + kg * 8
                            wps = []
                            for cb in range(2):
                                wp = C.wring.next()
                                c0 = n0 + nb2 * 1024 + cb * 512
                                P.dma("pool", lambda e, wp=wp, k0=k0, c0=c0: e.dma_start(
                                    out=wp[:, 0:8, 0:512], in_=wv[:, k0:k0 + 8, c0:c0 + 512]), Wb, wp)
                                wps.append(wp)
                            for kc in range(8):
                                first = (sup == 0 and kg == 0 and kc == 0)
                                last = (sup == nsup - 1 and kg == kcs // 8 - 1 and kc == 7)
                                for cb in range(2):
                                    for j in range(4):
                                        sig = last or (kc == 7 and j == 3)
                                        P.op("pe", lambda e, a=accs[cb * 4 + j], wp=wps[cb], xs=xs, kc=kc, j=j, kk=kg * 8 + kc, f=first, l=last: e.matmul(
                                            a[:, 0:512], xs[:, kk, j * 128:(j + 1) * 128], wp[:, kc, 0:512], start=f, stop=l),
                                            reads=[wps[cb], xs], writes=[accs[cb * 4 + j]], signal=sig)
                    for cb in range(2):
                        for j in range(4):
                            epi(tb, nb2 * 2 + cb, j, accs[cb * 4 + j], 512)
            return
        for tb in range(T // 512):
            ts = t0 + tb * 512
            xs = None
            for nb in range((N + 511) // 512):
                ncols = min(512, N - nb * 512)
                accs = [C.ps.next() for _ in range(4)]
                nj = 4 if form == "B" else (ncols + 127) // 128
                for sup in range(nsup):
                    if xs is None or nsup > 1:
                        xs = C.xring.next()
                        P.dma("sp", lambda e, xs=xs, sup=sup, ts=ts: e.dma_start(
                            out=xs[:, 0:kcs, :], in_=xv[:, sup * 32:sup * 32 + kcs, ts:ts + 512]), xT, xs)
                    for kg in range((kcs + 7) // 8):
                        nk = min(8, kcs - kg * 8)
                        wp = C.wring.next()
                        k0 = sup * 32 + kg * 8
                        P.dma("pool", lambda e, wp=wp, k0=k0, nk=nk, nb=nb, ncols=ncols: e.dma_start(
                            out=wp[:, 0:nk, 0:ncols], in_=wv[:, k0:k0 + nk, n0 + nb * 512:n0 + nb * 512 + ncols]), Wb, wp)
                        for kc in range(nk):
                            first = (sup == 0 and kg == 0 and kc == 0)
                            last = (sup == nsup - 1 and kg * 8 + kc == kcs - 1)
                            for j in range(nj):
                                sig = last or (kc == nk - 1 and j == nj - 1)
                                if form == "A":
                                    mc = min(128, ncols - j * 128)
                                    P.op("pe", lambda e, a=accs[j], wp=wp, xs=xs, kc=kc, j=j, mc=mc, kk=kg * 8 + kc, f=first, l=last: e.matmul(
                                        a[0:mc, :], wp[:, kc, j * 128:j * 128 + mc], xs[:, kk, :], start=f, stop=l),
                                        reads=[wp, xs], writes=[accs[j]], signal=sig)
                                else:
                                    P.op("pe", lambda e, a=accs[j], wp=wp, xs=xs, kc=kc, j=j, nc_=ncols, kk=kg * 8 + kc, f=first, l=last: e.matmul(
                                        a[:, 0:nc_], xs[:, kk, j * 128:(j + 1) * 128], wp[:, kc, 0:nc_], start=f, stop=l),
                                        reads=[wp, xs], writes=[accs[j]], signal=sig)
                if epi_grp is not None:
                    epi_grp(tb, nb, accs[:nj], ncols)
                else:
                    for j in range(nj):
                        epi(tb, nb, j, accs[j], ncols)
    C.gemm = gemm

    def epi_store_T(dst, dt, func=None, roff=0, coff=0):
        def epi(tb, nb, j, acc, ncols):
            mc = min(128, ncols - j * 128)
            sb = (C.st16 if dt == BF16 else C.st32).next()
            if func is None:
                P.op("dve", lambda e: e.tensor_copy(sb[0:mc, :], acc[0:mc, :]), reads=[acc], writes=[sb])
            else:
                P.op("act", lambda e: e.activation(out=sb[0:mc, :], in_=acc[0:mc, :], func=func), reads=[acc], writes=[sb])
            r0 = roff + nb * 512 + j * 128
            store(sb, sb[0:mc, :], dst, dst.t[r0:r0 + mc, coff + tb * 512:coff + (tb + 1) * 512])
        return epi
    C.epi_store_T = epi_store_T

    def epi_tm(dst, func, dt=F32, roff=0, coff=0):
        def epi(tb, nb, j, acc, ncols):
            t = (C.st16 if dt == BF16 else C.st32).next()
            P.op("act", lambda e: e.activation(out=t[:, 0:ncols], in_=acc[:, 0:ncols], func=func), reads=[acc], writes=[t])
            r0 = roff + tb * 512 + j * 128
            store(t, t[:, 0:ncols], dst, dst.t[r0:r0 + 128, coff + nb * 512:coff + nb * 512 + ncols])
        return epi
    C.epi_tm = epi_tm

    def rstd_op(ss, i, o, n):
        P.op("dve", lambda e: e.tensor_scalar(ss[:, o:o + 1], ss[:, i:i + 1], 1.0 / n, EPS, ALU.mult, ALU.add), reads=[ss], writes=[ss])
        P.op("act", lambda e: e.activation(out=ss[:, o:o + 1], in_=ss[:, o:o + 1], func=AF.Sqrt), reads=[ss], writes=[ss])
        P.op("dve", lambda e: e.reciprocal(ss[:, o:o + 1], ss[:, o:o + 1]), reads=[ss], writes=[ss])
    C.rstd_op = rstd_op

    def norm_T(src, srcT0, T, gain_ap, dstT):
        g = C.big[0]
        P.dma("sp", lambda e: e.dma_start(out=g[:], in_=bc_ap(gain_ap, 128)), P.ext(gain_ap), g)
        xr = Ring(C.big[1:3])
        hr = Ring(C.big[3:5])
        junk = C.junk
        KC = D // 128
        for tb in range(T // 512):
            hT = C.xring.next()
            for tt in range(4):
                r0 = srcT0 + tb * 512 + tt * 128
                xt = xr.next()
                P.dma("sp", lambda e, xt=xt, r0=r0: e.dma_start(out=xt[:], in_=src.t[r0:r0 + 128, :]), src, xt)
                ss = C.ssr.next()
                P.op("act", lambda e, xt=xt, ss=ss: e.activation(out=junk[:], in_=xt[:], func=AF.Square, accum_out=ss[:, 0:1]),
                     reads=[xt], writes=[junk, ss])
                rstd_op(ss, 0, 1, D)
                hb = hr.next()
                P.op("dve", lambda e, hb=hb, xt=xt, ss=ss: e.scalar_tensor_tensor(hb[:], xt[:], ss[:, 1:2], g[:], ALU.mult, ALU.mult),
                     reads=[xt, ss, g], writes=[hb])
                for k4 in range(KC // 4):
                    pt = C.ps.next()
                    for q in range(4):
                        kc = k4 * 4 + q
                        P.op("pe", lambda e, pt=pt, hb=hb, kc=kc, q=q: e.transpose(pt[:, q * 128:(q + 1) * 128], hb[:, kc * 128:(kc + 1) * 128], C.ident[:]),
                             reads=[hb, C.ident], writes=[pt])
                    if k4 % 2:
                        P.op("act", lambda e, pt=pt, hT=hT, k4=k4, tt=tt: e.activation(
                            out=hT[:, k4 * 4:k4 * 4 + 4, tt * 128:(tt + 1) * 128], in_=pt[:].rearrange("p (q t) -> p q t", q=4), func=AF.Copy),
                            reads=[pt], writes=[hT])
                    else:
                        P.op("dve", lambda e, pt=pt, hT=hT, k4=k4, tt=tt: e.tensor_copy(
                            hT[:, k4 * 4:k4 * 4 + 4, tt * 128:(tt + 1) * 128], pt[:].rearrange("p (q t) -> p q t", q=4)),
                            reads=[pt], writes=[hT])
            store(hT, hT[:, 0:KC, :], dstT, dstT.t.rearrange("(kc p) t -> p kc t", p=128)[:, :, tb * 512:(tb + 1) * 512])
    C.norm_T = norm_T
    return C


def attention(C, qaT, kaT, vtok, o_aT):
    P, I = C.P, C.I
    tiles, ktab = attn_tiles()
    P.begin_phase()
    amask = P.sbuf([128, 256], BF16)
    P.dma("pool", lambda e: e.dma_start(out=amask[:], in_=I.amask), P.ext(I.amask), amask)
    vcol = P.sbuf([128, len(ktab)], F32)
    P.dma("sp", lambda e: e.dma_start(out=vcol[:], in_=I.vcol), P.ext(I.vcol), vcol)
    ones = P.sbuf([128, 128], BF16)
    P.op("dve", lambda e: e.memset(ones[:], 1.0), writes=[ones])
    qTr = Ring([P.sbuf([128, TOWN], BF16) for _ in range(2)])
    kTr = Ring([P.sbuf([128, TA], BF16) for _ in range(2)])
    geo = {0: (1, 15, 9), 1: (4, 3, 3), 2: (16, 0, 2)}
    vts = {g: P.sbuf([128, geo[g][0], geo[g][2], 128], BF16) for g in range(3)}
    ULr = Ring([P.sbuf([128, 2, TOWN], F32) for _ in range(2)])
    Er = Ring([P.sbuf([128, 256], BF16) for _ in range(3)])
    Ewr = Ring([P.sbuf([128, 256], BF16) for _ in range(3)])
    ob = Ring([P.sbuf([128, TOWN], BF16) for _ in range(2)])
    rl = P.sbuf([128, TOWN], F32)
    scale = 128.0 ** -0.5
    for hh in range(8):
        UL = ULr.next()
        for g in range(3):
            d, bt0, nbt = geo[g]
            row0 = g * 1024 + hh * 128
            qT, kT, vt = qTr.next(), kTr.next(), vts[g]
            P.dma("sp", lambda e, qT=qT, row0=row0: e.dma_start(out=qT[:], in_=qaT.t[row0:row0 + 128, :]), qaT, qT)
            P.dma("sp", lambda e, kT=kT, row0=row0: e.dma_start(out=kT[:], in_=kaT.t[row0:row0 + 128, :]), kaT, kT)
            for r in range(d):
                if g < 2:
                    src = vtok.t[bass.ds(r + d * 128 * bt0, 128 * nbt, step=d), row0:row0 + 128].rearrange("(bt j) c -> j bt c", j=128)
                    P.dma("sp", lambda e, vt=vt, r=r, src=src: e.dma_start(out=vt[:, r, :, :], in_=src), vtok, vt)
                else:
                    src0 = vtok.t[bass.ds(r, 128, step=d), row0:row0 + 128]
                    src1 = vtok.t[bass.ds(r + d * 128, 64, step=d), row0:row0 + 128]
                    P.dma("sp", lambda e, vt=vt, r=r, src0=src0: e.dma_start(out=vt[:, r, 0, :], in_=src0), vtok, vt)
                    P.dma("sp", lambda e, vt=vt, r=r, src1=src1: e.dma_start(out=vt[0:64, r, 1, :], in_=src1), vtok, vt)
            for (tg, td, r, a, QW) in tiles:
                if tg != g:
                    continue
                k0 = ktab[(g, r, a - 128, 128)]
                k1 = ktab[(g, r, a, QW)]
                kc0 = bass.ds(r + d * (a - 128), 128, step=d)
                kc1 = bass.ds(r + d * a, QW, step=d)
                qc = bass.ds(r + d * a - HALO, QW, step=d)
                b0, b1 = (a - 128) // 128 - bt0, a // 128 - bt0
                ps_s, ps_u = C.ps.next(), C.ps.next()
                P.op("pe", lambda e, ps_s=ps_s, kT=kT, qT=qT, kc0=kc0, qc=qc, QW=QW: e.matmul(ps_s[:, 0:QW], kT[:, kc0], qT[:, qc], start=True, stop=True),
                     reads=[kT, qT], writes=[ps_s])
                P.op("pe", lambda e, ps_s=ps_s, kT=kT, qT=qT, kc1=kc1, qc=qc, QW=QW: e.matmul(ps_s[0:QW, 128:128 + QW], kT[:, kc1], qT[:, qc], start=True, stop=True),
                     reads=[kT, qT], writes=[ps_s])
                Er_, E = Er.next(), Ewr.next()
                P.op("act", lambda e, Er_=Er_, ps_s=ps_s, QW=QW: e.activation(out=Er_[:, 0:QW], in_=ps_s[:, 0:QW], func=AF.Exp, scale=scale), reads=[ps_s], writes=[Er_])
                P.op("act", lambda e, Er_=Er_, ps_s=ps_s, QW=QW: e.activation(out=Er_[0:QW, 128:128 + QW], in_=ps_s[0:QW, 128:128 + QW], func=AF.Exp, scale=scale), reads=[ps_s], writes=[Er_])
                P.op("dve", lambda e, E=E, Er_=Er_, k0=k0, QW=QW: e.scalar_tensor_tensor(E[:, 0:QW], Er_[:, 0:QW], vcol[:, k0:k0 + 1], amask[:, 0:QW], ALU.mult, ALU.mult),
                     reads=[Er_, vcol, amask], writes=[E])
                P.op("dve", lambda e, E=E, Er_=Er_, k1=k1, QW=QW: e.scalar_tensor_tensor(E[0:QW, 128:128 + QW], Er_[0:QW, 128:128 + QW], vcol[0:QW, k1:k1 + 1], amask[0:QW, 128:128 + QW], ALU.mult, ALU.mult),
                     reads=[Er_, vcol, amask], writes=[E])
                P.op("pe", lambda e, ps_u=ps_u, vt=vt, r=r, b0=b0, E=E, QW=QW: e.matmul(ps_u[:, 0:QW], vt[:, r, b0, :], E[:, 0:QW], start=True, stop=False), reads=[vt, E], writes=[ps_u])
                P.op("pe", lambda e, ps_u=ps_u, vt=vt, r=r, b1=b1, E=E, QW=QW: e.matmul(ps_u[:, 0:QW], vt[0:QW, r, b1, :], E[0:QW, 128:128 + QW], start=False, stop=True), reads=[vt, E], writes=[ps_u])
                P.op("pe", lambda e, ps_u=ps_u, E=E, QW=QW: e.matmul(ps_u[:, 128:128 + QW], ones[:, :], E[:, 0:QW], start=True, stop=False), reads=[ones, E], writes=[ps_u])
                P.op("pe", lambda e, ps_u=ps_u, E=E, QW=QW: e.matmul(ps_u[:, 128:128 + QW], ones[0:QW, :], E[0:QW, 128:128 + QW], start=False, stop=True), reads=[ones, E], writes=[ps_u])
                for w in range(2):
                    if g == 0:
                        P.op("dve", lambda e, UL=UL, ps_u=ps_u, w=w, qc=qc, QW=QW: e.tensor_copy(UL[:, w, qc], ps_u[:, w * 128:w * 128 + QW]), reads=[ps_u], writes=[UL])
                    else:
                        P.op("dve", lambda e, UL=UL, ps_u=ps_u, w=w, qc=qc, QW=QW: e.tensor_tensor(UL[:, w, qc], UL[:, w, qc], ps_u[:, w * 128:w * 128 + QW], ALU.add), reads=[ps_u, UL], writes=[UL])
        o = ob.next()
        P.op("dve", lambda e, UL=UL: e.reciprocal(rl[:], UL[:, 1, :]), reads=[UL], writes=[rl])
        P.op("dve", lambda e, UL=UL, o=o: e.tensor_tensor(o[:], UL[:, 0, :], rl[:], ALU.mult), reads=[UL, rl], writes=[o])
        C.store(o, o[:], o_aT, o_aT.t[hh * 128:(hh + 1) * 128, :])
    P.end_phase()


def v3(ap, h):
    return ap.rearrange("p (h j) -> p h j", h=h)


def bcl(ap2, n):
    p, h = ap2.shape
    return ap2.unsqueeze(2).broadcast_to([p, h, n])


def dn_prep_bufs(C):
    P, I = C.P, C.I
    Z = Ctx()
    Z.cw = P.sbuf([128, 48, 4], F32)
    P.dma("sp", lambda e: e.dma_start(out=Z.cw[:], in_=I.conv_wT.rearrange("(c p) j -> p c j", p=128)), P.ext(I.conv_wT), Z.cw)
    Z.ones = P.sbuf([128, 128], F32)
    P.op("dve", lambda e: e.memset(Z.ones[:], 1.0), writes=[Z.ones])
    Z.halo = P.sbuf([128, 48, 3], F32)
    P.op("dve", lambda e: e.memset(Z.halo[:], 0.0), writes=[Z.halo])
    mk = lambda w, n=4: Ring([P.sbuf([128, w], F32) for _ in range(n)])
    Z.xr, Z.yr, Z.y2r, Z.sqr, Z.rr, Z.y3r = mk(515), mk(512), mk(512, 8), mk(512, 8), mk(512), mk(512)
    Z.pending = None
    return Z


def dn_prep_epi(C, Z, fc0, dst_of):
    P = C.P
    cw, ones, halo = Z.cw, Z.ones, Z.halo

    def epi_grp(tb, nb, accs, ncols):
        st = []
        for j, acc in enumerate(accs):
            fc = fc0 + nb * 4 + j
            dst, drow = dst_of(fc)
            x = Z.xr.next()
            P.op("act", lambda e, x=x, acc=acc: e.activation(out=x[:, 3:515], in_=acc[:, :], func=AF.Copy), reads=[acc], writes=[x])
            st.append(dict(fc=fc, x=x, y=Z.yr.next(), y2=Z.y2r.next(), dst=dst, drow=drow, blk=tb))
        for t in st:
            P.op("dve", lambda e, t=t: e.tensor_copy(t["x"][:, 0:3], halo[:, t["fc"], :]), reads=[halo], writes=[t["x"]])
        for t in st:
            P.op("dve", lambda e, t=t: e.tensor_copy(halo[:, t["fc"], :], t["x"][:, 512:515]), reads=[t["x"]], writes=[halo])
        for t in st:
            P.op("dve", lambda e, t=t: e.tensor_scalar_mul(t["y"][:], t["x"][:, 3:515], cw[:, t["fc"], 3:4]), reads=[t["x"], cw], writes=[t["y"]])
        for j in (2, 1, 0):
            for t in st:
                P.op("dve", lambda e, t=t, j=j: e.scalar_tensor_tensor(t["y"][:], t["x"][:, j:j + 512], cw[:, t["fc"], j:j + 1], t["y"][:], ALU.mult, ALU.add),
                     reads=[t["x"], cw, t["y"]], writes=[t["y"]])
        for t in st:
            P.op("act", lambda e, t=t: e.activation(out=t["y2"][:], in_=t["y"][:], func=AF.Silu), reads=[t["y"]], writes=[t["y2"]])
        nt = [t for t in st if t["fc"] < 32]
        for t in nt:
            t["sq"] = Z.sqr.next()
            P.op("dve", lambda e, t=t: e.tensor_tensor(t["sq"][:], t["y2"][:], t["y2"][:], ALU.mult), reads=[t["y2"]], writes=[t["sq"]])
        prev, Z.pending = Z.pending, st
        if prev is not None:
            tail(prev)

    def tail(st):
        nt = [t for t in st if t["fc"] < 32]
        for t in nt:
            t["r"], t["y3"], t["ps"] = Z.rr.next(), Z.y3r.next(), C.ps.next()
            P.op("pe", lambda e, t=t: e.matmul(t["ps"][:, :], ones[:, :], t["sq"][:, :], start=True, stop=True), reads=[ones, t["sq"]], writes=[t["ps"]])
        for t in nt:
            P.op("act", lambda e, t=t: e.activation(out=t["r"][:], in_=t["ps"][:], func=AF.Sqrt, bias=EPS), reads=[t["ps"]], writes=[t["r"]])
        for t in nt:
            P.op("dve", lambda e, t=t: e.reciprocal(t["r"][:], t["r"][:]), reads=[t["r"]], writes=[t["r"]])
        for t in nt:
            sc = 128.0 ** -0.5 if t["fc"] < 16 else 1.0
            P.op("dve", lambda e, t=t, sc=sc: e.scalar_tensor_tensor(t["y3"][:], t["y2"][:], sc, t["r"][:], ALU.mult, ALU.mult), reads=[t["y2"], t["r"]], writes=[t["y3"]])
        for t in st:
            yo = t["y3"] if t["fc"] < 32 else t["y2"]
            C.store(yo, yo[:], t["dst"], t["dst"].t[t["drow"]:t["drow"] + 128, t["blk"] * 512:(t["blk"] + 1) * 512])

    def flush():
        prev, Z.pending = Z.pending, None
        if prev is not None:
            tail(prev)
    epi_grp.flush = flush
    return epi_grp


def dn_scan(C, ba, khT, vT, qhT, ztm, o_bT):
    P, I = C.P, C.I
    NCH = S // 64
    OWN0 = NCH - TOWN // 64
    TQ = qhT.t.shape[1]
    P.begin_phase()
    dnc = P.sbuf([128, 6 * 512], F32)
    P.dma("sp", lambda e: e.dma_start(out=dnc[:], in_=I.dnc), P.ext(I.dnc), dnc)
    Irep, strictrep, strictTrep = v3(dnc[0:64, 0:512], 8), v3(dnc[0:64, 512:1024], 8), v3(dnc[0:64, 1024:1536], 8)
    mneg, mnegT = dnc[0:64, 1536:2048], dnc[0:64, 2048:2560]
    Utri, ones, negones, negI64 = dnc[0:64, 2560:2624], dnc[:, 2624:2752], dnc[:, 2752:2880], dnc[0:64, 2880:2944]
    I64 = C.ident[0:64, 0:64]
    ident = C.ident
    gc_all = P.sbuf([64, NCH, 16], F32)
    beta_all = P.sbuf([64, NCH, 16], F32)
    kdecs_all = P.sbuf([64, NCH, 16], F32)
    egl_all = P.sbuf([128, NCH, 16], F32)
    alog = P.sbuf([64, 16], F32)
    dtb = P.sbuf([64, 16], F32)
    P.dma("sp", lambda e: e.dma_start(out=alog[:], in_=bc_ap(I.dn_a_log, 64)), P.ext(I.dn_a_log), alog)
    P.dma("sp", lambda e: e.dma_start(out=dtb[:], in_=bc_ap(I.dn_dt_bias, 64)), P.ext(I.dn_dt_bias), dtb)
    P.op("act", lambda e: e.activation(out=alog[:], in_=alog[:], func=AF.Exp), reads=[alog], writes=[alog])
    P.op("dve", lambda e: e.tensor_scalar_mul(alog[:], alog[:], -1.0), reads=[alog], writes=[alog])
    dnn = P.sbuf([64, 128], F32)
    P.dma("sp", lambda e: e.dma_start(out=dnn[:], in_=bc_ap(I.dn_norm, 64)), P.ext(I.dn_norm), dnn)
    QC = 16
    baq = P.sbuf([64, QC, 32], F32)
    spq = P.sbuf([64, QC, 16], F32)
    gq = P.sbuf([64, QC, 16], F32)
    tq = P.sbuf([64, QC, 16], F32)
    bav = ba.t.rearrange("(c t) n -> t c n", t=64)
    for q in range(NCH // QC if DN_SET >= 1 else 0):
        cs = slice(q * QC, (q + 1) * QC)
        P.dma("sp", lambda e, cs=cs: e.dma_start(out=baq[:], in_=bav[:, cs, :]), ba, baq)
        if DN_SET < 2:
            continue
        P.op("act", lambda e, cs=cs: e.activation(out=beta_all[:, cs, :], in_=baq[:, :, 0:16], func=AF.Sigmoid), reads=[baq], writes=[beta_all])
        if DN_SET < 3:
            continue
        P.op("dve", lambda e: e.tensor_tensor(spq[:], baq[:, :, 16:32], dtb[:].unsqueeze(1).broadcast_to([64, QC, 16]), ALU.add), reads=[baq, dtb], writes=[spq])
        P.op("act", lambda e: e.activation(out=spq[:], in_=spq[:], func=AF.Exp), reads=[spq], writes=[spq])
        P.op("act", lambda e: e.activation(out=spq[:], in_=spq[:], func=AF.Ln, bias=1.0), reads=[spq], writes=[spq])
        P.op("dve", lambda e: e.tensor_tensor(gq[:], spq[:], alog[:].unsqueeze(1).broadcast_to([64, QC, 16]), ALU.mult), reads=[spq, alog], writes=[gq])
        if DN_SET < 4:
            continue
        ps1, ps2 = C.ps.next(), C.ps.next()
        gflat = gq[:].rearrange("p c h -> p (c h)")
        P.op("pe", lambda e, ps1=ps1: e.matmul(ps1[0:64, 0:QC * 16], Utri, gflat, start=True, stop=True), reads=[dnc, gq], writes=[ps1])
        P.op("pe", lambda e, ps2=ps2: e.matmul(ps2[:, 0:QC * 16], ones[0:64, :], gflat, start=True, stop=True), reads=[dnc, gq], writes=[ps2])
        if DN_SET < 5:
            continue
        P.op("dve", lambda e, ps1=ps1, cs=cs: e.tensor_copy(gc_all[:, cs, :], v3(ps1[0:64, 0:QC * 16], QC)), reads=[ps1], writes=[gc_all])
        if DN_SET < 6:
            continue
        if DN_SET < 7:
            continue
        P.op("act", lambda e, ps2=ps2, cs=cs: e.activation(out=egl_all[:, cs, :], in_=v3(ps2[:, 0:QC * 16], QC), func=AF.Exp), reads=[ps2], writes=[egl_all])
        if DN_SET < 8:
            continue
        P.op("dve", lambda e, ps2=ps2, cs=cs: e.tensor_tensor(tq[:], v3(ps2[0:64, 0:QC * 16], QC), gc_all[:, cs, :], ALU.subtract), reads=[ps2, gc_all], writes=[tq])
        if DN_SET < 9:
            continue
        P.op("act", lambda e, cs=cs: e.activation(out=kdecs_all[:, cs, :], in_=tq[:], func=AF.Exp), reads=[tq], writes=[kdecs_all])
    if hasattr(C, "dbg"):
        for nm, b_ in (("gc_all", gc_all), ("beta_all", beta_all), ("kdecs_all", kdecs_all), ("egl_all", egl_all)):
            C.dbg(nm, b_, b_[:].rearrange("p c h -> p (c h)"))
    S4 = [P.sbuf([128, 4, 128], F32) for _ in range(4)]
    for s_ in S4:
        P.op("dve", lambda e, s_=s_: e.memset(s_[:], 0.0), writes=[s_])
    kcr = Ring([P.sbuf([128, 16, 64], F32) for _ in range(2)])
    vcr = Ring([P.sbuf([128, 16, 64], F32) for _ in range(2)])
    qc_ = P.sbuf([128, 16, 64], F32)
    zc_ = P.sbuf([64, 2048], F32)
    obT = P.sbuf([128, 16, 64], BF16)
    nb_ = P.sbuf([64, 16], F32)
    kbgs = P.sbuf([64, 16], F32)
    egc = P.sbuf([64, 16], F32)
    T = lambda: P.sbuf([64, 8, 64], F32)
    T4 = lambda: P.sbuf([64, 8, 128], F32)
    F = lambda b: b[:].rearrange("p h j -> p (h j)")

    def mk():
        X = Ctx()
        X.diagG, X.diagB, X.diagE, X.Dm, X.DmT, X.t1, X.t2, X.intraT = (T() for _ in range(8))
        X.sets = [(T(), T(), T()) for _ in range(2)]
        X.qdec, X.wT = P.sbuf([128, 8, 64], F32), P.sbuf([128, 8, 64], F32)
        X.kbg, X.kdec, X.vb, X.u, X.vnew = (T4() for _ in range(5))
        X.osb, X.on = P.sbuf([64, 4, 128], F32), P.sbuf([64, 4, 128], F32)
        X.ss = P.sbuf([64, 8], F32)
        return X
    XS = [mk(), mk()]
    chunks = range(NCH) if DN_CHUNKS is None else list(range(DN_CHUNKS)) + ([OWN0] if DN_CUT >= 8 else [])
    for c in (chunks if DN_CUT >= 1 else []):
        own = c >= OWN0 and DN_CUT >= 8
        kc_, vc_ = kcr.next(), vcr.next()
        P.dma("sp", lambda e, kc_=kc_, c=c: e.dma_start(out=kc_[:], in_=khT.t.rearrange("(h d) t -> d h t", d=128)[:, :, c * 64:(c + 1) * 64]), khT, kc_)
        P.dma("sp", lambda e, vc_=vc_, c=c: e.dma_start(out=vc_[:], in_=vT.t.rearrange("(h d) t -> d h t", d=128)[:, :, c * 64:(c + 1) * 64]), vT, vc_)
        if own:
            q0 = (c - OWN0) * 64 + (TQ - TOWN)
            P.dma("sp", lambda e, q0=q0: e.dma_start(out=qc_[:], in_=qhT.t.rearrange("(h d) t -> d h t", d=128)[:, :, q0:q0 + 64]), qhT, qc_)
            P.dma("sp", lambda e, c=c: e.dma_start(out=zc_[:], in_=ztm.t[(c - OWN0) * 64:(c - OWN0 + 1) * 64, :]), ztm, zc_)
        P.op("dve", lambda e, c=c: e.tensor_scalar_mul(nb_[:], beta_all[:, c, :], -1.0), reads=[beta_all], writes=[nb_])
        P.op("act", lambda e, c=c: e.activation(out=egc[:], in_=gc_all[:, c, :], func=AF.Exp), reads=[gc_all], writes=[egc])
        P.op("dve", lambda e, c=c: e.tensor_tensor(kbgs[:], beta_all[:, c, :], egc[:], ALU.mult), reads=[beta_all, egc], writes=[kbgs])
        HG = (0, 1)
        hsl = lambda hg: slice(hg * 8, hg * 8 + 8)
        for hg in HG:
            X = XS[hg]
            X.gcs = bcl(gc_all[:, c, hsl(hg)], 64)
            P.op("dve", lambda e, X=X, g_=X.gcs: e.tensor_tensor(X.diagG[:], Irep, g_, ALU.mult), reads=[dnc, gc_all], writes=[X.diagG])
            P.op("dve", lambda e, X=X, c=c, hg=hg: e.tensor_tensor(X.diagB[:], Irep, bcl(beta_all[:, c, hsl(hg)], 64), ALU.mult), reads=[dnc, beta_all], writes=[X.diagB])
        for hg in HG:
            X = XS[hg]
            h0 = hg * 8
            X.p1, X.p3, X.p4, X.p5 = C.ps.next(), C.ps.next(), C.ps.next(), C.ps.next()
            for h in range(8):
                P.op("pe", lambda e, p1=X.p1, kc_=kc_, h=h, h0=h0: e.matmul(p1[0:64, h * 64:(h + 1) * 64], kc_[:, h0 + h, :], kc_[:, h0 + h, :], start=True, stop=True),
                     reads=[kc_], writes=[X.p1], signal=(h == 7))
            P.op("pe", lambda e, X=X, p3=X.p3: e.matmul(p3[0:64, :], negones[0:64, 0:64], F(X.diagG), start=True, stop=False), reads=[dnc, X.diagG], writes=[X.p3])
            P.op("pe", lambda e, X=X, g_=X.gcs, p3=X.p3: e.matmul(p3[0:64, :], I64, g_, start=False, stop=False), reads=[ident, gc_all], writes=[X.p3])
            P.op("pe", lambda e, X=X, p3=X.p3: e.matmul(p3[0:64, :], I64, mneg, start=False, stop=True), reads=[ident, dnc], writes=[X.p3])
            P.op("pe", lambda e, X=X, p4=X.p4: e.matmul(p4[0:64, :], ones[0:64, 0:64], F(X.diagG), start=True, stop=False), reads=[dnc, X.diagG], writes=[X.p4])
            P.op("pe", lambda e, X=X, g_=X.gcs, p4=X.p4: e.matmul(p4[0:64, :], negI64, g_, start=False, stop=False), reads=[dnc, gc_all], writes=[X.p4])
            P.op("pe", lambda e, X=X, p4=X.p4: e.matmul(p4[0:64, :], I64, mnegT, start=False, stop=True), reads=[ident, dnc], writes=[X.p4])
            P.op("pe", lambda e, X=X, p5=X.p5: e.matmul(p5[0:64, :], negones[0:64, 0:64], F(X.diagB), start=True, stop=True), reads=[dnc, X.diagB], writes=[X.p5])
        for hg in HG:
            X = XS[hg]
            P.op("act", lambda e, X=X, p3=X.p3: e.activation(out=F(X.Dm), in_=p3[0:64, :], func=AF.Exp), reads=[X.p3], writes=[X.Dm])
            P.op("act", lambda e, X=X, p4=X.p4: e.activation(out=F(X.DmT), in_=p4[0:64, :], func=AF.Exp), reads=[X.p4], writes=[X.DmT])
        if DN_CUT < 3:
            continue
        for hg in HG:
            X = XS[hg]
            P.op("dve", lambda e, X=X, p1=X.p1: e.tensor_tensor(X.t1[:], v3(p1[0:64, :], 8), strictrep, ALU.mult), reads=[X.p1, dnc], writes=[X.t1])
            P.op("dve", lambda e, X=X, p1=X.p1: e.tensor_tensor(X.t2[:], v3(p1[0:64, :], 8), strictTrep, ALU.mult), reads=[X.p1, dnc], writes=[X.t2])
        for hg in HG:
            X = XS[hg]
            P.op("dve", lambda e, X=X: e.tensor_tensor(X.t1[:], X.t1[:], X.Dm[:], ALU.mult), reads=[X.t1, X.Dm], writes=[X.t1])
            P.op("dve", lambda e, X=X: e.tensor_tensor(X.t2[:], X.t2[:], X.DmT[:], ALU.mult), reads=[X.t2, X.DmT], writes=[X.t2])
        for hg in HG:
            X = XS[hg]
            M, MT, R = X.sets[0]
            P.op("dve", lambda e, X=X, M=M, hg=hg: e.tensor_tensor(M[:], X.t1[:], bcl(nb_[:, hsl(hg)], 64), ALU.mult), reads=[X.t1, nb_], writes=[M])
            P.op("dve", lambda e, X=X, MT=MT, p5=X.p5: e.tensor_tensor(MT[:], X.t2[:], v3(p5[0:64, :], 8), ALU.mult), reads=[X.t2, X.p5], writes=[MT])
        for hg in HG:
            M, MT, R = XS[hg].sets[0]
            P.op("dve", lambda e, R=R, MT=MT: e.tensor_tensor(R[:], MT[:], Irep, ALU.add), reads=[MT, dnc], writes=[R])
        if DN_CUT < 4:
            continue
        cur = 0
        for m in range(1, 6):
            for hg in HG:
                X = XS[hg]
                M, MT, R = X.sets[cur]
                X.pm, X.pmt = C.ps.next(), C.ps.next()
                for h in range(8):
                    P.op("pe", lambda e, pm=X.pm, M=M, MT=MT, h=h: e.matmul(pm[0:64, h * 64:(h + 1) * 64], MT[:, h, :], M[:, h, :], start=True, stop=True), reads=[M, MT], writes=[X.pm], signal=(h == 7))
                if m < 5:
                    for h in range(8):
                        P.op("pe", lambda e, pmt=X.pmt, M=M, MT=MT, h=h: e.matmul(pmt[0:64, h * 64:(h + 1) * 64], M[:, h, :], MT[:, h, :], start=True, stop=True), reads=[M, MT], writes=[X.pmt], signal=(h == 7))
            for hg in HG:
                X = XS[hg]
                Mn, MTn, Rn = X.sets[1 - cur]
                P.op("act", lambda e, Mn=Mn, pm=X.pm: e.activation(out=F(Mn), in_=pm[0:64, :], func=AF.Copy), reads=[X.pm], writes=[Mn])
                if m < 5:
                    P.op("act", lambda e, MTn=MTn, pmt=X.pmt: e.activation(out=F(MTn), in_=pmt[0:64, :], func=AF.Copy), reads=[X.pmt], writes=[MTn])
            for hg in HG:
                X = XS[hg]
                M, MT, R = X.sets[cur]
                Mn, MTn, Rn = X.sets[1 - cur]
                X.pr = C.ps.next()
                for h in range(8):
                    P.op("pe", lambda e, pr=X.pr, Mn=Mn, R=R, h=h: e.matmul(pr[0:64, h * 64:(h + 1) * 64], Mn[:, h, :], R[:, h, :], start=True, stop=True), reads=[Mn, R], writes=[X.pr], signal=(h == 7))
            for hg in HG:
                X = XS[hg]
                M, MT, R = X.sets[cur]
                Mn, MTn, Rn = X.sets[1 - cur]
                P.op("dve", lambda e, Rn=Rn, R=R, pr=X.pr: e.tensor_tensor(F(Rn), F(R), pr[0:64, :], ALU.add), reads=[R, X.pr], writes=[Rn])
            cur = 1 - cur
        if DN_CUT < 5:
            continue
        for half in range(2):
            for hg in HG:
                X = XS[hg]
                h0 = hg * 8
                X.pk, X.pv = C.ps.next(), C.ps.next()
                for hq in range(4):
                    h = h0 + half * 4 + hq
                    P.op("pe", lambda e, pk=X.pk, kc_=kc_, h=h, hq=hq: e.transpose(pk[0:64, hq * 128:(hq + 1) * 128], kc_[:, h, :], ident[:]), reads=[kc_, ident], writes=[X.pk])
                    P.op("pe", lambda e, pv=X.pv, vc_=vc_, h=h, hq=hq: e.transpose(pv[0:64, hq * 128:(hq + 1) * 128], vc_[:, h, :], ident[:]), reads=[vc_, ident], writes=[X.pv])
            for hg in HG:
                X = XS[hg]
                h0 = hg * 8
                a4 = slice(half * 4, half * 4 + 4)
                g4 = slice(h0 + half * 4, h0 + half * 4 + 4)
                P.op("dve", lambda e, X=X, pk=X.pk, a4=a4, g4=g4: e.tensor_tensor(X.kbg[:, a4, :], v3(pk[0:64, :], 4), bcl(kbgs[:, g4], 128), ALU.mult), reads=[X.pk, kbgs], writes=[X.kbg])
                P.op("dve", lambda e, X=X, pk=X.pk, a4=a4, g4=g4, c=c: e.tensor_tensor(X.kdec[:, a4, :], v3(pk[0:64, :], 4), bcl(kdecs_all[:, c, g4], 128), ALU.mult), reads=[X.pk, kdecs_all], writes=[X.kdec])
                P.op("dve", lambda e, X=X, pv=X.pv, a4=a4, g4=g4, c=c: e.tensor_tensor(X.vb[:, a4, :], v3(pv[0:64, :], 4), bcl(beta_all[:, c, g4], 128), ALU.mult), reads=[X.pv, beta_all], writes=[X.vb])
        if DN_CUT < 6:
            continue
        for hg in HG:
            X = XS[hg]
            R = X.sets[cur][2]
            X.pw = C.ps.next()
            for h in range(8):
                P.op("pe", lambda e, X=X, pw=X.pw, R=R, h=h: e.matmul(pw[:, h * 64:(h + 1) * 64], X.kbg[:, h, :], R[:, h, :], start=True, stop=True), reads=[X.kbg, R], writes=[X.pw], signal=(h == 7))
        for hg in HG:
            X = XS[hg]
            P.op("act", lambda e, X=X, pw=X.pw: e.activation(out=F(X.wT), in_=pw[:, :], func=AF.Copy), reads=[X.pw], writes=[X.wT])
        for half in range(2):
            for hg in HG:
                X = XS[hg]
                R = X.sets[cur][2]
                X.pu = C.ps.next()
                for hq in range(4):
                    h = half * 4 + hq
                    P.op("pe", lambda e, X=X, pu=X.pu, R=R, h=h, hq=hq: e.matmul(pu[0:64, hq * 128:(hq + 1) * 128], R[:, h, :], X.vb[:, h, :], start=True, stop=True), reads=[R, X.vb], writes=[X.pu], signal=(hq == 3))
            for hg in HG:
                X = XS[hg]
                P.op("act", lambda e, X=X, pu=X.pu, half=half: e.activation(out=X.u[:, half * 4:half * 4 + 4, :], in_=v3(pu[0:64, :], 4), func=AF.Copy), reads=[X.pu], writes=[X.u])
        if own:
            for hg in HG:
                X = XS[hg]
                h0 = hg * 8
                X.p2, X.p6 = C.ps.next(), C.ps.next()
                for h in range(8):
                    P.op("pe", lambda e, p2=X.p2, kc_=kc_, h=h, h0=h0: e.matmul(p2[0:64, h * 64:(h + 1) * 64], kc_[:, h0 + h, :], qc_[:, h0 + h, :], start=True, stop=True),
                         reads=[kc_, qc_], writes=[X.p2], signal=(h == 7))
                P.op("dve", lambda e, X=X, hg=hg: e.tensor_tensor(X.diagE[:], Irep, bcl(egc[:, hsl(hg)], 64), ALU.mult), reads=[dnc, egc], writes=[X.diagE])
            for hg in HG:
                X = XS[hg]
                P.op("dve", lambda e, X=X, p2=X.p2: e.tensor_tensor(X.intraT[:], v3(p2[0:64, :], 8), X.DmT[:], ALU.mult), reads=[X.p2, X.DmT], writes=[X.intraT])
                P.op("pe", lambda e, X=X, p6=X.p6: e.matmul(p6[:, :], ones[0:64, :], F(X.diagE), start=True, stop=True), reads=[dnc, X.diagE], writes=[X.p6])
            for hg in HG:
                X = XS[hg]
                P.op("dve", lambda e, X=X, p6=X.p6, hg=hg: e.tensor_tensor(X.qdec[:], qc_[:, hsl(hg), :], v3(p6[:, :], 8), ALU.mult), reads=[qc_, X.p6], writes=[X.qdec])
        if DN_CUT < 7:
            continue
        for half in range(2):
            a4 = slice(half * 4, half * 4 + 4)
            for hg in HG:
                X = XS[hg]
                Sb = S4[hg * 2 + half]
                X.pws = C.ps.next()
                for hq in range(4):
                    h = half * 4 + hq
                    P.op("pe", lambda e, X=X, pws=X.pws, Sb=Sb, h=h, hq=hq: e.matmul(pws[0:64, hq * 128:(hq + 1) * 128], X.wT[:, h, :], Sb[:, hq, :], start=True, stop=True), reads=[X.wT, Sb], writes=[X.pws], signal=(hq == 3))
            for hg in HG:
                X = XS[hg]
                P.op("dve", lambda e, X=X, pws=X.pws, a4=a4: e.tensor_tensor(X.vnew[:, a4, :], X.u[:, a4, :], v3(pws[0:64, :], 4), ALU.subtract), reads=[X.u, X.pws], writes=[X.vnew])
            for hg in HG:
                X = XS[hg]
                Sb = S4[hg * 2 + half]
                if own:
                    X.po = C.ps.next()
                    for hq in range(4):
                        h = half * 4 + hq
                        P.op("pe", lambda e, X=X, po=X.po, Sb=Sb, h=h, hq=hq: e.matmul(po[0:64, hq * 128:(hq + 1) * 128], X.qdec[:, h, :], Sb[:, hq, :], start=True, stop=False), reads=[X.qdec, Sb], writes=[X.po])
                        P.op("pe", lambda e, X=X, po=X.po, h=h, hq=hq: e.matmul(po[0:64, hq * 128:(hq + 1) * 128], X.intraT[:, h, :], X.vnew[:, h, :], start=False, stop=True), reads=[X.intraT, X.vnew], writes=[X.po])
                X.psu = C.ps.next()
                for hq in range(4):
                    h = half * 4 + hq
                    P.op("pe", lambda e, X=X, psu=X.psu, h=h, hq=hq: e.matmul(psu[:, hq * 128:(hq + 1) * 128], X.kdec[:, h, :], X.vnew[:, h, :], start=True, stop=True), reads=[X.kdec, X.vnew], writes=[X.psu], signal=(hq == 3))
            for hg in HG:
                Sb = S4[hg * 2 + half]
                g4 = slice(hg * 8 + half * 4, hg * 8 + half * 4 + 4)
                P.op("dve", lambda e, Sb=Sb, c=c, g4=g4: e.tensor_tensor(Sb[:], Sb[:], bcl(egl_all[:, c, g4], 128), ALU.mult), reads=[Sb, egl_all], writes=[Sb])
            for hg in HG:
                X = XS[hg]
                Sb = S4[hg * 2 + half]
                P.op("dve", lambda e, Sb=Sb, psu=X.psu: e.tensor_tensor(Sb[:], Sb[:], v3(psu[:, :], 4), ALU.add), reads=[Sb, X.psu], writes=[Sb])
            if own:
                for hg in HG:
                    X = XS[hg]
                    P.op("act", lambda e, X=X, po=X.po: e.activation(out=X.osb[:], in_=v3(po[0:64, :], 4), func=AF.Copy), reads=[X.po], writes=[X.osb])
                for hg in HG:
                    X = XS[hg]
                    P.op("dve", lambda e, X=X: e.tensor_tensor(X.on[:], X.osb[:], X.osb[:], ALU.mult), reads=[X.osb], writes=[X.on])
                for hg in HG:
                    X = XS[hg]
                    P.op("dve", lambda e, X=X: e.tensor_reduce(X.ss[:, 0:4], X.on[:], AX.X, ALU.add), reads=[X.on], writes=[X.ss])
                for hg in HG:
                    X = XS[hg]
                    P.op("dve", lambda e, X=X: e.tensor_scalar(X.ss[:, 4:8], X.ss[:, 0:4], 1.0 / 128, EPS, ALU.mult, ALU.add), reads=[X.ss], writes=[X.ss])
                for hg in HG:
                    X = XS[hg]
                    P.op("act", lambda e, X=X: e.activation(out=X.ss[:, 4:8], in_=X.ss[:, 4:8], func=AF.Sqrt), reads=[X.ss], writes=[X.ss])
                for hg in HG:
                    X = XS[hg]
                    P.op("dve", lambda e, X=X: e.reciprocal(X.ss[:, 4:8], X.ss[:, 4:8]), reads=[X.ss], writes=[X.ss])
                for hg in HG:
                    X = XS[hg]
                    P.op("dve", lambda e, X=X: e.tensor_tensor(X.on[:], X.osb[:], bcl(X.ss[:, 4:8], 128), ALU.mult), reads=[X.osb, X.ss], writes=[X.on])
                for hg in HG:
                    X = XS[hg]
                    P.op("dve", lambda e, X=X: e.tensor_tensor(X.on[:], X.on[:], dnn[:].unsqueeze(1).broadcast_to([64, 4, 128]), ALU.mult), reads=[X.on, dnn], writes=[X.on])
                for hg in HG:
                    X = XS[hg]
                    z0 = (hg * 8 + half * 4) * 128
                    P.op("dve", lambda e, X=X, z0=z0: e.tensor_tensor(X.on[:], X.on[:], v3(zc_[:, z0:z0 + 512], 4), ALU.mult), reads=[X.on, zc_], writes=[X.on])
                for hg in HG:
                    X = XS[hg]
                    X.pt = C.ps.next()
                    for hq in range(4):
                        P.op("pe", lambda e, X=X, pt=X.pt, hq=hq: e.transpose(pt[:, hq * 64:(hq + 1) * 64], X.on[:, hq, :], I64), reads=[X.on, ident], writes=[X.pt], signal=(hq == 3))
                for hg in HG:
                    X = XS[hg]
                    g4 = slice(hg * 8 + half * 4, hg * 8 + half * 4 + 4)
                    P.op("act", lambda e, pt=X.pt, g4=g4: e.activation(out=obT[:, g4, :], in_=v3(pt[:, 0:256], 4), func=AF.Copy), reads=[X.pt], writes=[obT])
        if own:
            C.store(obT, obT[:], o_bT, o_bT.t.rearrange("(h d) t -> d h t", d=128)[:, :, (c - OWN0) * 64:(c - OWN0 + 1) * 64])
    if hasattr(C, "dbg"):
        X = XS[1]
        for nm, b_ in (("Dm", X.Dm), ("DmT", X.DmT), ("R0", X.sets[0][2]), ("R1", X.sets[1][2]), ("M0", X.sets[0][0]), ("intraT", X.intraT)):
            C.dbg(nm, b_, b_[:].rearrange("p h j -> p (h j)"))
        for nm, b_ in (("u", X.u), ("vnew", X.vnew), ("kbg", X.kbg), ("kdec", X.kdec), ("vb", X.vb)):
            C.dbg(nm, b_, b_[:].rearrange("p h j -> p (h j)"))
        C.dbg("wT", X.wT, X.wT[:].rearrange("p h j -> p (h j)"))
        for i_, s_ in enumerate(S4):
            C.dbg("S%d" % i_, s_, s_[:].rearrange("p h j -> p (h j)"))
    P.end_phase()


def build_all():
    C = build()
    P, I, nc, out = C.P, C.I, C.nc, C.out
    gemm, store, norm_T, dram = C.gemm, C.store, C.norm_T, C.dram
    xrB = P.ext(I.xr)
    T0 = S - TOWN
    TQ = TOWN + 512

    P.begin_phase()
    C.gemm_bufs()
    hT = dram("hT", [D, S], BF16)
    norm_T(xrB, 0, S, I.norm_mix, hT)
    P.end_phase()
    P.begin_phase()
    C.gemm_bufs(norm=False)
    gT = dram("gT", [2 * D, TOWN], BF16)
    gemm(hT, T0, TOWN, I.w_in, GA, 2 * D, D, "A", C.epi_store_T(gT, BF16, AF.Sigmoid))
    o_aT = dram("o_aT", [1024, TOWN], BF16)
    o_bT = dram("o_bT", [2048, TOWN], BF16)
    if "attn" in STAGES:
        qaT = dram("qaT", [3072, TOWN], BF16)
        kaT = dram("kaT", [3072, TA], BF16)
        vtok = dram("vtok", [TA, 3072], BF16)
        gemm(hT, T0, TOWN, I.w_in, QA, 3072, D, "A", C.epi_store_T(qaT, BF16))
        TS = TOWN + 512
        gemm(hT, S - TS, TS, I.w_in, KA, 2048, D, "A", C.epi_store_T(kaT, BF16, coff=TA - TS))
        gemm(hT, S - TA, TA, I.w_in, KA + 2048, 1024, D, "A", C.epi_store_T(kaT, BF16, roff=2048))
        gemm(hT, S - TS, TS, I.w_in, VA, 2048, D, "B", C.epi_tm(vtok, AF.Copy, BF16, roff=TA - TS))
        gemm(hT, S - TA, TA, I.w_in, VA + 2048, 1024, D, "B", C.epi_tm(vtok, AF.Copy, BF16, coff=2048))
    if "dn" in STAGES:
        ztm = dram("ztm", [TOWN, 2048], F32)
        ba = dram("ba", [S, 32], F32)
        khT = dram("khT", [2048, S], F32)
        vT = dram("vT", [2048, S], F32)
        qhT = dram("qhT", [2048, TQ], F32)
        Z = dn_prep_bufs(C)
        kv_dst = lambda fc: (khT, (fc - 16) * 128) if fc < 32 else (vT, (fc - 32) * 128)
        eg = dn_prep_epi(C, Z, 16, kv_dst)
        gemm(hT, 0, S, I.w_in, KB, 4096, D, "A", None, epi_grp=eg)
        eg.flush()
        eg = dn_prep_epi(C, Z, 0, lambda fc: (qhT, fc * 128))
        gemm(hT, S - TQ, TQ, I.w_in, QB, 2048, D, "A", None, epi_grp=eg)
        eg.flush()
        gemm(hT, T0, TOWN, I.w_in, ZB, 2048, D, "B", C.epi_tm(ztm, AF.Silu))
        gemm(hT, 0, S, I.w_in, BETA, 32, D, "B", C.epi_tm(ba, AF.Copy))
    P.end_phase()

    if "attn" in STAGES:
        attention(C, qaT, kaT, vtok, o_aT)
    if "dn" in STAGES:
        if "noscan" not in STAGES:
            dn_scan(C, ba, khT, vT, qhT, ztm, o_bT)

    P.begin_phase()
    C.gemm_bufs()
    e32, e16 = C.st32, C.st16
    AT = dram("AT", [D, TOWN], F32)
    mT = dram("mT", [D, TOWN], BF16)

    def epi_gate_a(tb, nb, j, acc, ncols):
        r0 = nb * 512 + j * 128
        sl = (slice(r0, r0 + 128), slice(tb * 512, (tb + 1) * 512))
        ga, a_ = e16.next(), e32.next()
        P.dma("sp", lambda e: e.dma_start(out=ga[:], in_=gT.t[sl]), gT, ga)
        P.op("dve", lambda e: e.tensor_tensor(a_[:], acc[:], ga[:], ALU.mult), reads=[acc, ga], writes=[a_])
        store(a_, a_[:], AT, AT.t[sl])

    def epi_gate_b(tb, nb, j, acc, ncols):
        r0 = nb * 512 + j * 128
        sl = (slice(r0, r0 + 128), slice(tb * 512, (tb + 1) * 512))
        gb, a_, t_, m = e16.next(), e32.next(), e32.next(), e16.next()
        P.dma("sp", lambda e: e.dma_start(out=gb[:], in_=gT.t[D + r0:D + r0 + 128, sl[1]]), gT, gb)
        P.dma("sp", lambda e: e.dma_start(out=a_[:], in_=AT.t[sl]), AT, a_)
        P.op("dve", lambda e: e.tensor_tensor(t_[:], acc[:], gb[:], ALU.mult), reads=[acc, gb], writes=[t_])
        P.op("dve", lambda e: e.tensor_tensor(m[:], t_[:], a_[:], ALU.add), reads=[t_, a_], writes=[m])
        store(m, m[:], mT, mT.t[sl])

    gemm(o_aT, 0, TOWN, I.w_attn_up, 0, D, 1024, "A", epi_gate_a)
    gemm(o_bT, 0, TOWN, I.w_dn_up, 0, D, 2048, "A", epi_gate_b)

    def epi_resid(src, src_r0, dst):
        def epi(tb, nb, j, acc, ncols):
            r0 = tb * 512 + j * 128
            xt = e32.next()
            P.dma("sp", lambda e: e.dma_start(out=xt[:, 0:ncols], in_=src.t[src_r0 + r0:src_r0 + r0 + 128, nb * 512:nb * 512 + ncols]), src, xt)
            P.op("dve", lambda e: e.tensor_tensor(xt[:, 0:ncols], xt[:, 0:ncols], acc[:, 0:ncols], ALU.add), reads=[xt, acc], writes=[xt])
            store(xt, xt[:, 0:ncols], dst, dst.t[r0:r0 + 128, nb * 512:nb * 512 + ncols])
        return epi

    x1 = dram("x1", [TOWN, D], F32)
    gemm(mT, 0, TOWN, I.w_out, 0, D, D, "B", epi_resid(xrB, T0, x1))
    h2T = dram("h2T", [D, TOWN], BF16)
    norm_T(x1, 0, TOWN, I.norm_mlp, h2T)
    hidT = dram("hidT", [DFF, TOWN], BF16)

    def epi_relu2(tb, nb, j, acc, ncols):
        t, sb = e32.next(), e16.next()
        P.op("act", lambda e: e.activation(out=t[:], in_=acc[:], func=AF.Relu), reads=[acc], writes=[t])
        P.op("dve", lambda e: e.tensor_tensor(sb[:], t[:], t[:], ALU.mult), reads=[t], writes=[sb])
        r0 = nb * 512 + j * 128
        store(sb, sb[:], hidT, hidT.t[r0:r0 + 128, tb * 512:(tb + 1) * 512])

    gemm(h2T, 0, TOWN, I.w_mlp_up, 0, DFF, D, "A", epi_relu2)
    x2 = dram("x2", [TOWN, D], F32)
    gemm(hidT, 0, TOWN, I.w_mlp_down, 0, D, DFF, "B", epi_resid(x1, 0, x2))
    h3T = dram("h3T", [D, TOWN], BF16)
    norm_T(x2, 0, TOWN, I.norm_ple, h3T)
    gate = dram("gate", [TOWN, D], F32)
    proj = dram("proj", [TOWN, D], F32)
    gemm(h3T, 0, TOWN, I.w_ple_gate, 0, D, D, "B", C.epi_tm(gate, AF.Sigmoid))
    pT = dram("pT", [256, TOWN], BF16)
    ppB = P.ext(I.pp)
    for tb in range(TOWN // 512):
        pTs = e16.next(), e16.next()
        for tt in range(4):
            pt_in = e32.next()
            r0 = tb * 512 + tt * 128
            P.dma("sp", lambda e, pt_in=pt_in, r0=r0: e.dma_start(out=pt_in[:, 0:256], in_=I.pp[r0:r0 + 128, :]), ppB, pt_in)
            ps = C.ps.next()
            for q in range(2):
                P.op("pe", lambda e, ps=ps, pt_in=pt_in, q=q: e.transpose(ps[:, q * 128:(q + 1) * 128], pt_in[:, q * 128:(q + 1) * 128], C.ident[:]),
                     reads=[pt_in, C.ident], writes=[ps])
            for q in range(2):
                P.op("dve", lambda e, ps=ps, q=q, tt=tt, pTs=pTs: e.tensor_copy(pTs[q][:, tt * 128:(tt + 1) * 128], ps[:, q * 128:(q + 1) * 128]),
                     reads=[ps], writes=[pTs[q]])
        for q in range(2):
            store(pTs[q], pTs[q][:], pT, pT.t[q * 128:(q + 1) * 128, tb * 512:(tb + 1) * 512])
    gemm(pT, 0, TOWN, I.w_ple_proj, 0, D, 256, "B", C.epi_tm(proj, AF.Copy))

    gpp, gfn = C.big[0], C.big[1]
    P.dma("sp", lambda e: e.dma_start(out=gpp[:], in_=bc_ap(I.ple_post, 128)), P.ext(I.ple_post), gpp)
    P.dma("sp", lambda e: e.dma_start(out=gfn[:], in_=bc_ap(I.final_norm, 128)), P.ext(I.final_norm), gfn)
    big = Ring(C.big[2:5])
    junk = C.junk
    outB = P.ext(out)
    for tt in range(TOWN // 128):
        rs = slice(tt * 128, (tt + 1) * 128)
        pr, gt, x2t = big.next(), big.next(), big.next()
        P.dma("sp", lambda e, pr=pr, rs=rs: e.dma_start(out=pr[:], in_=proj.t[rs, :]), proj, pr)
        P.dma("sp", lambda e, gt=gt, rs=rs: e.dma_start(out=gt[:], in_=gate.t[rs, :]), gate, gt)
        P.dma("sp", lambda e, x2t=x2t, rs=rs: e.dma_start(out=x2t[:], in_=x2.t[rs, :]), x2, x2t)
        ss = C.ssr.next()
        P.op("act", lambda e, pr=pr, ss=ss: e.activation(out=junk[:], in_=pr[:], func=AF.Square, accum_out=ss[:, 0:1]), reads=[pr], writes=[junk, ss])
        C.rstd_op(ss, 0, 1, D)
        P.op("dve", lambda e, pr=pr, ss=ss: e.scalar_tensor_tensor(pr[:], pr[:], ss[:, 1:2], gpp[:], ALU.mult, ALU.mult), reads=[pr, ss, gpp], writes=[pr])
        P.op("dve", lambda e, pr=pr, gt=gt: e.tensor_tensor(pr[:], pr[:], gt[:], ALU.mult), reads=[pr, gt], writes=[pr])
        P.op("dve", lambda e, pr=pr, x2t=x2t: e.tensor_tensor(x2t[:], x2t[:], pr[:], ALU.add), reads=[pr, x2t], writes=[x2t])
        P.op("act", lambda e, x2t=x2t, ss=ss: e.activation(out=junk[:], in_=x2t[:], func=AF.Square, accum_out=ss[:, 2:3]), reads=[x2t], writes=[junk, ss])
        C.rstd_op(ss, 2, 3, D)
        P.op("dve", lambda e, x2t=x2t, ss=ss, gt=gt: e.scalar_tensor_tensor(gt[:], x2t[:], ss[:, 3:4], gfn[:], ALU.mult, ALU.mult), reads=[x2t, ss, gfn], writes=[gt])
        store(gt, gt[:], outB, out[rs, :])
    fin = [outB]
    for name in DUMP:
        src = C.dr[name]
        o = nc.dram_tensor("dump_" + name, list(src.t.shape), src.t.dtype, kind="ExternalOutput").ap()
        ob_ = P.ext(o)
        P.dma("sp", lambda e, o=o, src=src: e.dma_start(out=o, in_=src.t), src, ob_)
        fin.append(ob_)
    P.wait_all("sp", fin)
    P.end_phase()
    P.emit()
    return nc


def host_consts(c):
    tiles, ktab = attn_tiles()
    m = {"ident": np.eye(128, dtype=np.float32)}
    j = np.arange(128)[:, None]
    i = np.arange(128)[None, :]
    m["amask"] = np.concatenate([(j >= i), (j <= i)], 1).astype(np.float32)
    vcol = np.zeros((128, len(ktab)), np.float32)
    first_valid = TA - (c + 1) * TOWN
    for (g, r, s0, n), col in ktab.items():
        d = (1, 4, 16)[g]
        u = r + d * (s0 + np.arange(n))
        vcol[:n, col] = (u >= first_valid)
    m["vcol"] = vcol
    dnc = np.zeros((128, 6 * 512), np.float32)
    a = np.arange(64)
    I64 = np.eye(64, dtype=np.float32)
    low_incl = (a[:, None] >= a[None, :]).astype(np.float32)
    low_strict = (a[:, None] > a[None, :]).astype(np.float32)
    dnc[:64, 0:512] = np.tile(I64, (1, 8))
    dnc[:64, 512:1024] = np.tile(low_strict, (1, 8))
    dnc[:64, 1024:1536] = np.tile(low_strict.T, (1, 8))
    dnc[:64, 1536:2048] = np.tile((1 - low_incl) * -30000.0, (1, 8))
    dnc[:64, 2048:2560] = np.tile((1 - low_incl.T) * -30000.0, (1, 8))
    dnc[:64, 2560:2624] = (a[:, None] <= a[None, :])
    dnc[:, 2624:2752] = 1.0
    dnc[:, 2752:2880] = -1.0
    dnc[:64, 2880:2944] = -I64
    m["dnc"] = dnc
    return m


def kernel(**inputs):
    f = lambda k: np.ascontiguousarray(np.asarray(inputs[k], np.float32)[0])
    x = f("x")
    p = np.asarray(inputs["p"], np.float32)[0, 0]
    shared = {k: f(k) for k in ("w_in", "w_attn_up", "w_dn_up", "w_out", "w_mlp_up", "w_mlp_down", "w_ple_gate",
                                "w_ple_proj", "norm_mix", "norm_mlp", "norm_ple", "ple_post_norm", "dn_a_log", "dn_dt_bias", "dn_norm")}
    shared["final_norm"] = np.ascontiguousarray(np.asarray(inputs["final_norm"], np.float32))
    shared["conv_wT"] = np.ascontiguousarray(f("conv_w").T)
    in_maps = []
    for c in range(NCORES):
        m = dict(shared)
        m.update(host_consts(c))
        xr = np.zeros((S, D), np.float32)
        xr[S - (c + 1) * TOWN:] = x[:(c + 1) * TOWN]
        m["xr"] = xr
        m["pp"] = np.ascontiguousarray(p[c * TOWN:(c + 1) * TOWN])
        in_maps.append(m)
    nc = build_all()
    res = run_bass_kernel_spmd(nc, in_maps, core_ids=list(range(NCORES)), **({"trace": True} if TRACE else {}))
    kernel.last = res
    return np.concatenate([r["out"] for r in res.results], 0)[None].astype(np.float32)
```
